# Optimizing a Trainium2 kernel written in Bass

```python
import math
import jax
import jax.numpy as jnp
from jax import lax
import numpy as np

D_MODEL = 1024
BATCH = 4
SEQ = 8192
DEPTH = 1
DEC_BATCH = 32
DEC_SEQ = 8
PAST_LEN = 16384
PAGE_SIZE = 128

N_Q_HEADS = D_MODEL // 128
N_KV_HEADS = max(1, N_Q_HEADS // 4)
GROUP = N_Q_HEADS // N_KV_HEADS
HEAD_DIM = 64
D_ATTN = N_Q_HEADS * HEAD_DIM
CMP_BLOCK = 32
CMP_STRIDE = 16
CMP_RATIO = CMP_BLOCK // CMP_STRIDE
SEL_BLOCK = 64
N_SEL = 16
WINDOW = 512
Q_BLOCK = 128
N_KV_SLOTS = 6
D_SSM = D_MODEL // 2
SSM_GROUP = 16
N_SSM_GROUPS = D_SSM // SSM_GROUP
SSM_STATE = 64
SSM_CHUNK = 256
DT_MIN = 0.001
DT_MAX = 0.1
RMS_EPS = 1e-6
PROJ_SPLITS = (D_ATTN, N_KV_SLOTS * N_KV_HEADS * HEAD_DIM, 3 * N_Q_HEADS, D_ATTN, D_SSM, D_SSM, 2 * D_MODEL)
D_IN = sum(PROJ_SPLITS)

kernel_name = 'nsa_s5_gated_hybrid_step'


def rmsnorm(x, g):
    xf = x.astype(jnp.float32)
    xf = xf * lax.rsqrt(jnp.mean(xf * xf, axis=-1, keepdims=True) + RMS_EPS)
    return xf.astype(x.dtype) * g


def masked_softmax(s, mask, axis):
    s = jnp.where(mask, s.astype(jnp.float32), -jnp.inf)
    m = jnp.max(s, axis=axis, keepdims=True)
    m = jnp.where(jnp.isfinite(m), m, 0.0)
    e = jnp.exp(s - m)
    return e / jnp.maximum(jnp.sum(e, axis=axis, keepdims=True), 1e-30)


def split_proj(p):
    B, L = p.shape[:2]
    idx = np.cumsum(PROJ_SPLITS)[:-1].tolist()
    q, kv, g_nsa, z_attn, u, z_ssm, g_merge = jnp.split(p, idx, axis=-1)
    q = q.reshape(B, L, N_KV_HEADS, GROUP, HEAD_DIM)
    kv = kv.reshape(B, L, N_KV_SLOTS, N_KV_HEADS, HEAD_DIM)
    g_nsa = jax.nn.sigmoid(g_nsa).reshape(B, L, N_KV_HEADS, GROUP, 3)
    g_merge = jax.nn.sigmoid(g_merge).reshape(B, L, 2, D_MODEL)
    return q, kv, g_nsa, z_attn, u, z_ssm, g_merge


def compress(x, pe, w1, b1, w2):
    B, T = x.shape[:2]
    n_ch = -(-T // CMP_STRIDE)
    x = jnp.pad(x, ((0, 0), (0, n_ch * CMP_STRIDE - T), (0, 0), (0, 0)))
    ch = x.reshape(B, n_ch, CMP_STRIDE, N_KV_HEADS, HEAD_DIM)
    pe = pe.reshape(CMP_RATIO, CMP_STRIDE, 1, HEAD_DIM)
    w1 = w1.reshape(CMP_RATIO, CMP_STRIDE, HEAD_DIM, HEAD_DIM)
    n_cmp = n_ch - CMP_RATIO + 1
    hid = b1
    for r in range(CMP_RATIO):
        hid = hid + jnp.einsum('bnjhd,jde->bnhe', ch[:, r:r + n_cmp] + pe[r], w1[r])
    return jnp.einsum('bnhe,ed->bnhd', jax.nn.gelu(hid), w2)


def cmp_to_sel(n_cmp, n_sel):
    c0 = jnp.arange(n_cmp)[:, None] * CMP_STRIDE
    s0 = jnp.arange(n_sel)[None, :] * SEL_BLOCK
    shared = jnp.minimum(c0 + CMP_BLOCK, s0 + SEL_BLOCK) - jnp.maximum(c0, s0)
    return jnp.clip(shared, 0, None).astype(jnp.float32) / CMP_BLOCK


def nsa_attention(q, gates, full, win, q_pos0, cmp_pe, cmp_w1, cmp_b1, cmp_w2):
    B, L = q.shape[:2]
    T = full.shape[1]
    kc = compress(full[:, :, 0], cmp_pe[0], cmp_w1[0], cmp_b1[0], cmp_w2[0])
    vc = compress(full[:, :, 1], cmp_pe[1], cmp_w1[1], cmp_b1[1], cmp_w2[1])
    n_cmp = kc.shape[1]
    cmp_end = jnp.arange(n_cmp) * CMP_STRIDE + (CMP_BLOCK - 1)
    n_sel = -(-T // SEL_BLOCK)
    sel = jnp.pad(full[:, :, 2:4], ((0, 0), (0, n_sel * SEL_BLOCK - T), (0, 0), (0, 0), (0, 0)))
    sel = sel.reshape(B, n_sel, SEL_BLOCK, 2, N_KV_HEADS, HEAD_DIM)
    ks_blk = jnp.transpose(sel[:, :, :, 0], (0, 3, 1, 2, 4))
    vs_blk = jnp.transpose(sel[:, :, :, 1], (0, 3, 1, 2, 4))
    cmp_sel = cmp_to_sel(n_cmp, n_sel)
    k_top = min(N_SEL, n_sel)
    blk = jnp.arange(n_sel)
    bi = jnp.arange(B)[:, None, None, None]
    hi = jnp.arange(N_KV_HEADS)[None, None, :, None]
    qb = math.gcd(L, Q_BLOCK)
    scale = HEAD_DIM ** -0.5

    def one_block(i):
        q0 = i * qb
        qi = lax.dynamic_slice_in_dim(q, q0, qb, axis=1) * scale
        gi = lax.dynamic_slice_in_dim(gates, q0, qb, axis=1)
        qpos = q_pos0 + q0 + jnp.arange(qb)
        s = jnp.einsum('bqhgd,bnhd->bqhgn', qi, kc)
        p_c = masked_softmax(s, (cmp_end[None, :] <= qpos[:, None])[None, :, None, None, :], -1)
        o_c = jnp.einsum('bqhgn,bnhd->bqhgd', p_c.astype(vc.dtype), vc)
        imp = jnp.einsum('bqhgn,nj->bqhj', p_c, cmp_sel)
        forced = (blk[None, :] == 0) | (blk[None, :] == (qpos // SEL_BLOCK)[:, None])
        causal = blk[None, :] * SEL_BLOCK <= qpos[:, None]
        imp = jnp.where(forced[None, :, None, :], jnp.inf,
                        jnp.where(causal[None, :, None, :], imp, -jnp.inf))
        top_v, top_i = lax.top_k(imp, k_top)
        kg = ks_blk[bi, hi, top_i]
        vg = vs_blk[bi, hi, top_i]
        kpos = top_i[..., None] * SEL_BLOCK + jnp.arange(SEL_BLOCK)
        mask_s = (top_v > -jnp.inf)[..., None] & (kpos <= qpos[None, :, None, None, None])
        s = jnp.einsum('bqhgd,bqhksd->bqhgks', qi, kg)
        p_s = masked_softmax(s, mask_s[:, :, :, None], (-2, -1))
        o_s = jnp.einsum('bqhgks,bqhksd->bqhgd', p_s.astype(vg.dtype), vg)
        wi = lax.dynamic_slice_in_dim(win, q0, WINDOW + qb, axis=1)
        kpos_w = q_pos0 - WINDOW + q0 + jnp.arange(WINDOW + qb)
        dist = qpos[:, None] - kpos_w[None, :]
        mask_w = (kpos_w[None, :] >= 0) & (dist >= 0) & (dist < WINDOW)
        s = jnp.einsum('bqhgd,bkhd->bqhgk', qi, wi[:, :, 0])
        p_w = masked_softmax(s, mask_w[None, :, None, None, :], -1)
        o_w = jnp.einsum('bqhgk,bkhd->bqhgd', p_w.astype(wi.dtype), wi[:, :, 1])
        return gi[..., 0:1] * o_c + gi[..., 1:2] * o_s + gi[..., 2:3] * o_w

    out = lax.map(one_block, jnp.arange(L // qb))
    return jnp.moveaxis(out, 0, 1).reshape(B, L, D_ATTN)


def s5_scan(u, h_re, h_im, lam_re, lam_im, log_dt, b_re, b_im, c_re, c_im):
    f32 = jnp.float32
    u, h_re, h_im = u.astype(f32), h_re.astype(f32), h_im.astype(f32)
    lam_re, lam_im = lam_re.astype(f32), lam_im.astype(f32)
    b_re, b_im, c_re, c_im = b_re.astype(f32), b_im.astype(f32), c_re.astype(f32), c_im.astype(f32)
    dt = jnp.exp(log_dt.astype(f32))[:, None]
    mag = jnp.exp(lam_re * dt)
    a_re = mag * jnp.cos(lam_im * dt)
    a_im = mag * jnp.sin(lam_im * dt)
    den = lam_re * lam_re + lam_im * lam_im
    f_re = ((a_re - 1.0) * lam_re + a_im * lam_im) / den
    f_im = (a_im * lam_re - (a_re - 1.0) * lam_im) / den
    bb_re = f_re[..., None] * b_re - f_im[..., None] * b_im
    bb_im = f_re[..., None] * b_im + f_im[..., None] * b_re
    B, L = u.shape[:2]
    tc = math.gcd(L, SSM_CHUNK)
    uc = jnp.swapaxes(u.reshape(B, L // tc, tc, N_SSM_GROUPS, SSM_GROUP), 0, 1)

    def combine(e1, e2):
        a1r, a1i, b1r, b1i = e1
        a2r, a2i, b2r, b2i = e2
        return (a2r * a1r - a2i * a1i, a2r * a1i + a2i * a1r,
                a2r * b1r - a2i * b1i + b2r, a2r * b1i + a2i * b1r + b2i)

    def step(carry, u_t):
        hr, hi = carry
        bu_re = jnp.einsum('btgc,gpc->btgp', u_t, bb_re)
        bu_im = jnp.einsum('btgc,gpc->btgp', u_t, bb_im)
        ar = jnp.broadcast_to(a_re, bu_re.shape)
        ai = jnp.broadcast_to(a_im, bu_re.shape)
        cr, ci, sr, si = lax.associative_scan(combine, (ar, ai, bu_re, bu_im), axis=1)
        hr_t = sr + cr * hr[:, None] - ci * hi[:, None]
        hi_t = si + cr * hi[:, None] + ci * hr[:, None]
        y = jnp.einsum('btgp,gcp->btgc', hr_t, c_re) - jnp.einsum('btgp,gcp->btgc', hi_t, c_im)
        return (hr_t[:, -1], hi_t[:, -1]), y

    (h_re, h_im), ys = lax.scan(step, (h_re, h_im), uc)
    return jnp.swapaxes(ys, 0, 1).reshape(B, L, N_SSM_GROUPS, SSM_GROUP), h_re, h_im


def mixer(q, full, win, g_nsa, z_attn, u, z_ssm, g_merge, q_pos0, h_re, h_im, lw):
    (cmp_pe, cmp_w1, cmp_b1, cmp_w2, lam_re, lam_im, log_dt, b_re, b_im, c_re, c_im,
     d_skip, w_glu, b_glu, w_lift_attn, w_lift_ssm, w_out) = lw
    B, L = u.shape[:2]
    o_attn = nsa_attention(q, g_nsa, full, win, q_pos0, cmp_pe, cmp_w1, cmp_b1, cmp_w2)
    branch_a = (o_attn * jax.nn.silu(z_attn)) @ w_lift_attn
    y, h_re, h_im = s5_scan(u.reshape(B, L, N_SSM_GROUPS, SSM_GROUP), h_re, h_im,
                            lam_re, lam_im, log_dt, b_re, b_im, c_re, c_im)
    y = y.reshape(B, L, D_SSM).astype(u.dtype) + d_skip * u
    y = jax.nn.gelu(y)
    y = y * jax.nn.sigmoid(y @ w_glu + b_glu)
    branch_b = (y * jax.nn.silu(z_ssm)) @ w_lift_ssm
    merged = g_merge[:, :, 0] * branch_a + g_merge[:, :, 1] * branch_b
    return merged @ w_out, h_re, h_im


def setup_inputs(seed: int = 0) -> dict:
    key = jax.random.key(seed)
    ks = jax.random.split(key, 32)
    f32 = jnp.float32
    n_pages = PAST_LEN // PAGE_SIZE
    n_used = DEC_BATCH * n_pages
    n_phys = n_used + max(1, n_used // 4)
    w_buf = min(WINDOW, PAST_LEN)

    def nrm(k, shape, s):
        return s * jax.random.normal(k, shape, f32)

    page_table = jax.random.permutation(ks[6], n_phys)[:n_used].reshape(DEC_BATCH, n_pages).astype(jnp.int32)
    lam_im = math.pi * jnp.arange(SSM_STATE, dtype=f32) + nrm(ks[13], (DEPTH, N_SSM_GROUPS, SSM_STATE), 0.01)
    return {
        'x_prompt': nrm(ks[0], (BATCH, SEQ, D_MODEL), 1.0),
        'x_sample': nrm(ks[1], (DEC_BATCH, DEC_SEQ, D_MODEL), 1.0),
        'cache_kv': nrm(ks[2], (DEPTH, n_phys, PAGE_SIZE, 4, N_KV_HEADS, HEAD_DIM), 1.0),
        'cache_win_kv': nrm(ks[3], (DEPTH, DEC_BATCH, w_buf, 2, N_KV_HEADS, HEAD_DIM), 1.0),
        'state_ssm_re': nrm(ks[4], (DEPTH, DEC_BATCH, N_SSM_GROUPS, SSM_STATE), 0.1),
        'state_ssm_im': nrm(ks[5], (DEPTH, DEC_BATCH, N_SSM_GROUPS, SSM_STATE), 0.1),
        'page_table': page_table,
        'norm_g': 1.0 + nrm(ks[7], (DEPTH, D_MODEL), 0.01),
        'w_in': nrm(ks[8], (DEPTH, D_MODEL, D_IN), D_MODEL ** -0.5),
        'cmp_pe': nrm(ks[9], (DEPTH, 2, CMP_BLOCK, HEAD_DIM), 0.02),
        'cmp_w1': nrm(ks[10], (DEPTH, 2, CMP_BLOCK, HEAD_DIM, HEAD_DIM), (CMP_BLOCK * HEAD_DIM) ** -0.5),
        'cmp_b1': nrm(ks[11], (DEPTH, 2, HEAD_DIM), 0.01),
        'cmp_w2': nrm(ks[12], (DEPTH, 2, HEAD_DIM, HEAD_DIM), HEAD_DIM ** -0.5),
        'ssm_lam_re': -0.5 + nrm(ks[14], (DEPTH, N_SSM_GROUPS, SSM_STATE), 0.01),
        'ssm_lam_im': lam_im,
        'ssm_log_dt': jax.random.uniform(ks[15], (DEPTH, N_SSM_GROUPS), f32, math.log(DT_MIN), math.log(DT_MAX)),
        'ssm_b_re': nrm(ks[16], (DEPTH, N_SSM_GROUPS, SSM_STATE, SSM_GROUP), (2 * SSM_GROUP) ** -0.5),
        'ssm_b_im': nrm(ks[17], (DEPTH, N_SSM_GROUPS, SSM_STATE, SSM_GROUP), (2 * SSM_GROUP) ** -0.5),
        'ssm_c_re': nrm(ks[18], (DEPTH, N_SSM_GROUPS, SSM_GROUP, SSM_STATE), SSM_STATE ** -0.5),
        'ssm_c_im': nrm(ks[19], (DEPTH, N_SSM_GROUPS, SSM_GROUP, SSM_STATE), SSM_STATE ** -0.5),
        'ssm_d': nrm(ks[20], (DEPTH, D_SSM), 1.0),
        'w_glu': nrm(ks[21], (DEPTH, D_SSM, D_SSM), D_SSM ** -0.5),
        'b_glu': nrm(ks[22], (DEPTH, D_SSM), 0.01),
        'w_lift_attn': nrm(ks[23], (DEPTH, D_ATTN, D_MODEL), D_ATTN ** -0.5),
        'w_lift_ssm': nrm(ks[24], (DEPTH, D_SSM, D_MODEL), D_SSM ** -0.5),
        'w_out': nrm(ks[25], (DEPTH, D_MODEL, D_MODEL), D_MODEL ** -0.5),
        'final_g': 1.0 + nrm(ks[26], (D_MODEL,), 0.01),
    }


def reference(x_prompt, x_sample, cache_kv, cache_win_kv, state_ssm_re, state_ssm_im, page_table,
              norm_g, w_in, cmp_pe, cmp_w1, cmp_b1, cmp_w2, ssm_lam_re, ssm_lam_im, ssm_log_dt,
              ssm_b_re, ssm_b_im, ssm_c_re, ssm_c_im, ssm_d, w_glu, b_glu, w_lift_attn, w_lift_ssm,
              w_out, final_g):
    past_len = page_table.shape[1] * cache_kv.shape[2]
    w_buf = cache_win_kv.shape[2]
    xp, xs = x_prompt, x_sample
    Bp, Lp = xp.shape[:2]
    Bs, Ls = xs.shape[:2]
    kvp, winp, srp, sip = [], [], [], []
    kvs, wins, srs, sis = [], [], [], []
    for l in range(DEPTH):
        lw = (cmp_pe[l], cmp_w1[l], cmp_b1[l], cmp_w2[l], ssm_lam_re[l], ssm_lam_im[l], ssm_log_dt[l],
              ssm_b_re[l], ssm_b_im[l], ssm_c_re[l], ssm_c_im[l], ssm_d[l], w_glu[l], b_glu[l],
              w_lift_attn[l], w_lift_ssm[l], w_out[l])
        q, kv, g_nsa, z_a, u, z_s, g_m = split_proj(rmsnorm(xp, norm_g[l]) @ w_in[l])
        win = jnp.pad(kv[:, :, 4:], ((0, 0), (WINDOW, 0), (0, 0), (0, 0), (0, 0)))
        h0 = jnp.zeros((Bp, N_SSM_GROUPS, SSM_STATE), jnp.float32)
        y, hr, hi = mixer(q, kv[:, :, :4], win, g_nsa, z_a, u, z_s, g_m, 0, h0, h0, lw)
        xp = xp + y
        kvp.append(kv[:, :, :4])
        winp.append(kv[:, Lp - min(WINDOW, Lp):, 4:])
        srp.append(hr)
        sip.append(hi)
        q, kv, g_nsa, z_a, u, z_s, g_m = split_proj(rmsnorm(xs, norm_g[l]) @ w_in[l])
        past = cache_kv[l][page_table].reshape(Bs, past_len, 4, N_KV_HEADS, HEAD_DIM)
        full = jnp.concatenate([past, kv[:, :, :4]], axis=1)
        buf = jnp.concatenate([cache_win_kv[l], kv[:, :, 4:]], axis=1)
        win = jnp.pad(buf, ((0, 0), (WINDOW - w_buf, 0), (0, 0), (0, 0), (0, 0)))
        y, hr, hi = mixer(q, full, win, g_nsa, z_a, u, z_s, g_m, past_len,
                          state_ssm_re[l], state_ssm_im[l], lw)
        xs = xs + y
        kvs.append(kv[:, :, :4])
        n_keep = min(WINDOW, w_buf + Ls)
        wins.append(buf[:, buf.shape[1] - n_keep:])
        srs.append(hr)
        sis.append(hi)
    y_prompt = rmsnorm(xp, final_g)
    y_sample = rmsnorm(xs, final_g)
    kv_prompt = jnp.stack(kvp)
    win_prompt = jnp.stack(winp)
    ssm_re_prompt = jnp.stack(srp)
    ssm_im_prompt = jnp.stack(sip)
    kv_sample = jnp.stack(kvs)
    win_sample = jnp.stack(wins)
    ssm_re_sample = jnp.stack(srs)
    ssm_im_sample = jnp.stack(sis)
    return (y_prompt, y_sample, kv_prompt, win_prompt, ssm_re_prompt, ssm_im_prompt,
            kv_sample, win_sample, ssm_re_sample, ssm_im_sample)
```

```python
import contextlib
import os
import math
import numpy as np
import ml_dtypes
import concourse.bass as bass
import concourse.mybir as mybir
from concourse.bass_utils import run_bass_kernel_spmd

F32 = mybir.dt.float32
BF16 = mybir.dt.bfloat16
I32 = mybir.dt.int32
AF = mybir.ActivationFunctionType
ALU = mybir.AluOpType
AX = mybir.AxisListType

EPOCH = int(os.environ.get("EPOCH", "3000"))
NEG = -30000.0
GC = math.sqrt(2.0 / math.pi)


class Eng:
    def __init__(self, fw, name, raw):
        self.fw, self.name, self.raw = fw, name, raw
        self.sems = []
        self.count = 0
        self.known = {}

    def sem_for(self, epoch):
        while len(self.sems) <= epoch:
            self.sems.append(self.fw.new_sem(f"s_{self.name}_{len(self.sems)}"))
        return self.sems[epoch]


class Buf:
    def __init__(self, ap, name="", excl=False):
        self.ap = ap
        self.name = name
        self.writer = None
        self.readers = {}
        self.excl = excl

    def __getitem__(self, idx):
        return self.ap[idx]


class FW:
    def __init__(self, nc, stack):
        self.nc = nc
        self.stack = stack
        self.semstack = stack
        self.engs = {}
        for n, raw in (("pe", nc.tensor), ("act", nc.scalar), ("dve", nc.vector),
                       ("pool", nc.gpsimd), ("sp", nc.sync)):
            self.engs[n] = Eng(self, n, raw)
        self.dma_slots = []
        self.dma_next = 0
        self.n_dma_slots = 32
        self.ninst = 0

    def new_sem(self, name):
        return self.semstack.enter_context(self.nc.semaphore(name))

    def barrier(self):
        for e in self.engs.values():
            for e2 in self.engs.values():
                if e2 is e or e2.count == 0:
                    continue
                self._need(e, (e2, e2.count))
            for si, slot in enumerate(self.dma_slots):
                if slot[1] > 0:
                    self._need(e, ('dma', slot[0], slot[1], f"dma{si}"))

    def sbuf(self, name, shape, dtype):
        t = self.stack.enter_context(self.nc.sbuf_tensor("sb_" + name, list(shape), dtype))
        return Buf(t, name)

    def psum(self, name, shape, dtype):
        t = self.stack.enter_context(self.nc.psum_tensor("pp_" + name, list(shape), dtype))
        return Buf(t, name, excl=True)

    def _need(self, eng, dep):
        if dep is None:
            return
        if dep[0] == 'dma':
            _, sem, val, key = dep
            if eng.known.get(key, 0) >= val:
                return
            eng.raw.wait_ge(sem, val)
            eng.known[key] = val
            return
        e2, cnt = dep
        if e2 is eng and eng.name == "pe":
            return
        if eng.known.get(e2.name, 0) >= cnt:
            return
        ep = (cnt - 1) // EPOCH
        eng.raw.wait_ge(e2.sem_for(ep), cnt - ep * EPOCH)
        eng.known[e2.name] = cnt

    def _deps(self, eng, reads, writes):
        for b in reads:
            self._need(eng, b.writer)
            if b.excl:
                for r in list(b.readers.values()):
                    self._need(eng, r)
        for b in writes:
            self._need(eng, b.writer)
            for r in list(b.readers.values()):
                self._need(eng, r)

    def op(self, engname, fn, reads=(), writes=()):
        eng = self.engs[engname]
        self._deps(eng, reads, writes)
        inst = fn(eng.raw)
        eng.count += 1
        ep = (eng.count - 1) // EPOCH
        inst.then_inc(eng.sem_for(ep), 1)
        tag = (eng, eng.count)
        for b in reads:
            b.readers[eng.name] = tag
        for b in writes:
            b.writer = tag
            b.readers = {}
        self.ninst += 1
        return inst

    def dma(self, out_ap, in_ap, reads=(), writes=(), q="sp", **kw):
        eng = self.engs[q]
        self._deps(eng, reads, writes)
        if len(self.dma_slots) < self.n_dma_slots:
            self.dma_slots.append([self.new_sem(f"d{len(self.dma_slots)}"), 0])
        si = self.dma_next % self.n_dma_slots
        self.dma_next += 1
        slot = self.dma_slots[si]
        key = f"dma{si}"
        if slot[1] > 0 and eng.known.get(key, 0) < slot[1]:
            eng.raw.wait_ge(slot[0], slot[1])
            eng.known[key] = slot[1]
        slot[1] += 16
        eng.raw.dma_start(out=out_ap, in_=in_ap, **kw).then_inc(slot[0], 16)
        tag = ('dma', slot[0], slot[1], key)
        for b in reads:
            b.readers[key] = tag
        for b in writes:
            b.writer = tag
            b.readers = {}
        self.ninst += 1

    def idma(self, out_ap, rows_ap, idx_ap, reads=(), writes=()):
        eng = self.engs["pool"]
        self._deps(eng, reads, writes)
        if len(self.dma_slots) < self.n_dma_slots:
            self.dma_slots.append([self.new_sem(f"d{len(self.dma_slots)}"), 0])
        si = self.dma_next % self.n_dma_slots
        self.dma_next += 1
        slot = self.dma_slots[si]
        key = f"dma{si}"
        if slot[1] > 0 and eng.known.get(key, 0) < slot[1]:
            eng.raw.wait_ge(slot[0], slot[1])
            eng.known[key] = slot[1]
        slot[1] += 16
        eng.raw.indirect_dma_start(out=out_ap, out_offset=None, in_=rows_ap,
                                   in_offset=bass.IndirectOffsetOnAxis(ap=idx_ap, axis=0)).then_inc(slot[0], 16)
        tag = ('dma', slot[0], slot[1], key)
        for b in reads:
            b.readers[key] = tag
        for b in writes:
            b.writer = tag
            b.readers = {}
        self.ninst += 1

    def finish(self):
        eng = self.engs["sp"]
        for slot in self.dma_slots:
            if slot[1] > 0:
                eng.raw.wait_ge(slot[0], slot[1])


class Ring:
    def __init__(self, bufs):
        self.bufs = bufs
        self.i = 0

    def get(self):
        b = self.bufs[self.i % len(self.bufs)]
        self.i += 1
        return b


D = 1024
NH, NKV, G, DH = 8, 2, 4, 64
D_ATTN = 512
D_SSM = 512
NST = 16
O_Q, O_KV, O_GN, O_ZA, O_U, O_ZS, O_GM = 0, 512, 1280, 1304, 1816, 2328, 2840
NWF = 1536
NWT = 3352
T_GROUPS = [(0, 512), (512, 792), (792, 1304), (1304, 1816), (1816, 2328), (2328, 2840), (2840, 3352)]


def wf_cols():
    q = [h * 256 + g * 64 + d for g in range(4) for h in range(2) for d in range(64)]
    return np.array(q + list(range(O_U, O_U + 512)) + list(range(O_ZS, O_ZS + 512)))


def wt_cols():
    return np.array(list(range(O_KV, O_KV + 768)) + list(range(O_GN, O_GN + 24)) +
                    list(range(O_ZA, O_ZA + 512)) + list(range(O_GM, O_GM + 2048)))


import os
DBG = int(os.environ.get("DBG_STAGE", "99"))
SKIP = os.environ.get("DBG_SKIP", "")


def build(NBP, NBO, NSB=4, NPHYS=5120):
    NBT = NBP + NBO
    TP, TO = NBP * 128, NBO * 128
    NWIN = min(4, NBO)
    nc = bass.Bass("TRN2", target_bir_lowering=False)

    def din(name, shape, dt=F32):
        return nc.dram_tensor(name, list(shape), dt, kind="ExternalInput").ap()

    def dout(name, shape, dt=F32):
        return nc.dram_tensor(name, list(shape), dt, kind="ExternalOutput").ap()

    def dscr(name, shape, dt):
        return nc.dram_tensor(name, list(shape), dt, kind="Internal").ap()

    xo = din("xo", [TO, D])
    xp = din("xp", [TP, D])
    wf_d = din("wf", [128, 8, NWF])
    wt_d = din("wt", [128, 8, NWT])
    ng_d = din("ng", [128, 8])
    fg_d = din("fg", [128, D])
    w1_d = din("w1", [128, 2, 32, 64])
    w2_d = din("w2", [128, 2, 64])
    pe_d = din("pe", [128, 2, 32])
    b1_d = din("b1", [128, 2])
    cmpsel_d = din("cmpsel", [128, 4, 128], BF16)
    cmpb_d = din("cmpb", [NBO, 4, 128, 128], BF16)
    winb_d = din("winb", [NBO, 5, 128, 128], BF16)
    tka_d = din("tka", [NBO, 128, 128])
    tkb_d = din("tkb", [NBO, 128, 128])
    caus_d = din("caus", [128, 128], BF16)
    EW = max(NBT, 64 if NSB else 0) * 128
    E_d = din("E", [128, EW], BF16)
    lamre_d = din("lamre", [128, 16])
    lamim_d = din("lamim", [128, 16])
    logdt_d = din("logdt", [128, 16])
    bre_d = din("bre", [128, 16, 16])
    bim_d = din("bim", [128, 16, 16])
    cre_d = din("cre", [128, 16, 128])
    cim_d = din("cim", [128, 16, 128])
    mgh_d = din("mgh", [128, 2])
    dsk_d = din("dsk", [128, 4])
    bgl_d = din("bgl", [128, 4])
    wglu_d = din("wglu", [128, 4, 512])
    wla_d = din("wla", [128, 4, D])
    wls_d = din("wls", [128, 4, D])
    wo_d = din("wo", [128, 8, D])

    y_o = dout("y", [TO, D])
    kv_o = dout("kvp", [TO, 512])
    win_o = dout("winp", [NWIN * 128, 256])
    ssm_o = dout("ssmp", [128, 2, 16])

    sc_q = dscr("sc_q", [NBO, 128, 512], BF16)
    sc_u = dscr("sc_u", [NBT, 128, 512], BF16)
    sc_zs = dscr("sc_zs", [NBO, 128, 512], BF16)
    sc_za = dscr("sc_za", [NBO, 128, 512], BF16)
    sc_gm = dscr("sc_gm", [NBO, 128, 2048], BF16)
    sc_gn = dscr("sc_gn", [NBO, 128, 24], F32)
    NS1 = max(NSB, 1)
    xs_d = din("xs", [NS1, 128, D])
    cache_d = din("cache", [NPHYS * 256, 256])
    pt_d = din("ptab", [NS1, 128, 128], I32)
    cwin_d = din("cwin", [NS1, 512, 256])
    sst_d = din("sst", [NS1, 128, 2, 16])
    cmpsel_s_d = din("cmpsel_s", [128, 8, 256], BF16)
    cb0_d = din("cb0", [128, 128], BF16)
    swb0_d = din("swb0", [128, 128], BF16)
    ys_o = dout("ys", [NS1, 8, D])
    kvs_o = dout("kvs", [NS1, 8, 512])
    wins_o = dout("wins", [NS1, 512, 256])
    ssms_o = dout("ssms", [NS1, 128, 2, 16])
    sc_q_s = dscr("sc_q_s", [NS1, 128, 512], BF16)
    sc_u_s = dscr("sc_u_s", [NS1, 128, 512], BF16)
    sc_zs_s = dscr("sc_zs_s", [NS1, 128, 512], BF16)
    sc_za_s = dscr("sc_za_s", [NS1, 128, 512], BF16)
    sc_gm_s = dscr("sc_gm_s", [NS1, 128, 2048], BF16)
    sc_gn_s = dscr("sc_gn_s", [NS1, 128, 24], F32)
    sc_nk = dscr("sc_nk", [NS1, 128, 256], BF16)
    sc_nv = dscr("sc_nv", [NS1, 128, 260], BF16)
    sc_kc = dscr("sc_kc", [NS1, 128, 1024], BF16)
    sc_vcT = dscr("sc_vcT", [NS1, 128, 1024], BF16)
    sc_ys_s = dscr("sc_ys_s", [NS1, 128, 512], BF16)
    sc_wk = dscr("sc_wk", [NBT, 128, 128], BF16)
    sc_wv = dscr("sc_wv", [NBT, 128, 130], BF16)

    with contextlib.ExitStack() as top:
        fw = FW(nc, top)
        op, dma = fw.op, fw.dma

        identb = fw.sbuf("identb", [128, 128], BF16)
        identf = fw.sbuf("identf", [128, 128], F32)
        ps_bufs = [fw.psum(f"ps{i}", [128, 512], F32) for i in range(8)]
        pstore = contextlib.ExitStack()
        pstore.__enter__()
        fw.stack = pstore
        selKT = fw.sbuf("selKT", [128, NBT * 128], BF16)
        selV = fw.sbuf("selV", [128, NBT, 2, 65], BF16)
        winKT = fw.sbuf("winKT", [128, 8 * 128], BF16)
        winV = fw.sbuf("winV", [128, 8, 2, 65], BF16)
        kcT = fw.sbuf("kcT", [128, 512], BF16)
        vcT = fw.sbuf("vcT", [128, 512], BF16)
        vc = fw.sbuf("vc", [128, 4, 2, 65], BF16)
        cmpsel = fw.sbuf("cmpsel", [128, 4, 128], BF16)
        ring = Ring(ps_bufs[4:])
        accO = Ring(ps_bufs[0:2])
        accI = ps_bufs[2]
        ypsb = ps_bufs[3]

        def bfview(b):
            return b.ap[:].bitcast(BF16)

        op("pool", lambda e: e.memset(identf[:], 1.0), writes=[identf])
        op("pool", lambda e: e.affine_select(out=identf[:], in_=identf[:], pattern=[[-1, 128]],
                                             compare_op=ALU.is_equal, fill=0.0, base=0,
                                             channel_multiplier=1), reads=[identf], writes=[identf])
        op("dve", lambda e: e.tensor_copy(out=identb[:], in_=identf[:]), reads=[identf], writes=[identb])
        op("pool", lambda e: e.memset(selV[:], 1.0), writes=[selV])
        op("pool", lambda e: e.memset(winV[:], 1.0), writes=[winV])
        op("pool", lambda e: e.memset(vc[:], 1.0), writes=[vc])
        op("pool", lambda e: e.memset(kcT[:], 0.0), writes=[kcT])
        op("pool", lambda e: e.memset(vcT[:], 0.0), writes=[vcT])
        op("pool", lambda e: e.memset(winKT[:], 0.0), writes=[winKT])
        dma(cmpsel[:], cmpsel_d, writes=[cmpsel])

        with contextlib.ExitStack() as p1:
            fw.stack = p1
            WF = fw.sbuf("WF", [128, 8, NWF], BF16)
            WT = fw.sbuf("WT", [128, 8, NWT], BF16)
            ngt = fw.sbuf("ngt", [128, 8], F32)
            W1r = fw.sbuf("W1r", [128, 2, 32, 64], BF16)
            W2r = fw.sbuf("W2r", [128, 2, 64], BF16)
            peT = fw.sbuf("peT", [128, 2, 32], BF16)
            b1p = fw.sbuf("b1p", [128, 2], F32)
            cmpraw = fw.sbuf("cmpraw", [128, 2, 144], BF16)
            stg = Ring([fw.sbuf(f"stg{i}", [128, 1024], F32) for i in range(2)])
            dma(ngt[:], ng_d, writes=[ngt])
            k = 0
            for (wd, W, ncol) in ((wf_d, WF, NWF), (wt_d, WT, NWT)):
                for c in range(8):
                    for c0 in range(0, ncol, 1024):
                        c1 = min(ncol, c0 + 1024)
                        s = stg.get()
                        dma(s[:, 0:c1 - c0], wd[:, c, c0:c1], writes=[s])
                        en = "dve" if k % 2 == 0 else "pool"
                        k += 1
                        op(en, lambda e: e.tensor_scalar(out=W[:, c, c0:c1], in0=s[:, 0:c1 - c0],
                                                         scalar1=ngt[:, c:c + 1], scalar2=None, op0=ALU.mult),
                           reads=[s, ngt], writes=[W])
            for kv in range(2):
                for jh in range(2):
                    s = stg.get()
                    dma(s[:, 0:1024].rearrange("p (j e) -> p j e", e=64), w1_d[:, kv, jh * 16:(jh + 1) * 16], writes=[s])
                    op("dve", lambda e: e.tensor_copy(out=W1r[:, kv, jh * 16:(jh + 1) * 16],
                                                      in_=s[:, 0:1024].rearrange("p (j e) -> p j e", e=64)),
                       reads=[s], writes=[W1r])
            s = stg.get()
            dma(s[:, 0:128].rearrange("p (k e) -> p k e", e=64), w2_d, writes=[s])
            op("dve", lambda e: e.tensor_copy(out=W2r[:], in_=s[:, 0:128].rearrange("p (k e) -> p k e", e=64)),
               reads=[s], writes=[W2r])
            s = stg.get()
            dma(s[:, 0:64].rearrange("p (k e) -> p k e", e=32), pe_d, writes=[s])
            op("dve", lambda e: e.tensor_copy(out=peT[:], in_=s[:, 0:64].rearrange("p (k e) -> p k e", e=32)),
               reads=[s], writes=[peT])
            dma(b1p[:], b1_d, writes=[b1p])
            op("pool", lambda e: e.memset(cmpraw[:], 0.0), writes=[cmpraw])
            pb = ring.get()
            for kv in range(2 if DBG >= 2 else 0):
                for h in range(2):
                    rows = slice(h * 64, (h + 1) * 64)
                    for j in range(32):
                        op("pe", lambda e: e.matmul(pb[rows, kv:kv + 1], lhsT=W1r[rows, kv, j, :],
                                                    rhs=peT[rows, kv, j:j + 1], start=(j == 0), stop=(j == 31)),
                           reads=[W1r, peT], writes=[pb])
            if DBG >= 2:
                op("dve", lambda e: e.tensor_tensor(out=b1p[:], in0=pb[:, 0:2], in1=b1p[:], op=ALU.add),
                   reads=[pb, b1p], writes=[b1p])

            xr = Ring([fw.sbuf(f"xt{i}", [128, D], F32) for i in range(2)])
            junk = fw.sbuf("junk", [128, D], BF16)
            ssr = Ring([fw.sbuf(f"ss{i}", [128, 1], F32) for i in range(2)])
            xbr = Ring([fw.sbuf(f"xb{i}", [128, D], BF16) for i in range(2)])
            xTr = Ring([fw.sbuf(f"xT{i}", [128, 8, 128], BF16) for i in range(2)])
            kvor = Ring([fw.sbuf(f"kvo{i}", [128, 512], F32) for i in range(2)])
            kv1r = Ring([fw.sbuf(f"kv1f{i}", [128, 280], F32) for i in range(2)])
            kvbr = Ring([fw.sbuf(f"kvb{i}", [128, 768], BF16) for i in range(2)])
            zar = Ring([fw.sbuf(f"za{i}", [128, 512], BF16) for i in range(2)])
            gmr = Ring([fw.sbuf(f"gmb{i}", [128, 2048], BF16) for i in range(2)])
            f4r = Ring([fw.sbuf(f"f4{i}", [128, 4, 128], BF16) for i in range(3)])
            hidr = Ring([fw.sbuf(f"hid{i}", [128, 4, 128], F32) for i in range(2)])
            gTr = Ring([fw.sbuf(f"gT{i}", [128, 2, 64], BF16) for i in range(2)])
            tmpK = fw.sbuf("tmpK", [128, 2, 128], BF16)
            tmpV = fw.sbuf("tmpV", [128, 2, 2, 65], BF16)
            op("pool", lambda e: e.memset(tmpV[:], 1.0), writes=[tmpV])

            def compress(raw, nb, dst_k, dst_v, col0):
                W = 2 * nb
                hp = ring.get()
                hv = hp.ap[:, 0:W].rearrange("p (k m) -> p k m", m=nb)
                cr = raw.ap[:].rearrange("p k (m s) -> p k m s", s=16)
                for kv in range(2):
                    for h in range(2):
                        rows = slice(h * 64, (h + 1) * 64)
                        for j in range(32):
                            rhs = cr[rows, kv, 0:nb, j] if j < 16 else cr[rows, kv, 1:nb + 1, j - 16]
                            op("pe", lambda e: e.matmul(hv[rows, kv, :], lhsT=W1r[rows, kv, j, :], rhs=rhs,
                                                        start=(j == 0), stop=(j == 31)),
                               reads=[W1r, raw], writes=[hp])
                hd = hidr.get()
                op("dve", lambda e: e.tensor_tensor(out=hd[:, 0, 0:W].rearrange("p (k m) -> p k m", m=nb), in0=hv,
                                                    in1=b1p[:].unsqueeze(2).to_broadcast([128, 2, nb]),
                                                    op=ALU.add), reads=[hp, b1p], writes=[hd])
                op("dve", lambda e: e.tensor_tensor(out=hd[:, 1, 0:W], in0=hd[:, 0, 0:W], in1=hd[:, 0, 0:W], op=ALU.mult),
                   reads=[hd], writes=[hd])
                op("dve", lambda e: e.tensor_scalar(out=hd[:, 1, 0:W], in0=hd[:, 1, 0:W], scalar1=0.044715, scalar2=1.0,
                                                    op0=ALU.mult, op1=ALU.add), reads=[hd], writes=[hd])
                op("dve", lambda e: e.tensor_tensor(out=hd[:, 2, 0:W], in0=hd[:, 1, 0:W], in1=hd[:, 0, 0:W], op=ALU.mult),
                   reads=[hd], writes=[hd])
                op("act", lambda e: e.activation(out=hd[:, 3, 0:W], in_=hd[:, 2, 0:W], func=AF.Exp, scale=-2.0 * GC),
                   reads=[hd], writes=[hd])
                op("dve", lambda e: e.tensor_scalar(out=hd[:, 3, 0:W], in0=hd[:, 3, 0:W], scalar1=1.0, scalar2=None,
                                                    op0=ALU.add), reads=[hd], writes=[hd])
                op("dve", lambda e: e.reciprocal(out=hd[:, 3, 0:W], in_=hd[:, 3, 0:W]), reads=[hd], writes=[hd])
                gT = gTr.get()
                op("dve", lambda e: e.tensor_tensor(out=gT[:, :, 0:nb], in0=hd[:, 0, 0:W].rearrange("p (k m) -> p k m", m=nb),
                                                    in1=hd[:, 3, 0:W].rearrange("p (k m) -> p k m", m=nb), op=ALU.mult),
                   reads=[hd], writes=[gT])
                kp = ring.get()
                kpv = kp.ap[:, 0:W].rearrange("p (k m) -> p k m", m=nb)
                for kv in range(2):
                    for h in range(2):
                        rows = slice(h * 64, (h + 1) * 64)
                        op("pe", lambda e: e.matmul(kpv[rows, kv, :], lhsT=W2r[rows, kv, :], rhs=gT[rows, kv, 0:nb],
                                                    start=True, stop=True), reads=[W2r, gT], writes=[kp])
                op("act", lambda e: e.copy(out=dst_k[:, col0:col0 + nb], in_=kpv[:, 0, :]), reads=[kp], writes=[dst_k])
                op("act", lambda e: e.copy(out=dst_v[:, col0:col0 + nb], in_=kpv[:, 1, :]), reads=[kp], writes=[dst_v])

            def front(bg, samp=None):
                sm = samp is not None
                own = sm or bg >= NBP
                i = samp if sm else bg - NBP
                d_gn, d_za, d_gm, d_q, d_zs = ((sc_gn_s, sc_za_s, sc_gm_s, sc_q_s, sc_zs_s) if sm
                                               else (sc_gn, sc_za, sc_gm, sc_q, sc_zs))
                xt = xr.get()
                if sm:
                    src = xs_d[samp]
                else:
                    src = xo[i * 128:(i + 1) * 128, :] if own else xp[bg * 128:(bg + 1) * 128, :]
                dma(xt[:], src, writes=[xt])
                ss = ssr.get()
                op("act", lambda e: e.activation(out=junk[:], in_=xt[:], func=AF.Square, accum_out=ss[:]),
                   reads=[xt], writes=[junk, ss])
                op("act", lambda e: e.activation(out=ss[:], in_=ss[:], func=AF.Ln, scale=1.0 / D, bias=1e-6),
                   reads=[ss], writes=[ss])
                op("act", lambda e: e.activation(out=ss[:], in_=ss[:], func=AF.Exp, scale=-0.5),
                   reads=[ss], writes=[ss])
                xb = xbr.get()
                op("dve", lambda e: e.tensor_scalar(out=xb[:], in0=xt[:], scalar1=ss[:, 0:1], scalar2=None,
                                                    op0=ALU.mult), reads=[xt, ss], writes=[xb])
                pt = ring.get()
                ptv = bfview(pt).rearrange("p (a b) -> p a b", b=128)
                for c in range(8):
                    op("pe", lambda e: e.transpose(out=ptv[:, c, :], in_=xb[:, c * 128:(c + 1) * 128],
                                                   identity=identb[:]), reads=[xb, identb], writes=[pt])
                xT = xTr.get()
                op("act", lambda e: e.copy(out=xT[:], in_=ptv), reads=[pt], writes=[xT])

                def tproj(gi):
                    c0, c1 = T_GROUPS[gi]
                    ps = ring.get()
                    for c in range(8):
                        op("pe", lambda e: e.matmul(ps[:, 0:c1 - c0], lhsT=xT[:, c, :], rhs=WT[:, c, c0:c1],
                                                    start=(c == 0), stop=(c == 7)), reads=[xT, WT], writes=[ps])
                    return ps

                kvb = kvbr.get()
                ps = tproj(0)
                if own:
                    kvo = kvor.get()
                    op("dve", lambda e: e.tensor_copy(out=kvo[:], in_=ps[:, :]), reads=[ps], writes=[kvo])
                    if sm:
                        dma(kvs_o[samp], kvo[0:8, :], reads=[kvo])
                    else:
                        dma(kv_o[i * 128:(i + 1) * 128, :], kvo[:], reads=[kvo])
                op("act", lambda e: e.copy(out=kvb[:, 0:512], in_=ps[:, :]), reads=[ps], writes=[kvb])
                ps = tproj(1)
                op("act", lambda e: e.copy(out=kvb[:, 512:768], in_=ps[:, 0:256]), reads=[ps], writes=[kvb])
                if own:
                    kv1 = kv1r.get()
                    op("dve", lambda e: e.tensor_copy(out=kv1[:], in_=ps[:, 0:280]), reads=[ps], writes=[kv1])
                    dma(d_gn[i], kv1[:, 256:280], reads=[kv1])
                    if sm:
                        dma(wins_o[samp, 504:512, :], kv1[0:8, 0:256], reads=[kv1])
                        dma(wins_o[samp, 0:504, :], cwin_d[samp, 8:512, :])
                    elif i >= NBO - NWIN:
                        w = i - (NBO - NWIN)
                        dma(win_o[w * 128:(w + 1) * 128, :], kv1[:, 0:256], reads=[kv1])
                    ps = tproj(2)
                    za = zar.get()
                    op("act", lambda e: e.copy(out=za[:], in_=ps[:, :]), reads=[ps], writes=[za])
                    dma(d_za[i], za[:], reads=[za])
                    gmb = gmr.get()
                    for q4 in range(4):
                        ps = tproj(3 + q4)
                        if q4 % 2 == 0:
                            op("dve", lambda e: e.tensor_copy(out=gmb[:, q4 * 512:(q4 + 1) * 512], in_=ps[:, :]),
                               reads=[ps], writes=[gmb])
                        else:
                            op("act", lambda e: e.copy(out=gmb[:, q4 * 512:(q4 + 1) * 512], in_=ps[:, :]),
                               reads=[ps], writes=[gmb])
                    dma(d_gm[i], gmb[:], reads=[gmb])
                pt2 = ring.get()
                p2v = bfview(pt2).rearrange("p (a b) -> p a b", b=128)
                for kk, c0 in enumerate((0, 128, 256, 512)):
                    op("pe", lambda e: e.transpose(out=p2v[:, kk, :], in_=kvb[:, c0:c0 + 128], identity=identb[:]),
                       reads=[kvb, identb], writes=[pt2])
                if sm:
                    op("act", lambda e: e.copy(out=tmpK[:], in_=p2v[:, 2:4, :]), reads=[pt2], writes=[tmpK])
                    op("pool", lambda e: e.tensor_copy(out=tmpV[:, 0, :, 0:64],
                                                       in_=kvb[:, 384:512].rearrange("p (h d) -> p h d", d=64)),
                       reads=[kvb], writes=[tmpV])
                    op("pool", lambda e: e.tensor_copy(out=tmpV[:, 1, :, 0:64],
                                                       in_=kvb[:, 640:768].rearrange("p (h d) -> p h d", d=64)),
                       reads=[kvb], writes=[tmpV])
                    dma(sc_nk[samp], tmpK[:].rearrange("p a b -> p (a b)"), reads=[tmpK])
                    dma(sc_nv[samp], tmpV[:].rearrange("p a h c -> p (a h c)"), reads=[tmpV])
                else:
                    op("pool", lambda e: e.tensor_copy(out=cmpraw[:, :, 0:16], in_=cmpraw[:, :, 128:144]),
                       reads=[cmpraw], writes=[cmpraw])
                    op("dve", lambda e: e.tensor_copy(out=cmpraw[:, :, 16:144], in_=p2v[:, 0:2, :]),
                       reads=[pt2, cmpraw], writes=[cmpraw])
                    op("act", lambda e: e.copy(out=selKT[:, bg * 128:(bg + 1) * 128], in_=p2v[:, 2, :]),
                       reads=[pt2], writes=[selKT])
                    wslot = bg % 8
                    op("act", lambda e: e.copy(out=winKT[:, wslot * 128:(wslot + 1) * 128], in_=p2v[:, 3, :]),
                       reads=[pt2], writes=[winKT])
                    op("pool", lambda e: e.tensor_copy(out=selV[:, bg, :, 0:64],
                                                       in_=kvb[:, 384:512].rearrange("p (h d) -> p h d", d=64)),
                       reads=[kvb], writes=[selV])
                    op("pool", lambda e: e.tensor_copy(out=winV[:, wslot, :, 0:64],
                                                       in_=kvb[:, 640:768].rearrange("p (h d) -> p h d", d=64)),
                       reads=[kvb], writes=[winV])
                    if bg >= NBP - 4:
                        dma(sc_wk[bg], winKT[:, wslot * 128:(wslot + 1) * 128], reads=[winKT])
                        dma(sc_wv[bg], winV[:, wslot].rearrange("p h c -> p (h c)"), reads=[winV])

                def fproj(col0, scale):
                    ps4 = ring.get()
                    p4 = ps4.ap[:].rearrange("p (a b) -> p a b", b=128)
                    for k4 in range(4):
                        for c in range(8):
                            op("pe", lambda e: e.matmul(p4[:, k4, :], lhsT=WF[:, c, col0 + 128 * k4:col0 + 128 * (k4 + 1)],
                                                        rhs=xT[:, c, :], start=(c == 0), stop=(c == 7)),
                               reads=[WF, xT], writes=[ps4])
                    f4 = f4r.get()
                    op("act", lambda e: e.activation(out=f4[:], in_=p4, func=AF.Copy, scale=scale),
                       reads=[ps4], writes=[f4])
                    return f4

                f4 = fproj(512, 1.0)
                dma((sc_u_s[samp] if sm else sc_u[bg]).rearrange("p (a b) -> p a b", b=128), f4[:], reads=[f4])
                if own:
                    f4 = fproj(0, 0.125)
                    dma(d_q[i].rearrange("p (a b) -> p a b", b=128), f4[:], reads=[f4])
                    f4 = fproj(1024, 1.0)
                    dma(d_zs[i].rearrange("p (a b) -> p a b", b=128), f4[:], reads=[f4])
                if sm:
                    return
                compress(cmpraw, 8, kcT, vcT, 8 * bg)
                nt = bg // 16
                pv = ring.get()
                pvv = bfview(pv)[:, 0:128]
                op("pe", lambda e: e.transpose(out=pvv, in_=vcT[:, nt * 128:(nt + 1) * 128], identity=identb[:]),
                   reads=[vcT, identb], writes=[pv])
                op("dve", lambda e: e.tensor_copy(out=vc[:, nt, :, 0:64], in_=pvv.rearrange("p (h d) -> p h d", d=64)),
                   reads=[pv], writes=[vc])

            if NSB > 0:
                rawS = fw.sbuf("rawS", [128, 2, 1040], BF16)
                kcS = fw.sbuf("kcS", [128, 1024], BF16)
                vcS = fw.sbuf("vcS", [128, 1024], BF16)
                ptb = fw.sbuf("ptb", [128, 128], I32)
                ptf = fw.sbuf("ptf", [128, 128], F32)
                io2 = fw.sbuf("io2", [128, 1], F32)
                idxc = fw.sbuf("idxc", [128, 128], I32)
                pgr = Ring([fw.sbuf(f"pg{i}", [128, 256], F32) for i in range(4)])
                pgbr = Ring([fw.sbuf(f"pgb{i}", [128, 256], BF16) for i in range(2)])
                op("pool", lambda e: e.memset(rawS[:], 0.0), writes=[rawS])
                op("pool", lambda e: e.iota(io2[:], pattern=[[0, 1]], base=0, channel_multiplier=2,
                                            allow_small_or_imprecise_dtypes=True), writes=[io2])

            for bg in range(NBT):
                front(bg)
            for b in range(NSB):
                front(None, samp=b)
                dma(ptb[:], pt_d[b], writes=[ptb])
                op("dve", lambda e: e.tensor_copy(out=ptf[:], in_=ptb[:]), reads=[ptb], writes=[ptf])
                op("dve", lambda e: e.tensor_scalar(out=ptf[:], in0=ptf[:], scalar1=256.0, scalar2=io2[:, 0:1],
                                                    op0=ALU.mult, op1=ALU.add), reads=[ptf, io2], writes=[ptf])
                op("dve", lambda e: e.tensor_copy(out=idxc[:], in_=ptf[:]), reads=[ptf], writes=[idxc])
                for kt in range(128):
                    pg = pgr.get()
                    fw.idma(pg[:], cache_d, idxc[:, kt:kt + 1], reads=[idxc], writes=[pg])
                    pb_ = pgbr.get()
                    if kt % 2 == 0:
                        op("dve", lambda e: e.tensor_copy(out=pb_[:], in_=pg[:]), reads=[pg], writes=[pb_])
                    else:
                        op("act", lambda e: e.copy(out=pb_[:], in_=pg[:]), reads=[pg], writes=[pb_])
                    pt2 = ring.get()
                    p2v = bfview(pt2).rearrange("p (a b) -> p a b", b=128)
                    for kk in range(2):
                        op("pe", lambda e: e.transpose(out=p2v[:, kk, :], in_=pb_[:, kk * 128:(kk + 1) * 128],
                                                       identity=identb[:]), reads=[pb_, identb], writes=[pt2])
                    slot = kt % 8
                    if kt % 2 == 0:
                        op("act", lambda e: e.copy(out=rawS[:, :, 16 + 128 * slot:16 + 128 * (slot + 1)], in_=p2v[:, 0:2, :]),
                           reads=[pt2], writes=[rawS])
                    else:
                        op("dve", lambda e: e.tensor_copy(out=rawS[:, :, 16 + 128 * slot:16 + 128 * (slot + 1)],
                                                          in_=p2v[:, 0:2, :]), reads=[pt2], writes=[rawS])
                    if slot == 7:
                        compress(rawS, 64, kcS, vcS, 64 * (kt // 8))
                        op("dve", lambda e: e.tensor_copy(out=rawS[:, :, 0:16], in_=rawS[:, :, 1024:1040]),
                           reads=[rawS], writes=[rawS])
                dma(sc_kc[b], kcS[:], reads=[kcS])
                dma(sc_vcT[b], vcS[:], reads=[vcS])
        fw.barrier()
        fw.stack = pstore


        with contextlib.ExitStack() as p2:
            fw.stack = p2
            wla = fw.sbuf("wla", [128, 4, D], BF16)
            wls = fw.sbuf("wls", [128, 4, D], BF16)
            wo = fw.sbuf("wo", [128, 8, D], BF16)
            wglu = fw.sbuf("wglu", [128, 4, 512], BF16)
            fg = fw.sbuf("fgt", [128, D], F32)
            Esb = fw.sbuf("Esb", [128, EW], BF16)
            caus = fw.sbuf("caust", [128, 128], BF16)
            pset = contextlib.ExitStack()
            pset.__enter__()
            BbT = fw.sbuf("BbT", [128, 2, 16, 128], BF16)
            CTre = fw.sbuf("CTre", [128, 16, 128], BF16)
            CTimn = fw.sbuf("CTimn", [128, 16, 128], BF16)
            cosT = fw.sbuf("cosT", [128, 16, 128], F32)
            sinT = fw.sbuf("sinT", [128, 16, 128], F32)
            p2small = [fw.sbuf(f"p2s{i}", [128, 16], F32) for i in range(24)]
            dsk = fw.sbuf("dsk", [128, 4], F32)
            nbgl = fw.sbuf("nbgl", [128, 4], F32)
            fw.stack = pset
            stg = Ring([fw.sbuf(f"stg2_{i}", [128, 1024], F32) for i in range(2)])
            dma(fg[:], fg_d, writes=[fg])
            dma(Esb[:], E_d, writes=[Esb])
            dma(caus[:], caus_d, writes=[caus])
            k = 0
            for (wd, W, nk, ncol) in ((wla_d, wla, 4, D), (wls_d, wls, 4, D), (wo_d, wo, 8, D), (wglu_d, wglu, 4, 512)):
                for c in range(nk):
                    s_ = stg.get()
                    dma(s_[:, 0:ncol], wd[:, c, :], writes=[s_])
                    en = "dve" if k % 2 == 0 else "pool"
                    k += 1
                    op(en, lambda e: e.tensor_copy(out=W[:, c, :], in_=s_[:, 0:ncol]), reads=[s_], writes=[W])

            def small(name, shape=(128, 16), dt=F32):
                if tuple(shape) == (128, 16) and dt == F32 and p2small:
                    return p2small.pop()
                return fw.sbuf(name, list(shape), dt)

            lre, lim, ldt = small("lre"), small("lim"), small("ldt")
            dma(lre[:], lamre_d, writes=[lre])
            dma(lim[:], lamim_d, writes=[lim])
            dma(ldt[:], logdt_d, writes=[ldt])
            mgh = small("mgh", (128, 2))
            dma(dsk[:], dsk_d, writes=[dsk])
            dma(nbgl[:], bgl_d, writes=[nbgl])
            dma(mgh[:], mgh_d, writes=[mgh])
            op("dve", lambda e: e.tensor_scalar(out=nbgl[:], in0=nbgl[:], scalar1=-1.0, scalar2=None, op0=ALU.mult),
               reads=[nbgl], writes=[nbgl])
            tmp = [small(f"s5t{i}") for i in range(8)]
            rho, c1, s1 = small("rho"), small("c1"), small("s1")
            ki = small("ki", dt=I32)

            def tt(out, a, b, o, eng="dve"):
                op(eng, lambda e: e.tensor_tensor(out=out[:], in0=a[:], in1=b[:], op=o), reads=[a, b], writes=[out])

            def ts(out, a, s1_, s2_, o0, o1=None, eng="dve"):
                if o1 is None:
                    op(eng, lambda e: e.tensor_scalar(out=out[:], in0=a[:], scalar1=s1_, scalar2=None, op0=o0),
                       reads=[a], writes=[out])
                else:
                    op(eng, lambda e: e.tensor_scalar(out=out[:], in0=a[:], scalar1=s1_, scalar2=s2_, op0=o0, op1=o1),
                       reads=[a], writes=[out])

            dtt, lr, th, r_ = tmp[0], tmp[1], tmp[2], tmp[3]
            op("act", lambda e: e.activation(out=dtt[:], in_=ldt[:], func=AF.Exp), reads=[ldt], writes=[dtt])
            tt(lr, lre, dtt, ALU.mult)
            tt(th, lim, dtt, ALU.mult)
            op("act", lambda e: e.activation(out=rho[:], in_=lr[:], func=AF.Exp), reads=[lr], writes=[rho])
            ts(tmp[4], th, 1.0 / (2 * math.pi), None, ALU.mult)
            op("dve", lambda e: e.tensor_copy(out=ki[:], in_=tmp[4][:]), reads=[tmp[4]], writes=[ki])
            op("dve", lambda e: e.tensor_copy(out=tmp[4][:], in_=ki[:]), reads=[ki], writes=[tmp[4]])
            op("dve", lambda e: e.scalar_tensor_tensor(out=r_[:], in0=tmp[4][:], scalar=-2.0 * math.pi, in1=th[:],
                                                       op0=ALU.mult, op1=ALU.add), reads=[tmp[4], th], writes=[r_])
            s4, c4, s2, c2 = tmp[4], tmp[5], tmp[6], tmp[7]
            hp_ = small("halfpi", (128, 1))
            op("pool", lambda e: e.memset(hp_[:], math.pi / 2), writes=[hp_])
            op("act", lambda e: e.activation(out=s4[:], in_=r_[:], func=AF.Sin, scale=0.25), reads=[r_], writes=[s4])
            op("act", lambda e: e.activation(out=c4[:], in_=r_[:], func=AF.Sin, scale=0.25, bias=hp_[:, 0:1]),
               reads=[r_, hp_], writes=[c4])
            op("dve", lambda e: e.scalar_tensor_tensor(out=s2[:], in0=s4[:], scalar=2.0, in1=c4[:], op0=ALU.mult,
                                                       op1=ALU.mult), reads=[s4, c4], writes=[s2])
            tt(c2, s4, s4, ALU.mult)
            ts(c2, c2, -2.0, 1.0, ALU.mult, ALU.add)
            op("dve", lambda e: e.scalar_tensor_tensor(out=s1[:], in0=s2[:], scalar=2.0, in1=c2[:], op0=ALU.mult,
                                                       op1=ALU.mult), reads=[s2, c2], writes=[s1])
            tt(c1, s2, s2, ALU.mult)
            ts(c1, c1, -2.0, 1.0, ALU.mult, ALU.add)
            are, aim, am1, den = tmp[0], tmp[1], tmp[2], tmp[3]
            tt(are, rho, c1, ALU.mult)
            tt(aim, rho, s1, ALU.mult)
            ts(am1, are, -1.0, None, ALU.add)
            tt(den, lre, lre, ALU.mult)
            tt(tmp[4], lim, lim, ALU.mult)
            tt(den, den, tmp[4], ALU.add)
            op("dve", lambda e: e.reciprocal(out=den[:], in_=den[:]), reads=[den], writes=[den])
            fre, fim = small("fre"), small("fim")
            tt(tmp[4], am1, lre, ALU.mult)
            tt(tmp[5], aim, lim, ALU.mult)
            tt(tmp[4], tmp[4], tmp[5], ALU.add)
            tt(fre, tmp[4], den, ALU.mult)
            tt(tmp[4], aim, lre, ALU.mult)
            tt(tmp[5], am1, lim, ALU.mult)
            tt(tmp[4], tmp[4], tmp[5], ALU.subtract)
            tt(fim, tmp[4], den, ALU.mult)
            bre, bim = small("bre", (128, 16, 16)), small("bim", (128, 16, 16))
            dma(bre[:], bre_d, writes=[bre])
            dma(bim[:], bim_d, writes=[bim])
            bt = [small(f"bt{i}", (128, 16, 16)) for i in range(3)]
            Mb = fw.sbuf("Mb", [128, 16, 2, 16], BF16)
            Mz = fw.sbuf("Mz", [128, 16, 4, 32], BF16)
            op("pool", lambda e: e.memset(Mz[:], 0.0), writes=[Mz])

            def bc16(t_):
                return t_.ap[:].unsqueeze(2).to_broadcast([128, 16, 16])

            for ri in range(2):
                fa, fb_, sgn = (fre, fim, ALU.subtract) if ri == 0 else (fim, fre, ALU.add)
                op("dve", lambda e: e.tensor_tensor(out=bt[0][:], in0=bre[:], in1=bc16(fa), op=ALU.mult),
                   reads=[bre, fa], writes=[bt[0]])
                op("dve", lambda e: e.tensor_tensor(out=bt[1][:], in0=bim[:], in1=bc16(fb_), op=ALU.mult),
                   reads=[bim, fb_], writes=[bt[1]])
                op("dve", lambda e: e.tensor_tensor(out=bt[2][:], in0=bt[0][:], in1=bt[1][:], op=sgn),
                   reads=[bt[0], bt[1]], writes=[bt[2]])
                op("dve", lambda e: e.tensor_tensor(
                    out=Mb[:], in0=bt[2].ap[:].unsqueeze(2).to_broadcast([128, 16, 2, 16]),
                    in1=mgh.ap[:].unsqueeze(1).unsqueeze(3).to_broadcast([128, 16, 2, 16]), op=ALU.mult),
                   reads=[bt[2], mgh], writes=[Mb])
                Mz5 = Mz.ap[:].rearrange("p (a q) r c -> p a q r c", q=4)
                Mb4 = Mb.ap[:].rearrange("p (a q) g c -> p a q (g c)", q=4)
                for q_ in range(4):
                    op("dve", lambda e: e.tensor_copy(out=Mz5[:, :, q_, q_, :], in_=Mb4[:, :, q_, :]),
                       reads=[Mb], writes=[Mz])
                for st in range(16):
                    pz = ring.get()
                    pzv = bfview(pz)[:, 0:128]
                    op("pe", lambda e: e.transpose(out=pzv, in_=Mz[:, st].rearrange("p r c -> p (r c)"),
                                                   identity=identb[:]), reads=[Mz, identb], writes=[pz])
                    op("act", lambda e: e.copy(out=BbT[:, ri, st, :], in_=pzv), reads=[pz], writes=[BbT])
            for half_ in range(2):
                s_ = stg.get()
                dma(s_[:, 0:1024].rearrange("p (a b) -> p a b", b=128), cre_d[:, half_ * 8:(half_ + 1) * 8, :], writes=[s_])
                op("dve", lambda e: e.tensor_copy(out=CTre[:, half_ * 8:(half_ + 1) * 8, :],
                                                  in_=s_[:, 0:1024].rearrange("p (a b) -> p a b", b=128)),
                   reads=[s_], writes=[CTre])
                s_ = stg.get()
                dma(s_[:, 0:1024].rearrange("p (a b) -> p a b", b=128), cim_d[:, half_ * 8:(half_ + 1) * 8, :], writes=[s_])
                op("dve", lambda e: e.tensor_scalar(out=CTimn[:, half_ * 8:(half_ + 1) * 8, :],
                                                    in0=s_[:, 0:1024].rearrange("p (a b) -> p a b", b=128),
                                                    scalar1=-1.0, scalar2=None, op0=ALU.mult), reads=[s_], writes=[CTimn])
            op("pool", lambda e: e.memset(cosT[:], 1.0), writes=[cosT])
            op("pool", lambda e: e.memset(sinT[:], 0.0), writes=[sinT])
            emr, emi = small("emr"), small("emi")
            op("dve", lambda e: e.tensor_copy(out=emr[:], in_=c1[:]), reads=[c1], writes=[emr])
            op("dve", lambda e: e.tensor_copy(out=emi[:], in_=s1[:]), reads=[s1], writes=[emi])
            tb = [fw.sbuf(f"tb{i}", [128, 16, 64], F32) for i in range(2)]
            for kk in range(7):
                m = 1 << kk
                ebr = emr.ap[:].unsqueeze(2).to_broadcast([128, 16, m])
                ebi = emi.ap[:].unsqueeze(2).to_broadcast([128, 16, m])
                op("dve", lambda e: e.tensor_tensor(out=tb[0][:, :, 0:m], in0=cosT[:, :, 0:m], in1=ebr, op=ALU.mult),
                   reads=[cosT, emr], writes=[tb[0]])
                op("dve", lambda e: e.tensor_tensor(out=tb[1][:, :, 0:m], in0=sinT[:, :, 0:m], in1=ebi, op=ALU.mult),
                   reads=[sinT, emi], writes=[tb[1]])
                op("dve", lambda e: e.tensor_tensor(out=cosT[:, :, m:2 * m], in0=tb[0][:, :, 0:m], in1=tb[1][:, :, 0:m],
                                                    op=ALU.subtract), reads=[tb[0], tb[1]], writes=[cosT])
                op("dve", lambda e: e.tensor_tensor(out=tb[0][:, :, 0:m], in0=cosT[:, :, 0:m], in1=ebi, op=ALU.mult),
                   reads=[cosT, emi], writes=[tb[0]])
                op("dve", lambda e: e.tensor_tensor(out=tb[1][:, :, 0:m], in0=sinT[:, :, 0:m], in1=ebr, op=ALU.mult),
                   reads=[sinT, emr], writes=[tb[1]])
                op("dve", lambda e: e.tensor_tensor(out=sinT[:, :, m:2 * m], in0=tb[0][:, :, 0:m], in1=tb[1][:, :, 0:m],
                                                    op=ALU.add), reads=[tb[0], tb[1]], writes=[sinT])
                tt(tmp[0], emr, emr, ALU.mult)
                tt(tmp[1], emi, emi, ALU.mult)
                tt(tmp[2], emr, emi, ALU.mult)
                tt(emr, tmp[0], tmp[1], ALU.subtract)
                ts(emi, tmp[2], 2.0, None, ALU.mult)

            hre_p, him_p = small("hre_p"), small("him_p")
            g0r, g0i = small("g0r"), small("g0i")
            glr, gli = small("glr"), small("gli")
            fw.barrier()
            pset.close()
            fw.stack = p2
            op("pool", lambda e: e.memset(hre_p[:], 0.0), writes=[hre_p])
            op("pool", lambda e: e.memset(him_p[:], 0.0), writes=[him_p])

            uTr = Ring([fw.sbuf(f"uT{i}", [128, 4, 128], BF16) for i in range(2)])
            wk = Ring([fw.sbuf(f"wk{i}", [128, 128], F32) for i in range(8)])
            gbr = Ring([fw.sbuf(f"gb{i}", [128, 128], F32) for i in range(4)])
            hbr = Ring([fw.sbuf(f"hb{i}", [128, 2, 128], BF16) for i in range(3)])

            def cmul(or_, oi_, ar, ai, br_, bi_):
                tt(tmp[0], ar, br_, ALU.mult)
                tt(tmp[1], ai, bi_, ALU.mult, eng="pool")
                tt(tmp[2], ar, bi_, ALU.mult)
                tt(tmp[3], ai, br_, ALU.mult, eng="pool")
                tt(or_, tmp[0], tmp[1], ALU.subtract)
                tt(oi_, tmp[2], tmp[3], ALU.add, eng="pool")

            class T16:
                def __init__(self, b):
                    self.b = b

            def s5_block(bg, samp=None):
                own = samp is not None or bg >= NBP
                lc = 7 if samp is not None else 127
                uT = uTr.get()
                dma(uT[:], (sc_u_s[samp] if samp is not None else sc_u[bg]).rearrange("p (a b) -> p a b", b=128), writes=[uT])
                cmul(g0r, g0i, c1, s1, hre_p, him_p)
                for st in range(16):
                    ct, q_ = st // 4, st % 4
                    rows = slice(32 * q_, 32 * q_ + 32)
                    bps = ring.get()
                    bv = bps.ap[:, 0:256].rearrange("p (a b) -> p a b", b=128)
                    for ri in range(2):
                        op("pe", lambda e: e.matmul(bv[:, ri, :], lhsT=BbT[:, ri, st, :], rhs=uT[:, ct, :],
                                                    start=True, stop=True), reads=[BbT, uT], writes=[bps])
                    t1, t2, t3, t4 = wk.get(), wk.get(), wk.get(), wk.get()
                    cs, sn = cosT.ap[:, st, :], sinT.ap[:, st, :]
                    op("dve", lambda e: e.tensor_tensor(out=t1[:], in0=bv[:, 0, :], in1=cs, op=ALU.mult),
                       reads=[bps, cosT], writes=[t1])
                    op("dve", lambda e: e.tensor_tensor(out=t2[:], in0=bv[:, 1, :], in1=sn, op=ALU.mult),
                       reads=[bps, sinT], writes=[t2])
                    op("dve", lambda e: e.tensor_tensor(out=t3[:], in0=bv[:, 1, :], in1=cs, op=ALU.mult),
                       reads=[bps, cosT], writes=[t3])
                    op("dve", lambda e: e.tensor_tensor(out=t4[:], in0=bv[:, 0, :], in1=sn, op=ALU.mult),
                       reads=[bps, sinT], writes=[t4])
                    op("pool", lambda e: e.tensor_tensor(out=t1[:], in0=t1[:], in1=t2[:], op=ALU.add),
                       reads=[t1, t2], writes=[t1])
                    op("pool", lambda e: e.tensor_tensor(out=t3[:], in0=t3[:], in1=t4[:], op=ALU.subtract),
                       reads=[t3, t4], writes=[t3])
                    gr, gi = gbr.get(), gbr.get()
                    rb = rho.ap[:, st:st + 1].to_broadcast([128, 128])
                    op("dve", lambda e: e.tensor_tensor_scan(out=gr[:], data0=rb, data1=t1[:], initial=g0r[:, st:st + 1],
                                                             op0=ALU.mult, op1=ALU.add), reads=[rho, t1, g0r], writes=[gr])
                    op("dve", lambda e: e.tensor_tensor_scan(out=gi[:], data0=rb, data1=t3[:], initial=g0i[:, st:st + 1],
                                                             op0=ALU.mult, op1=ALU.add), reads=[rho, t3, g0i], writes=[gi])
                    op("pool", lambda e: e.tensor_copy(out=glr[:, st:st + 1], in_=gr[:, lc:lc + 1]), reads=[gr], writes=[glr])
                    op("pool", lambda e: e.tensor_copy(out=gli[:, st:st + 1], in_=gi[:, lc:lc + 1]), reads=[gi], writes=[gli])
                    if own:
                        t5, t6, t7, t8 = wk.get(), wk.get(), wk.get(), wk.get()
                        hb = hbr.get()
                        op("dve", lambda e: e.tensor_tensor(out=t5[:], in0=gr[:], in1=cs, op=ALU.mult),
                           reads=[gr, cosT], writes=[t5])
                        op("pool", lambda e: e.tensor_tensor(out=t6[:], in0=gi[:], in1=sn, op=ALU.mult),
                           reads=[gi, sinT], writes=[t6])
                        op("dve", lambda e: e.tensor_tensor(out=t7[:], in0=gi[:], in1=cs, op=ALU.mult),
                           reads=[gi, cosT], writes=[t7])
                        op("pool", lambda e: e.tensor_tensor(out=t8[:], in0=gr[:], in1=sn, op=ALU.mult),
                           reads=[gr, sinT], writes=[t8])
                        op("dve", lambda e: e.tensor_tensor(out=hb[:, 0, :], in0=t5[:], in1=t6[:], op=ALU.subtract),
                           reads=[t5, t6], writes=[hb])
                        op("pool", lambda e: e.tensor_tensor(out=hb[:, 1, :], in0=t7[:], in1=t8[:], op=ALU.add),
                           reads=[t7, t8], writes=[hb])
                        yv_ = ypsb.ap[:].rearrange("p (a b) -> p a b", b=128)
                        op("pe", lambda e: e.matmul(yv_[:, ct, :], lhsT=CTre[:, st, :], rhs=hb[:, 0, :],
                                                    start=(q_ == 0), stop=False), reads=[CTre, hb], writes=[ypsb])
                        op("pe", lambda e: e.matmul(yv_[:, ct, :], lhsT=CTimn[:, st, :], rhs=hb[:, 1, :],
                                                    start=False, stop=(q_ == 3)), reads=[CTimn, hb], writes=[ypsb])
                c127 = tmp[4]
                s127 = tmp[5]
                op("dve", lambda e: e.tensor_copy(out=c127[:], in_=cosT[:, :, lc]), reads=[cosT], writes=[c127])
                op("dve", lambda e: e.tensor_copy(out=s127[:], in_=sinT[:, :, lc]), reads=[sinT], writes=[s127])
                cmul(hre_p, him_p, c127, s127, glr, gli)
                return uT

            wkr = Ring([fw.sbuf(f"wkt{i}", [128, 5, 128], BF16) for i in range(2)])
            wvr = Ring([fw.sbuf(f"wvt{i}", [128, 5, 130], BF16) for i in range(2)])
            qTr = Ring([fw.sbuf(f"qT{i}", [128, 512], BF16) for i in range(2)])
            qzr = Ring([fw.sbuf(f"qz{i}", [128, 2, 512], BF16) for i in range(1)])
            gnr = Ring([fw.sbuf(f"gn{i}", [128, 24], F32) for i in range(2)])
            cbr = Ring([fw.sbuf(f"cb{i}", [128, 128], BF16) for i in range(10)])
            tkr = Ring([fw.sbuf(f"tk{i}", [128, 128], F32) for i in range(2)])
            Pr = Ring([fw.sbuf(f"P{i}", [128, 512], BF16) for i in range(3)])
            OTr = Ring([fw.sbuf(f"OT{i}", [128, 512], F32) for i in range(2)])
            oattn_r = Ring([fw.sbuf(f"oat{i}", [128, 512], F32) for i in range(1)])
            smr = Ring([fw.sbuf(f"sm{i}", [128, 8], F32) for i in range(12)])
            impr = Ring([fw.sbuf(f"imp{i}", [128, 128], F32) for i in range(2)])
            selmb_r = Ring([fw.sbuf(f"selmb{i}", [128, 128], BF16) for i in range(2)])
            selmT_r = Ring([fw.sbuf(f"selmT{i}", [128, 2, 128], BF16) for i in range(2)])
            big = Ring([fw.sbuf(f"big{i}", [128, 512], F32) for i in range(4)])
            bigb = Ring([fw.sbuf(f"bigb{i}", [128, 512], BF16) for i in range(4)])
            gmr2 = Ring([fw.sbuf(f"gm2_{i}", [128, 2048], BF16) for i in range(1)])
            sgm = fw.sbuf("sgm", [128, 2048], BF16)
            sgt = fw.sbuf("sgt", [128, 512], F32)
            mbt = fw.sbuf("mbt", [128, D], BF16)
            mT = fw.sbuf("mT", [128, 8, 128], BF16)
            xr2 = Ring([fw.sbuf(f"xr2_{i}", [128, D], F32) for i in range(1)])

            def sigmoid_from(out, src_ap, src_bufs, scale_in=1.0, shape_ap=None):
                op("act", lambda e: e.activation(out=out[:], in_=src_ap, func=AF.Exp, scale=-1.0 * scale_in),
                   reads=src_bufs, writes=[out])
                op("pool", lambda e: e.tensor_scalar(out=out[:], in0=out[:], scalar1=1.0, scalar2=None, op0=ALU.add),
                   reads=[out], writes=[out])
                op("dve", lambda e: e.reciprocal(out=out[:], in_=out[:]), reads=[out], writes=[out])

            def attn_pass(h, qz, units, acc, extras=()):
                n = len(units)
                for ui, (kT, kreads, biases, V, vreads) in enumerate(units):
                    S = ring.get()
                    op("pe", lambda e: e.matmul(S[:, :], lhsT=kT, rhs=qz[:, h, :], start=True, stop=(len(biases) == 0)),
                       reads=kreads + [qz], writes=[S])
                    for bi_, (l_, r_ap, rd) in enumerate(biases):
                        op("pe", lambda e: e.matmul(S.ap[:].rearrange("p (a b) -> p a b", b=128), lhsT=l_, rhs=r_ap,
                                                    start=False, stop=(bi_ == len(biases) - 1)), reads=rd, writes=[S])
                    P = Pr.get()
                    op("act", lambda e: e.activation(out=P[:], in_=S[:, :], func=AF.Exp), reads=[S], writes=[P])
                    op("pe", lambda e: e.matmul(acc[0:65, :], lhsT=V, rhs=P[:], start=(ui == 0), stop=(ui == n - 1)),
                       reads=vreads + [P], writes=[acc])
                    for (xacc, xfn, lo, hi, xrd) in extras:
                        if lo <= ui <= hi:
                            op("pe", lambda e: e.matmul(xacc[:, :], lhsT=xfn(ui), rhs=P[:], start=(ui == lo),
                                                        stop=(ui == hi)), reads=xrd + [P], writes=[xacc])

            def finish_pass(h, br, acc, gate, oattn, first, clampz=False):
                OT = OTr.get()
                op("act", lambda e: e.copy(out=OT[0:65, :], in_=acc[0:65, :]), reads=[acc], writes=[OT])
                tp = ring.get()
                tpv = tp.ap[:, 0:260].rearrange("p (g c) -> p g c", c=65)
                for g in range(4):
                    op("pe", lambda e: e.transpose(out=tpv[:, g, :], in_=OT[0:65, g * 128:(g + 1) * 128],
                                                   identity=identf[0:65, 0:65]), reads=[OT, identf], writes=[tp])
                rz = smr.get()
                if clampz:
                    op("dve", lambda e: e.tensor_scalar(out=rz[:, 0:4], in0=tpv[:, :, 64], scalar1=1e-30, scalar2=None,
                                                        op0=ALU.max), reads=[tp], writes=[rz])
                    op("dve", lambda e: e.reciprocal(out=rz[:, 0:4], in_=rz[:, 0:4]), reads=[rz], writes=[rz])
                else:
                    op("dve", lambda e: e.reciprocal(out=rz[:, 0:4], in_=tpv[:, :, 64]), reads=[tp], writes=[rz])
                w_ = smr.get()
                gv = gate.ap[:].rearrange("p (h g b) -> p h g b", h=2, g=4)[:, h, :, br]
                op("dve", lambda e: e.tensor_tensor(out=w_[:, 0:4], in0=rz[:, 0:4], in1=gv, op=ALU.mult),
                   reads=[rz, gate], writes=[w_])
                for g in range(4):
                    dst = oattn[:, (h * 4 + g) * 64:(h * 4 + g + 1) * 64]
                    if first:
                        op("dve", lambda e: e.tensor_scalar(out=dst, in0=tpv[:, g, 0:64], scalar1=w_[:, g:g + 1],
                                                            scalar2=None, op0=ALU.mult), reads=[tp, w_], writes=[oattn])
                    else:
                        op("dve", lambda e: e.scalar_tensor_tensor(out=dst, in0=tpv[:, g, 0:64], scalar=w_[:, g:g + 1],
                                                                   in1=dst, op0=ALU.mult, op1=ALU.add),
                           reads=[tp, w_, oattn], writes=[oattn])
                return rz

            def own_block(i, uT, samp=None):
                sm = samp is not None
                bg = NBP + i if not sm else None
                yv_ = ypsb.ap[:].rearrange("p (a b) -> p a b", b=128)
                yv = big.get()
                yv3 = yv.ap[:].rearrange("p (a b) -> p a b", b=128)
                for ct in range(4):
                    op("dve", lambda e: e.scalar_tensor_tensor(out=yv3[:, ct, :], in0=uT[:, ct, :], scalar=dsk[:, ct:ct + 1],
                                                               in1=yv_[:, ct, :], op0=ALU.mult, op1=ALU.add),
                       reads=[uT, dsk, ypsb], writes=[yv])
                t_a, t_b = big.get(), big.get()
                op("pool", lambda e: e.tensor_tensor(out=t_a[:], in0=yv[:], in1=yv[:], op=ALU.mult), reads=[yv], writes=[t_a])
                op("pool", lambda e: e.tensor_scalar(out=t_a[:], in0=t_a[:], scalar1=0.044715, scalar2=1.0, op0=ALU.mult,
                                                     op1=ALU.add), reads=[t_a], writes=[t_a])
                op("pool", lambda e: e.tensor_tensor(out=t_a[:], in0=t_a[:], in1=yv[:], op=ALU.mult), reads=[t_a, yv], writes=[t_a])
                sigmoid_from(t_b, t_a[:], [t_a], scale_in=2.0 * GC)
                yg = big.get()
                op("dve", lambda e: e.tensor_tensor(out=yg[:], in0=yv[:], in1=t_b[:], op=ALU.mult), reads=[yv, t_b], writes=[yg])
                ygb = bigb.get()
                op("act", lambda e: e.copy(out=ygb[:], in_=yg[:]), reads=[yg], writes=[ygb])
                gl = ring.get()
                glv = gl.ap[:].rearrange("p (a b) -> p a b", b=128)
                ygb3 = ygb.ap[:].rearrange("p (a b) -> p a b", b=128)
                for co in range(4):
                    for ci in range(4):
                        op("pe", lambda e: e.matmul(glv[:, co, :], lhsT=wglu[:, ci, co * 128:(co + 1) * 128], rhs=ygb3[:, ci, :],
                                                    start=(ci == 0), stop=(ci == 3)), reads=[wglu, ygb], writes=[gl])
                sg = t_a
                sg3 = sg.ap[:].rearrange("p (a b) -> p a b", b=128)
                for co in range(4):
                    op("act", lambda e: e.activation(out=sg3[:, co, :], in_=glv[:, co, :], func=AF.Exp, scale=-1.0,
                                                     bias=nbgl[:, co:co + 1]), reads=[gl, nbgl], writes=[sg])
                op("pool", lambda e: e.tensor_scalar(out=sg[:], in0=sg[:], scalar1=1.0, scalar2=None, op0=ALU.add),
                   reads=[sg], writes=[sg])
                op("dve", lambda e: e.reciprocal(out=sg[:], in_=sg[:]), reads=[sg], writes=[sg])
                op("pool", lambda e: e.tensor_tensor(out=yg[:], in0=yg[:], in1=sg[:], op=ALU.mult), reads=[yg, sg], writes=[yg])
                zs = bigb.get()
                dma(zs[:], sc_zs_s[samp] if sm else sc_zs[i], writes=[zs])
                sz = t_b
                sigmoid_from(sz, zs[:], [zs])
                op("pool", lambda e: e.tensor_tensor(out=sz[:], in0=sz[:], in1=zs[:], op=ALU.mult), reads=[sz, zs], writes=[sz])
                ysT = bigb.get()
                op("dve", lambda e: e.tensor_tensor(out=ysT[:], in0=yg[:], in1=sz[:], op=ALU.mult), reads=[yg, sz], writes=[ysT])

                qT = qTr.get()
                dma(qT[:], sc_q_s[samp] if sm else sc_q[i], writes=[qT])
                qz = qzr.get()
                op("pool", lambda e: e.memset(qz[:], 0.0), writes=[qz])
                for h in range(2):
                    rows = slice(h * 64, (h + 1) * 64)
                    op("pool", lambda e: e.tensor_copy(out=qz[rows, h, :], in_=qT[rows, :]), reads=[qT], writes=[qz])
                gn = gnr.get()
                dma(gn[:], sc_gn_s[samp] if sm else sc_gn[i], writes=[gn])
                gate = gnr.get()
                sigmoid_from(gate, gn[:], [gn])
                oattn = oattn_r.get()
                if sm:
                    sample_attention(samp, qz, gate, oattn)
                else:
                    prompt_attention(i, bg, qz, gate, oattn)
                out_chain(i, samp, oattn, ysT)

            def prompt_attention(i, bg, qz, gate, oattn):
                nt_hi = (8 * bg + 7) // 128
                cbs = []
                for nt in range(nt_hi + 1):
                    cb = cbr.get()
                    dma(cb[:], cmpb_d[i, nt], writes=[cb])
                    cbs.append(cb)
                ta, tb_ = tkr.get(), tkr.get()
                dma(ta[:], tka_d[i], writes=[ta])
                dma(tb_[:], tkb_d[i], writes=[tb_])
                selmT = selmT_r.get()
                for h in range(2):
                    units = []
                    for nt in range(nt_hi + 1):
                        units.append((kcT[:, nt * 128:(nt + 1) * 128], [kcT],
                                      [(identb[:], cbs[nt].ap[:].unsqueeze(1).to_broadcast([128, 4, 128]), [identb, cbs[nt]])],
                                      vc[:, nt, h, :], [vc]))
                    acc = accO.get()
                    attn_pass(h, qz, units, acc, extras=[(accI, lambda ui: cmpsel[:, ui, :], 0, nt_hi, [cmpsel])])
                    rz = finish_pass(h, 0, acc, gate, oattn, first=True, clampz=True)
                    IT = OTr.get()
                    op("act", lambda e: e.copy(out=IT[:], in_=accI[:, :]), reads=[accI], writes=[IT])
                    tpi = ring.get()
                    tpiv = tpi.ap[:].rearrange("p (g c) -> p g c", c=128)
                    for g in range(4):
                        op("pe", lambda e: e.transpose(out=tpiv[:, g, :], in_=IT[:, g * 128:(g + 1) * 128], identity=identf[:]),
                           reads=[IT, identf], writes=[tpi])
                    imp = impr.get()
                    op("dve", lambda e: e.tensor_scalar(out=imp[:], in0=tpiv[:, 0, :], scalar1=rz[:, 0:1], scalar2=None,
                                                        op0=ALU.mult), reads=[tpi, rz], writes=[imp])
                    for g in range(1, 4):
                        op("dve", lambda e: e.scalar_tensor_tensor(out=imp[:], in0=tpiv[:, g, :], scalar=rz[:, g:g + 1],
                                                                   in1=imp[:], op0=ALU.mult, op1=ALU.add),
                           reads=[tpi, rz, imp], writes=[imp])
                    op("dve", lambda e: e.tensor_tensor(out=imp[:], in0=imp[:], in1=ta[:], op=ALU.mult), reads=[imp, ta], writes=[imp])
                    op("dve", lambda e: e.tensor_tensor(out=imp[:], in0=imp[:], in1=tb_[:], op=ALU.add), reads=[imp, tb_], writes=[imp])
                    m1, m2 = smr.get(), smr.get()
                    imp2 = impr.get()
                    op("dve", lambda e: e.max(out=m1[:], in_=imp[:]), reads=[imp], writes=[m1])
                    op("dve", lambda e: e.match_replace(out=imp2[:], in_to_replace=m1[:], in_values=imp[:], imm_value=-1e9),
                       reads=[m1, imp], writes=[imp2])
                    op("dve", lambda e: e.max(out=m2[:], in_=imp2[:]), reads=[imp2], writes=[m2])
                    op("dve", lambda e: e.scalar_tensor_tensor(out=imp2[:], in0=imp[:], scalar=m2[:, 7:8], in1=ta[:],
                                                               op0=ALU.is_ge, op1=ALU.mult), reads=[imp, m2, ta], writes=[imp2])
                    selmb = selmb_r.get()
                    op("dve", lambda e: e.tensor_scalar(out=selmb[:], in0=imp2[:], scalar1=-1.0, scalar2=None, op0=ALU.add),
                       reads=[imp2], writes=[selmb])
                    pst = ring.get()
                    pstv = bfview(pst)[:, 0:128]
                    op("pe", lambda e: e.transpose(out=pstv, in_=selmb[:], identity=identb[:]), reads=[selmb, identb], writes=[pst])
                    op("act", lambda e: e.copy(out=selmT[:, h, :], in_=pstv), reads=[pst], writes=[selmT])
                wbs = {}
                for r in range(5):
                    kt = bg - 4 + r
                    if kt < 0:
                        continue
                    cb = cbr.get()
                    dma(cb[:], winb_d[i, r], writes=[cb])
                    wbs[kt] = cb
                kt0 = max(0, bg - 4)
                nwt = bg - kt0 + 1
                wkt, wvt = wkr.get(), wvr.get()
                dma(wkt[:, 0:nwt, :], sc_wk[kt0:bg + 1].rearrange("n p c -> p n c"), writes=[wkt])
                dma(wvt[:, 0:nwt, :], sc_wv[kt0:bg + 1].rearrange("n p c -> p n c"), writes=[wvt])
                for h in range(2):
                    units = []
                    for kt in range(bg + 1):
                        biases = [(Esb[:, kt * 128:(kt + 1) * 128], selmT.ap[:, h:h + 1, :].to_broadcast([128, 4, 128]),
                                   [Esb, selmT])]
                        if kt == bg:
                            biases.append((identb[:], caus.ap[:].unsqueeze(1).to_broadcast([128, 4, 128]), [identb, caus]))
                        units.append((selKT[:, kt * 128:(kt + 1) * 128], [selKT], biases, selV[:, kt, h, :], [selV]))
                    acc = accO.get()
                    attn_pass(h, qz, units, acc)
                    finish_pass(h, 1, acc, gate, oattn, first=False)
                    units = []
                    for kt in sorted(wbs):
                        ws = kt - kt0
                        units.append((wkt[:, ws, :], [wkt],
                                      [(identb[:], wbs[kt].ap[:].unsqueeze(1).to_broadcast([128, 4, 128]), [identb, wbs[kt]])],
                                      wvt[:, ws, h * 65:(h + 1) * 65], [wvt]))
                    acc = accO.get()
                    attn_pass(h, qz, units, acc)
                    finish_pass(h, 2, acc, gate, oattn, first=False)

            def out_chain(i, samp, oattn, ysT):
                sm = samp is not None
                za = bigb.get()
                dma(za[:], sc_za_s[samp] if sm else sc_za[i], writes=[za])
                sza = big.get()
                sigmoid_from(sza, za[:], [za])
                op("pool", lambda e: e.tensor_tensor(out=sza[:], in0=sza[:], in1=za[:], op=ALU.mult), reads=[sza, za], writes=[sza])
                ozb = bigb.get()
                op("dve", lambda e: e.tensor_tensor(out=ozb[:], in0=oattn[:], in1=sza[:], op=ALU.mult), reads=[oattn, sza], writes=[ozb])
                pzt = ring.get()
                pztv = bfview(pzt)[:, 0:512].rearrange("p (a b) -> p a b", b=128)
                for k4 in range(4):
                    op("pe", lambda e: e.transpose(out=pztv[:, k4, :], in_=ozb[:, k4 * 128:(k4 + 1) * 128], identity=identb[:]),
                       reads=[ozb, identb], writes=[pzt])
                ozT = bigb.get()
                op("act", lambda e: e.copy(out=ozT[:].rearrange("p (a b) -> p a b", b=128), in_=pztv), reads=[pzt], writes=[ozT])
                ozT3 = ozT.ap[:].rearrange("p (a b) -> p a b", b=128)
                ysT3 = ysT.ap[:].rearrange("p (a b) -> p a b", b=128)
                gmb = gmr2.get()
                dma(gmb[:], sc_gm_s[samp] if sm else sc_gm[i], writes=[gmb])
                for pc in range(4):
                    op("act", lambda e: e.activation(out=sgt[:], in_=gmb[:, pc * 512:(pc + 1) * 512], func=AF.Exp, scale=-1.0),
                       reads=[gmb], writes=[sgt])
                    op("pool", lambda e: e.tensor_scalar(out=sgt[:], in0=sgt[:], scalar1=1.0, scalar2=None, op0=ALU.add),
                       reads=[sgt], writes=[sgt])
                    with nc.allow_low_precision(reason="sigmoid gate stored in bf16"):
                        op("dve", lambda e: e.reciprocal(out=sgm[:, pc * 512:(pc + 1) * 512], in_=sgt[:]), reads=[sgt], writes=[sgm])
                for cg in range(2):
                    pa = ring.get()
                    for k4 in range(4):
                        op("pe", lambda e: e.matmul(pa[:, :], lhsT=ozT3[:, k4, :], rhs=wla[:, k4, cg * 512:(cg + 1) * 512],
                                                    start=(k4 == 0), stop=(k4 == 3)), reads=[ozT, wla], writes=[pa])
                    pb_ = ring.get()
                    for k4 in range(4):
                        op("pe", lambda e: e.matmul(pb_[:, :], lhsT=ysT3[:, k4, :], rhs=wls[:, k4, cg * 512:(cg + 1) * 512],
                                                    start=(k4 == 0), stop=(k4 == 3)), reads=[ysT, wls], writes=[pb_])
                    m1_, m2_ = big.get(), big.get()
                    op("dve", lambda e: e.tensor_tensor(out=m1_[:], in0=pa[:, :], in1=sgm[:, cg * 512:(cg + 1) * 512], op=ALU.mult),
                       reads=[pa, sgm], writes=[m1_])
                    op("dve", lambda e: e.tensor_tensor(out=m2_[:], in0=pb_[:, :], in1=sgm[:, 1024 + cg * 512:1024 + (cg + 1) * 512],
                                                        op=ALU.mult), reads=[pb_, sgm], writes=[m2_])
                    op("pool", lambda e: e.tensor_tensor(out=mbt[:, cg * 512:(cg + 1) * 512], in0=m1_[:], in1=m2_[:], op=ALU.add),
                       reads=[m1_, m2_], writes=[mbt])
                pmt = ring.get()
                pmtv = bfview(pmt).rearrange("p (a b) -> p a b", b=128)
                for k8 in range(8):
                    op("pe", lambda e: e.transpose(out=pmtv[:, k8, :], in_=mbt[:, k8 * 128:(k8 + 1) * 128], identity=identb[:]),
                       reads=[mbt, identb], writes=[pmt])
                op("act", lambda e: e.copy(out=mT[:], in_=pmtv), reads=[pmt], writes=[mT])
                xt = xr2.get()
                dma(xt[:], xs_d[samp] if sm else xo[i * 128:(i + 1) * 128, :], writes=[xt])
                res = xt
                for cg in range(2):
                    py = ring.get()
                    for k8 in range(8):
                        op("pe", lambda e: e.matmul(py[:, :], lhsT=mT[:, k8, :], rhs=wo[:, k8, cg * 512:(cg + 1) * 512],
                                                    start=(k8 == 0), stop=(k8 == 7)), reads=[mT, wo], writes=[py])
                    op("dve", lambda e: e.tensor_tensor(out=res[:, cg * 512:(cg + 1) * 512], in0=py[:, :],
                                                        in1=xt[:, cg * 512:(cg + 1) * 512], op=ALU.add), reads=[py, xt], writes=[xt])
                ss = smr.get()
                op("act", lambda e: e.activation(out=mbt[:], in_=res[:], func=AF.Square, accum_out=ss[:, 0:1]),
                   reads=[res], writes=[mbt, ss])
                op("act", lambda e: e.activation(out=ss[:, 0:1], in_=ss[:, 0:1], func=AF.Ln, scale=1.0 / D, bias=1e-6),
                   reads=[ss], writes=[ss])
                op("act", lambda e: e.activation(out=ss[:, 0:1], in_=ss[:, 0:1], func=AF.Exp, scale=-0.5), reads=[ss], writes=[ss])
                op("dve", lambda e: e.scalar_tensor_tensor(out=res[:], in0=res[:], scalar=ss[:, 0:1], in1=fg[:],
                                                           op0=ALU.mult, op1=ALU.mult), reads=[res, ss, fg], writes=[res])
                if sm:
                    dma(ys_o[samp], res[0:8, :], reads=[res])
                else:
                    dma(y_o[i * 128:(i + 1) * 128, :], res[:], reads=[res])

            if NSB > 0:
                class _View:
                    def __init__(self, ap):
                        self.ap = ap

                    def __getitem__(self, idx):
                        return self.ap[idx]
                kcs_p = vcTs_p = sgm
                kcs = _View(sgm.ap[:, 1024:2048])
                vcTs = _View(sgm.ap[:, 0:1024])
                cmpsel_p = gmr2.bufs[0]
                cmpsel_s = _View(cmpsel_p.ap[:].rearrange("p (a b) -> p a b", b=256))
                vcs = fw.sbuf("vcs", [128, 8, 2, 65], BF16)
                cb0 = fw.sbuf("cb0_t", [128, 128], BF16)
                swb0 = fw.sbuf("swb0_t", [128, 128], BF16)
                ptb2 = fw.sbuf("ptb2", [128, 128], I32)
                ptf2 = fw.sbuf("ptf2", [128, 128], F32)
                io3 = fw.sbuf("io3", [128, 1], F32)
                idxs = fw.sbuf("idxs", [128, 128], I32)
                pgr2 = Ring([fw.sbuf(f"pgs{i}", [128, 256], F32) for i in range(2)])
                kbr = Ring([fw.sbuf(f"kbs{i}", [128, 128], BF16) for i in range(2)])
                KTr = Ring([fw.sbuf(f"KTs{i}", [128, 128], BF16) for i in range(3)])
                Vtr = Ring([fw.sbuf(f"Vts{i}", [128, 2, 65], BF16) for i in range(3)])
                selmT_s = fw.sbuf("selmT_s", [128, 4, 128], BF16)
                selmb_s = fw.sbuf("selmb_s", [128, 256], BF16)
                nkt = fw.sbuf("nkt", [128, 256], BF16)
                nvt = fw.sbuf("nvt", [128, 260], BF16)
                stt = fw.sbuf("stt", [128, 2, 16], F32)
                op("pool", lambda e: e.memset(vcs[:], 1.0), writes=[vcs])
                for vt_ in Vtr.bufs:
                    op("pool", lambda e: e.memset(vt_[:], 1.0), writes=[vt_])
                op("pool", lambda e: e.iota(io3[:], pattern=[[0, 1]], base=1, channel_multiplier=2,
                                            allow_small_or_imprecise_dtypes=True), writes=[io3])
                dma(cb0[:], cb0_d, writes=[cb0])
                dma(swb0[:], swb0_d, writes=[swb0])

            def stream_pass(ntiles, prep, biasfn, qz, gate, oattn, br):
                accs = [ps_bufs[0], ps_bufs[1]]
                for kt in range(ntiles):
                    KT_ap, kreads, Vfn = prep(kt)
                    for h in range(2):
                        S = ring.get()
                        biases = biasfn(kt, h)
                        op("pe", lambda e: e.matmul(S[:, :], lhsT=KT_ap, rhs=qz[:, h, :], start=True, stop=(len(biases) == 0)),
                           reads=kreads + [qz], writes=[S])
                        for bi_, (l_, r_ap, rd) in enumerate(biases):
                            op("pe", lambda e: e.matmul(S.ap[:].rearrange("p (a b) -> p a b", b=128), lhsT=l_, rhs=r_ap,
                                                        start=False, stop=(bi_ == len(biases) - 1)), reads=rd, writes=[S])
                        P = Pr.get()
                        op("act", lambda e: e.activation(out=P[:], in_=S[:, :], func=AF.Exp), reads=[S], writes=[P])
                        V_ap, vreads = Vfn(h)
                        op("pe", lambda e: e.matmul(accs[h][0:65, :], lhsT=V_ap, rhs=P[:], start=(kt == 0), stop=(kt == ntiles - 1)),
                           reads=vreads + [P], writes=[accs[h]])
                for h in range(2):
                    finish_pass(h, br, accs[h], gate, oattn, first=False)

            def sample_attention(b, qz, gate, oattn):
                dma(kcs[:], sc_kc[b], writes=[sgm])
                dma(vcTs[:], sc_vcT[b], writes=[sgm])
                dma(cmpsel_s[:], cmpsel_s_d, writes=[cmpsel_p])
                dma(nkt[:], sc_nk[b], writes=[nkt])
                dma(nvt[:], sc_nv[b], writes=[nvt])
                for nt in range(8):
                    pv = ring.get()
                    pvv = bfview(pv)[:, 0:128]
                    op("pe", lambda e: e.transpose(out=pvv, in_=vcTs[:, nt * 128:(nt + 1) * 128], identity=identb[:]),
                       reads=[sgm, identb], writes=[pv])
                    op("dve", lambda e: e.tensor_copy(out=vcs[:, nt, :, 0:64], in_=pvv.rearrange("p (h d) -> p h d", d=64)),
                       reads=[pv], writes=[vcs])
                for h in range(2):
                    units = []
                    for nt in range(8):
                        bl = [(identb[:], cb0.ap[:].unsqueeze(1).to_broadcast([128, 4, 128]), [identb, cb0])] if nt == 0 else []
                        units.append((kcs[:, nt * 128:(nt + 1) * 128], [sgm], bl, vcs[:, nt, h, :], [vcs]))
                    acc = accO.get()
                    attn_pass(h, qz, units, acc, extras=[(accI, lambda ui: cmpsel_s[:, ui, 0:128], 0, 3, [cmpsel_p]),
                                                         (ypsb, lambda ui: cmpsel_s[:, ui, 128:256], 3, 7, [cmpsel_p])])
                    rz = finish_pass(h, 0, acc, gate, oattn, first=True, clampz=True)
                    imp_p, imp2_p = big.bufs[0], big.bufs[1]
                    imp, imp2 = _View(imp_p.ap[:, 0:256]), _View(imp2_p.ap[:, 0:256])
                    for t_, accb in enumerate((accI, ypsb)):
                        IT = OTr.get()
                        op("act", lambda e: e.copy(out=IT[:], in_=accb[:, :]), reads=[accb], writes=[IT])
                        tpi = ring.get()
                        tpiv = tpi.ap[:].rearrange("p (g c) -> p g c", c=128)
                        for g in range(4):
                            op("pe", lambda e: e.transpose(out=tpiv[:, g, :], in_=IT[:, g * 128:(g + 1) * 128], identity=identf[:]),
                               reads=[IT, identf], writes=[tpi])
                        dsti = imp[:, t_ * 128:(t_ + 1) * 128]
                        op("dve", lambda e: e.tensor_scalar(out=dsti, in0=tpiv[:, 0, :], scalar1=rz[:, 0:1], scalar2=None,
                                                            op0=ALU.mult), reads=[tpi, rz], writes=[imp_p])
                        for g in range(1, 4):
                            op("dve", lambda e: e.scalar_tensor_tensor(out=dsti, in0=tpiv[:, g, :], scalar=rz[:, g:g + 1],
                                                                       in1=dsti, op0=ALU.mult, op1=ALU.add),
                               reads=[tpi, rz, imp_p], writes=[imp_p])
                    op("dve", lambda e: e.memset(imp[:, 0:1], 100.0), writes=[imp_p])
                    m1, m2 = smr.get(), smr.get()
                    op("dve", lambda e: e.max(out=m1[:], in_=imp[:]), reads=[imp_p], writes=[m1])
                    op("dve", lambda e: e.match_replace(out=imp2[:], in_to_replace=m1[:], in_values=imp[:], imm_value=-1e9),
                       reads=[m1, imp_p], writes=[imp2_p])
                    op("dve", lambda e: e.max(out=m2[:], in_=imp2[:]), reads=[imp2_p], writes=[m2])
                    op("dve", lambda e: e.tensor_scalar(out=selmb_s[:], in0=imp[:], scalar1=m2[:, 6:7], scalar2=-1.0,
                                                        op0=ALU.is_ge, op1=ALU.add), reads=[imp_p, m2], writes=[selmb_s])
                    for t_ in range(2):
                        pst = ring.get()
                        pstv = bfview(pst)[:, 0:128]
                        op("pe", lambda e: e.transpose(out=pstv, in_=selmb_s[:, t_ * 128:(t_ + 1) * 128], identity=identb[:]),
                           reads=[selmb_s, identb], writes=[pst])
                        op("act", lambda e: e.copy(out=selmT_s[:, t_ * 2 + h, :], in_=pstv), reads=[pst], writes=[selmT_s])
                dma(ptb2[:], pt_d[b], writes=[ptb2])
                op("dve", lambda e: e.tensor_copy(out=ptf2[:], in_=ptb2[:]), reads=[ptb2], writes=[ptf2])
                op("dve", lambda e: e.tensor_scalar(out=ptf2[:], in0=ptf2[:], scalar1=256.0, scalar2=io3[:, 0:1],
                                                    op0=ALU.mult, op1=ALU.add), reads=[ptf2, io3], writes=[ptf2])
                op("dve", lambda e: e.tensor_copy(out=idxs[:], in_=ptf2[:]), reads=[ptf2], writes=[idxs])

                def tile_from_rows(pg, k):
                    kb = kbr.get()
                    Vt = Vtr.get()
                    if k % 2 == 0:
                        op("dve", lambda e: e.tensor_copy(out=kb[:], in_=pg[:, 0:128]), reads=[pg], writes=[kb])
                        op("act", lambda e: e.copy(out=Vt[:, :, 0:64], in_=pg[:, 128:256].rearrange("p (h d) -> p h d", d=64)),
                           reads=[pg], writes=[Vt])
                    else:
                        op("act", lambda e: e.copy(out=kb[:], in_=pg[:, 0:128]), reads=[pg], writes=[kb])
                        op("dve", lambda e: e.tensor_copy(out=Vt[:, :, 0:64], in_=pg[:, 128:256].rearrange("p (h d) -> p h d", d=64)),
                           reads=[pg], writes=[Vt])
                    pt2 = ring.get()
                    p2v = bfview(pt2)[:, 0:128]
                    op("pe", lambda e: e.transpose(out=p2v, in_=kb[:], identity=identb[:]), reads=[kb, identb], writes=[pt2])
                    KT = KTr.get()
                    if k % 2 == 0:
                        op("act", lambda e: e.copy(out=KT[:], in_=p2v), reads=[pt2], writes=[KT])
                    else:
                        op("dve", lambda e: e.tensor_copy(out=KT[:], in_=p2v), reads=[pt2], writes=[KT])
                    return KT, Vt

                def prep_sel(kt):
                    if kt < 128:
                        pg = pgr2.get()
                        fw.idma(pg[:], cache_d, idxs[:, kt:kt + 1], reads=[idxs], writes=[pg])
                        KT, Vt = tile_from_rows(pg, kt)
                        return KT[:], [KT], (lambda h: (Vt[:, h, :], [Vt]))
                    return nkt[:, 0:128], [nkt], (lambda h: (nvt[:, h * 65:(h + 1) * 65], [nvt]))

                def bias_sel(kt, h):
                    if kt < 128:
                        t_ = kt // 64
                        return [(Esb[:, (kt % 64) * 128:(kt % 64 + 1) * 128],
                                 selmT_s.ap[:, t_ * 2 + h:t_ * 2 + h + 1, :].to_broadcast([128, 4, 128]), [Esb, selmT_s])]
                    return [(identb[:], caus.ap[:].unsqueeze(1).to_broadcast([128, 4, 128]), [identb, caus])]

                stream_pass(129, prep_sel, bias_sel, qz, gate, oattn, 1)

                def prep_win(r):
                    if r < 4:
                        pg = pgr2.get()
                        dma(pg[:], cwin_d[b, r * 128:(r + 1) * 128, :], writes=[pg])
                        KT, Vt = tile_from_rows(pg, r)
                        return KT[:], [KT], (lambda h: (Vt[:, h, :], [Vt]))
                    return nkt[:, 128:256], [nkt], (lambda h: (nvt[:, 130 + h * 65:130 + (h + 1) * 65], [nvt]))

                def bias_win(r, h):
                    if r == 0:
                        return [(identb[:], swb0.ap[:].unsqueeze(1).to_broadcast([128, 4, 128]), [identb, swb0])]
                    if r == 4:
                        return [(identb[:], caus.ap[:].unsqueeze(1).to_broadcast([128, 4, 128]), [identb, caus])]
                    return []

                stream_pass(5, prep_win, bias_win, qz, gate, oattn, 2)

            for bg in range(NBT):
                uT = s5_block(bg)
                if bg >= NBP:
                    own_block(bg - NBP, uT)
            sso = fw.sbuf("sso", [128, 2, 16], F32)
            op("dve", lambda e: e.tensor_copy(out=sso[:, 0, :], in_=hre_p[:]), reads=[hre_p], writes=[sso])
            op("dve", lambda e: e.tensor_copy(out=sso[:, 1, :], in_=him_p[:]), reads=[him_p], writes=[sso])
            dma(ssm_o, sso[:], reads=[sso])
            for b in range(NSB):
                dma(stt[:], sst_d[b], writes=[stt])
                op("dve", lambda e: e.tensor_copy(out=hre_p[:], in_=stt[:, 0, :]), reads=[stt], writes=[hre_p])
                op("dve", lambda e: e.tensor_copy(out=him_p[:], in_=stt[:, 1, :]), reads=[stt], writes=[him_p])
                uT = s5_block(None, samp=b)
                op("dve", lambda e: e.tensor_copy(out=sso[:, 0, :], in_=hre_p[:]), reads=[hre_p], writes=[sso])
                op("dve", lambda e: e.tensor_copy(out=sso[:, 1, :], in_=him_p[:]), reads=[him_p], writes=[sso])
                dma(ssms_o[b], sso[:], reads=[sso])
                own_block(b, uT, samp=b)
            fw.barrier()
        fw.stack = top
        pstore.close()
        fw.finish()
    return nc, fw


def _bf(a):
    return np.ascontiguousarray(a).astype(ml_dtypes.bfloat16)


def host_consts(NBP, NBO, half, NSB=4):
    NBT = NBP + NBO
    c = {}
    n = np.arange(512) - 1
    blk = np.arange(128)
    c0 = n[:, None] * 16
    s0 = blk[None, :] * 64
    shared = np.minimum(c0 + 32, s0 + 64) - np.maximum(c0, s0)
    cs = np.clip(shared, 0, None).astype(np.float32) / 32.0
    cs[0, :] = 0.0
    c["cmpsel"] = _bf(cs.reshape(4, 128, 128).transpose(1, 0, 2))
    first_real_tok = 0 if half == 1 else NBP * 128
    q = np.arange(128)
    cmpb = np.zeros((NBO, 4, 128, 128), np.float32)
    for i in range(NBO):
        qpos = (NBP + i) * 128 + q
        for nt in range(4):
            nn = nt * 128 + np.arange(128) - 1
            vis = (nn[:, None] * 16 + 31 <= qpos[None, :]) & (nn[:, None] >= 0) & (nn[:, None] * 16 >= first_real_tok)
            cmpb[i, nt] = np.where(vis, 0.0, NEG)
    c["cmpb"] = _bf(cmpb)
    winb = np.zeros((NBO, 5, 128, 128), np.float32)
    for i in range(NBO):
        bg = NBP + i
        qpos = bg * 128 + q
        for r in range(5):
            kpos = (bg - 4 + r) * 128 + np.arange(128)
            dist = qpos[None, :] - kpos[:, None]
            vis = (kpos[:, None] >= first_real_tok) & (dist >= 0) & (dist < 512)
            winb[i, r] = np.where(vis, 0.0, NEG)
    c["winb"] = _bf(winb)
    tka = np.zeros((NBO, 128, 128), np.float32)
    tkb = np.zeros((NBO, 128, 128), np.float32)
    fb = first_real_tok // 64
    for i in range(NBO):
        qpos = (NBP + i) * 128 + q
        valid = (blk[None, :] * 64 <= qpos[:, None]) & (blk[None, :] >= fb)
        forced = (blk[None, :] == fb) | (blk[None, :] == (qpos // 64)[:, None])
        tka[i] = valid.astype(np.float32)
        tkb[i] = np.where(valid, np.where(forced, 100.0, 0.0), -100.0)
    c["tka"], c["tkb"] = tka, tkb
    key = np.arange(128)
    c["caus"] = _bf(np.where(key[:, None] <= q[None, :], 0.0, NEG))
    EW = max(NBT, 64 if NSB else 0) * 128
    E = np.zeros((128, EW), np.float32)
    kk = np.arange(EW)
    E[kk // 64, kk] = -NEG
    c["E"] = _bf(E)
    mgh = np.zeros((128, 2), np.float32)
    mgh[:64, 0] = 1.0
    mgh[64:, 1] = 1.0
    c["mgh"] = mgh
    n = np.arange(1024) - 1
    blk = np.arange(256)
    c0 = n[:, None] * 16
    s0 = blk[None, :] * 64
    shared = np.minimum(c0 + 32, s0 + 64) - np.maximum(c0, s0)
    cs = np.clip(shared, 0, None).astype(np.float32) / 32.0
    cs[0, :] = 0.0
    c["cmpsel_s"] = _bf(cs.reshape(8, 128, 256).transpose(1, 0, 2))
    cb0 = np.zeros((128, 128), np.float32)
    cb0[0, :] = NEG
    c["cb0"] = _bf(cb0)
    c["swb0"] = _bf(np.where(key[:, None] > q[None, :], 0.0, NEG))
    return c


def host_weights(inp):
    w = {}
    w_in = inp["w_in"][0]
    wf = w_in[:, wf_cols()]
    wt = w_in[:, wt_cols()]
    w["wf"] = np.ascontiguousarray(wf.reshape(8, 128, NWF).transpose(1, 0, 2))
    w["wt"] = np.ascontiguousarray(wt.reshape(8, 128, NWT).transpose(1, 0, 2))
    w["ng"] = np.ascontiguousarray(inp["norm_g"][0].reshape(8, 128).T)
    w["fg"] = np.ascontiguousarray(np.broadcast_to(inp["final_g"][None, :], (128, D)))
    w1 = inp["cmp_w1"][0]
    w1l = w1.transpose(2, 0, 1, 3)
    w["w1"] = np.ascontiguousarray(np.concatenate([w1l, w1l], 0))
    w2 = inp["cmp_w2"][0].transpose(1, 0, 2)
    w["w2"] = np.ascontiguousarray(np.concatenate([w2, w2], 0))
    pe = inp["cmp_pe"][0].transpose(2, 0, 1)
    w["pe"] = np.ascontiguousarray(np.concatenate([pe, pe], 0))
    b1 = inp["cmp_b1"][0].T
    w["b1"] = np.ascontiguousarray(np.concatenate([b1, b1], 0))

    def st_layout(a):
        a = a.reshape((16, 2, 64) + a.shape[2:])
        a = np.moveaxis(a, 0, 2)
        return np.ascontiguousarray(a.reshape((128, 16) + a.shape[3:]))
    w["lamre"] = st_layout(inp["ssm_lam_re"][0])
    w["lamim"] = st_layout(inp["ssm_lam_im"][0])
    w["logdt"] = st_layout(np.broadcast_to(inp["ssm_log_dt"][0][:, None], (32, 64)))
    w["bre"] = st_layout(inp["ssm_b_re"][0])
    w["bim"] = st_layout(inp["ssm_b_im"][0])
    for nm, key in (("cre", "ssm_c_re"), ("cim", "ssm_c_im")):
        cc = inp[key][0].transpose(0, 2, 1)
        cl = st_layout(cc)
        bd = np.zeros((128, 16, 4, 2, 16), np.float32)
        for st in range(16):
            bd[:64, st, st % 4, 0, :] = cl[:64, st]
            bd[64:, st, st % 4, 1, :] = cl[64:, st]
        w[nm] = bd.reshape(128, 16, 128)
    w["dsk"] = np.ascontiguousarray(inp["ssm_d"][0].reshape(4, 128).T)
    w["bgl"] = np.ascontiguousarray(inp["b_glu"][0].reshape(4, 128).T)
    w["wglu"] = np.ascontiguousarray(inp["w_glu"][0].reshape(4, 128, 512).transpose(1, 0, 2))
    w["wla"] = np.ascontiguousarray(inp["w_lift_attn"][0].reshape(4, 128, D).transpose(1, 0, 2))
    w["wls"] = np.ascontiguousarray(inp["w_lift_ssm"][0].reshape(4, 128, D).transpose(1, 0, 2))
    w["wo"] = np.ascontiguousarray(inp["w_out"][0].reshape(8, 128, D).transpose(1, 0, 2))
    return w


def make_in_maps(inp, NBP, NBO, NSB=4):
    w = host_weights(inp)
    consts = [host_consts(NBP, NBO, 0, NSB), host_consts(NBP, NBO, 1, NSB)]
    NS1 = max(NSB, 1)
    cache = np.ascontiguousarray(inp["cache_kv"][0]).reshape(-1, 256)

    def st_layout(a):
        a = a.reshape((16, 2, 64))
        a = np.moveaxis(a, 0, 2)
        return a.reshape(128, 16)
    maps = []
    TP, TO = NBP * 128, NBO * 128
    for c in range(8):
        s, half = c // 2, c % 2
        m = dict(w)
        m.update(consts[half])
        xs = inp["x_prompt"][s]
        m["xo"] = np.ascontiguousarray(xs[half * TP:half * TP + TO])
        m["xp"] = np.ascontiguousarray(xs[0:TP]) if half == 1 else np.zeros((TP, D), np.float32)
        bs = [min(NS1 * c + j, 31) for j in range(NS1)]
        xsp = np.zeros((NS1, 128, D), np.float32)
        xsp[:, 0:8, :] = inp["x_sample"][bs]
        m["xs"] = xsp
        m["cache"] = cache
        m["ptab"] = np.ascontiguousarray(np.broadcast_to(inp["page_table"][bs][:, None, :], (NS1, 128, 128))).astype(np.int32)
        m["cwin"] = np.ascontiguousarray(inp["cache_win_kv"][0][bs]).reshape(NS1, 512, 256)
        sst = np.zeros((NS1, 128, 2, 16), np.float32)
        for j, b in enumerate(bs):
            sst[j, :, 0, :] = st_layout(inp["state_ssm_re"][0, b])
            sst[j, :, 1, :] = st_layout(inp["state_ssm_im"][0, b])
        m["sst"] = sst
        maps.append(m)
    return maps


def _unstate(a):
    return a.reshape(2, 64, 16).transpose(2, 0, 1).reshape(32, 64)


_CACHE = {}


def kernel(**inp):
    inp = {k: np.asarray(v) for k, v in inp.items()}
    NB, NSB = 32, 4
    if "nc" not in _CACHE:
        _CACHE["nc"] = build(NB, NB, NSB, inp["cache_kv"].shape[1])[0]
    nc = _CACHE["nc"]
    maps = make_in_maps(inp, NB, NB, NSB)
    res = run_bass_kernel_spmd(nc, maps, core_ids=list(range(8))).results
    T = NB * 128
    y_p = np.zeros((4, 8192, D), np.float32)
    kv_p = np.zeros((1, 4, 8192, 4, 2, 64), np.float32)
    win_p = np.zeros((1, 4, 512, 2, 2, 64), np.float32)
    sre_p = np.zeros((1, 4, 32, 64), np.float32)
    sim_p = np.zeros((1, 4, 32, 64), np.float32)
    y_s = np.zeros((32, 8, D), np.float32)
    kv_s = np.zeros((1, 32, 8, 4, 2, 64), np.float32)
    win_s = np.zeros((1, 32, 512, 2, 2, 64), np.float32)
    sre_s = np.zeros((1, 32, 32, 64), np.float32)
    sim_s = np.zeros((1, 32, 32, 64), np.float32)
    for c in range(8):
        s_, half = c // 2, c % 2
        r = res[c]
        y_p[s_, half * T:(half + 1) * T] = r["y"]
        kv_p[0, s_, half * T:(half + 1) * T] = r["kvp"].reshape(T, 4, 2, 64)
        if half == 1:
            win_p[0, s_] = r["winp"].reshape(512, 2, 2, 64)
            sre_p[0, s_] = _unstate(r["ssmp"][:, 0, :])
            sim_p[0, s_] = _unstate(r["ssmp"][:, 1, :])
        for j in range(NSB):
            b = NSB * c + j
            y_s[b] = r["ys"][j]
            kv_s[0, b] = r["kvs"][j].reshape(8, 4, 2, 64)
            win_s[0, b] = r["wins"][j].reshape(512, 2, 2, 64)
            sre_s[0, b] = _unstate(r["ssms"][j, :, 0, :])
            sim_s[0, b] = _unstate(r["ssms"][j, :, 1, :])
    return (y_p, y_s, kv_p, win_p, sre_p, sim_p, kv_s, win_s, sre_s, sim_s)
```

```python
import contextlib
import os
import math
import numpy as np
import ml_dtypes
import concourse.bass as bass
import concourse.mybir as mybir
from concourse.bass_utils import run_bass_kernel_spmd

F32 = mybir.dt.float32
BF16 = mybir.dt.bfloat16
I32 = mybir.dt.int32
AF = mybir.ActivationFunctionType
ALU = mybir.AluOpType
AX = mybir.AxisListType

EPOCH = int(os.environ.get("EPOCH", "3000"))
NEG = -30000.0
GC = math.sqrt(2.0 / math.pi)


class Eng:
    def __init__(self, fw, name, raw):
        self.fw, self.name, self.raw = fw, name, raw
        self.sems = []
        self.count = 0
        self.known = {}

    def sem_for(self, epoch):
        while len(self.sems) <= epoch:
            self.sems.append(self.fw.new_sem(f"s_{self.name}_{len(self.sems)}"))
        return self.sems[epoch]


class Buf:
    def __init__(self, ap, name="", excl=False):
        self.ap = ap
        self.name = name
        self.writer = None
        self.readers = {}
        self.excl = excl

    def __getitem__(self, idx):
        return self.ap[idx]


class FW:
    def __init__(self, nc, stack):
        self.nc = nc
        self.stack = stack
        self.semstack = stack
        self.engs = {}
        for n, raw in (("pe", nc.tensor), ("act", nc.scalar), ("dve", nc.vector),
                       ("pool", nc.gpsimd), ("sp", nc.sync)):
            self.engs[n] = Eng(self, n, raw)
        self.dma_slots = []
        self.dma_next = 0
        self.n_dma_slots = 32
        self.ninst = 0

    def new_sem(self, name):
        return self.semstack.enter_context(self.nc.semaphore(name))

    def barrier(self):
        for e in self.engs.values():
            for e2 in self.engs.values():
                if e2 is e or e2.count == 0:
                    continue
                self._need(e, (e2, e2.count))
            for si, slot in enumerate(self.dma_slots):
                if slot[1] > 0:
                    self._need(e, ('dma', slot[0], slot[1], f"dma{si}"))

    def sbuf(self, name, shape, dtype):
        t = self.stack.enter_context(self.nc.sbuf_tensor("sb_" + name, list(shape), dtype))
        return Buf(t, name)

    def psum(self, name, shape, dtype):
        t = self.stack.enter_context(self.nc.psum_tensor("pp_" + name, list(shape), dtype))
        return Buf(t, name, excl=True)

    def _need(self, eng, dep):
        if dep is None:
            return
        if dep[0] == 'dma':
            _, sem, val, key = dep
            if eng.known.get(key, 0) >= val:
                return
            eng.raw.wait_ge(sem, val)
            eng.known[key] = val
            return
        e2, cnt = dep
        if e2 is eng and eng.name == "pe":
            return
        if eng.known.get(e2.name, 0) >= cnt:
            return
        ep = (cnt - 1) // EPOCH
        eng.raw.wait_ge(e2.sem_for(ep), cnt - ep * EPOCH)
        eng.known[e2.name] = cnt

    def _deps(self, eng, reads, writes):
        for b in reads:
            self._need(eng, b.writer)
            if b.excl:
                for r in list(b.readers.values()):
                    self._need(eng, r)
        for b in writes:
            self._need(eng, b.writer)
            for r in list(b.readers.values()):
                self._need(eng, r)

    def op(self, engname, fn, reads=(), writes=()):
        eng = self.engs[engname]
        self._deps(eng, reads, writes)
        inst = fn(eng.raw)
        eng.count += 1
        ep = (eng.count - 1) // EPOCH
        inst.then_inc(eng.sem_for(ep), 1)
        tag = (eng, eng.count)
        for b in reads:
            b.readers[eng.name] = tag
        for b in writes:
            b.writer = tag
            b.readers = {}
        self.ninst += 1
        return inst

    def dma(self, out_ap, in_ap, reads=(), writes=(), q="sp", **kw):
        eng = self.engs[q]
        self._deps(eng, reads, writes)
        if len(self.dma_slots) < self.n_dma_slots:
            self.dma_slots.append([self.new_sem(f"d{len(self.dma_slots)}"), 0])
        si = self.dma_next % self.n_dma_slots
        self.dma_next += 1
        slot = self.dma_slots[si]
        key = f"dma{si}"
        if slot[1] > 0 and eng.known.get(key, 0) < slot[1]:
            eng.raw.wait_ge(slot[0], slot[1])
            eng.known[key] = slot[1]
        slot[1] += 16
        eng.raw.dma_start(out=out_ap, in_=in_ap, **kw).then_inc(slot[0], 16)
        tag = ('dma', slot[0], slot[1], key)
        for b in reads:
            b.readers[key] = tag
        for b in writes:
            b.writer = tag
            b.readers = {}
        self.ninst += 1

    def idma(self, out_ap, rows_ap, idx_ap, reads=(), writes=()):
        eng = self.engs["pool"]
        self._deps(eng, reads, writes)
        if len(self.dma_slots) < self.n_dma_slots:
            self.dma_slots.append([self.new_sem(f"d{len(self.dma_slots)}"), 0])
        si = self.dma_next % self.n_dma_slots
        self.dma_next += 1
        slot = self.dma_slots[si]
        key = f"dma{si}"
        if slot[1] > 0 and eng.known.get(key, 0) < slot[1]:
            eng.raw.wait_ge(slot[0], slot[1])
            eng.known[key] = slot[1]
        slot[1] += 16
        eng.raw.indirect_dma_start(out=out_ap, out_offset=None, in_=rows_ap,
                                   in_offset=bass.IndirectOffsetOnAxis(ap=idx_ap, axis=0)).then_inc(slot[0], 16)
        tag = ('dma', slot[0], slot[1], key)
        for b in reads:
            b.readers[key] = tag
        for b in writes:
            b.writer = tag
            b.readers = {}
        self.ninst += 1

    def finish(self):
        eng = self.engs["sp"]
        for slot in self.dma_slots:
            if slot[1] > 0:
                eng.raw.wait_ge(slot[0], slot[1])


class Ring:
    def __init__(self, bufs):
        self.bufs = bufs
        self.i = 0

    def get(self):
        b = self.bufs[self.i % len(self.bufs)]
        self.i += 1
        return b


D = 1024
NH, NKV, G, DH = 8, 2, 4, 64
D_ATTN = 512
D_SSM = 512
NST = 16
O_Q, O_KV, O_GN, O_ZA, O_U, O_ZS, O_GM = 0, 512, 1280, 1304, 1816, 2328, 2840
NWF = 1536
NWT = 3352
T_GROUPS = [(0, 512), (512, 792), (792, 1304), (1304, 1816), (1816, 2328), (2328, 2840), (2840, 3352)]


def wf_cols():
    q = [h * 256 + g * 64 + d for g in range(4) for h in range(2) for d in range(64)]
    return np.array(q + list(range(O_U, O_U + 512)) + list(range(O_ZS, O_ZS + 512)))


def wt_cols():
    return np.array(list(range(O_KV, O_KV + 768)) + list(range(O_GN, O_GN + 24)) +
                    list(range(O_ZA, O_ZA + 512)) + list(range(O_GM, O_GM + 2048)))


import os
DBG = int(os.environ.get("DBG_STAGE", "99"))
SKIP = os.environ.get("DBG_SKIP", "")


def build(NBP, NBO, NSB=4, NPHYS=5120):
    NBT = NBP + NBO
    TP, TO = NBP * 128, NBO * 128
    NWIN = min(4, NBO)
    nc = bass.Bass("TRN2", target_bir_lowering=False)

    def din(name, shape, dt=F32):
        return nc.dram_tensor(name, list(shape), dt, kind="ExternalInput").ap()

    def dout(name, shape, dt=F32):
        return nc.dram_tensor(name, list(shape), dt, kind="ExternalOutput").ap()

    def dscr(name, shape, dt):
        return nc.dram_tensor(name, list(shape), dt, kind="Internal").ap()

    xo = din("xo", [TO, D])
    xp = din("xp", [TP, D])
    wf_d = din("wf", [128, 8, NWF])
    wt_d = din("wt", [128, 8, NWT])
    ng_d = din("ng", [128, 8])
    fg_d = din("fg", [128, D])
    w1_d = din("w1", [128, 2, 32, 64])
    w2_d = din("w2", [128, 2, 64])
    pe_d = din("pe", [128, 2, 32])
    b1_d = din("b1", [128, 2])
    cmpsel_d = din("cmpsel", [128, 4, 128], BF16)
    cmpb_d = din("cmpb", [NBO, 4, 128, 128], BF16)
    winb_d = din("winb", [NBO, 5, 128, 128], BF16)
    tka_d = din("tka", [NBO, 128, 128])
    tkb_d = din("tkb", [NBO, 128, 128])
    caus_d = din("caus", [128, 128], BF16)
    EW = max(NBT, 64 if NSB else 0) * 128
    E_d = din("E", [128, EW], BF16)
    lamre_d = din("lamre", [128, 16])
    lamim_d = din("lamim", [128, 16])
    logdt_d = din("logdt", [128, 16])
    bre_d = din("bre", [128, 16, 16])
    bim_d = din("bim", [128, 16, 16])
    cre_d = din("cre", [128, 16, 128])
    cim_d = din("cim", [128, 16, 128])
    mgh_d = din("mgh", [128, 2])
    dsk_d = din("dsk", [128, 4])
    bgl_d = din("bgl", [128, 4])
    wglu_d = din("wglu", [128, 4, 512])
    wla_d = din("wla", [128, 4, D])
    wls_d = din("wls", [128, 4, D])
    wo_d = din("wo", [128, 8, D])

    y_o = dout("y", [TO, D])
    kv_o = dout("kvp", [TO, 512])
    win_o = dout("winp", [NWIN * 128, 256])
    ssm_o = dout("ssmp", [128, 2, 16])

    sc_q = dscr("sc_q", [NBO, 128, 512], BF16)
    sc_u = dscr("sc_u", [NBT, 128, 512], BF16)
    sc_zs = dscr("sc_zs", [NBO, 128, 512], BF16)
    sc_za = dscr("sc_za", [NBO, 128, 512], BF16)
    sc_gm = dscr("sc_gm", [NBO, 128, 2048], BF16)
    sc_gn = dscr("sc_gn", [NBO, 128, 24], F32)
    NS1 = max(NSB, 1)
    xs_d = din("xs", [NS1, 128, D])
    cache_d = din("cache", [NPHYS * 256, 256])
    pt_d = din("ptab", [NS1, 128, 128], I32)
    cwin_d = din("cwin", [NS1, 512, 256])
    sst_d = din("sst", [NS1, 128, 2, 16])
    cmpsel_s_d = din("cmpsel_s", [128, 8, 256], BF16)
    cb0_d = din("cb0", [128, 128], BF16)
    swb0_d = din("swb0", [128, 128], BF16)
    ys_o = dout("ys", [NS1, 8, D])
    kvs_o = dout("kvs", [NS1, 8, 512])
    wins_o = dout("wins", [NS1, 512, 256])
    ssms_o = dout("ssms", [NS1, 128, 2, 16])
    sc_q_s = dscr("sc_q_s", [NS1, 128, 512], BF16)
    sc_u_s = dscr("sc_u_s", [NS1, 128, 512], BF16)
    sc_zs_s = dscr("sc_zs_s", [NS1, 128, 512], BF16)
    sc_za_s = dscr("sc_za_s", [NS1, 128, 512], BF16)
    sc_gm_s = dscr("sc_gm_s", [NS1, 128, 2048], BF16)
    sc_gn_s = dscr("sc_gn_s", [NS1, 128, 24], F32)
    sc_nk = dscr("sc_nk", [NS1, 128, 256], BF16)
    sc_nv = dscr("sc_nv", [NS1, 128, 260], BF16)
    sc_kc = dscr("sc_kc", [NS1, 128, 1024], BF16)
    sc_vcT = dscr("sc_vcT", [NS1, 128, 1024], BF16)
    sc_ys_s = dscr("sc_ys_s", [NS1, 128, 512], BF16)
    sc_wk = dscr("sc_wk", [NBT, 128, 128], BF16)
    sc_wv = dscr("sc_wv", [NBT, 128, 130], BF16)

    with contextlib.ExitStack() as top:
        fw = FW(nc, top)
        op, dma = fw.op, fw.dma

        identb = fw.sbuf("identb", [128, 128], BF16)
        identf = fw.sbuf("identf", [128, 128], F32)
        ps_bufs = [fw.psum(f"ps{i}", [128, 512], F32) for i in range(8)]
        pstore = contextlib.ExitStack()
        pstore.__enter__()
        fw.stack = pstore
        selKT = fw.sbuf("selKT", [128, NBT * 128], BF16)
        selV = fw.sbuf("selV", [128, NBT, 2, 65], BF16)
        winKT = fw.sbuf("winKT", [128, 8 * 128], BF16)
        winV = fw.sbuf("winV", [128, 8, 2, 65], BF16)
        kcT = fw.sbuf("kcT", [128, 512], BF16)
        vcT = fw.sbuf("vcT", [128, 512], BF16)
        vc = fw.sbuf("vc", [128, 4, 2, 65], BF16)
        cmpsel = fw.sbuf("cmpsel", [128, 4, 128], BF16)
        ring = Ring(ps_bufs[4:])
        accO = Ring(ps_bufs[0:2])
        accI = ps_bufs[2]
        ypsb = ps_bufs[3]

        def bfview(b):
            return b.ap[:].bitcast(BF16)

        op("pool", lambda e: e.memset(identf[:], 1.0), writes=[identf])
        op("pool", lambda e: e.affine_select(out=identf[:], in_=identf[:], pattern=[[-1, 128]],
                                             compare_op=ALU.is_equal, fill=0.0, base=0,
                                             channel_multiplier=1), reads=[identf], writes=[identf])
        op("dve", lambda e: e.tensor_copy(out=identb[:], in_=identf[:]), reads=[identf], writes=[identb])
        op("pool", lambda e: e.memset(selV[:], 1.0), writes=[selV])
        op("pool", lambda e: e.memset(winV[:], 1.0), writes=[winV])
        op("pool", lambda e: e.memset(vc[:], 1.0), writes=[vc])
        op("pool", lambda e: e.memset(kcT[:], 0.0), writes=[kcT])
        op("pool", lambda e: e.memset(vcT[:], 0.0), writes=[vcT])
        op("pool", lambda e: e.memset(winKT[:], 0.0), writes=[winKT])
        dma(cmpsel[:], cmpsel_d, writes=[cmpsel])

        with contextlib.ExitStack() as p1:
            fw.stack = p1
            WF = fw.sbuf("WF", [128, 8, NWF], BF16)
            WT = fw.sbuf("WT", [128, 8, NWT], BF16)
            ngt = fw.sbuf("ngt", [128, 8], F32)
            W1r = fw.sbuf("W1r", [128, 2, 32, 64], BF16)
            W2r = fw.sbuf("W2r", [128, 2, 64], BF16)
            peT = fw.sbuf("peT", [128, 2, 32], BF16)
            b1p = fw.sbuf("b1p", [128, 2], F32)
            cmpraw = fw.sbuf("cmpraw", [128, 2, 144], BF16)
            stg = Ring([fw.sbuf(f"stg{i}", [128, 1024], F32) for i in range(2)])
            dma(ngt[:], ng_d, writes=[ngt])
            k = 0
            for (wd, W, ncol) in ((wf_d, WF, NWF), (wt_d, WT, NWT)):
                for c in range(8):
                    for c0 in range(0, ncol, 1024):
                        c1 = min(ncol, c0 + 1024)
                        s = stg.get()
                        dma(s[:, 0:c1 - c0], wd[:, c, c0:c1], writes=[s])
                        en = "dve" if k % 2 == 0 else "pool"
                        k += 1
                        op(en, lambda e: e.tensor_scalar(out=W[:, c, c0:c1], in0=s[:, 0:c1 - c0],
                                                         scalar1=ngt[:, c:c + 1], scalar2=1.0, op0=ALU.mult, op1=ALU.mult),
                           reads=[s, ngt], writes=[W])
            for kv in range(2):
                for jh in range(2):
                    s = stg.get()
                    dma(s[:, 0:1024].rearrange("p (j e) -> p j e", e=64), w1_d[:, kv, jh * 16:(jh + 1) * 16], writes=[s])
                    op("dve", lambda e: e.tensor_copy(out=W1r[:, kv, jh * 16:(jh + 1) * 16],
                                                      in_=s[:, 0:1024].rearrange("p (j e) -> p j e", e=64)),
                       reads=[s], writes=[W1r])
            s = stg.get()
            dma(s[:, 0:128].rearrange("p (k e) -> p k e", e=64), w2_d, writes=[s])
            op("dve", lambda e: e.tensor_copy(out=W2r[:], in_=s[:, 0:128].rearrange("p (k e) -> p k e", e=64)),
               reads=[s], writes=[W2r])
            s = stg.get()
            dma(s[:, 0:64].rearrange("p (k e) -> p k e", e=32), pe_d, writes=[s])
            op("dve", lambda e: e.tensor_copy(out=peT[:], in_=s[:, 0:64].rearrange("p (k e) -> p k e", e=32)),
               reads=[s], writes=[peT])
            dma(b1p[:], b1_d, writes=[b1p])
            op("pool", lambda e: e.memset(cmpraw[:], 0.0), writes=[cmpraw])
            pb = ring.get()
            for kv in range(2 if DBG >= 2 else 0):
                for h in range(2):
                    rows = slice(h * 64, (h + 1) * 64)
                    for j in range(32):
                        op("pe", lambda e: e.matmul(pb[rows, kv:kv + 1], lhsT=W1r[rows, kv, j, :],
                                                    rhs=peT[rows, kv, j:j + 1], start=(j == 0), stop=(j == 31)),
                           reads=[W1r, peT], writes=[pb])
            if DBG >= 2:
                op("dve", lambda e: e.tensor_tensor(out=b1p[:], in0=pb[:, 0:2], in1=b1p[:], op=ALU.add),
                   reads=[pb, b1p], writes=[b1p])

            xr = Ring([fw.sbuf(f"xt{i}", [128, D], F32) for i in range(2)])
            junk = fw.sbuf("junk", [128, D], BF16)
            ssr = Ring([fw.sbuf(f"ss{i}", [128, 1], F32) for i in range(2)])
            xbr = Ring([fw.sbuf(f"xb{i}", [128, D], BF16) for i in range(2)])
            xTr = Ring([fw.sbuf(f"xT{i}", [128, 8, 128], BF16) for i in range(2)])
            kvor = Ring([fw.sbuf(f"kvo{i}", [128, 512], F32) for i in range(2)])
            kv1r = Ring([fw.sbuf(f"kv1f{i}", [128, 280], F32) for i in range(2)])
            kvbr = Ring([fw.sbuf(f"kvb{i}", [128, 768], BF16) for i in range(2)])
            zar = Ring([fw.sbuf(f"za{i}", [128, 512], BF16) for i in range(2)])
            gmr = Ring([fw.sbuf(f"gmb{i}", [128, 2048], BF16) for i in range(2)])
            f4r = Ring([fw.sbuf(f"f4{i}", [128, 4, 128], BF16) for i in range(3)])
            hidr = Ring([fw.sbuf(f"hid{i}", [128, 4, 128], F32) for i in range(2)])
            gTr = Ring([fw.sbuf(f"gT{i}", [128, 2, 64], BF16) for i in range(2)])
            tmpK = fw.sbuf("tmpK", [128, 2, 128], BF16)
            tmpV = fw.sbuf("tmpV", [128, 2, 2, 65], BF16)
            op("pool", lambda e: e.memset(tmpV[:], 1.0), writes=[tmpV])

            def compress(raw, nb, dst_k, dst_v, col0):
                W = 2 * nb
                hp = ring.get()
                hv = hp.ap[:, 0:W].rearrange("p (k m) -> p k m", m=nb)
                cr = raw.ap[:].rearrange("p k (m s) -> p k m s", s=16)
                for kv in range(2):
                    for h in range(2):
                        rows = slice(h * 64, (h + 1) * 64)
                        for j in range(32):
                            rhs = cr[rows, kv, 0:nb, j] if j < 16 else cr[rows, kv, 1:nb + 1, j - 16]
                            op("pe", lambda e: e.matmul(hv[rows, kv, :], lhsT=W1r[rows, kv, j, :], rhs=rhs,
                                                        start=(j == 0), stop=(j == 31)),
                               reads=[W1r, raw], writes=[hp])
                hd = hidr.get()
                op("dve", lambda e: e.tensor_tensor(out=hd[:, 0, 0:W].rearrange("p (k m) -> p k m", m=nb), in0=hv,
                                                    in1=b1p[:].unsqueeze(2).to_broadcast([128, 2, nb]),
                                                    op=ALU.add), reads=[hp, b1p], writes=[hd])
                op("dve", lambda e: e.tensor_tensor(out=hd[:, 1, 0:W], in0=hd[:, 0, 0:W], in1=hd[:, 0, 0:W], op=ALU.mult),
                   reads=[hd], writes=[hd])
                op("dve", lambda e: e.tensor_scalar(out=hd[:, 1, 0:W], in0=hd[:, 1, 0:W], scalar1=0.044715, scalar2=1.0,
                                                    op0=ALU.mult, op1=ALU.add), reads=[hd], writes=[hd])
                op("dve", lambda e: e.tensor_tensor(out=hd[:, 2, 0:W], in0=hd[:, 1, 0:W], in1=hd[:, 0, 0:W], op=ALU.mult),
                   reads=[hd], writes=[hd])
                op("act", lambda e: e.activation(out=hd[:, 3, 0:W], in_=hd[:, 2, 0:W], func=AF.Exp, scale=-2.0 * GC),
                   reads=[hd], writes=[hd])
                op("dve", lambda e: e.tensor_scalar(out=hd[:, 3, 0:W], in0=hd[:, 3, 0:W], scalar1=1.0, scalar2=None,
                                                    op0=ALU.add), reads=[hd], writes=[hd])
                op("dve", lambda e: e.reciprocal(out=hd[:, 3, 0:W], in_=hd[:, 3, 0:W]), reads=[hd], writes=[hd])
                gT = gTr.get()
                op("dve", lambda e: e.tensor_tensor(out=gT[:, :, 0:nb], in0=hd[:, 0, 0:W].rearrange("p (k m) -> p k m", m=nb),
                                                    in1=hd[:, 3, 0:W].rearrange("p (k m) -> p k m", m=nb), op=ALU.mult),
                   reads=[hd], writes=[gT])
                kp = ring.get()
                kpv = kp.ap[:, 0:W].rearrange("p (k m) -> p k m", m=nb)
                for kv in range(2):
                    for h in range(2):
                        rows = slice(h * 64, (h + 1) * 64)
                        op("pe", lambda e: e.matmul(kpv[rows, kv, :], lhsT=W2r[rows, kv, :], rhs=gT[rows, kv, 0:nb],
                                                    start=True, stop=True), reads=[W2r, gT], writes=[kp])
                op("act", lambda e: e.copy(out=dst_k[:, col0:col0 + nb], in_=kpv[:, 0, :]), reads=[kp], writes=[dst_k])
                op("act", lambda e: e.copy(out=dst_v[:, col0:col0 + nb], in_=kpv[:, 1, :]), reads=[kp], writes=[dst_v])

            def front(bg, samp=None):
                sm = samp is not None
                own = sm or bg >= NBP
                i = samp if sm else bg - NBP
                d_gn, d_za, d_gm, d_q, d_zs = ((sc_gn_s, sc_za_s, sc_gm_s, sc_q_s, sc_zs_s) if sm
                                               else (sc_gn, sc_za, sc_gm, sc_q, sc_zs))
                xt = xr.get()
                if sm:
                    src = xs_d[samp]
                else:
                    src = xo[i * 128:(i + 1) * 128, :] if own else xp[bg * 128:(bg + 1) * 128, :]
                dma(xt[:], src, writes=[xt])
                ss = ssr.get()
                op("act", lambda e: e.activation(out=junk[:], in_=xt[:], func=AF.Square, accum_out=ss[:]),
                   reads=[xt], writes=[junk, ss])
                op("act", lambda e: e.activation(out=ss[:], in_=ss[:], func=AF.Ln, scale=1.0 / D, bias=1e-6),
                   reads=[ss], writes=[ss])
                op("act", lambda e: e.activation(out=ss[:], in_=ss[:], func=AF.Exp, scale=-0.5),
                   reads=[ss], writes=[ss])
                xb = xbr.get()
                op("dve", lambda e: e.tensor_scalar(out=xb[:], in0=xt[:], scalar1=ss[:, 0:1], scalar2=None,
                                                    op0=ALU.mult), reads=[xt, ss], writes=[xb])
                pt = ring.get()
                ptv = bfview(pt).rearrange("p (a b) -> p a b", b=128)
                for c in range(8):
                    op("pe", lambda e: e.transpose(out=ptv[:, c, :], in_=xb[:, c * 128:(c + 1) * 128],
                                                   identity=identb[:]), reads=[xb, identb], writes=[pt])
                xT = xTr.get()
                op("act", lambda e: e.copy(out=xT[:], in_=ptv), reads=[pt], writes=[xT])

                def tproj(gi):
                    c0, c1 = T_GROUPS[gi]
                    ps = ring.get()
                    for c in range(8):
                        op("pe", lambda e: e.matmul(ps[:, 0:c1 - c0], lhsT=xT[:, c, :], rhs=WT[:, c, c0:c1],
                                                    start=(c == 0), stop=(c == 7)), reads=[xT, WT], writes=[ps])
                    return ps

                kvb = kvbr.get()
                ps = tproj(0)
                if own:
                    kvo = kvor.get()
                    op("dve", lambda e: e.tensor_copy(out=kvo[:], in_=ps[:, :]), reads=[ps], writes=[kvo])
                    if sm:
                        dma(kvs_o[samp], kvo[0:8, :], reads=[kvo])
                    else:
                        dma(kv_o[i * 128:(i + 1) * 128, :], kvo[:], reads=[kvo])
                op("act", lambda e: e.copy(out=kvb[:, 0:512], in_=ps[:, :]), reads=[ps], writes=[kvb])
                ps = tproj(1)
                op("act", lambda e: e.copy(out=kvb[:, 512:768], in_=ps[:, 0:256]), reads=[ps], writes=[kvb])
                if own:
                    kv1 = kv1r.get()
                    op("dve", lambda e: e.tensor_copy(out=kv1[:], in_=ps[:, 0:280]), reads=[ps], writes=[kv1])
                    dma(d_gn[i], kv1[:, 256:280], reads=[kv1])
                    if sm:
                        dma(wins_o[samp, 504:512, :], kv1[0:8, 0:256], reads=[kv1])
                        dma(wins_o[samp, 0:504, :], cwin_d[samp, 8:512, :])
                    elif i >= NBO - NWIN:
                        w = i - (NBO - NWIN)
                        dma(win_o[w * 128:(w + 1) * 128, :], kv1[:, 0:256], reads=[kv1])
                    ps = tproj(2)
                    za = zar.get()
                    op("act", lambda e: e.copy(out=za[:], in_=ps[:, :]), reads=[ps], writes=[za])
                    dma(d_za[i], za[:], reads=[za])
                    gmb = gmr.get()
                    for q4 in range(4):
                        ps = tproj(3 + q4)
                        if q4 % 2 == 0:
                            op("dve", lambda e: e.tensor_copy(out=gmb[:, q4 * 512:(q4 + 1) * 512], in_=ps[:, :]),
                               reads=[ps], writes=[gmb])
                        else:
                            op("act", lambda e: e.copy(out=gmb[:, q4 * 512:(q4 + 1) * 512], in_=ps[:, :]),
                               reads=[ps], writes=[gmb])
                    dma(d_gm[i], gmb[:], reads=[gmb])
                pt2 = ring.get()
                p2v = bfview(pt2).rearrange("p (a b) -> p a b", b=128)
                for kk, c0 in enumerate((0, 128, 256, 512)):
                    op("pe", lambda e: e.transpose(out=p2v[:, kk, :], in_=kvb[:, c0:c0 + 128], identity=identb[:]),
                       reads=[kvb, identb], writes=[pt2])
                if sm:
                    op("act", lambda e: e.copy(out=tmpK[:], in_=p2v[:, 2:4, :]), reads=[pt2], writes=[tmpK])
                    op("pool", lambda e: e.tensor_copy(out=tmpV[:, 0, :, 0:64],
                                                       in_=kvb[:, 384:512].rearrange("p (h d) -> p h d", d=64)),
                       reads=[kvb], writes=[tmpV])
                    op("pool", lambda e: e.tensor_copy(out=tmpV[:, 1, :, 0:64],
                                                       in_=kvb[:, 640:768].rearrange("p (h d) -> p h d", d=64)),
                       reads=[kvb], writes=[tmpV])
                    dma(sc_nk[samp], tmpK[:].rearrange("p a b -> p (a b)"), reads=[tmpK])
                    dma(sc_nv[samp], tmpV[:].rearrange("p a h c -> p (a h c)"), reads=[tmpV])
                else:
                    op("pool", lambda e: e.tensor_copy(out=cmpraw[:, :, 0:16], in_=cmpraw[:, :, 128:144]),
                       reads=[cmpraw], writes=[cmpraw])
                    op("dve", lambda e: e.tensor_copy(out=cmpraw[:, :, 16:144], in_=p2v[:, 0:2, :]),
                       reads=[pt2, cmpraw], writes=[cmpraw])
                    op("act", lambda e: e.copy(out=selKT[:, bg * 128:(bg + 1) * 128], in_=p2v[:, 2, :]),
                       reads=[pt2], writes=[selKT])
                    wslot = bg % 8
                    op("act", lambda e: e.copy(out=winKT[:, wslot * 128:(wslot + 1) * 128], in_=p2v[:, 3, :]),
                       reads=[pt2], writes=[winKT])
                    op("pool", lambda e: e.tensor_copy(out=selV[:, bg, :, 0:64],
                                                       in_=kvb[:, 384:512].rearrange("p (h d) -> p h d", d=64)),
                       reads=[kvb], writes=[selV])
                    op("pool", lambda e: e.tensor_copy(out=winV[:, wslot, :, 0:64],
                                                       in_=kvb[:, 640:768].rearrange("p (h d) -> p h d", d=64)),
                       reads=[kvb], writes=[winV])
                    if bg >= NBP - 4:
                        dma(sc_wk[bg], winKT[:, wslot * 128:(wslot + 1) * 128], reads=[winKT])
                        dma(sc_wv[bg], winV[:, wslot].rearrange("p h c -> p (h c)"), reads=[winV])

                def fproj(col0, scale):
                    ps4 = ring.get()
                    p4 = ps4.ap[:].rearrange("p (a b) -> p a b", b=128)
                    for k4 in range(4):
                        for c in range(8):
                            op("pe", lambda e: e.matmul(p4[:, k4, :], lhsT=WF[:, c, col0 + 128 * k4:col0 + 128 * (k4 + 1)],
                                                        rhs=xT[:, c, :], start=(c == 0), stop=(c == 7)),
                               reads=[WF, xT], writes=[ps4])
                    f4 = f4r.get()
                    op("act", lambda e: e.activation(out=f4[:], in_=p4, func=AF.Copy, scale=scale),
                       reads=[ps4], writes=[f4])
                    return f4

                f4 = fproj(512, 1.0)
                dma((sc_u_s[samp] if sm else sc_u[bg]).rearrange("p (a b) -> p a b", b=128), f4[:], reads=[f4])
                if own:
                    f4 = fproj(0, 0.125)
                    dma(d_q[i].rearrange("p (a b) -> p a b", b=128), f4[:], reads=[f4])
                    f4 = fproj(1024, 1.0)
                    dma(d_zs[i].rearrange("p (a b) -> p a b", b=128), f4[:], reads=[f4])
                if sm:
                    return
                compress(cmpraw, 8, kcT, vcT, 8 * bg)
                nt = bg // 16
                pv = ring.get()
                pvv = bfview(pv)[:, 0:128]
                op("pe", lambda e: e.transpose(out=pvv, in_=vcT[:, nt * 128:(nt + 1) * 128], identity=identb[:]),
                   reads=[vcT, identb], writes=[pv])
                op("dve", lambda e: e.tensor_copy(out=vc[:, nt, :, 0:64], in_=pvv.rearrange("p (h d) -> p h d", d=64)),
                   reads=[pv], writes=[vc])

            if NSB > 0:
                rawS = fw.sbuf("rawS", [128, 2, 1040], BF16)
                kcS = fw.sbuf("kcS", [128, 1024], BF16)
                vcS = fw.sbuf("vcS", [128, 1024], BF16)
                ptb = fw.sbuf("ptb", [128, 128], I32)
                ptf = fw.sbuf("ptf", [128, 128], F32)
                io2 = fw.sbuf("io2", [128, 1], F32)
                idxc = fw.sbuf("idxc", [128, 128], I32)
                pgr = Ring([fw.sbuf(f"pg{i}", [128, 256], F32) for i in range(4)])
                pgbr = Ring([fw.sbuf(f"pgb{i}", [128, 256], BF16) for i in range(2)])
                op("pool", lambda e: e.memset(rawS[:], 0.0), writes=[rawS])
                op("pool", lambda e: e.iota(io2[:], pattern=[[0, 1]], base=0, channel_multiplier=2,
                                            allow_small_or_imprecise_dtypes=True), writes=[io2])

            for bg in range(NBT):
                front(bg)
            for b in range(NSB):
                front(None, samp=b)
                dma(ptb[:], pt_d[b], writes=[ptb])
                op("dve", lambda e: e.tensor_copy(out=ptf[:], in_=ptb[:]), reads=[ptb], writes=[ptf])
                op("dve", lambda e: e.tensor_scalar(out=ptf[:], in0=ptf[:], scalar1=256.0, scalar2=io2[:, 0:1],
                                                    op0=ALU.mult, op1=ALU.add), reads=[ptf, io2], writes=[ptf])
                op("dve", lambda e: e.tensor_copy(out=idxc[:], in_=ptf[:]), reads=[ptf], writes=[idxc])
                for kt in range(128):
                    pg = pgr.get()
                    fw.idma(pg[:], cache_d, idxc[:, kt:kt + 1], reads=[idxc], writes=[pg])
                    pb_ = pgbr.get()
                    if kt % 2 == 0:
                        op("dve", lambda e: e.tensor_copy(out=pb_[:], in_=pg[:]), reads=[pg], writes=[pb_])
                    else:
                        op("act", lambda e: e.copy(out=pb_[:], in_=pg[:]), reads=[pg], writes=[pb_])
                    pt2 = ring.get()
                    p2v = bfview(pt2).rearrange("p (a b) -> p a b", b=128)
                    for kk in range(2):
                        op("pe", lambda e: e.transpose(out=p2v[:, kk, :], in_=pb_[:, kk * 128:(kk + 1) * 128],
                                                       identity=identb[:]), reads=[pb_, identb], writes=[pt2])
                    slot = kt % 8
                    if kt % 2 == 0:
                        op("act", lambda e: e.copy(out=rawS[:, :, 16 + 128 * slot:16 + 128 * (slot + 1)], in_=p2v[:, 0:2, :]),
                           reads=[pt2], writes=[rawS])
                    else:
                        op("dve", lambda e: e.tensor_copy(out=rawS[:, :, 16 + 128 * slot:16 + 128 * (slot + 1)],
                                                          in_=p2v[:, 0:2, :]), reads=[pt2], writes=[rawS])
                    if slot == 7:
                        compress(rawS, 64, kcS, vcS, 64 * (kt // 8))
                        op("dve", lambda e: e.tensor_copy(out=rawS[:, :, 0:16], in_=rawS[:, :, 1024:1040]),
                           reads=[rawS], writes=[rawS])
                dma(sc_kc[b], kcS[:], reads=[kcS])
                dma(sc_vcT[b], vcS[:], reads=[vcS])
        fw.barrier()
        fw.stack = pstore


        with contextlib.ExitStack() as p2:
            fw.stack = p2
            wla = fw.sbuf("wla", [128, 4, D], BF16)
            wls = fw.sbuf("wls", [128, 4, D], BF16)
            wo = fw.sbuf("wo", [128, 8, D], BF16)
            wglu = fw.sbuf("wglu", [128, 4, 512], BF16)
            fg = fw.sbuf("fgt", [128, D], F32)
            Esb = fw.sbuf("Esb", [128, EW], BF16)
            caus = fw.sbuf("caust", [128, 128], BF16)
            pset = contextlib.ExitStack()
            pset.__enter__()
            BbT = fw.sbuf("BbT", [128, 2, 16, 128], BF16)
            CTre = fw.sbuf("CTre", [128, 16, 128], BF16)
            CTimn = fw.sbuf("CTimn", [128, 16, 128], BF16)
            cosT = fw.sbuf("cosT", [128, 16, 128], F32)
            sinT = fw.sbuf("sinT", [128, 16, 128], F32)
            p2small = [fw.sbuf(f"p2s{i}", [128, 16], F32) for i in range(24)]
            dsk = fw.sbuf("dsk", [128, 4], F32)
            nbgl = fw.sbuf("nbgl", [128, 4], F32)
            fw.stack = pset
            stg = Ring([fw.sbuf(f"stg2_{i}", [128, 1024], F32) for i in range(2)])
            dma(fg[:], fg_d, writes=[fg])
            dma(Esb[:], E_d, writes=[Esb])
            dma(caus[:], caus_d, writes=[caus])
            k = 0
            for (wd, W, nk, ncol) in ((wla_d, wla, 4, D), (wls_d, wls, 4, D), (wo_d, wo, 8, D), (wglu_d, wglu, 4, 512)):
                for c in range(nk):
                    s_ = stg.get()
                    dma(s_[:, 0:ncol], wd[:, c, :], writes=[s_])
                    en = "dve" if k % 2 == 0 else "pool"
                    k += 1
                    op(en, lambda e: e.tensor_copy(out=W[:, c, :], in_=s_[:, 0:ncol]), reads=[s_], writes=[W])

            def small(name, shape=(128, 16), dt=F32):
                if tuple(shape) == (128, 16) and dt == F32 and p2small:
                    return p2small.pop()
                return fw.sbuf(name, list(shape), dt)

            lre, lim, ldt = small("lre"), small("lim"), small("ldt")
            dma(lre[:], lamre_d, writes=[lre])
            dma(lim[:], lamim_d, writes=[lim])
            dma(ldt[:], logdt_d, writes=[ldt])
            mgh = small("mgh", (128, 2))
            dma(dsk[:], dsk_d, writes=[dsk])
            dma(nbgl[:], bgl_d, writes=[nbgl])
            dma(mgh[:], mgh_d, writes=[mgh])
            op("dve", lambda e: e.tensor_scalar(out=nbgl[:], in0=nbgl[:], scalar1=-1.0, scalar2=None, op0=ALU.mult),
               reads=[nbgl], writes=[nbgl])
            tmp = [small(f"s5t{i}") for i in range(8)]
            rho, c1, s1 = small("rho"), small("c1"), small("s1")
            ki = small("ki", dt=I32)

            def tt(out, a, b, o, eng="dve"):
                op(eng, lambda e: e.tensor_tensor(out=out[:], in0=a[:], in1=b[:], op=o), reads=[a, b], writes=[out])

            def ts(out, a, s1_, s2_, o0, o1=None, eng="dve"):
                if o1 is None:
                    op(eng, lambda e: e.tensor_scalar(out=out[:], in0=a[:], scalar1=s1_, scalar2=None, op0=o0),
                       reads=[a], writes=[out])
                else:
                    op(eng, lambda e: e.tensor_scalar(out=out[:], in0=a[:], scalar1=s1_, scalar2=s2_, op0=o0, op1=o1),
                       reads=[a], writes=[out])

            dtt, lr, th, r_ = tmp[0], tmp[1], tmp[2], tmp[3]
            op("act", lambda e: e.activation(out=dtt[:], in_=ldt[:], func=AF.Exp), reads=[ldt], writes=[dtt])
            tt(lr, lre, dtt, ALU.mult)
            tt(th, lim, dtt, ALU.mult)
            op("act", lambda e: e.activation(out=rho[:], in_=lr[:], func=AF.Exp), reads=[lr], writes=[rho])
            ts(tmp[4], th, 1.0 / (2 * math.pi), None, ALU.mult)
            op("dve", lambda e: e.tensor_copy(out=ki[:], in_=tmp[4][:]), reads=[tmp[4]], writes=[ki])
            op("dve", lambda e: e.tensor_copy(out=tmp[4][:], in_=ki[:]), reads=[ki], writes=[tmp[4]])
            op("dve", lambda e: e.scalar_tensor_tensor(out=r_[:], in0=tmp[4][:], scalar=-2.0 * math.pi, in1=th[:],
                                                       op0=ALU.mult, op1=ALU.add), reads=[tmp[4], th], writes=[r_])
            s4, c4, s2, c2 = tmp[4], tmp[5], tmp[6], tmp[7]
            hp_ = small("halfpi", (128, 1))
            op("pool", lambda e: e.memset(hp_[:], math.pi / 2), writes=[hp_])
            op("act", lambda e: e.activation(out=s4[:], in_=r_[:], func=AF.Sin, scale=0.25), reads=[r_], writes=[s4])
            op("act", lambda e: e.activation(out=c4[:], in_=r_[:], func=AF.Sin, scale=0.25, bias=hp_[:, 0:1]),
               reads=[r_, hp_], writes=[c4])
            op("dve", lambda e: e.scalar_tensor_tensor(out=s2[:], in0=s4[:], scalar=2.0, in1=c4[:], op0=ALU.mult,
                                                       op1=ALU.mult), reads=[s4, c4], writes=[s2])
            tt(c2, s4, s4, ALU.mult)
            ts(c2, c2, -2.0, 1.0, ALU.mult, ALU.add)
            op("dve", lambda e: e.scalar_tensor_tensor(out=s1[:], in0=s2[:], scalar=2.0, in1=c2[:], op0=ALU.mult,
                                                       op1=ALU.mult), reads=[s2, c2], writes=[s1])
            tt(c1, s2, s2, ALU.mult)
            ts(c1, c1, -2.0, 1.0, ALU.mult, ALU.add)
            are, aim, am1, den = tmp[0], tmp[1], tmp[2], tmp[3]
            tt(are, rho, c1, ALU.mult)
            tt(aim, rho, s1, ALU.mult)
            ts(am1, are, -1.0, None, ALU.add)
            tt(den, lre, lre, ALU.mult)
            tt(tmp[4], lim, lim, ALU.mult)
            tt(den, den, tmp[4], ALU.add)
            op("dve", lambda e: e.reciprocal(out=den[:], in_=den[:]), reads=[den], writes=[den])
            fre, fim = small("fre"), small("fim")
            tt(tmp[4], am1, lre, ALU.mult)
            tt(tmp[5], aim, lim, ALU.mult)
            tt(tmp[4], tmp[4], tmp[5], ALU.add)
            tt(fre, tmp[4], den, ALU.mult)
            tt(tmp[4], aim, lre, ALU.mult)
            tt(tmp[5], am1, lim, ALU.mult)
            tt(tmp[4], tmp[4], tmp[5], ALU.subtract)
            tt(fim, tmp[4], den, ALU.mult)
            bre, bim = small("bre", (128, 16, 16)), small("bim", (128, 16, 16))
            dma(bre[:], bre_d, writes=[bre])
            dma(bim[:], bim_d, writes=[bim])
            bt = [small(f"bt{i}", (128, 16, 16)) for i in range(3)]
            Mb = fw.sbuf("Mb", [128, 16, 2, 16], BF16)
            Mz = fw.sbuf("Mz", [128, 16, 4, 32], BF16)
            op("pool", lambda e: e.memset(Mz[:], 0.0), writes=[Mz])

            def bc16(t_):
                return t_.ap[:].unsqueeze(2).to_broadcast([128, 16, 16])

            for ri in range(2):
                fa, fb_, sgn = (fre, fim, ALU.subtract) if ri == 0 else (fim, fre, ALU.add)
                op("dve", lambda e: e.tensor_tensor(out=bt[0][:], in0=bre[:], in1=bc16(fa), op=ALU.mult),
                   reads=[bre, fa], writes=[bt[0]])
                op("dve", lambda e: e.tensor_tensor(out=bt[1][:], in0=bim[:], in1=bc16(fb_), op=ALU.mult),
                   reads=[bim, fb_], writes=[bt[1]])
                op("dve", lambda e: e.tensor_tensor(out=bt[2][:], in0=bt[0][:], in1=bt[1][:], op=sgn),
                   reads=[bt[0], bt[1]], writes=[bt[2]])
                op("dve", lambda e: e.tensor_tensor(
                    out=Mb[:], in0=bt[2].ap[:].unsqueeze(2).to_broadcast([128, 16, 2, 16]),
                    in1=mgh.ap[:].unsqueeze(1).unsqueeze(3).to_broadcast([128, 16, 2, 16]), op=ALU.mult),
                   reads=[bt[2], mgh], writes=[Mb])
                Mz5 = Mz.ap[:].rearrange("p (a q) r c -> p a q r c", q=4)
                Mb4 = Mb.ap[:].rearrange("p (a q) g c -> p a q (g c)", q=4)
                for q_ in range(4):
                    op("dve", lambda e: e.tensor_copy(out=Mz5[:, :, q_, q_, :], in_=Mb4[:, :, q_, :]),
                       reads=[Mb], writes=[Mz])
                for st in range(16):
                    pz = ring.get()
                    pzv = bfview(pz)[:, 0:128]
                    op("pe", lambda e: e.transpose(out=pzv, in_=Mz[:, st].rearrange("p r c -> p (r c)"),
                                                   identity=identb[:]), reads=[Mz, identb], writes=[pz])
                    op("act", lambda e: e.copy(out=BbT[:, ri, st, :], in_=pzv), reads=[pz], writes=[BbT])
            for half_ in range(2):
                s_ = stg.get()
                dma(s_[:, 0:1024].rearrange("p (a b) -> p a b", b=128), cre_d[:, half_ * 8:(half_ + 1) * 8, :], writes=[s_])
                op("dve", lambda e: e.tensor_copy(out=CTre[:, half_ * 8:(half_ + 1) * 8, :],
                                                  in_=s_[:, 0:1024].rearrange("p (a b) -> p a b", b=128)),
                   reads=[s_], writes=[CTre])
                s_ = stg.get()
                dma(s_[:, 0:1024].rearrange("p (a b) -> p a b", b=128), cim_d[:, half_ * 8:(half_ + 1) * 8, :], writes=[s_])
                op("dve", lambda e: e.tensor_scalar(out=CTimn[:, half_ * 8:(half_ + 1) * 8, :],
                                                    in0=s_[:, 0:1024].rearrange("p (a b) -> p a b", b=128),
                                                    scalar1=-1.0, scalar2=None, op0=ALU.mult), reads=[s_], writes=[CTimn])
            op("pool", lambda e: e.memset(cosT[:], 1.0), writes=[cosT])
            op("pool", lambda e: e.memset(sinT[:], 0.0), writes=[sinT])
            emr, emi = small("emr"), small("emi")
            op("dve", lambda e: e.tensor_copy(out=emr[:], in_=c1[:]), reads=[c1], writes=[emr])
            op("dve", lambda e: e.tensor_copy(out=emi[:], in_=s1[:]), reads=[s1], writes=[emi])
            tb = [fw.sbuf(f"tb{i}", [128, 16, 64], F32) for i in range(2)]
            for kk in range(7):
                m = 1 << kk
                ebr = emr.ap[:].unsqueeze(2).to_broadcast([128, 16, m])
                ebi = emi.ap[:].unsqueeze(2).to_broadcast([128, 16, m])
                op("dve", lambda e: e.tensor_tensor(out=tb[0][:, :, 0:m], in0=cosT[:, :, 0:m], in1=ebr, op=ALU.mult),
                   reads=[cosT, emr], writes=[tb[0]])
                op("dve", lambda e: e.tensor_tensor(out=tb[1][:, :, 0:m], in0=sinT[:, :, 0:m], in1=ebi, op=ALU.mult),
                   reads=[sinT, emi], writes=[tb[1]])
                op("dve", lambda e: e.tensor_tensor(out=cosT[:, :, m:2 * m], in0=tb[0][:, :, 0:m], in1=tb[1][:, :, 0:m],
                                                    op=ALU.subtract), reads=[tb[0], tb[1]], writes=[cosT])
                op("dve", lambda e: e.tensor_tensor(out=tb[0][:, :, 0:m], in0=cosT[:, :, 0:m], in1=ebi, op=ALU.mult),
                   reads=[cosT, emi], writes=[tb[0]])
                op("dve", lambda e: e.tensor_tensor(out=tb[1][:, :, 0:m], in0=sinT[:, :, 0:m], in1=ebr, op=ALU.mult),
                   reads=[sinT, emr], writes=[tb[1]])
                op("dve", lambda e: e.tensor_tensor(out=sinT[:, :, m:2 * m], in0=tb[0][:, :, 0:m], in1=tb[1][:, :, 0:m],
                                                    op=ALU.add), reads=[tb[0], tb[1]], writes=[sinT])
                tt(tmp[0], emr, emr, ALU.mult)
                tt(tmp[1], emi, emi, ALU.mult)
                tt(tmp[2], emr, emi, ALU.mult)
                tt(emr, tmp[0], tmp[1], ALU.subtract)
                ts(emi, tmp[2], 2.0, None, ALU.mult)

            hre_p, him_p = small("hre_p"), small("him_p")
            g0r, g0i = small("g0r"), small("g0i")
            glr, gli = small("glr"), small("gli")
            fw.barrier()
            pset.close()
            fw.stack = p2
            op("pool", lambda e: e.memset(hre_p[:], 0.0), writes=[hre_p])
            op("pool", lambda e: e.memset(him_p[:], 0.0), writes=[him_p])

            uTr = Ring([fw.sbuf(f"uT{i}", [128, 4, 128], BF16) for i in range(2)])
            wk = Ring([fw.sbuf(f"wk{i}", [128, 128], F32) for i in range(8)])
            gbr = Ring([fw.sbuf(f"gb{i}", [128, 128], F32) for i in range(4)])
            hbr = Ring([fw.sbuf(f"hb{i}", [128, 2, 128], BF16) for i in range(4)])

            def cmul(or_, oi_, ar, ai, br_, bi_):
                tt(tmp[0], ar, br_, ALU.mult)
                tt(tmp[1], ai, bi_, ALU.mult, eng="pool")
                tt(tmp[2], ar, bi_, ALU.mult)
                tt(tmp[3], ai, br_, ALU.mult, eng="pool")
                tt(or_, tmp[0], tmp[1], ALU.subtract)
                tt(oi_, tmp[2], tmp[3], ALU.add, eng="pool")

            class T16:
                def __init__(self, b):
                    self.b = b

            def s5_gen(bg, samp, out):
                own = samp is not None or bg >= NBP
                lc = 7 if samp is not None else 127
                uT = uTr.get()
                out["uT"] = uT
                dma(uT[:], (sc_u_s[samp] if samp is not None else sc_u[bg]).rearrange("p (a b) -> p a b", b=128), writes=[uT])
                cmul(g0r, g0i, c1, s1, hre_p, him_p)
                yield
                pend = []
                for st in range(16):
                    ct, q_ = st // 4, st % 4
                    bps = ring.get()
                    bv = bps.ap[:, 0:256].rearrange("p (a b) -> p a b", b=128)
                    for ri in range(2):
                        op("pe", lambda e: e.matmul(bv[:, ri, :], lhsT=BbT[:, ri, st, :], rhs=uT[:, ct, :],
                                                    start=True, stop=True), reads=[BbT, uT], writes=[bps])
                    t1, t2, t3, t4 = wk.get(), wk.get(), wk.get(), wk.get()
                    cs, sn = cosT.ap[:, st, :], sinT.ap[:, st, :]
                    op("dve", lambda e: e.tensor_tensor(out=t1[:], in0=bv[:, 0, :], in1=cs, op=ALU.mult),
                       reads=[bps, cosT], writes=[t1])
                    op("dve", lambda e: e.tensor_tensor(out=t2[:], in0=bv[:, 1, :], in1=sn, op=ALU.mult),
                       reads=[bps, sinT], writes=[t2])
                    op("dve", lambda e: e.tensor_tensor(out=t3[:], in0=bv[:, 1, :], in1=cs, op=ALU.mult),
                       reads=[bps, cosT], writes=[t3])
                    op("dve", lambda e: e.tensor_tensor(out=t4[:], in0=bv[:, 0, :], in1=sn, op=ALU.mult),
                       reads=[bps, sinT], writes=[t4])
                    op("pool", lambda e: e.tensor_tensor(out=t1[:], in0=t1[:], in1=t2[:], op=ALU.add),
                       reads=[t1, t2], writes=[t1])
                    op("pool", lambda e: e.tensor_tensor(out=t3[:], in0=t3[:], in1=t4[:], op=ALU.subtract),
                       reads=[t3, t4], writes=[t3])
                    gr, gi = gbr.get(), gbr.get()
                    rb = rho.ap[:, st:st + 1].to_broadcast([128, 128])
                    op("dve", lambda e: e.tensor_tensor_scan(out=gr[:], data0=rb, data1=t1[:], initial=g0r[:, st:st + 1],
                                                             op0=ALU.mult, op1=ALU.add), reads=[rho, t1, g0r], writes=[gr])
                    op("dve", lambda e: e.tensor_tensor_scan(out=gi[:], data0=rb, data1=t3[:], initial=g0i[:, st:st + 1],
                                                             op0=ALU.mult, op1=ALU.add), reads=[rho, t3, g0i], writes=[gi])
                    op("pool", lambda e: e.tensor_copy(out=glr[:, st:st + 1], in_=gr[:, lc:lc + 1]), reads=[gr], writes=[glr])
                    op("pool", lambda e: e.tensor_copy(out=gli[:, st:st + 1], in_=gi[:, lc:lc + 1]), reads=[gi], writes=[gli])
                    this_y = None
                    if own:
                        t5, t6, t7, t8 = wk.get(), wk.get(), wk.get(), wk.get()
                        hb = hbr.get()
                        op("dve", lambda e: e.tensor_tensor(out=t5[:], in0=gr[:], in1=cs, op=ALU.mult),
                           reads=[gr, cosT], writes=[t5])
                        op("pool", lambda e: e.tensor_tensor(out=t6[:], in0=gi[:], in1=sn, op=ALU.mult),
                           reads=[gi, sinT], writes=[t6])
                        op("dve", lambda e: e.tensor_tensor(out=t7[:], in0=gi[:], in1=cs, op=ALU.mult),
                           reads=[gi, cosT], writes=[t7])
                        op("pool", lambda e: e.tensor_tensor(out=t8[:], in0=gr[:], in1=sn, op=ALU.mult),
                           reads=[gr, sinT], writes=[t8])
                        op("dve", lambda e: e.tensor_tensor(out=hb[:, 0, :], in0=t5[:], in1=t6[:], op=ALU.subtract),
                           reads=[t5, t6], writes=[hb])
                        op("pool", lambda e: e.tensor_tensor(out=hb[:, 1, :], in0=t7[:], in1=t8[:], op=ALU.add),
                           reads=[t7, t8], writes=[hb])

                        def this_y(hb=hb, st=st, ct=ct, q_=q_):
                            yv_ = ypsb.ap[:].rearrange("p (a b) -> p a b", b=128)
                            op("pe", lambda e: e.matmul(yv_[:, ct, :], lhsT=CTre[:, st, :], rhs=hb[:, 0, :],
                                                        start=(q_ == 0), stop=False), reads=[CTre, hb], writes=[ypsb])
                            op("pe", lambda e: e.matmul(yv_[:, ct, :], lhsT=CTimn[:, st, :], rhs=hb[:, 1, :],
                                                        start=False, stop=(q_ == 3)), reads=[CTimn, hb], writes=[ypsb])
                    yield
                    pend.append(this_y)
                    if len(pend) > int(os.environ.get('YDEF', '2')):
                        y_ = pend.pop(0)
                        if y_ is not None:
                            y_()
                while pend:
                    y_ = pend.pop(0)
                    if y_ is not None:
                        yield
                        y_()
                c127 = tmp[4]
                s127 = tmp[5]
                op("dve", lambda e: e.tensor_copy(out=c127[:], in_=cosT[:, :, lc]), reads=[cosT], writes=[c127])
                op("dve", lambda e: e.tensor_copy(out=s127[:], in_=sinT[:, :, lc]), reads=[sinT], writes=[s127])
                cmul(hre_p, him_p, c127, s127, glr, gli)

            def s5_block(bg, samp=None):
                out = {}
                for _ in s5_gen(bg, samp, out):
                    pass
                return out["uT"]

            wkr = Ring([fw.sbuf(f"wkt{i}", [128, 5, 128], BF16) for i in range(2)])
            wvr = Ring([fw.sbuf(f"wvt{i}", [128, 5, 130], BF16) for i in range(2)])
            qTr = Ring([fw.sbuf(f"qT{i}", [128, 512], BF16) for i in range(2)])
            qzr = Ring([fw.sbuf(f"qz{i}", [128, 2, 512], BF16) for i in range(1)])
            gnr = Ring([fw.sbuf(f"gn{i}", [128, 24], F32) for i in range(2)])
            cbr = Ring([fw.sbuf(f"cb{i}", [128, 128], BF16) for i in range(10)])
            tkr = Ring([fw.sbuf(f"tk{i}", [128, 128], F32) for i in range(2)])
            Pr = Ring([fw.sbuf(f"P{i}", [128, 512], BF16) for i in range(2)])
            OTr = Ring([fw.sbuf(f"OT{i}", [128, 512], F32) for i in range(1)])
            oattn_r = Ring([fw.sbuf(f"oat{i}", [128, 512], F32) for i in range(2)])
            smr = Ring([fw.sbuf(f"sm{i}", [128, 8], F32) for i in range(12)])
            impr = Ring([fw.sbuf(f"imp{i}", [128, 128], F32) for i in range(2)])
            selmb_r = Ring([fw.sbuf(f"selmb{i}", [128, 128], BF16) for i in range(2)])
            selmT_r = Ring([fw.sbuf(f"selmT{i}", [128, 2, 128], BF16) for i in range(2)])
            big = Ring([fw.sbuf(f"big{i}", [128, 512], F32) for i in range(4)])
            bigb = Ring([fw.sbuf(f"bigb{i}", [128, 512], BF16) for i in range(3)])
            ysTr = Ring([fw.sbuf(f"ysT{i}", [128, 512], BF16) for i in range(2)])
            gmr2 = Ring([fw.sbuf(f"gm2_{i}", [128, 2048], BF16) for i in range(1)])
            sgm = fw.sbuf("sgm", [128, 2048], BF16)
            sgt = fw.sbuf("sgt", [128, 512], F32)
            mbt = fw.sbuf("mbt", [128, D], BF16)
            mT = fw.sbuf("mT", [128, 8, 128], BF16)
            xr2 = Ring([fw.sbuf(f"xr2_{i}", [128, D], F32) for i in range(1)])

            def sigmoid_from(out, src_ap, src_bufs, scale_in=1.0, shape_ap=None):
                op("act", lambda e: e.activation(out=out[:], in_=src_ap, func=AF.Exp, scale=-1.0 * scale_in),
                   reads=src_bufs, writes=[out])
                op("pool", lambda e: e.tensor_scalar(out=out[:], in0=out[:], scalar1=1.0, scalar2=1.0, op0=ALU.add, op1=ALU.mult),
                   reads=[out], writes=[out])
                op("dve", lambda e: e.reciprocal(out=out[:], in_=out[:]), reads=[out], writes=[out])

            TICK = {"gens": [], "n": 0}

            def tick():
                gs = TICK["gens"]
                if not gs:
                    return
                TICK["n"] += 1
                g_ = gs[TICK["n"] % len(gs)]
                try:
                    next(g_)
                except StopIteration:
                    gs.remove(g_)

            def emit_scores(kT, kreads, biases, qz, h):
                S = ring.get()
                op("pe", lambda e: e.matmul(S[:, :], lhsT=kT, rhs=qz[:, h, :], start=True, stop=(len(biases) == 0)),
                   reads=kreads + [qz], writes=[S])
                for bi_, (l_, r_ap, rd) in enumerate(biases):
                    op("pe", lambda e: e.matmul(S.ap[:].rearrange("p (a b) -> p a b", b=128), lhsT=l_, rhs=r_ap,
                                                start=False, stop=(bi_ == len(biases) - 1)), reads=rd, writes=[S])
                return S

            def attn_pass(h, qz, units, acc, extras=()):
                n = len(units)
                LA = int(os.environ.get('LA', '1'))
                Sq = [emit_scores(units[j][0], units[j][1], units[j][2], qz, h) for j in range(min(LA, n))]
                for ui, (kT, kreads, biases, V, vreads) in enumerate(units):
                    S = Sq.pop(0)
                    if ui + LA < n:
                        u2 = units[ui + LA]
                        Sq.append(emit_scores(u2[0], u2[1], u2[2], qz, h))
                    P = Pr.get()
                    op("act", lambda e: e.activation(out=P[:], in_=S[:, :], func=AF.Exp), reads=[S], writes=[P])
                    op("pe", lambda e: e.matmul(acc[0:65, :], lhsT=V, rhs=P[:], start=(ui == 0), stop=(ui == n - 1)),
                       reads=vreads + [P], writes=[acc])
                    for (xacc, xfn, lo, hi, xrd) in extras:
                        if lo <= ui <= hi:
                            op("pe", lambda e: e.matmul(xacc[:, :], lhsT=xfn(ui), rhs=P[:], start=(ui == lo),
                                                        stop=(ui == hi)), reads=xrd + [P], writes=[xacc])
                    tick()

            def finish_pass(h, br, acc, gate, oattn, first, clampz=False):
                OT = OTr.get()
                op("act", lambda e: e.copy(out=OT[0:65, :], in_=acc[0:65, :]), reads=[acc], writes=[OT])
                tp = ring.get()
                tpv = tp.ap[:, 0:260].rearrange("p (g c) -> p g c", c=65)
                for g in range(4):
                    op("pe", lambda e: e.transpose(out=tpv[:, g, :], in_=OT[0:65, g * 128:(g + 1) * 128],
                                                   identity=identf[0:65, 0:65]), reads=[OT, identf], writes=[tp])
                rz = smr.get()
                if clampz:
                    op("dve", lambda e: e.tensor_scalar(out=rz[:, 0:4], in0=tpv[:, :, 64], scalar1=1e-30, scalar2=None,
                                                        op0=ALU.max), reads=[tp], writes=[rz])
                    op("dve", lambda e: e.reciprocal(out=rz[:, 0:4], in_=rz[:, 0:4]), reads=[rz], writes=[rz])
                else:
                    op("dve", lambda e: e.reciprocal(out=rz[:, 0:4], in_=tpv[:, :, 64]), reads=[tp], writes=[rz])
                w_ = smr.get()
                gv = gate.ap[:].rearrange("p (h g b) -> p h g b", h=2, g=4)[:, h, :, br]
                op("dve", lambda e: e.tensor_tensor(out=w_[:, 0:4], in0=rz[:, 0:4], in1=gv, op=ALU.mult),
                   reads=[rz, gate], writes=[w_])
                for g in range(4):
                    dst = oattn[:, (h * 4 + g) * 64:(h * 4 + g + 1) * 64]
                    if first:
                        op("dve", lambda e: e.tensor_scalar(out=dst, in0=tpv[:, g, 0:64], scalar1=w_[:, g:g + 1],
                                                            scalar2=None, op0=ALU.mult), reads=[tp, w_], writes=[oattn])
                    else:
                        op("dve", lambda e: e.scalar_tensor_tensor(out=dst, in0=tpv[:, g, 0:64], scalar=w_[:, g:g + 1],
                                                                   in1=dst, op0=ALU.mult, op1=ALU.add),
                           reads=[tp, w_, oattn], writes=[oattn])
                return rz

            def own_block(i, uT, samp=None):
                sm = samp is not None
                bg = NBP + i if not sm else None
                yv_ = ypsb.ap[:].rearrange("p (a b) -> p a b", b=128)
                yv = big.get()
                yv3 = yv.ap[:].rearrange("p (a b) -> p a b", b=128)
                for ct in range(4):
                    op("dve", lambda e: e.scalar_tensor_tensor(out=yv3[:, ct, :], in0=uT[:, ct, :], scalar=dsk[:, ct:ct + 1],
                                                               in1=yv_[:, ct, :], op0=ALU.mult, op1=ALU.add),
                       reads=[uT, dsk, ypsb], writes=[yv])
                t_a, t_b = big.get(), big.get()
                op("pool", lambda e: e.tensor_tensor(out=t_a[:], in0=yv[:], in1=yv[:], op=ALU.mult), reads=[yv], writes=[t_a])
                op("pool", lambda e: e.tensor_scalar(out=t_a[:], in0=t_a[:], scalar1=0.044715, scalar2=1.0, op0=ALU.mult,
                                                     op1=ALU.add), reads=[t_a], writes=[t_a])
                op("pool", lambda e: e.tensor_tensor(out=t_a[:], in0=t_a[:], in1=yv[:], op=ALU.mult), reads=[t_a, yv], writes=[t_a])
                sigmoid_from(t_b, t_a[:], [t_a], scale_in=2.0 * GC)
                yg = big.get()
                op("dve", lambda e: e.tensor_tensor(out=yg[:], in0=yv[:], in1=t_b[:], op=ALU.mult), reads=[yv, t_b], writes=[yg])
                ygb = bigb.get()
                op("act", lambda e: e.copy(out=ygb[:], in_=yg[:]), reads=[yg], writes=[ygb])
                gl = ring.get()
                glv = gl.ap[:].rearrange("p (a b) -> p a b", b=128)
                ygb3 = ygb.ap[:].rearrange("p (a b) -> p a b", b=128)
                for co in range(4):
                    for ci in range(4):
                        op("pe", lambda e: e.matmul(glv[:, co, :], lhsT=wglu[:, ci, co * 128:(co + 1) * 128], rhs=ygb3[:, ci, :],
                                                    start=(ci == 0), stop=(ci == 3)), reads=[wglu, ygb], writes=[gl])
                sg = t_a
                sg3 = sg.ap[:].rearrange("p (a b) -> p a b", b=128)
                for co in range(4):
                    op("act", lambda e: e.activation(out=sg3[:, co, :], in_=glv[:, co, :], func=AF.Exp, scale=-1.0,
                                                     bias=nbgl[:, co:co + 1]), reads=[gl, nbgl], writes=[sg])
                op("pool", lambda e: e.tensor_scalar(out=sg[:], in0=sg[:], scalar1=1.0, scalar2=1.0, op0=ALU.add, op1=ALU.mult),
                   reads=[sg], writes=[sg])
                op("dve", lambda e: e.reciprocal(out=sg[:], in_=sg[:]), reads=[sg], writes=[sg])
                op("pool", lambda e: e.tensor_tensor(out=yg[:], in0=yg[:], in1=sg[:], op=ALU.mult), reads=[yg, sg], writes=[yg])
                zs = bigb.get()
                dma(zs[:], sc_zs_s[samp] if sm else sc_zs[i], writes=[zs])
                sz = t_b
                sigmoid_from(sz, zs[:], [zs])
                op("pool", lambda e: e.tensor_tensor(out=sz[:], in0=sz[:], in1=zs[:], op=ALU.mult), reads=[sz, zs], writes=[sz])
                ysT = ysTr.get()
                op("dve", lambda e: e.tensor_tensor(out=ysT[:], in0=yg[:], in1=sz[:], op=ALU.mult), reads=[yg, sz], writes=[ysT])

                qT = qTr.get()
                dma(qT[:], sc_q_s[samp] if sm else sc_q[i], writes=[qT])
                qz = qzr.get()
                op("pool", lambda e: e.memset(qz[:], 0.0), writes=[qz])
                for h in range(2):
                    rows = slice(h * 64, (h + 1) * 64)
                    op("pool", lambda e: e.tensor_copy(out=qz[rows, h, :], in_=qT[rows, :]), reads=[qT], writes=[qz])
                gn = gnr.get()
                dma(gn[:], sc_gn_s[samp] if sm else sc_gn[i], writes=[gn])
                gate = gnr.get()
                sigmoid_from(gate, gn[:], [gn])
                oattn = oattn_r.get()
                if sm:
                    sample_attention(samp, qz, gate, oattn)
                    for _ in out_chain(i, samp, oattn, ysT):
                        pass
                    return None
                prompt_attention(i, bg, qz, gate, oattn)
                return out_chain(i, samp, oattn, ysT)

            def prompt_attention(i, bg, qz, gate, oattn):
                nt_hi = (8 * bg + 7) // 128
                cbs = []
                for nt in range(nt_hi + 1):
                    cb = cbr.get()
                    dma(cb[:], cmpb_d[i, nt], writes=[cb])
                    cbs.append(cb)
                ta, tb_ = tkr.get(), tkr.get()
                dma(ta[:], tka_d[i], writes=[ta])
                dma(tb_[:], tkb_d[i], writes=[tb_])
                selmT = selmT_r.get()
                selmbs = []
                for h in range(2):
                    units = []
                    for nt in range(nt_hi + 1):
                        units.append((kcT[:, nt * 128:(nt + 1) * 128], [kcT],
                                      [(identb[:], cbs[nt].ap[:].unsqueeze(1).to_broadcast([128, 4, 128]), [identb, cbs[nt]])],
                                      vc[:, nt, h, :], [vc]))
                    acc = accO.get()
                    attn_pass(h, qz, units, acc, extras=[(accI, lambda ui: cmpsel[:, ui, :], 0, nt_hi, [cmpsel])])
                    rz = finish_pass(h, 0, acc, gate, oattn, first=True, clampz=True)
                    IT = OTr.get()
                    op("act", lambda e: e.copy(out=IT[:], in_=accI[:, :]), reads=[accI], writes=[IT])
                    tpi = ring.get()
                    tpiv = tpi.ap[:].rearrange("p (g c) -> p g c", c=128)
                    for g in range(4):
                        op("pe", lambda e: e.transpose(out=tpiv[:, g, :], in_=IT[:, g * 128:(g + 1) * 128], identity=identf[:]),
                           reads=[IT, identf], writes=[tpi])
                    imp = impr.get()
                    op("dve", lambda e: e.tensor_scalar(out=imp[:], in0=tpiv[:, 0, :], scalar1=rz[:, 0:1], scalar2=None,
                                                        op0=ALU.mult), reads=[tpi, rz], writes=[imp])
                    for g in range(1, 4):
                        op("dve", lambda e: e.scalar_tensor_tensor(out=imp[:], in0=tpiv[:, g, :], scalar=rz[:, g:g + 1],
                                                                   in1=imp[:], op0=ALU.mult, op1=ALU.add),
                           reads=[tpi, rz, imp], writes=[imp])
                    op("dve", lambda e: e.tensor_tensor(out=imp[:], in0=imp[:], in1=ta[:], op=ALU.mult), reads=[imp, ta], writes=[imp])
                    op("dve", lambda e: e.tensor_tensor(out=imp[:], in0=imp[:], in1=tb_[:], op=ALU.add), reads=[imp, tb_], writes=[imp])
                    m1, m2 = smr.get(), smr.get()
                    imp2 = impr.get()
                    op("dve", lambda e: e.max(out=m1[:], in_=imp[:]), reads=[imp], writes=[m1])
                    op("dve", lambda e: e.match_replace(out=imp2[:], in_to_replace=m1[:], in_values=imp[:], imm_value=-1e9),
                       reads=[m1, imp], writes=[imp2])
                    op("dve", lambda e: e.max(out=m2[:], in_=imp2[:]), reads=[imp2], writes=[m2])
                    op("dve", lambda e: e.scalar_tensor_tensor(out=imp2[:], in0=imp[:], scalar=m2[:, 7:8], in1=ta[:],
                                                               op0=ALU.is_ge, op1=ALU.mult), reads=[imp, m2, ta], writes=[imp2])
                    selmb = selmb_r.get()
                    op("dve", lambda e: e.tensor_scalar(out=selmb[:], in0=imp2[:], scalar1=-1.0, scalar2=None, op0=ALU.add),
                       reads=[imp2], writes=[selmb])
                    selmbs.append(selmb)
                wbs = {}
                for r in range(5):
                    kt = bg - 4 + r
                    if kt < 0:
                        continue
                    cb = cbr.get()
                    dma(cb[:], winb_d[i, r], writes=[cb])
                    wbs[kt] = cb
                kt0 = max(0, bg - 4)
                nwt = bg - kt0 + 1
                wkt, wvt = wkr.get(), wvr.get()
                dma(wkt[:, 0:nwt, :], sc_wk[kt0:bg + 1].rearrange("n p c -> p n c"), writes=[wkt])
                dma(wvt[:, 0:nwt, :], sc_wv[kt0:bg + 1].rearrange("n p c -> p n c"), writes=[wvt])
                for h in range(2):
                    units = []
                    for kt in sorted(wbs):
                        ws = kt - kt0
                        units.append((wkt[:, ws, :], [wkt],
                                      [(identb[:], wbs[kt].ap[:].unsqueeze(1).to_broadcast([128, 4, 128]), [identb, wbs[kt]])],
                                      wvt[:, ws, h * 65:(h + 1) * 65], [wvt]))
                    acc = accO.get()
                    attn_pass(h, qz, units, acc)
                    finish_pass(h, 2, acc, gate, oattn, first=False)
                for h in range(2):
                    pst = ring.get()
                    pstv = bfview(pst)[:, 0:128]
                    op("pe", lambda e: e.transpose(out=pstv, in_=selmbs[h][:], identity=identb[:]), reads=[selmbs[h], identb], writes=[pst])
                    op("act", lambda e: e.copy(out=selmT[:, h, :], in_=pstv), reads=[pst], writes=[selmT])
                for h in range(2):
                    units = []
                    for kt in range(bg + 1):
                        biases = [(Esb[:, kt * 128:(kt + 1) * 128], selmT.ap[:, h:h + 1, :].to_broadcast([128, 4, 128]),
                                   [Esb, selmT])]
                        if kt == bg:
                            biases.append((identb[:], caus.ap[:].unsqueeze(1).to_broadcast([128, 4, 128]), [identb, caus]))
                        units.append((selKT[:, kt * 128:(kt + 1) * 128], [selKT], biases, selV[:, kt, h, :], [selV]))
                    acc = accO.get()
                    attn_pass(h, qz, units, acc)
                    finish_pass(h, 1, acc, gate, oattn, first=False)

            def out_chain(i, samp, oattn, ysT):
                sm = samp is not None
                za = bigb.get()
                dma(za[:], sc_za_s[samp] if sm else sc_za[i], writes=[za])
                sza = big.get()
                sigmoid_from(sza, za[:], [za])
                op("pool", lambda e: e.tensor_tensor(out=sza[:], in0=sza[:], in1=za[:], op=ALU.mult), reads=[sza, za], writes=[sza])
                ozb = bigb.get()
                op("dve", lambda e: e.tensor_tensor(out=ozb[:], in0=oattn[:], in1=sza[:], op=ALU.mult), reads=[oattn, sza], writes=[ozb])
                pzt = ring.get()
                pztv = bfview(pzt)[:, 0:512].rearrange("p (a b) -> p a b", b=128)
                for k4 in range(4):
                    op("pe", lambda e: e.transpose(out=pztv[:, k4, :], in_=ozb[:, k4 * 128:(k4 + 1) * 128], identity=identb[:]),
                       reads=[ozb, identb], writes=[pzt])
                ozT = bigb.get()
                op("act", lambda e: e.copy(out=ozT[:].rearrange("p (a b) -> p a b", b=128), in_=pztv), reads=[pzt], writes=[ozT])
                ozT3 = ozT.ap[:].rearrange("p (a b) -> p a b", b=128)
                ysT3 = ysT.ap[:].rearrange("p (a b) -> p a b", b=128)
                yield
                gmb = gmr2.get()
                dma(gmb[:], sc_gm_s[samp] if sm else sc_gm[i], writes=[gmb])
                for pc in range(4):
                    op("act", lambda e: e.activation(out=sgt[:], in_=gmb[:, pc * 512:(pc + 1) * 512], func=AF.Exp, scale=-1.0),
                       reads=[gmb], writes=[sgt])
                    op("pool", lambda e: e.tensor_scalar(out=sgt[:], in0=sgt[:], scalar1=1.0, scalar2=1.0, op0=ALU.add, op1=ALU.mult),
                       reads=[sgt], writes=[sgt])
                    with nc.allow_low_precision(reason="sigmoid gate stored in bf16"):
                        op("dve", lambda e: e.reciprocal(out=sgm[:, pc * 512:(pc + 1) * 512], in_=sgt[:]), reads=[sgt], writes=[sgm])
                for cg in range(2):
                    yield
                    pa = ring.get()
                    for k4 in range(4):
                        op("pe", lambda e: e.matmul(pa[:, :], lhsT=ozT3[:, k4, :], rhs=wla[:, k4, cg * 512:(cg + 1) * 512],
                                                    start=(k4 == 0), stop=(k4 == 3)), reads=[ozT, wla], writes=[pa])
                    pb_ = ring.get()
                    for k4 in range(4):
                        op("pe", lambda e: e.matmul(pb_[:, :], lhsT=ysT3[:, k4, :], rhs=wls[:, k4, cg * 512:(cg + 1) * 512],
                                                    start=(k4 == 0), stop=(k4 == 3)), reads=[ysT, wls], writes=[pb_])
                    m1_, m2_ = big.get(), big.get()
                    op("dve", lambda e: e.tensor_tensor(out=m1_[:], in0=pa[:, :], in1=sgm[:, cg * 512:(cg + 1) * 512], op=ALU.mult),
                       reads=[pa, sgm], writes=[m1_])
                    op("dve", lambda e: e.tensor_tensor(out=m2_[:], in0=pb_[:, :], in1=sgm[:, 1024 + cg * 512:1024 + (cg + 1) * 512],
                                                        op=ALU.mult), reads=[pb_, sgm], writes=[m2_])
                    op("pool", lambda e: e.tensor_tensor(out=mbt[:, cg * 512:(cg + 1) * 512], in0=m1_[:], in1=m2_[:], op=ALU.add),
                       reads=[m1_, m2_], writes=[mbt])
                yield
                pmt = ring.get()
                pmtv = bfview(pmt).rearrange("p (a b) -> p a b", b=128)
                for k8 in range(8):
                    op("pe", lambda e: e.transpose(out=pmtv[:, k8, :], in_=mbt[:, k8 * 128:(k8 + 1) * 128], identity=identb[:]),
                       reads=[mbt, identb], writes=[pmt])
                op("act", lambda e: e.copy(out=mT[:], in_=pmtv), reads=[pmt], writes=[mT])
                xt = xr2.get()
                dma(xt[:], xs_d[samp] if sm else xo[i * 128:(i + 1) * 128, :], writes=[xt])
                res = xt
                for cg in range(2):
                    yield
                    py = ring.get()
                    for k8 in range(8):
                        op("pe", lambda e: e.matmul(py[:, :], lhsT=mT[:, k8, :], rhs=wo[:, k8, cg * 512:(cg + 1) * 512],
                                                    start=(k8 == 0), stop=(k8 == 7)), reads=[mT, wo], writes=[py])
                    op("dve", lambda e: e.tensor_tensor(out=res[:, cg * 512:(cg + 1) * 512], in0=py[:, :],
                                                        in1=xt[:, cg * 512:(cg + 1) * 512], op=ALU.add), reads=[py, xt], writes=[xt])
                yield
                ss = smr.get()
                op("act", lambda e: e.activation(out=mbt[:], in_=res[:], func=AF.Square, accum_out=ss[:, 0:1]),
                   reads=[res], writes=[mbt, ss])
                op("act", lambda e: e.activation(out=ss[:, 0:1], in_=ss[:, 0:1], func=AF.Ln, scale=1.0 / D, bias=1e-6),
                   reads=[ss], writes=[ss])
                op("act", lambda e: e.activation(out=ss[:, 0:1], in_=ss[:, 0:1], func=AF.Exp, scale=-0.5), reads=[ss], writes=[ss])
                op("dve", lambda e: e.scalar_tensor_tensor(out=res[:], in0=res[:], scalar=ss[:, 0:1], in1=fg[:],
                                                           op0=ALU.mult, op1=ALU.mult), reads=[res, ss, fg], writes=[res])
                if sm:
                    dma(ys_o[samp], res[0:8, :], reads=[res])
                else:
                    dma(y_o[i * 128:(i + 1) * 128, :], res[:], reads=[res])

            if NSB > 0:
                class _View:
                    def __init__(self, ap):
                        self.ap = ap

                    def __getitem__(self, idx):
                        return self.ap[idx]
                kcs_p = vcTs_p = sgm
                kcs = _View(sgm.ap[:, 1024:2048])
                vcTs = _View(sgm.ap[:, 0:1024])
                cmpsel_p = gmr2.bufs[0]
                cmpsel_s = _View(cmpsel_p.ap[:].rearrange("p (a b) -> p a b", b=256))
                vcs = fw.sbuf("vcs", [128, 8, 2, 65], BF16)
                cb0 = fw.sbuf("cb0_t", [128, 128], BF16)
                swb0 = fw.sbuf("swb0_t", [128, 128], BF16)
                ptb2 = fw.sbuf("ptb2", [128, 128], I32)
                ptf2 = fw.sbuf("ptf2", [128, 128], F32)
                io3 = fw.sbuf("io3", [128, 1], F32)
                idxs = fw.sbuf("idxs", [128, 128], I32)
                pgr2 = Ring([fw.sbuf(f"pgs{i}", [128, 256], F32) for i in range(2)])
                kbr = Ring([fw.sbuf(f"kbs{i}", [128, 128], BF16) for i in range(2)])
                KTr = Ring([fw.sbuf(f"KTs{i}", [128, 128], BF16) for i in range(3)])
                Vtr = Ring([fw.sbuf(f"Vts{i}", [128, 2, 65], BF16) for i in range(3)])
                selmT_s = fw.sbuf("selmT_s", [128, 4, 128], BF16)
                selmb_s = fw.sbuf("selmb_s", [128, 256], BF16)
                nkt = fw.sbuf("nkt", [128, 256], BF16)
                nvt = fw.sbuf("nvt", [128, 260], BF16)
                stt = fw.sbuf("stt", [128, 2, 16], F32)
                op("pool", lambda e: e.memset(vcs[:], 1.0), writes=[vcs])
                for vt_ in Vtr.bufs:
                    op("pool", lambda e: e.memset(vt_[:], 1.0), writes=[vt_])
                op("pool", lambda e: e.iota(io3[:], pattern=[[0, 1]], base=1, channel_multiplier=2,
                                            allow_small_or_imprecise_dtypes=True), writes=[io3])
                dma(cb0[:], cb0_d, writes=[cb0])
                dma(swb0[:], swb0_d, writes=[swb0])

            def stream_pass(ntiles, prep, biasfn, qz, gate, oattn, br):
                accs = [ps_bufs[0], ps_bufs[1]]

                def scores(kt):
                    KT_ap, kreads, Vfn = prep(kt)
                    return [emit_scores(KT_ap, kreads, biasfn(kt, h), qz, h) for h in range(2)], Vfn

                nxt = scores(0)
                for kt in range(ntiles):
                    Ss, Vfn = nxt
                    if kt + 1 < ntiles:
                        nxt = scores(kt + 1)
                    for h in range(2):
                        P = Pr.get()
                        op("act", lambda e: e.activation(out=P[:], in_=Ss[h][:, :], func=AF.Exp), reads=[Ss[h]], writes=[P])
                        V_ap, vreads = Vfn(h)
                        op("pe", lambda e: e.matmul(accs[h][0:65, :], lhsT=V_ap, rhs=P[:], start=(kt == 0), stop=(kt == ntiles - 1)),
                           reads=vreads + [P], writes=[accs[h]])
                for h in range(2):
                    finish_pass(h, br, accs[h], gate, oattn, first=False)

            def sample_attention(b, qz, gate, oattn):
                dma(kcs[:], sc_kc[b], writes=[sgm])
                dma(vcTs[:], sc_vcT[b], writes=[sgm])
                dma(cmpsel_s[:], cmpsel_s_d, writes=[cmpsel_p])
                dma(nkt[:], sc_nk[b], writes=[nkt])
                dma(nvt[:], sc_nv[b], writes=[nvt])
                for nt in range(8):
                    pv = ring.get()
                    pvv = bfview(pv)[:, 0:128]
                    op("pe", lambda e: e.transpose(out=pvv, in_=vcTs[:, nt * 128:(nt + 1) * 128], identity=identb[:]),
                       reads=[sgm, identb], writes=[pv])
                    op("dve", lambda e: e.tensor_copy(out=vcs[:, nt, :, 0:64], in_=pvv.rearrange("p (h d) -> p h d", d=64)),
                       reads=[pv], writes=[vcs])
                for h in range(2):
                    units = []
                    for nt in range(8):
                        bl = [(identb[:], cb0.ap[:].unsqueeze(1).to_broadcast([128, 4, 128]), [identb, cb0])] if nt == 0 else []
                        units.append((kcs[:, nt * 128:(nt + 1) * 128], [sgm], bl, vcs[:, nt, h, :], [vcs]))
                    acc = accO.get()
                    attn_pass(h, qz, units, acc, extras=[(accI, lambda ui: cmpsel_s[:, ui, 0:128], 0, 3, [cmpsel_p]),
                                                         (ypsb, lambda ui: cmpsel_s[:, ui, 128:256], 3, 7, [cmpsel_p])])
                    rz = finish_pass(h, 0, acc, gate, oattn, first=True, clampz=True)
                    imp_p, imp2_p = big.bufs[0], big.bufs[1]
                    imp, imp2 = _View(imp_p.ap[:, 0:256]), _View(imp2_p.ap[:, 0:256])
                    for t_, accb in enumerate((accI, ypsb)):
                        IT = OTr.get()
                        op("act", lambda e: e.copy(out=IT[:], in_=accb[:, :]), reads=[accb], writes=[IT])
                        tpi = ring.get()
                        tpiv = tpi.ap[:].rearrange("p (g c) -> p g c", c=128)
                        for g in range(4):
                            op("pe", lambda e: e.transpose(out=tpiv[:, g, :], in_=IT[:, g * 128:(g + 1) * 128], identity=identf[:]),
                               reads=[IT, identf], writes=[tpi])
                        dsti = imp[:, t_ * 128:(t_ + 1) * 128]
                        op("dve", lambda e: e.tensor_scalar(out=dsti, in0=tpiv[:, 0, :], scalar1=rz[:, 0:1], scalar2=None,
                                                            op0=ALU.mult), reads=[tpi, rz], writes=[imp_p])
                        for g in range(1, 4):
                            op("dve", lambda e: e.scalar_tensor_tensor(out=dsti, in0=tpiv[:, g, :], scalar=rz[:, g:g + 1],
                                                                       in1=dsti, op0=ALU.mult, op1=ALU.add),
                               reads=[tpi, rz, imp_p], writes=[imp_p])
                    op("dve", lambda e: e.memset(imp[:, 0:1], 100.0), writes=[imp_p])
                    m1, m2 = smr.get(), smr.get()
                    op("dve", lambda e: e.max(out=m1[:], in_=imp[:]), reads=[imp_p], writes=[m1])
                    op("dve", lambda e: e.match_replace(out=imp2[:], in_to_replace=m1[:], in_values=imp[:], imm_value=-1e9),
                       reads=[m1, imp_p], writes=[imp2_p])
                    op("dve", lambda e: e.max(out=m2[:], in_=imp2[:]), reads=[imp2_p], writes=[m2])
                    op("dve", lambda e: e.tensor_scalar(out=selmb_s[:], in0=imp[:], scalar1=m2[:, 6:7], scalar2=-1.0,
                                                        op0=ALU.is_ge, op1=ALU.add), reads=[imp_p, m2], writes=[selmb_s])
                    for t_ in range(2):
                        pst = ring.get()
                        pstv = bfview(pst)[:, 0:128]
                        op("pe", lambda e: e.transpose(out=pstv, in_=selmb_s[:, t_ * 128:(t_ + 1) * 128], identity=identb[:]),
                           reads=[selmb_s, identb], writes=[pst])
                        op("act", lambda e: e.copy(out=selmT_s[:, t_ * 2 + h, :], in_=pstv), reads=[pst], writes=[selmT_s])
                dma(ptb2[:], pt_d[b], writes=[ptb2])
                op("dve", lambda e: e.tensor_copy(out=ptf2[:], in_=ptb2[:]), reads=[ptb2], writes=[ptf2])
                op("dve", lambda e: e.tensor_scalar(out=ptf2[:], in0=ptf2[:], scalar1=256.0, scalar2=io3[:, 0:1],
                                                    op0=ALU.mult, op1=ALU.add), reads=[ptf2, io3], writes=[ptf2])
                op("dve", lambda e: e.tensor_copy(out=idxs[:], in_=ptf2[:]), reads=[ptf2], writes=[idxs])

                ringT = Ring([accI, ypsb])

                def tile_from_rows(pg, k):
                    kb = kbr.get()
                    Vt = Vtr.get()
                    if k % 2 == 0:
                        op("dve", lambda e: e.tensor_copy(out=kb[:], in_=pg[:, 0:128]), reads=[pg], writes=[kb])
                        op("act", lambda e: e.copy(out=Vt[:, :, 0:64], in_=pg[:, 128:256].rearrange("p (h d) -> p h d", d=64)),
                           reads=[pg], writes=[Vt])
                    else:
                        op("act", lambda e: e.copy(out=kb[:], in_=pg[:, 0:128]), reads=[pg], writes=[kb])
                        op("dve", lambda e: e.tensor_copy(out=Vt[:, :, 0:64], in_=pg[:, 128:256].rearrange("p (h d) -> p h d", d=64)),
                           reads=[pg], writes=[Vt])
                    pt2 = ringT.get()
                    p2v = bfview(pt2)[:, 0:128]
                    op("pe", lambda e: e.transpose(out=p2v, in_=kb[:], identity=identb[:]), reads=[kb, identb], writes=[pt2])
                    KT = KTr.get()
                    if k % 2 == 0:
                        op("act", lambda e: e.copy(out=KT[:], in_=p2v), reads=[pt2], writes=[KT])
                    else:
                        op("dve", lambda e: e.tensor_copy(out=KT[:], in_=p2v), reads=[pt2], writes=[KT])
                    return KT, Vt

                def prep_sel(kt):
                    if kt < 128:
                        pg = pgr2.get()
                        fw.idma(pg[:], cache_d, idxs[:, kt:kt + 1], reads=[idxs], writes=[pg])
                        KT, Vt = tile_from_rows(pg, kt)
                        return KT[:], [KT], (lambda h: (Vt[:, h, :], [Vt]))
                    return nkt[:, 0:128], [nkt], (lambda h: (nvt[:, h * 65:(h + 1) * 65], [nvt]))

                def bias_sel(kt, h):
                    if kt < 128:
                        t_ = kt // 64
                        return [(Esb[:, (kt % 64) * 128:(kt % 64 + 1) * 128],
                                 selmT_s.ap[:, t_ * 2 + h:t_ * 2 + h + 1, :].to_broadcast([128, 4, 128]), [Esb, selmT_s])]
                    return [(identb[:], caus.ap[:].unsqueeze(1).to_broadcast([128, 4, 128]), [identb, caus])]

                stream_pass(129, prep_sel, bias_sel, qz, gate, oattn, 1)

                def prep_win(r):
                    if r < 4:
                        pg = pgr2.get()
                        dma(pg[:], cwin_d[b, r * 128:(r + 1) * 128, :], writes=[pg])
                        KT, Vt = tile_from_rows(pg, r)
                        return KT[:], [KT], (lambda h: (Vt[:, h, :], [Vt]))
                    return nkt[:, 128:256], [nkt], (lambda h: (nvt[:, 130 + h * 65:130 + (h + 1) * 65], [nvt]))

                def bias_win(r, h):
                    if r == 0:
                        return [(identb[:], swb0.ap[:].unsqueeze(1).to_broadcast([128, 4, 128]), [identb, swb0])]
                    if r == 4:
                        return [(identb[:], caus.ap[:].unsqueeze(1).to_broadcast([128, 4, 128]), [identb, caus])]
                    return []

                stream_pass(5, prep_win, bias_win, qz, gate, oattn, 2)

            uT_cur = s5_block(0)
            post = None
            for bg in range(NBT):
                nxt_out = {}
                gen_ = s5_gen(bg + 1, None, nxt_out) if bg + 1 < NBT else None
                if bg >= NBP:
                    if gen_ is not None:
                        next(gen_)
                    TICK["gens"] = [g for g in (gen_, post) if g is not None]
                    TICK["n"] = 0
                    newpost = own_block(bg - NBP, uT_cur)
                    if os.environ.get('DEFER_OUT', '1') == '0' and newpost is not None:
                        for _ in newpost:
                            pass
                        newpost = None
                    TICK["gens"] = []
                    if post is not None:
                        for _ in post:
                            pass
                    post = newpost
                if gen_ is not None:
                    for _ in gen_:
                        pass
                    uT_cur = nxt_out["uT"]
            if post is not None:
                for _ in post:
                    pass
            sso = fw.sbuf("sso", [128, 2, 16], F32)
            op("dve", lambda e: e.tensor_copy(out=sso[:, 0, :], in_=hre_p[:]), reads=[hre_p], writes=[sso])
            op("dve", lambda e: e.tensor_copy(out=sso[:, 1, :], in_=him_p[:]), reads=[him_p], writes=[sso])
            dma(ssm_o, sso[:], reads=[sso])
            for b in range(NSB):
                dma(stt[:], sst_d[b], writes=[stt])
                op("dve", lambda e: e.tensor_copy(out=hre_p[:], in_=stt[:, 0, :]), reads=[stt], writes=[hre_p])
                op("dve", lambda e: e.tensor_copy(out=him_p[:], in_=stt[:, 1, :]), reads=[stt], writes=[him_p])
                uT = s5_block(None, samp=b)
                op("dve", lambda e: e.tensor_copy(out=sso[:, 0, :], in_=hre_p[:]), reads=[hre_p], writes=[sso])
                op("dve", lambda e: e.tensor_copy(out=sso[:, 1, :], in_=him_p[:]), reads=[him_p], writes=[sso])
                dma(ssms_o[b], sso[:], reads=[sso])
                own_block(b, uT, samp=b)
            fw.barrier()
        fw.stack = top
        pstore.close()
        fw.finish()
    return nc, fw


def _bf(a):
    return np.ascontiguousarray(a).astype(ml_dtypes.bfloat16)


def host_consts(NBP, NBO, half, NSB=4):
    NBT = NBP + NBO
    c = {}
    n = np.arange(512) - 1
    blk = np.arange(128)
    c0 = n[:, None] * 16
    s0 = blk[None, :] * 64
    shared = np.minimum(c0 + 32, s0 + 64) - np.maximum(c0, s0)
    cs = np.clip(shared, 0, None).astype(np.float32) / 32.0
    cs[0, :] = 0.0
    c["cmpsel"] = _bf(cs.reshape(4, 128, 128).transpose(1, 0, 2))
    first_real_tok = 0 if half == 1 else NBP * 128
    q = np.arange(128)
    cmpb = np.zeros((NBO, 4, 128, 128), np.float32)
    for i in range(NBO):
        qpos = (NBP + i) * 128 + q
        for nt in range(4):
            nn = nt * 128 + np.arange(128) - 1
            vis = (nn[:, None] * 16 + 31 <= qpos[None, :]) & (nn[:, None] >= 0) & (nn[:, None] * 16 >= first_real_tok)
            cmpb[i, nt] = np.where(vis, 0.0, NEG)
    c["cmpb"] = _bf(cmpb)
    winb = np.zeros((NBO, 5, 128, 128), np.float32)
    for i in range(NBO):
        bg = NBP + i
        qpos = bg * 128 + q
        for r in range(5):
            kpos = (bg - 4 + r) * 128 + np.arange(128)
            dist = qpos[None, :] - kpos[:, None]
            vis = (kpos[:, None] >= first_real_tok) & (dist >= 0) & (dist < 512)
            winb[i, r] = np.where(vis, 0.0, NEG)
    c["winb"] = _bf(winb)
    tka = np.zeros((NBO, 128, 128), np.float32)
    tkb = np.zeros((NBO, 128, 128), np.float32)
    fb = first_real_tok // 64
    for i in range(NBO):
        qpos = (NBP + i) * 128 + q
        valid = (blk[None, :] * 64 <= qpos[:, None]) & (blk[None, :] >= fb)
        forced = (blk[None, :] == fb) | (blk[None, :] == (qpos // 64)[:, None])
        tka[i] = valid.astype(np.float32)
        tkb[i] = np.where(valid, np.where(forced, 100.0, 0.0), -100.0)
    c["tka"], c["tkb"] = tka, tkb
    key = np.arange(128)
    c["caus"] = _bf(np.where(key[:, None] <= q[None, :], 0.0, NEG))
    EW = max(NBT, 64 if NSB else 0) * 128
    E = np.zeros((128, EW), np.float32)
    kk = np.arange(EW)
    E[kk // 64, kk] = -NEG
    c["E"] = _bf(E)
    mgh = np.zeros((128, 2), np.float32)
    mgh[:64, 0] = 1.0
    mgh[64:, 1] = 1.0
    c["mgh"] = mgh
    n = np.arange(1024) - 1
    blk = np.arange(256)
    c0 = n[:, None] * 16
    s0 = blk[None, :] * 64
    shared = np.minimum(c0 + 32, s0 + 64) - np.maximum(c0, s0)
    cs = np.clip(shared, 0, None).astype(np.float32) / 32.0
    cs[0, :] = 0.0
    c["cmpsel_s"] = _bf(cs.reshape(8, 128, 256).transpose(1, 0, 2))
    cb0 = np.zeros((128, 128), np.float32)
    cb0[0, :] = NEG
    c["cb0"] = _bf(cb0)
    c["swb0"] = _bf(np.where(key[:, None] > q[None, :], 0.0, NEG))
    return c


def host_weights(inp):
    w = {}
    w_in = inp["w_in"][0]
    wf = w_in[:, wf_cols()]
    wt = w_in[:, wt_cols()]
    w["wf"] = np.ascontiguousarray(wf.reshape(8, 128, NWF).transpose(1, 0, 2))
    w["wt"] = np.ascontiguousarray(wt.reshape(8, 128, NWT).transpose(1, 0, 2))
    w["ng"] = np.ascontiguousarray(inp["norm_g"][0].reshape(8, 128).T)
    w["fg"] = np.ascontiguousarray(np.broadcast_to(inp["final_g"][None, :], (128, D)))
    w1 = inp["cmp_w1"][0]
    w1l = w1.transpose(2, 0, 1, 3)
    w["w1"] = np.ascontiguousarray(np.concatenate([w1l, w1l], 0))
    w2 = inp["cmp_w2"][0].transpose(1, 0, 2)
    w["w2"] = np.ascontiguousarray(np.concatenate([w2, w2], 0))
    pe = inp["cmp_pe"][0].transpose(2, 0, 1)
    w["pe"] = np.ascontiguousarray(np.concatenate([pe, pe], 0))
    b1 = inp["cmp_b1"][0].T
    w["b1"] = np.ascontiguousarray(np.concatenate([b1, b1], 0))

    def st_layout(a):
        a = a.reshape((16, 2, 64) + a.shape[2:])
        a = np.moveaxis(a, 0, 2)
        return np.ascontiguousarray(a.reshape((128, 16) + a.shape[3:]))
    w["lamre"] = st_layout(inp["ssm_lam_re"][0])
    w["lamim"] = st_layout(inp["ssm_lam_im"][0])
    w["logdt"] = st_layout(np.broadcast_to(inp["ssm_log_dt"][0][:, None], (32, 64)))
    w["bre"] = st_layout(inp["ssm_b_re"][0])
    w["bim"] = st_layout(inp["ssm_b_im"][0])
    for nm, key in (("cre", "ssm_c_re"), ("cim", "ssm_c_im")):
        cc = inp[key][0].transpose(0, 2, 1)
        cl = st_layout(cc)
        bd = np.zeros((128, 16, 4, 2, 16), np.float32)
        for st in range(16):
            bd[:64, st, st % 4, 0, :] = cl[:64, st]
            bd[64:, st, st % 4, 1, :] = cl[64:, st]
        w[nm] = bd.reshape(128, 16, 128)
    w["dsk"] = np.ascontiguousarray(inp["ssm_d"][0].reshape(4, 128).T)
    w["bgl"] = np.ascontiguousarray(inp["b_glu"][0].reshape(4, 128).T)
    w["wglu"] = np.ascontiguousarray(inp["w_glu"][0].reshape(4, 128, 512).transpose(1, 0, 2))
    w["wla"] = np.ascontiguousarray(inp["w_lift_attn"][0].reshape(4, 128, D).transpose(1, 0, 2))
    w["wls"] = np.ascontiguousarray(inp["w_lift_ssm"][0].reshape(4, 128, D).transpose(1, 0, 2))
    w["wo"] = np.ascontiguousarray(inp["w_out"][0].reshape(8, 128, D).transpose(1, 0, 2))
    return w


def make_in_maps(inp, NBP, NBO, NSB=4):
    w = host_weights(inp)
    consts = [host_consts(NBP, NBO, 0, NSB), host_consts(NBP, NBO, 1, NSB)]
    NS1 = max(NSB, 1)
    cache = (np.ascontiguousarray(inp["cache_kv"][0]).reshape(-1, 256) if NSB > 0
             else np.zeros((256, 256), np.float32))

    def st_layout(a):
        a = a.reshape((16, 2, 64))
        a = np.moveaxis(a, 0, 2)
        return a.reshape(128, 16)
    maps = []
    TP, TO = NBP * 128, NBO * 128
    for c in range(8):
        s, half = c // 2, c % 2
        m = dict(w)
        m.update(consts[half])
        xs = inp["x_prompt"][s]
        m["xo"] = np.ascontiguousarray(xs[half * TP:half * TP + TO])
        m["xp"] = np.ascontiguousarray(xs[0:TP]) if half == 1 else np.zeros((TP, D), np.float32)
        bs = [min(NS1 * c + j, 31) for j in range(NS1)]
        xsp = np.zeros((NS1, 128, D), np.float32)
        xsp[:, 0:8, :] = inp["x_sample"][bs]
        m["xs"] = xsp
        m["cache"] = cache
        m["ptab"] = np.ascontiguousarray(np.broadcast_to(inp["page_table"][bs][:, None, :], (NS1, 128, 128))).astype(np.int32)
        m["cwin"] = np.ascontiguousarray(inp["cache_win_kv"][0][bs]).reshape(NS1, 512, 256)
        sst = np.zeros((NS1, 128, 2, 16), np.float32)
        for j, b in enumerate(bs):
            sst[j, :, 0, :] = st_layout(inp["state_ssm_re"][0, b])
            sst[j, :, 1, :] = st_layout(inp["state_ssm_im"][0, b])
        m["sst"] = sst
        maps.append(m)
    return maps


def _unstate(a):
    return a.reshape(2, 64, 16).transpose(2, 0, 1).reshape(32, 64)


_CACHE = {}


def kernel(**inp):
    inp = {k: np.asarray(v) for k, v in inp.items()}
    NB, NSB = 32, 4
    if "nc" not in _CACHE:
        _CACHE["nc"] = build(NB, NB, NSB, inp["cache_kv"].shape[1])[0]
    nc = _CACHE["nc"]
    maps = make_in_maps(inp, NB, NB, NSB)
    res = run_bass_kernel_spmd(nc, maps, core_ids=list(range(8))).results
    T = NB * 128
    y_p = np.zeros((4, 8192, D), np.float32)
    kv_p = np.zeros((1, 4, 8192, 4, 2, 64), np.float32)
    win_p = np.zeros((1, 4, 512, 2, 2, 64), np.float32)
    sre_p = np.zeros((1, 4, 32, 64), np.float32)
    sim_p = np.zeros((1, 4, 32, 64), np.float32)
    y_s = np.zeros((32, 8, D), np.float32)
    kv_s = np.zeros((1, 32, 8, 4, 2, 64), np.float32)
    win_s = np.zeros((1, 32, 512, 2, 2, 64), np.float32)
    sre_s = np.zeros((1, 32, 32, 64), np.float32)
    sim_s = np.zeros((1, 32, 32, 64), np.float32)
    for c in range(8):
        s_, half = c // 2, c % 2
        r = res[c]
        y_p[s_, half * T:(half + 1) * T] = r["y"]
        kv_p[0, s_, half * T:(half + 1) * T] = r["kvp"].reshape(T, 4, 2, 64)
        if half == 1:
            win_p[0, s_] = r["winp"].reshape(512, 2, 2, 64)
            sre_p[0, s_] = _unstate(r["ssmp"][:, 0, :])
            sim_p[0, s_] = _unstate(r["ssmp"][:, 1, :])
        for j in range(NSB):
            b = NSB * c + j
            y_s[b] = r["ys"][j]
            kv_s[0, b] = r["kvs"][j].reshape(8, 4, 2, 64)
            win_s[0, b] = r["wins"][j].reshape(512, 2, 2, 64)
            sre_s[0, b] = _unstate(r["ssms"][j, :, 0, :])
            sim_s[0, b] = _unstate(r["ssms"][j, :, 1, :])
    return (y_p, y_s, kv_p, win_p, sre_p, sim_p, kv_s, win_s, sre_s, sim_s)
```

```python
import contextlib
import os
import math
import numpy as np
import ml_dtypes
import concourse.bass as bass
import concourse.mybir as mybir
from concourse.bass_utils import run_bass_kernel_spmd

F32 = mybir.dt.float32
BF16 = mybir.dt.bfloat16
I32 = mybir.dt.int32
AF = mybir.ActivationFunctionType
ALU = mybir.AluOpType
AX = mybir.AxisListType

EPOCH = int(os.environ.get("EPOCH", "3000"))
NEG = -30000.0
GC = math.sqrt(2.0 / math.pi)


class Eng:
    def __init__(self, fw, name, raw):
        self.fw, self.name, self.raw = fw, name, raw
        self.sems = []
        self.count = 0
        self.known = {}

    def sem_for(self, epoch):
        while len(self.sems) <= epoch:
            self.sems.append(self.fw.new_sem(f"s_{self.name}_{len(self.sems)}"))
        return self.sems[epoch]


class Buf:
    def __init__(self, ap, name="", excl=False):
        self.ap = ap
        self.name = name
        self.writer = None
        self.readers = {}
        self.excl = excl

    def __getitem__(self, idx):
        return self.ap[idx]


class FW:
    def __init__(self, nc, stack):
        self.nc = nc
        self.stack = stack
        self.semstack = stack
        self.engs = {}
        for n, raw in (("pe", nc.tensor), ("act", nc.scalar), ("dve", nc.vector),
                       ("pool", nc.gpsimd), ("sp", nc.sync)):
            self.engs[n] = Eng(self, n, raw)
        self.dma_slots = []
        self.dma_next = 0
        self.n_dma_slots = 32
        self.ninst = 0

    def new_sem(self, name):
        return self.semstack.enter_context(self.nc.semaphore(name))

    def barrier(self):
        for e in self.engs.values():
            for e2 in self.engs.values():
                if e2 is e or e2.count == 0:
                    continue
                self._need(e, (e2, e2.count))
            for si, slot in enumerate(self.dma_slots):
                if slot[1] > 0:
                    self._need(e, ('dma', slot[0], slot[1], f"dma{si}"))

    def sbuf(self, name, shape, dtype):
        t = self.stack.enter_context(self.nc.sbuf_tensor("sb_" + name, list(shape), dtype))
        return Buf(t, name)

    def psum(self, name, shape, dtype):
        t = self.stack.enter_context(self.nc.psum_tensor("pp_" + name, list(shape), dtype))
        return Buf(t, name, excl=True)

    def _need(self, eng, dep):
        if dep is None:
            return
        if dep[0] == 'dma':
            _, sem, val, key = dep
            if eng.known.get(key, 0) >= val:
                return
            eng.raw.wait_ge(sem, val)
            eng.known[key] = val
            return
        e2, cnt = dep
        if e2 is eng and eng.name == "pe":
            return
        if eng.known.get(e2.name, 0) >= cnt:
            return
        ep = (cnt - 1) // EPOCH
        eng.raw.wait_ge(e2.sem_for(ep), cnt - ep * EPOCH)
        eng.known[e2.name] = cnt

    def _deps(self, eng, reads, writes):
        for b in reads:
            self._need(eng, b.writer)
            if b.excl:
                for r in list(b.readers.values()):
                    self._need(eng, r)
        for b in writes:
            self._need(eng, b.writer)
            for r in list(b.readers.values()):
                self._need(eng, r)

    def op(self, engname, fn, reads=(), writes=()):
        eng = self.engs[engname]
        self._deps(eng, reads, writes)
        inst = fn(eng.raw)
        eng.count += 1
        ep = (eng.count - 1) // EPOCH
        inst.then_inc(eng.sem_for(ep), 1)
        tag = (eng, eng.count)
        for b in reads:
            b.readers[eng.name] = tag
        for b in writes:
            b.writer = tag
            b.readers = {}
        self.ninst += 1
        return inst

    def dma(self, out_ap, in_ap, reads=(), writes=(), q="sp", **kw):
        eng = self.engs[q]
        self._deps(eng, reads, writes)
        if len(self.dma_slots) < self.n_dma_slots:
            self.dma_slots.append([self.new_sem(f"d{len(self.dma_slots)}"), 0])
        si = self.dma_next % self.n_dma_slots
        self.dma_next += 1
        slot = self.dma_slots[si]
        key = f"dma{si}"
        if slot[1] > 0 and eng.known.get(key, 0) < slot[1]:
            eng.raw.wait_ge(slot[0], slot[1])
            eng.known[key] = slot[1]
        slot[1] += 16
        eng.raw.dma_start(out=out_ap, in_=in_ap, **kw).then_inc(slot[0], 16)
        tag = ('dma', slot[0], slot[1], key)
        for b in reads:
            b.readers[key] = tag
        for b in writes:
            b.writer = tag
            b.readers = {}
        self.ninst += 1

    def idma(self, out_ap, rows_ap, idx_ap, reads=(), writes=()):
        eng = self.engs["pool"]
        self._deps(eng, reads, writes)
        if len(self.dma_slots) < self.n_dma_slots:
            self.dma_slots.append([self.new_sem(f"d{len(self.dma_slots)}"), 0])
        si = self.dma_next % self.n_dma_slots
        self.dma_next += 1
        slot = self.dma_slots[si]
        key = f"dma{si}"
        if slot[1] > 0 and eng.known.get(key, 0) < slot[1]:
            eng.raw.wait_ge(slot[0], slot[1])
            eng.known[key] = slot[1]
        slot[1] += 16
        eng.raw.indirect_dma_start(out=out_ap, out_offset=None, in_=rows_ap,
                                   in_offset=bass.IndirectOffsetOnAxis(ap=idx_ap, axis=0)).then_inc(slot[0], 16)
        tag = ('dma', slot[0], slot[1], key)
        for b in reads:
            b.readers[key] = tag
        for b in writes:
            b.writer = tag
            b.readers = {}
        self.ninst += 1

    def finish(self):
        eng = self.engs["sp"]
        for slot in self.dma_slots:
            if slot[1] > 0:
                eng.raw.wait_ge(slot[0], slot[1])


class Ring:
    def __init__(self, bufs):
        self.bufs = bufs
        self.i = 0

    def get(self):
        b = self.bufs[self.i % len(self.bufs)]
        self.i += 1
        return b


D = 1024
NH, NKV, G, DH = 8, 2, 4, 64
D_ATTN = 512
D_SSM = 512
NST = 16
O_Q, O_KV, O_GN, O_ZA, O_U, O_ZS, O_GM = 0, 512, 1280, 1304, 1816, 2328, 2840
NWF = 1536
NWT = 3352
T_GROUPS = [(0, 512), (512, 792), (792, 1304), (1304, 1816), (1816, 2328), (2328, 2840), (2840, 3352)]


def wf_cols():
    q = [h * 256 + g * 64 + d for g in range(4) for h in range(2) for d in range(64)]
    return np.array(q + list(range(O_U, O_U + 512)) + list(range(O_ZS, O_ZS + 512)))


def wt_cols():
    return np.array(list(range(O_KV, O_KV + 768)) + list(range(O_GN, O_GN + 24)) +
                    list(range(O_ZA, O_ZA + 512)) + list(range(O_GM, O_GM + 2048)))


import os
DBG = int(os.environ.get("DBG_STAGE", "99"))
SKIP = os.environ.get("DBG_SKIP", "")


def build(NBP, NBO, NSB=4, NPHYS=5120):
    NBT = NBP + NBO
    TP, TO = NBP * 128, NBO * 128
    NWIN = min(4, NBO)
    nc = bass.Bass("TRN2", target_bir_lowering=False)

    def din(name, shape, dt=F32):
        return nc.dram_tensor(name, list(shape), dt, kind="ExternalInput").ap()

    def dout(name, shape, dt=F32):
        return nc.dram_tensor(name, list(shape), dt, kind="ExternalOutput").ap()

    def dscr(name, shape, dt):
        return nc.dram_tensor(name, list(shape), dt, kind="Internal").ap()

    xo = din("xo", [TO, D])
    xp = din("xp", [TP, D])
    wf_d = din("wf", [128, 8, NWF])
    wt_d = din("wt", [128, 8, NWT])
    ng_d = din("ng", [128, 8])
    fg_d = din("fg", [128, D])
    w1_d = din("w1", [128, 2, 32, 64])
    w2_d = din("w2", [128, 2, 64])
    pe_d = din("pe", [128, 2, 32])
    b1_d = din("b1", [128, 2])
    cmpsel_d = din("cmpsel", [128, 4, 128], BF16)
    cmpb_d = din("cmpb", [NBO, 4, 128, 128], BF16)
    winb_d = din("winb", [NBO, 5, 128, 128], BF16)
    tka_d = din("tka", [NBO, 128, 128])
    tkb_d = din("tkb", [NBO, 128, 128])
    caus_d = din("caus", [128, 128], BF16)
    EW = max(NBT, 64 if NSB else 0) * 128
    E_d = din("E", [128, EW], BF16)
    lamre_d = din("lamre", [128, 16])
    lamim_d = din("lamim", [128, 16])
    logdt_d = din("logdt", [128, 16])
    bre_d = din("bre", [128, 16, 16])
    bim_d = din("bim", [128, 16, 16])
    cre_d = din("cre", [128, 16, 128])
    cim_d = din("cim", [128, 16, 128])
    mgh_d = din("mgh", [128, 2])
    dsk_d = din("dsk", [128, 4])
    bgl_d = din("bgl", [128, 4])
    wglu_d = din("wglu", [128, 4, 512])
    wla_d = din("wla", [128, 4, D])
    wls_d = din("wls", [128, 4, D])
    wo_d = din("wo", [128, 8, D])

    y_o = dout("y", [TO, D])
    kv_o = dout("kvp", [TO, 512])
    win_o = dout("winp", [NWIN * 128, 256])
    ssm_o = dout("ssmp", [128, 2, 16])

    sc_q = dscr("sc_q", [NBO, 128, 512], BF16)
    sc_u = dscr("sc_u", [NBT, 128, 512], BF16)
    sc_zs = dscr("sc_zs", [NBO, 128, 512], BF16)
    sc_za = dscr("sc_za", [NBO, 128, 512], BF16)
    sc_gm = dscr("sc_gm", [NBO, 128, 2048], BF16)
    sc_gn = dscr("sc_gn", [NBO, 128, 24], F32)
    NS1 = max(NSB, 1)
    xs_d = din("xs", [NS1, 128, D])
    cache_d = din("cache", [NPHYS * 256, 256])
    pt_d = din("ptab", [NS1, 128, 128], I32)
    cwin_d = din("cwin", [NS1, 512, 256])
    sst_d = din("sst", [NS1, 128, 2, 16])
    cmpsel_s_d = din("cmpsel_s", [128, 8, 256], BF16)
    cb0_d = din("cb0", [128, 128], BF16)
    swb0_d = din("swb0", [128, 128], BF16)
    ys_o = dout("ys", [NS1, 8, D])
    kvs_o = dout("kvs", [NS1, 8, 512])
    wins_o = dout("wins", [NS1, 512, 256])
    ssms_o = dout("ssms", [NS1, 128, 2, 16])
    sc_q_s = dscr("sc_q_s", [NS1, 128, 512], BF16)
    sc_u_s = dscr("sc_u_s", [NS1, 128, 512], BF16)
    sc_zs_s = dscr("sc_zs_s", [NS1, 128, 512], BF16)
    sc_za_s = dscr("sc_za_s", [NS1, 128, 512], BF16)
    sc_gm_s = dscr("sc_gm_s", [NS1, 128, 2048], BF16)
    sc_gn_s = dscr("sc_gn_s", [NS1, 128, 24], F32)
    sc_nk = dscr("sc_nk", [NS1, 128, 256], BF16)
    sc_nv = dscr("sc_nv", [NS1, 128, 260], BF16)
    sc_kc = dscr("sc_kc", [NS1, 128, 1024], BF16)
    sc_vcT = dscr("sc_vcT", [NS1, 128, 1024], BF16)
    sc_ys_s = dscr("sc_ys_s", [NS1, 128, 512], BF16)
    sc_wk = dscr("sc_wk", [NBT, 128, 128], BF16)
    sc_wv = dscr("sc_wv", [NBT, 128, 130], BF16)

    with contextlib.ExitStack() as top:
        fw = FW(nc, top)
        op, dma = fw.op, fw.dma

        identb = fw.sbuf("identb", [128, 128], BF16)
        identf = fw.sbuf("identf", [128, 128], F32)
        ps_bufs = [fw.psum(f"ps{i}", [128, 512], F32) for i in range(8)]
        pstore = contextlib.ExitStack()
        pstore.__enter__()
        fw.stack = pstore
        selKT = fw.sbuf("selKT", [128, NBT * 128], BF16)
        selV = fw.sbuf("selV", [128, NBT, 2, 65], BF16)
        winKT = fw.sbuf("winKT", [128, 8 * 128], BF16)
        winV = fw.sbuf("winV", [128, 8, 2, 65], BF16)
        kcT = fw.sbuf("kcT", [128, 512], BF16)
        vcT = fw.sbuf("vcT", [128, 512], BF16)
        vc = fw.sbuf("vc", [128, 4, 2, 65], BF16)
        cmpsel = fw.sbuf("cmpsel", [128, 4, 128], BF16)
        ring = Ring(ps_bufs[4:])
        accO = Ring(ps_bufs[0:2])
        accI = ps_bufs[2]
        ypsb = ps_bufs[3]

        def bfview(b):
            return b.ap[:].bitcast(BF16)

        op("pool", lambda e: e.memset(identf[:], 1.0), writes=[identf])
        op("pool", lambda e: e.affine_select(out=identf[:], in_=identf[:], pattern=[[-1, 128]],
                                             compare_op=ALU.is_equal, fill=0.0, base=0,
                                             channel_multiplier=1), reads=[identf], writes=[identf])
        op("dve", lambda e: e.tensor_copy(out=identb[:], in_=identf[:]), reads=[identf], writes=[identb])
        op("pool", lambda e: e.memset(selV[:], 1.0), writes=[selV])
        op("pool", lambda e: e.memset(winV[:], 1.0), writes=[winV])
        op("pool", lambda e: e.memset(vc[:], 1.0), writes=[vc])
        op("pool", lambda e: e.memset(kcT[:], 0.0), writes=[kcT])
        op("pool", lambda e: e.memset(vcT[:], 0.0), writes=[vcT])
        op("pool", lambda e: e.memset(winKT[:], 0.0), writes=[winKT])
        dma(cmpsel[:], cmpsel_d, writes=[cmpsel])

        with contextlib.ExitStack() as p1:
            fw.stack = p1
            WF = fw.sbuf("WF", [128, 8, NWF], BF16)
            WT = fw.sbuf("WT", [128, 8, NWT], BF16)
            ngt = fw.sbuf("ngt", [128, 8], F32)
            W1r = fw.sbuf("W1r", [128, 2, 32, 64], BF16)
            W2r = fw.sbuf("W2r", [128, 2, 64], BF16)
            peT = fw.sbuf("peT", [128, 2, 32], BF16)
            b1p = fw.sbuf("b1p", [128, 2], F32)
            cmpraw = fw.sbuf("cmpraw", [128, 2, 144], BF16)
            stg = Ring([fw.sbuf(f"stg{i}", [128, 1024], F32) for i in range(2)])
            dma(ngt[:], ng_d, writes=[ngt])
            k = 0
            for (wd, W, ncol) in ((wf_d, WF, NWF), (wt_d, WT, NWT)):
                for c in range(8):
                    for c0 in range(0, ncol, 1024):
                        c1 = min(ncol, c0 + 1024)
                        s = stg.get()
                        dma(s[:, 0:c1 - c0], wd[:, c, c0:c1], writes=[s])
                        en = "dve" if k % 2 == 0 else "pool"
                        k += 1
                        op(en, lambda e: e.tensor_scalar(out=W[:, c, c0:c1], in0=s[:, 0:c1 - c0],
                                                         scalar1=ngt[:, c:c + 1], scalar2=1.0, op0=ALU.mult, op1=ALU.mult),
                           reads=[s, ngt], writes=[W])
            for kv in range(2):
                for jh in range(2):
                    s = stg.get()
                    dma(s[:, 0:1024].rearrange("p (j e) -> p j e", e=64), w1_d[:, kv, jh * 16:(jh + 1) * 16], writes=[s])
                    op("dve", lambda e: e.tensor_copy(out=W1r[:, kv, jh * 16:(jh + 1) * 16],
                                                      in_=s[:, 0:1024].rearrange("p (j e) -> p j e", e=64)),
                       reads=[s], writes=[W1r])
            s = stg.get()
            dma(s[:, 0:128].rearrange("p (k e) -> p k e", e=64), w2_d, writes=[s])
            op("dve", lambda e: e.tensor_copy(out=W2r[:], in_=s[:, 0:128].rearrange("p (k e) -> p k e", e=64)),
               reads=[s], writes=[W2r])
            s = stg.get()
            dma(s[:, 0:64].rearrange("p (k e) -> p k e", e=32), pe_d, writes=[s])
            op("dve", lambda e: e.tensor_copy(out=peT[:], in_=s[:, 0:64].rearrange("p (k e) -> p k e", e=32)),
               reads=[s], writes=[peT])
            dma(b1p[:], b1_d, writes=[b1p])
            op("pool", lambda e: e.memset(cmpraw[:], 0.0), writes=[cmpraw])
            pb = ring.get()
            for kv in range(2 if DBG >= 2 else 0):
                for h in range(2):
                    rows = slice(h * 64, (h + 1) * 64)
                    for j in range(32):
                        op("pe", lambda e: e.matmul(pb[rows, kv:kv + 1], lhsT=W1r[rows, kv, j, :],
                                                    rhs=peT[rows, kv, j:j + 1], start=(j == 0), stop=(j == 31)),
                           reads=[W1r, peT], writes=[pb])
            if DBG >= 2:
                op("dve", lambda e: e.tensor_tensor(out=b1p[:], in0=pb[:, 0:2], in1=b1p[:], op=ALU.add),
                   reads=[pb, b1p], writes=[b1p])

            xr = Ring([fw.sbuf(f"xt{i}", [128, D], F32) for i in range(2)])
            junk = fw.sbuf("junk", [128, D], BF16)
            ssr = Ring([fw.sbuf(f"ss{i}", [128, 1], F32) for i in range(2)])
            xbr = Ring([fw.sbuf(f"xb{i}", [128, D], BF16) for i in range(2)])
            xTr = Ring([fw.sbuf(f"xT{i}", [128, 8, 128], BF16) for i in range(2)])
            kvor = Ring([fw.sbuf(f"kvo{i}", [128, 512], F32) for i in range(2)])
            kv1r = Ring([fw.sbuf(f"kv1f{i}", [128, 280], F32) for i in range(2)])
            kvbr = Ring([fw.sbuf(f"kvb{i}", [128, 768], BF16) for i in range(2)])
            zar = Ring([fw.sbuf(f"za{i}", [128, 512], BF16) for i in range(2)])
            gmr = Ring([fw.sbuf(f"gmb{i}", [128, 2048], BF16) for i in range(2)])
            f4r = Ring([fw.sbuf(f"f4{i}", [128, 4, 128], BF16) for i in range(3)])
            hidr = Ring([fw.sbuf(f"hid{i}", [128, 4, 128], F32) for i in range(2)])
            gTr = Ring([fw.sbuf(f"gT{i}", [128, 2, 64], BF16) for i in range(2)])
            tmpK = fw.sbuf("tmpK", [128, 2, 128], BF16)
            tmpV = fw.sbuf("tmpV", [128, 2, 2, 65], BF16)
            op("pool", lambda e: e.memset(tmpV[:], 1.0), writes=[tmpV])

            def compress(raw, nb, dst_k, dst_v, col0):
                W = 2 * nb
                hp = ring.get()
                hv = hp.ap[:, 0:W].rearrange("p (k m) -> p k m", m=nb)
                cr = raw.ap[:].rearrange("p k (m s) -> p k m s", s=16)
                for kv in range(2):
                    for h in range(2):
                        rows = slice(h * 64, (h + 1) * 64)
                        for j in range(32):
                            rhs = cr[rows, kv, 0:nb, j] if j < 16 else cr[rows, kv, 1:nb + 1, j - 16]
                            op("pe", lambda e: e.matmul(hv[rows, kv, :], lhsT=W1r[rows, kv, j, :], rhs=rhs,
                                                        start=(j == 0), stop=(j == 31)),
                               reads=[W1r, raw], writes=[hp])
                hd = hidr.get()
                op("dve", lambda e: e.tensor_tensor(out=hd[:, 0, 0:W].rearrange("p (k m) -> p k m", m=nb), in0=hv,
                                                    in1=b1p[:].unsqueeze(2).to_broadcast([128, 2, nb]),
                                                    op=ALU.add), reads=[hp, b1p], writes=[hd])
                op("dve", lambda e: e.tensor_tensor(out=hd[:, 1, 0:W], in0=hd[:, 0, 0:W], in1=hd[:, 0, 0:W], op=ALU.mult),
                   reads=[hd], writes=[hd])
                op("dve", lambda e: e.tensor_scalar(out=hd[:, 1, 0:W], in0=hd[:, 1, 0:W], scalar1=0.044715, scalar2=1.0,
                                                    op0=ALU.mult, op1=ALU.add), reads=[hd], writes=[hd])
                op("dve", lambda e: e.tensor_tensor(out=hd[:, 2, 0:W], in0=hd[:, 1, 0:W], in1=hd[:, 0, 0:W], op=ALU.mult),
                   reads=[hd], writes=[hd])
                op("act", lambda e: e.activation(out=hd[:, 3, 0:W], in_=hd[:, 2, 0:W], func=AF.Exp, scale=-2.0 * GC),
                   reads=[hd], writes=[hd])
                op("dve", lambda e: e.tensor_scalar(out=hd[:, 3, 0:W], in0=hd[:, 3, 0:W], scalar1=1.0, scalar2=None,
                                                    op0=ALU.add), reads=[hd], writes=[hd])
                op("dve", lambda e: e.reciprocal(out=hd[:, 3, 0:W], in_=hd[:, 3, 0:W]), reads=[hd], writes=[hd])
                gT = gTr.get()
                op("dve", lambda e: e.tensor_tensor(out=gT[:, :, 0:nb], in0=hd[:, 0, 0:W].rearrange("p (k m) -> p k m", m=nb),
                                                    in1=hd[:, 3, 0:W].rearrange("p (k m) -> p k m", m=nb), op=ALU.mult),
                   reads=[hd], writes=[gT])
                kp = ring.get()
                kpv = kp.ap[:, 0:W].rearrange("p (k m) -> p k m", m=nb)
                for kv in range(2):
                    for h in range(2):
                        rows = slice(h * 64, (h + 1) * 64)
                        op("pe", lambda e: e.matmul(kpv[rows, kv, :], lhsT=W2r[rows, kv, :], rhs=gT[rows, kv, 0:nb],
                                                    start=True, stop=True), reads=[W2r, gT], writes=[kp])
                op("act", lambda e: e.copy(out=dst_k[:, col0:col0 + nb], in_=kpv[:, 0, :]), reads=[kp], writes=[dst_k])
                op("act", lambda e: e.copy(out=dst_v[:, col0:col0 + nb], in_=kpv[:, 1, :]), reads=[kp], writes=[dst_v])

            def front(bg, samp=None):
                sm = samp is not None
                own = sm or bg >= NBP
                i = samp if sm else bg - NBP
                d_gn, d_za, d_gm, d_q, d_zs = ((sc_gn_s, sc_za_s, sc_gm_s, sc_q_s, sc_zs_s) if sm
                                               else (sc_gn, sc_za, sc_gm, sc_q, sc_zs))
                xt = xr.get()
                if sm:
                    src = xs_d[samp]
                else:
                    src = xo[i * 128:(i + 1) * 128, :] if own else xp[bg * 128:(bg + 1) * 128, :]
                dma(xt[:], src, writes=[xt])
                ss = ssr.get()
                op("act", lambda e: e.activation(out=junk[:], in_=xt[:], func=AF.Square, accum_out=ss[:]),
                   reads=[xt], writes=[junk, ss])
                op("act", lambda e: e.activation(out=ss[:], in_=ss[:], func=AF.Ln, scale=1.0 / D, bias=1e-6),
                   reads=[ss], writes=[ss])
                op("act", lambda e: e.activation(out=ss[:], in_=ss[:], func=AF.Exp, scale=-0.5),
                   reads=[ss], writes=[ss])
                xb = xbr.get()
                op("dve", lambda e: e.tensor_scalar(out=xb[:], in0=xt[:], scalar1=ss[:, 0:1], scalar2=None,
                                                    op0=ALU.mult), reads=[xt, ss], writes=[xb])
                pt = ring.get()
                ptv = bfview(pt).rearrange("p (a b) -> p a b", b=128)
                for c in range(8):
                    op("pe", lambda e: e.transpose(out=ptv[:, c, :], in_=xb[:, c * 128:(c + 1) * 128],
                                                   identity=identb[:]), reads=[xb, identb], writes=[pt])
                xT = xTr.get()
                op("act", lambda e: e.copy(out=xT[:], in_=ptv), reads=[pt], writes=[xT])

                def tproj(gi):
                    c0, c1 = T_GROUPS[gi]
                    ps = ring.get()
                    for c in range(8):
                        op("pe", lambda e: e.matmul(ps[:, 0:c1 - c0], lhsT=xT[:, c, :], rhs=WT[:, c, c0:c1],
                                                    start=(c == 0), stop=(c == 7)), reads=[xT, WT], writes=[ps])
                    return ps

                kvb = kvbr.get()
                ps = tproj(0)
                if own:
                    kvo = kvor.get()
                    op("dve", lambda e: e.tensor_copy(out=kvo[:], in_=ps[:, :]), reads=[ps], writes=[kvo])
                    if sm:
                        dma(kvs_o[samp], kvo[0:8, :], reads=[kvo])
                    else:
                        dma(kv_o[i * 128:(i + 1) * 128, :], kvo[:], reads=[kvo])
                op("act", lambda e: e.copy(out=kvb[:, 0:512], in_=ps[:, :]), reads=[ps], writes=[kvb])
                ps = tproj(1)
                op("act", lambda e: e.copy(out=kvb[:, 512:768], in_=ps[:, 0:256]), reads=[ps], writes=[kvb])
                if own:
                    kv1 = kv1r.get()
                    op("dve", lambda e: e.tensor_copy(out=kv1[:], in_=ps[:, 0:280]), reads=[ps], writes=[kv1])
                    dma(d_gn[i], kv1[:, 256:280], reads=[kv1])
                    if sm:
                        dma(wins_o[samp, 504:512, :], kv1[0:8, 0:256], reads=[kv1])
                        dma(wins_o[samp, 0:504, :], cwin_d[samp, 8:512, :])
                    elif i >= NBO - NWIN:
                        w = i - (NBO - NWIN)
                        dma(win_o[w * 128:(w + 1) * 128, :], kv1[:, 0:256], reads=[kv1])
                    ps = tproj(2)
                    za = zar.get()
                    op("act", lambda e: e.copy(out=za[:], in_=ps[:, :]), reads=[ps], writes=[za])
                    dma(d_za[i], za[:], reads=[za])
                    gmb = gmr.get()
                    for q4 in range(4):
                        ps = tproj(3 + q4)
                        if q4 % 2 == 0:
                            op("dve", lambda e: e.tensor_copy(out=gmb[:, q4 * 512:(q4 + 1) * 512], in_=ps[:, :]),
                               reads=[ps], writes=[gmb])
                        else:
                            op("act", lambda e: e.copy(out=gmb[:, q4 * 512:(q4 + 1) * 512], in_=ps[:, :]),
                               reads=[ps], writes=[gmb])
                    dma(d_gm[i], gmb[:], reads=[gmb])
                pt2 = ring.get()
                p2v = bfview(pt2).rearrange("p (a b) -> p a b", b=128)
                for kk, c0 in enumerate((0, 128, 256, 512)):
                    op("pe", lambda e: e.transpose(out=p2v[:, kk, :], in_=kvb[:, c0:c0 + 128], identity=identb[:]),
                       reads=[kvb, identb], writes=[pt2])
                if sm:
                    op("act", lambda e: e.copy(out=tmpK[:], in_=p2v[:, 2:4, :]), reads=[pt2], writes=[tmpK])
                    op("pool", lambda e: e.tensor_copy(out=tmpV[:, 0, :, 0:64],
                                                       in_=kvb[:, 384:512].rearrange("p (h d) -> p h d", d=64)),
                       reads=[kvb], writes=[tmpV])
                    op("pool", lambda e: e.tensor_copy(out=tmpV[:, 1, :, 0:64],
                                                       in_=kvb[:, 640:768].rearrange("p (h d) -> p h d", d=64)),
                       reads=[kvb], writes=[tmpV])
                    dma(sc_nk[samp], tmpK[:].rearrange("p a b -> p (a b)"), reads=[tmpK])
                    dma(sc_nv[samp], tmpV[:].rearrange("p a h c -> p (a h c)"), reads=[tmpV])
                else:
                    op("pool", lambda e: e.tensor_copy(out=cmpraw[:, :, 0:16], in_=cmpraw[:, :, 128:144]),
                       reads=[cmpraw], writes=[cmpraw])
                    op("dve", lambda e: e.tensor_copy(out=cmpraw[:, :, 16:144], in_=p2v[:, 0:2, :]),
                       reads=[pt2, cmpraw], writes=[cmpraw])
                    op("act", lambda e: e.copy(out=selKT[:, bg * 128:(bg + 1) * 128], in_=p2v[:, 2, :]),
                       reads=[pt2], writes=[selKT])
                    wslot = bg % 8
                    op("act", lambda e: e.copy(out=winKT[:, wslot * 128:(wslot + 1) * 128], in_=p2v[:, 3, :]),
                       reads=[pt2], writes=[winKT])
                    op("pool", lambda e: e.tensor_copy(out=selV[:, bg, :, 0:64],
                                                       in_=kvb[:, 384:512].rearrange("p (h d) -> p h d", d=64)),
                       reads=[kvb], writes=[selV])
                    op("pool", lambda e: e.tensor_copy(out=winV[:, wslot, :, 0:64],
                                                       in_=kvb[:, 640:768].rearrange("p (h d) -> p h d", d=64)),
                       reads=[kvb], writes=[winV])
                    if bg >= NBP - 4:
                        dma(sc_wk[bg], winKT[:, wslot * 128:(wslot + 1) * 128], reads=[winKT])
                        dma(sc_wv[bg], winV[:, wslot].rearrange("p h c -> p (h c)"), reads=[winV])

                def fproj(col0, scale):
                    ps4 = ring.get()
                    p4 = ps4.ap[:].rearrange("p (a b) -> p a b", b=128)
                    for k4 in range(4):
                        for c in range(8):
                            op("pe", lambda e: e.matmul(p4[:, k4, :], lhsT=WF[:, c, col0 + 128 * k4:col0 + 128 * (k4 + 1)],
                                                        rhs=xT[:, c, :], start=(c == 0), stop=(c == 7)),
                               reads=[WF, xT], writes=[ps4])
                    f4 = f4r.get()
                    op("act", lambda e: e.activation(out=f4[:], in_=p4, func=AF.Copy, scale=scale),
                       reads=[ps4], writes=[f4])
                    return f4

                f4 = fproj(512, 1.0)
                dma((sc_u_s[samp] if sm else sc_u[bg]).rearrange("p (a b) -> p a b", b=128), f4[:], reads=[f4])
                if own:
                    f4 = fproj(0, 0.125)
                    dma(d_q[i].rearrange("p (a b) -> p a b", b=128), f4[:], reads=[f4])
                    f4 = fproj(1024, 1.0)
                    dma(d_zs[i].rearrange("p (a b) -> p a b", b=128), f4[:], reads=[f4])
                if sm:
                    return
                compress(cmpraw, 8, kcT, vcT, 8 * bg)
                nt = bg // 16
                pv = ring.get()
                pvv = bfview(pv)[:, 0:128]
                op("pe", lambda e: e.transpose(out=pvv, in_=vcT[:, nt * 128:(nt + 1) * 128], identity=identb[:]),
                   reads=[vcT, identb], writes=[pv])
                op("dve", lambda e: e.tensor_copy(out=vc[:, nt, :, 0:64], in_=pvv.rearrange("p (h d) -> p h d", d=64)),
                   reads=[pv], writes=[vc])

            if NSB > 0:
                rawS = fw.sbuf("rawS", [128, 2, 1040], BF16)
                kcS = fw.sbuf("kcS", [128, 1024], BF16)
                vcS = fw.sbuf("vcS", [128, 1024], BF16)
                ptb = fw.sbuf("ptb", [128, 128], I32)
                ptf = fw.sbuf("ptf", [128, 128], F32)
                io2 = fw.sbuf("io2", [128, 1], F32)
                idxc = fw.sbuf("idxc", [128, 128], I32)
                pgr = Ring([fw.sbuf(f"pg{i}", [128, 256], F32) for i in range(4)])
                pgbr = Ring([fw.sbuf(f"pgb{i}", [128, 256], BF16) for i in range(2)])
                op("pool", lambda e: e.memset(rawS[:], 0.0), writes=[rawS])
                op("pool", lambda e: e.iota(io2[:], pattern=[[0, 1]], base=0, channel_multiplier=2,
                                            allow_small_or_imprecise_dtypes=True), writes=[io2])

            for bg in range(NBT):
                front(bg)
            for b in range(NSB):
                front(None, samp=b)
                dma(ptb[:], pt_d[b], writes=[ptb])
                op("dve", lambda e: e.tensor_copy(out=ptf[:], in_=ptb[:]), reads=[ptb], writes=[ptf])
                op("dve", lambda e: e.tensor_scalar(out=ptf[:], in0=ptf[:], scalar1=256.0, scalar2=io2[:, 0:1],
                                                    op0=ALU.mult, op1=ALU.add), reads=[ptf, io2], writes=[ptf])
                op("dve", lambda e: e.tensor_copy(out=idxc[:], in_=ptf[:]), reads=[ptf], writes=[idxc])
                for kt in range(128):
                    pg = pgr.get()
                    fw.idma(pg[:], cache_d, idxc[:, kt:kt + 1], reads=[idxc], writes=[pg])
                    pb_ = pgbr.get()
                    if kt % 2 == 0:
                        op("dve", lambda e: e.tensor_copy(out=pb_[:], in_=pg[:]), reads=[pg], writes=[pb_])
                    else:
                        op("act", lambda e: e.copy(out=pb_[:], in_=pg[:]), reads=[pg], writes=[pb_])
                    pt2 = ring.get()
                    p2v = bfview(pt2).rearrange("p (a b) -> p a b", b=128)
                    for kk in range(2):
                        op("pe", lambda e: e.transpose(out=p2v[:, kk, :], in_=pb_[:, kk * 128:(kk + 1) * 128],
                                                       identity=identb[:]), reads=[pb_, identb], writes=[pt2])
                    slot = kt % 8
                    if kt % 2 == 0:
                        op("act", lambda e: e.copy(out=rawS[:, :, 16 + 128 * slot:16 + 128 * (slot + 1)], in_=p2v[:, 0:2, :]),
                           reads=[pt2], writes=[rawS])
                    else:
                        op("dve", lambda e: e.tensor_copy(out=rawS[:, :, 16 + 128 * slot:16 + 128 * (slot + 1)],
                                                          in_=p2v[:, 0:2, :]), reads=[pt2], writes=[rawS])
                    if slot == 7:
                        compress(rawS, 64, kcS, vcS, 64 * (kt // 8))
                        op("dve", lambda e: e.tensor_copy(out=rawS[:, :, 0:16], in_=rawS[:, :, 1024:1040]),
                           reads=[rawS], writes=[rawS])
                dma(sc_kc[b], kcS[:], reads=[kcS])
                dma(sc_vcT[b], vcS[:], reads=[vcS])
        fw.barrier()
        fw.stack = pstore


        with contextlib.ExitStack() as p2:
            fw.stack = p2
            wla = fw.sbuf("wla", [128, 4, D], BF16)
            wls = fw.sbuf("wls", [128, 4, D], BF16)
            wo = fw.sbuf("wo", [128, 8, D], BF16)
            wglu = fw.sbuf("wglu", [128, 4, 512], BF16)
            fg = fw.sbuf("fgt", [128, D], F32)
            Esb = fw.sbuf("Esb", [128, EW], BF16)
            caus = fw.sbuf("caust", [128, 128], BF16)
            pset = contextlib.ExitStack()
            pset.__enter__()
            BbT = fw.sbuf("BbT", [128, 2, 16, 128], BF16)
            CTre = fw.sbuf("CTre", [128, 16, 128], BF16)
            CTimn = fw.sbuf("CTimn", [128, 16, 128], BF16)
            cosT = fw.sbuf("cosT", [128, 16, 128], F32)
            sinT = fw.sbuf("sinT", [128, 16, 128], F32)
            p2small = [fw.sbuf(f"p2s{i}", [128, 16], F32) for i in range(24)]
            dsk = fw.sbuf("dsk", [128, 4], F32)
            nbgl = fw.sbuf("nbgl", [128, 4], F32)
            fw.stack = pset
            stg = Ring([fw.sbuf(f"stg2_{i}", [128, 1024], F32) for i in range(2)])
            dma(fg[:], fg_d, writes=[fg])
            dma(Esb[:], E_d, writes=[Esb])
            dma(caus[:], caus_d, writes=[caus])
            k = 0
            for (wd, W, nk, ncol) in ((wla_d, wla, 4, D), (wls_d, wls, 4, D), (wo_d, wo, 8, D), (wglu_d, wglu, 4, 512)):
                for c in range(nk):
                    s_ = stg.get()
                    dma(s_[:, 0:ncol], wd[:, c, :], writes=[s_])
                    en = "dve" if k % 2 == 0 else "pool"
                    k += 1
                    op(en, lambda e: e.tensor_copy(out=W[:, c, :], in_=s_[:, 0:ncol]), reads=[s_], writes=[W])

            def small(name, shape=(128, 16), dt=F32):
                if tuple(shape) == (128, 16) and dt == F32 and p2small:
                    return p2small.pop()
                return fw.sbuf(name, list(shape), dt)

            lre, lim, ldt = small("lre"), small("lim"), small("ldt")
            dma(lre[:], lamre_d, writes=[lre])
            dma(lim[:], lamim_d, writes=[lim])
            dma(ldt[:], logdt_d, writes=[ldt])
            mgh = small("mgh", (128, 2))
            dma(dsk[:], dsk_d, writes=[dsk])
            dma(nbgl[:], bgl_d, writes=[nbgl])
            dma(mgh[:], mgh_d, writes=[mgh])
            tmp = [small(f"s5t{i}") for i in range(8)]
            rho, c1, s1 = small("rho"), small("c1"), small("s1")
            ki = small("ki", dt=I32)

            def tt(out, a, b, o, eng="dve"):
                op(eng, lambda e: e.tensor_tensor(out=out[:], in0=a[:], in1=b[:], op=o), reads=[a, b], writes=[out])

            def ts(out, a, s1_, s2_, o0, o1=None, eng="dve"):
                if o1 is None:
                    op(eng, lambda e: e.tensor_scalar(out=out[:], in0=a[:], scalar1=s1_, scalar2=None, op0=o0),
                       reads=[a], writes=[out])
                else:
                    op(eng, lambda e: e.tensor_scalar(out=out[:], in0=a[:], scalar1=s1_, scalar2=s2_, op0=o0, op1=o1),
                       reads=[a], writes=[out])

            dtt, lr, th, r_ = tmp[0], tmp[1], tmp[2], tmp[3]
            op("act", lambda e: e.activation(out=dtt[:], in_=ldt[:], func=AF.Exp), reads=[ldt], writes=[dtt])
            tt(lr, lre, dtt, ALU.mult)
            tt(th, lim, dtt, ALU.mult)
            op("act", lambda e: e.activation(out=rho[:], in_=lr[:], func=AF.Exp), reads=[lr], writes=[rho])
            ts(tmp[4], th, 1.0 / (2 * math.pi), None, ALU.mult)
            op("dve", lambda e: e.tensor_copy(out=ki[:], in_=tmp[4][:]), reads=[tmp[4]], writes=[ki])
            op("dve", lambda e: e.tensor_copy(out=tmp[4][:], in_=ki[:]), reads=[ki], writes=[tmp[4]])
            op("dve", lambda e: e.scalar_tensor_tensor(out=r_[:], in0=tmp[4][:], scalar=-2.0 * math.pi, in1=th[:],
                                                       op0=ALU.mult, op1=ALU.add), reads=[tmp[4], th], writes=[r_])
            s4, c4, s2, c2 = tmp[4], tmp[5], tmp[6], tmp[7]
            hp_ = small("halfpi", (128, 1))
            op("pool", lambda e: e.memset(hp_[:], math.pi / 2), writes=[hp_])
            op("act", lambda e: e.activation(out=s4[:], in_=r_[:], func=AF.Sin, scale=0.25), reads=[r_], writes=[s4])
            op("act", lambda e: e.activation(out=c4[:], in_=r_[:], func=AF.Sin, scale=0.25, bias=hp_[:, 0:1]),
               reads=[r_, hp_], writes=[c4])
            op("dve", lambda e: e.scalar_tensor_tensor(out=s2[:], in0=s4[:], scalar=2.0, in1=c4[:], op0=ALU.mult,
                                                       op1=ALU.mult), reads=[s4, c4], writes=[s2])
            tt(c2, s4, s4, ALU.mult)
            ts(c2, c2, -2.0, 1.0, ALU.mult, ALU.add)
            op("dve", lambda e: e.scalar_tensor_tensor(out=s1[:], in0=s2[:], scalar=2.0, in1=c2[:], op0=ALU.mult,
                                                       op1=ALU.mult), reads=[s2, c2], writes=[s1])
            tt(c1, s2, s2, ALU.mult)
            ts(c1, c1, -2.0, 1.0, ALU.mult, ALU.add)
            are, aim, am1, den = tmp[0], tmp[1], tmp[2], tmp[3]
            tt(are, rho, c1, ALU.mult)
            tt(aim, rho, s1, ALU.mult)
            ts(am1, are, -1.0, None, ALU.add)
            tt(den, lre, lre, ALU.mult)
            tt(tmp[4], lim, lim, ALU.mult)
            tt(den, den, tmp[4], ALU.add)
            op("dve", lambda e: e.reciprocal(out=den[:], in_=den[:]), reads=[den], writes=[den])
            fre, fim = small("fre"), small("fim")
            tt(tmp[4], am1, lre, ALU.mult)
            tt(tmp[5], aim, lim, ALU.mult)
            tt(tmp[4], tmp[4], tmp[5], ALU.add)
            tt(fre, tmp[4], den, ALU.mult)
            tt(tmp[4], aim, lre, ALU.mult)
            tt(tmp[5], am1, lim, ALU.mult)
            tt(tmp[4], tmp[4], tmp[5], ALU.subtract)
            tt(fim, tmp[4], den, ALU.mult)
            bre, bim = small("bre", (128, 16, 16)), small("bim", (128, 16, 16))
            dma(bre[:], bre_d, writes=[bre])
            dma(bim[:], bim_d, writes=[bim])
            bt = [small(f"bt{i}", (128, 16, 16)) for i in range(3)]
            Mb = fw.sbuf("Mb", [128, 16, 2, 16], BF16)
            Mz = fw.sbuf("Mz", [128, 16, 4, 32], BF16)
            op("pool", lambda e: e.memset(Mz[:], 0.0), writes=[Mz])

            def bc16(t_):
                return t_.ap[:].unsqueeze(2).to_broadcast([128, 16, 16])

            for ri in range(2):
                fa, fb_, sgn = (fre, fim, ALU.subtract) if ri == 0 else (fim, fre, ALU.add)
                op("dve", lambda e: e.tensor_tensor(out=bt[0][:], in0=bre[:], in1=bc16(fa), op=ALU.mult),
                   reads=[bre, fa], writes=[bt[0]])
                op("dve", lambda e: e.tensor_tensor(out=bt[1][:], in0=bim[:], in1=bc16(fb_), op=ALU.mult),
                   reads=[bim, fb_], writes=[bt[1]])
                op("dve", lambda e: e.tensor_tensor(out=bt[2][:], in0=bt[0][:], in1=bt[1][:], op=sgn),
                   reads=[bt[0], bt[1]], writes=[bt[2]])
                op("dve", lambda e: e.tensor_tensor(
                    out=Mb[:], in0=bt[2].ap[:].unsqueeze(2).to_broadcast([128, 16, 2, 16]),
                    in1=mgh.ap[:].unsqueeze(1).unsqueeze(3).to_broadcast([128, 16, 2, 16]), op=ALU.mult),
                   reads=[bt[2], mgh], writes=[Mb])
                Mz5 = Mz.ap[:].rearrange("p (a q) r c -> p a q r c", q=4)
                Mb4 = Mb.ap[:].rearrange("p (a q) g c -> p a q (g c)", q=4)
                for q_ in range(4):
                    op("dve", lambda e: e.tensor_copy(out=Mz5[:, :, q_, q_, :], in_=Mb4[:, :, q_, :]),
                       reads=[Mb], writes=[Mz])
                for st in range(16):
                    pz = ring.get()
                    pzv = bfview(pz)[:, 0:128]
                    op("pe", lambda e: e.transpose(out=pzv, in_=Mz[:, st].rearrange("p r c -> p (r c)"),
                                                   identity=identb[:]), reads=[Mz, identb], writes=[pz])
                    op("act", lambda e: e.copy(out=BbT[:, ri, st, :], in_=pzv), reads=[pz], writes=[BbT])
            for half_ in range(2):
                s_ = stg.get()
                dma(s_[:, 0:1024].rearrange("p (a b) -> p a b", b=128), cre_d[:, half_ * 8:(half_ + 1) * 8, :], writes=[s_])
                op("dve", lambda e: e.tensor_copy(out=CTre[:, half_ * 8:(half_ + 1) * 8, :],
                                                  in_=s_[:, 0:1024].rearrange("p (a b) -> p a b", b=128)),
                   reads=[s_], writes=[CTre])
                s_ = stg.get()
                dma(s_[:, 0:1024].rearrange("p (a b) -> p a b", b=128), cim_d[:, half_ * 8:(half_ + 1) * 8, :], writes=[s_])
                op("dve", lambda e: e.tensor_scalar(out=CTimn[:, half_ * 8:(half_ + 1) * 8, :],
                                                    in0=s_[:, 0:1024].rearrange("p (a b) -> p a b", b=128),
                                                    scalar1=-1.0, scalar2=None, op0=ALU.mult), reads=[s_], writes=[CTimn])
            op("pool", lambda e: e.memset(cosT[:], 1.0), writes=[cosT])
            op("pool", lambda e: e.memset(sinT[:], 0.0), writes=[sinT])
            emr, emi = small("emr"), small("emi")
            op("dve", lambda e: e.tensor_copy(out=emr[:], in_=c1[:]), reads=[c1], writes=[emr])
            op("dve", lambda e: e.tensor_copy(out=emi[:], in_=s1[:]), reads=[s1], writes=[emi])
            tb = [fw.sbuf(f"tb{i}", [128, 16, 64], F32) for i in range(2)]
            for kk in range(7):
                m = 1 << kk
                ebr = emr.ap[:].unsqueeze(2).to_broadcast([128, 16, m])
                ebi = emi.ap[:].unsqueeze(2).to_broadcast([128, 16, m])
                op("dve", lambda e: e.tensor_tensor(out=tb[0][:, :, 0:m], in0=cosT[:, :, 0:m], in1=ebr, op=ALU.mult),
                   reads=[cosT, emr], writes=[tb[0]])
                op("dve", lambda e: e.tensor_tensor(out=tb[1][:, :, 0:m], in0=sinT[:, :, 0:m], in1=ebi, op=ALU.mult),
                   reads=[sinT, emi], writes=[tb[1]])
                op("dve", lambda e: e.tensor_tensor(out=cosT[:, :, m:2 * m], in0=tb[0][:, :, 0:m], in1=tb[1][:, :, 0:m],
                                                    op=ALU.subtract), reads=[tb[0], tb[1]], writes=[cosT])
                op("dve", lambda e: e.tensor_tensor(out=tb[0][:, :, 0:m], in0=cosT[:, :, 0:m], in1=ebi, op=ALU.mult),
                   reads=[cosT, emi], writes=[tb[0]])
                op("dve", lambda e: e.tensor_tensor(out=tb[1][:, :, 0:m], in0=sinT[:, :, 0:m], in1=ebr, op=ALU.mult),
                   reads=[sinT, emr], writes=[tb[1]])
                op("dve", lambda e: e.tensor_tensor(out=sinT[:, :, m:2 * m], in0=tb[0][:, :, 0:m], in1=tb[1][:, :, 0:m],
                                                    op=ALU.add), reads=[tb[0], tb[1]], writes=[sinT])
                tt(tmp[0], emr, emr, ALU.mult)
                tt(tmp[1], emi, emi, ALU.mult)
                tt(tmp[2], emr, emi, ALU.mult)
                tt(emr, tmp[0], tmp[1], ALU.subtract)
                ts(emi, tmp[2], 2.0, None, ALU.mult)

            hre_p, him_p = small("hre_p"), small("him_p")
            g0r, g0i = small("g0r"), small("g0i")
            glr, gli = small("glr"), small("gli")
            fw.barrier()
            pset.close()
            fw.stack = p2
            op("pool", lambda e: e.memset(hre_p[:], 0.0), writes=[hre_p])
            op("pool", lambda e: e.memset(him_p[:], 0.0), writes=[him_p])

            uTr = Ring([fw.sbuf(f"uT{i}", [128, 4, 128], BF16) for i in range(2)])
            wk = Ring([fw.sbuf(f"wk{i}", [128, 128], F32) for i in range(8)])
            gbr = Ring([fw.sbuf(f"gb{i}", [128, 128], F32) for i in range(4)])
            hbr = Ring([fw.sbuf(f"hb{i}", [128, 2, 128], BF16) for i in range(4)])

            def cmul(or_, oi_, ar, ai, br_, bi_):
                tt(tmp[0], ar, br_, ALU.mult)
                tt(tmp[1], ai, bi_, ALU.mult, eng="pool")
                tt(tmp[2], ar, bi_, ALU.mult)
                tt(tmp[3], ai, br_, ALU.mult, eng="pool")
                tt(or_, tmp[0], tmp[1], ALU.subtract)
                tt(oi_, tmp[2], tmp[3], ALU.add, eng="pool")

            class T16:
                def __init__(self, b):
                    self.b = b

            def s5_gen(bg, samp, out):
                own = samp is not None or bg >= NBP
                lc = 7 if samp is not None else 127
                uT = uTr.get()
                out["uT"] = uT
                dma(uT[:], (sc_u_s[samp] if samp is not None else sc_u[bg]).rearrange("p (a b) -> p a b", b=128), writes=[uT])
                cmul(g0r, g0i, c1, s1, hre_p, him_p)
                yield
                pend = []
                for st in range(16):
                    ct, q_ = st // 4, st % 4
                    bps = ring.get()
                    bv = bps.ap[:, 0:256].rearrange("p (a b) -> p a b", b=128)
                    for ri in range(2):
                        op("pe", lambda e: e.matmul(bv[:, ri, :], lhsT=BbT[:, ri, st, :], rhs=uT[:, ct, :],
                                                    start=True, stop=True), reads=[BbT, uT], writes=[bps])
                    t1, t2, t3, t4 = wk.get(), wk.get(), wk.get(), wk.get()
                    cs, sn = cosT.ap[:, st, :], sinT.ap[:, st, :]
                    op("dve", lambda e: e.tensor_tensor(out=t1[:], in0=bv[:, 0, :], in1=cs, op=ALU.mult),
                       reads=[bps, cosT], writes=[t1])
                    op("dve", lambda e: e.tensor_tensor(out=t2[:], in0=bv[:, 1, :], in1=sn, op=ALU.mult),
                       reads=[bps, sinT], writes=[t2])
                    op("dve", lambda e: e.tensor_tensor(out=t3[:], in0=bv[:, 1, :], in1=cs, op=ALU.mult),
                       reads=[bps, cosT], writes=[t3])
                    op("dve", lambda e: e.tensor_tensor(out=t4[:], in0=bv[:, 0, :], in1=sn, op=ALU.mult),
                       reads=[bps, sinT], writes=[t4])
                    op("pool", lambda e: e.tensor_tensor(out=t1[:], in0=t1[:], in1=t2[:], op=ALU.add),
                       reads=[t1, t2], writes=[t1])
                    op("pool", lambda e: e.tensor_tensor(out=t3[:], in0=t3[:], in1=t4[:], op=ALU.subtract),
                       reads=[t3, t4], writes=[t3])
                    gr, gi = gbr.get(), gbr.get()
                    rb = rho.ap[:, st:st + 1].to_broadcast([128, 128])
                    op("dve", lambda e: e.tensor_tensor_scan(out=gr[:], data0=rb, data1=t1[:], initial=g0r[:, st:st + 1],
                                                             op0=ALU.mult, op1=ALU.add), reads=[rho, t1, g0r], writes=[gr])
                    op("dve", lambda e: e.tensor_tensor_scan(out=gi[:], data0=rb, data1=t3[:], initial=g0i[:, st:st + 1],
                                                             op0=ALU.mult, op1=ALU.add), reads=[rho, t3, g0i], writes=[gi])
                    op("pool", lambda e: e.tensor_copy(out=glr[:, st:st + 1], in_=gr[:, lc:lc + 1]), reads=[gr], writes=[glr])
                    op("pool", lambda e: e.tensor_copy(out=gli[:, st:st + 1], in_=gi[:, lc:lc + 1]), reads=[gi], writes=[gli])
                    this_y = None
                    if own:
                        t5, t6, t7, t8 = wk.get(), wk.get(), wk.get(), wk.get()
                        hb = hbr.get()
                        op("dve", lambda e: e.tensor_tensor(out=t5[:], in0=gr[:], in1=cs, op=ALU.mult),
                           reads=[gr, cosT], writes=[t5])
                        op("pool", lambda e: e.tensor_tensor(out=t6[:], in0=gi[:], in1=sn, op=ALU.mult),
                           reads=[gi, sinT], writes=[t6])
                        op("dve", lambda e: e.tensor_tensor(out=t7[:], in0=gi[:], in1=cs, op=ALU.mult),
                           reads=[gi, cosT], writes=[t7])
                        op("pool", lambda e: e.tensor_tensor(out=t8[:], in0=gr[:], in1=sn, op=ALU.mult),
                           reads=[gr, sinT], writes=[t8])
                        op("dve", lambda e: e.tensor_tensor(out=hb[:, 0, :], in0=t5[:], in1=t6[:], op=ALU.subtract),
                           reads=[t5, t6], writes=[hb])
                        op("pool", lambda e: e.tensor_tensor(out=hb[:, 1, :], in0=t7[:], in1=t8[:], op=ALU.add),
                           reads=[t7, t8], writes=[hb])

                        def this_y(hb=hb, st=st, ct=ct, q_=q_):
                            yv_ = ypsb.ap[:].rearrange("p (a b) -> p a b", b=128)
                            op("pe", lambda e: e.matmul(yv_[:, ct, :], lhsT=CTre[:, st, :], rhs=hb[:, 0, :],
                                                        start=(q_ == 0), stop=False), reads=[CTre, hb], writes=[ypsb])
                            op("pe", lambda e: e.matmul(yv_[:, ct, :], lhsT=CTimn[:, st, :], rhs=hb[:, 1, :],
                                                        start=False, stop=(q_ == 3)), reads=[CTimn, hb], writes=[ypsb])
                    yield
                    pend.append(this_y)
                    if len(pend) > int(os.environ.get('YDEF', '2')):
                        y_ = pend.pop(0)
                        if y_ is not None:
                            y_()
                while pend:
                    y_ = pend.pop(0)
                    if y_ is not None:
                        yield
                        y_()
                c127 = tmp[4]
                s127 = tmp[5]
                op("dve", lambda e: e.tensor_copy(out=c127[:], in_=cosT[:, :, lc]), reads=[cosT], writes=[c127])
                op("dve", lambda e: e.tensor_copy(out=s127[:], in_=sinT[:, :, lc]), reads=[sinT], writes=[s127])
                cmul(hre_p, him_p, c127, s127, glr, gli)

            def s5_block(bg, samp=None):
                out = {}
                for _ in s5_gen(bg, samp, out):
                    pass
                return out["uT"]

            wkr = Ring([fw.sbuf(f"wkt{i}", [128, 5, 128], BF16) for i in range(2)])
            wvr = Ring([fw.sbuf(f"wvt{i}", [128, 5, 130], BF16) for i in range(2)])
            qTr = Ring([fw.sbuf(f"qT{i}", [128, 512], BF16) for i in range(2)])
            qzr = Ring([fw.sbuf(f"qz{i}", [128, 2, 512], BF16) for i in range(1)])
            gnr = Ring([fw.sbuf(f"gn{i}", [128, 24], F32) for i in range(2)])
            cbr = Ring([fw.sbuf(f"cb{i}", [128, 128], BF16) for i in range(10)])
            tkr = Ring([fw.sbuf(f"tk{i}", [128, 128], F32) for i in range(2)])
            Pr = Ring([fw.sbuf(f"P{i}", [128, 512], BF16) for i in range(2)])
            OTr = Ring([fw.sbuf(f"OT{i}", [128, 512], F32) for i in range(1)])
            oattn_r = Ring([fw.sbuf(f"oat{i}", [128, 512], F32) for i in range(2)])
            smr = Ring([fw.sbuf(f"sm{i}", [128, 8], F32) for i in range(12)])
            impr = Ring([fw.sbuf(f"imp{i}", [128, 128], F32) for i in range(2)])
            selmb_r = Ring([fw.sbuf(f"selmb{i}", [128, 128], BF16) for i in range(2)])
            selmT_r = Ring([fw.sbuf(f"selmT{i}", [128, 2, 128], BF16) for i in range(2)])
            big = Ring([fw.sbuf(f"big{i}", [128, 512], F32) for i in range(4)])
            bigb = Ring([fw.sbuf(f"bigb{i}", [128, 512], BF16) for i in range(3)])
            ysTr = Ring([fw.sbuf(f"ysT{i}", [128, 512], BF16) for i in range(2)])
            gmr2 = Ring([fw.sbuf(f"gm2_{i}", [128, 2048], BF16) for i in range(1)])
            sgm = fw.sbuf("sgm", [128, 2048], BF16)
            sgt = fw.sbuf("sgt", [128, 512], F32)
            mbt = fw.sbuf("mbt", [128, D], BF16)
            mT = fw.sbuf("mT", [128, 8, 128], BF16)
            xr2 = Ring([fw.sbuf(f"xr2_{i}", [128, D], F32) for i in range(1)])

            def sigmoid_from(out, src_ap, src_bufs, scale_in=1.0, big_recip=True):
                op("act", lambda e: e.activation(out=out[:], in_=src_ap, func=AF.Sigmoid, scale=scale_in),
                   reads=src_bufs, writes=[out])

            TICK = {"gens": [], "n": 0}

            def tick():
                gs = TICK["gens"]
                if not gs:
                    return
                TICK["n"] += 1
                g_ = gs[TICK["n"] % len(gs)]
                try:
                    next(g_)
                except StopIteration:
                    gs.remove(g_)

            def emit_scores(kT, kreads, biases, qz, h):
                S = ring.get()
                op("pe", lambda e: e.matmul(S[:, :], lhsT=kT, rhs=qz[:, h, :], start=True, stop=(len(biases) == 0)),
                   reads=kreads + [qz], writes=[S])
                for bi_, (l_, r_ap, rd) in enumerate(biases):
                    op("pe", lambda e: e.matmul(S.ap[:].rearrange("p (a b) -> p a b", b=128), lhsT=l_, rhs=r_ap,
                                                start=False, stop=(bi_ == len(biases) - 1)), reads=rd, writes=[S])
                return S

            def attn_pass(h, qz, units, acc, extras=()):
                n = len(units)
                LA = int(os.environ.get('LA', '1'))
                Sq = [emit_scores(units[j][0], units[j][1], units[j][2], qz, h) for j in range(min(LA, n))]
                for ui, (kT, kreads, biases, V, vreads) in enumerate(units):
                    S = Sq.pop(0)
                    if ui + LA < n:
                        u2 = units[ui + LA]
                        Sq.append(emit_scores(u2[0], u2[1], u2[2], qz, h))
                    P = Pr.get()
                    op("act", lambda e: e.activation(out=P[:], in_=S[:, :], func=AF.Exp), reads=[S], writes=[P])
                    op("pe", lambda e: e.matmul(acc[0:65, :], lhsT=V, rhs=P[:], start=(ui == 0), stop=(ui == n - 1)),
                       reads=vreads + [P], writes=[acc])
                    for (xacc, xfn, lo, hi, xrd) in extras:
                        if lo <= ui <= hi:
                            op("pe", lambda e: e.matmul(xacc[:, :], lhsT=xfn(ui), rhs=P[:], start=(ui == lo),
                                                        stop=(ui == hi)), reads=xrd + [P], writes=[xacc])
                    tick()

            def finish_pass(h, br, acc, gate, oattn, first, clampz=False):
                OT = OTr.get()
                op("act", lambda e: e.copy(out=OT[0:65, :], in_=acc[0:65, :]), reads=[acc], writes=[OT])
                tp = ring.get()
                tpv = tp.ap[:, 0:260].rearrange("p (g c) -> p g c", c=65)
                for g in range(4):
                    op("pe", lambda e: e.transpose(out=tpv[:, g, :], in_=OT[0:65, g * 128:(g + 1) * 128],
                                                   identity=identf[0:65, 0:65]), reads=[OT, identf], writes=[tp])
                rz = smr.get()
                if clampz:
                    op("dve", lambda e: e.tensor_scalar(out=rz[:, 0:4], in0=tpv[:, :, 64], scalar1=1e-30, scalar2=None,
                                                        op0=ALU.max), reads=[tp], writes=[rz])
                    op("dve", lambda e: e.reciprocal(out=rz[:, 0:4], in_=rz[:, 0:4]), reads=[rz], writes=[rz])
                else:
                    op("dve", lambda e: e.reciprocal(out=rz[:, 0:4], in_=tpv[:, :, 64]), reads=[tp], writes=[rz])
                w_ = smr.get()
                gv = gate.ap[:].rearrange("p (h g b) -> p h g b", h=2, g=4)[:, h, :, br]
                op("dve", lambda e: e.tensor_tensor(out=w_[:, 0:4], in0=rz[:, 0:4], in1=gv, op=ALU.mult),
                   reads=[rz, gate], writes=[w_])
                for g in range(4):
                    dst = oattn[:, (h * 4 + g) * 64:(h * 4 + g + 1) * 64]
                    if first:
                        op("dve", lambda e: e.tensor_scalar(out=dst, in0=tpv[:, g, 0:64], scalar1=w_[:, g:g + 1],
                                                            scalar2=None, op0=ALU.mult), reads=[tp, w_], writes=[oattn])
                    else:
                        op("dve", lambda e: e.scalar_tensor_tensor(out=dst, in0=tpv[:, g, 0:64], scalar=w_[:, g:g + 1],
                                                                   in1=dst, op0=ALU.mult, op1=ALU.add),
                           reads=[tp, w_, oattn], writes=[oattn])
                return rz

            def own_block(i, uT, samp=None):
                sm = samp is not None
                bg = NBP + i if not sm else None
                yv_ = ypsb.ap[:].rearrange("p (a b) -> p a b", b=128)
                yv = big.get()
                yv3 = yv.ap[:].rearrange("p (a b) -> p a b", b=128)
                for ct in range(4):
                    op("dve", lambda e: e.scalar_tensor_tensor(out=yv3[:, ct, :], in0=uT[:, ct, :], scalar=dsk[:, ct:ct + 1],
                                                               in1=yv_[:, ct, :], op0=ALU.mult, op1=ALU.add),
                       reads=[uT, dsk, ypsb], writes=[yv])
                t_a, t_b = big.get(), big.get()
                op("pool", lambda e: e.tensor_tensor(out=t_a[:], in0=yv[:], in1=yv[:], op=ALU.mult), reads=[yv], writes=[t_a])
                op("pool", lambda e: e.tensor_scalar(out=t_a[:], in0=t_a[:], scalar1=0.044715, scalar2=1.0, op0=ALU.mult,
                                                     op1=ALU.add), reads=[t_a], writes=[t_a])
                op("pool", lambda e: e.tensor_tensor(out=t_a[:], in0=t_a[:], in1=yv[:], op=ALU.mult), reads=[t_a, yv], writes=[t_a])
                sigmoid_from(t_b, t_a[:], [t_a], scale_in=2.0 * GC)
                yg = big.get()
                op("dve", lambda e: e.tensor_tensor(out=yg[:], in0=yv[:], in1=t_b[:], op=ALU.mult), reads=[yv, t_b], writes=[yg])
                ygb = bigb.get()
                op("act", lambda e: e.copy(out=ygb[:], in_=yg[:]), reads=[yg], writes=[ygb])
                def readout_B():
                    for _ in range(6):
                        yield
                    gl = ring.get()
                    glv = gl.ap[:].rearrange("p (a b) -> p a b", b=128)
                    ygb3 = ygb.ap[:].rearrange("p (a b) -> p a b", b=128)
                    for co in range(4):
                        for ci in range(4):
                            op("pe", lambda e: e.matmul(glv[:, co, :], lhsT=wglu[:, ci, co * 128:(co + 1) * 128], rhs=ygb3[:, ci, :],
                                                        start=(ci == 0), stop=(ci == 3)), reads=[wglu, ygb], writes=[gl])
                    sg = t_a
                    sg3 = sg.ap[:].rearrange("p (a b) -> p a b", b=128)
                    for co in range(4):
                        op("act", lambda e: e.activation(out=sg3[:, co, :], in_=glv[:, co, :], func=AF.Sigmoid, scale=1.0,
                                                         bias=nbgl[:, co:co + 1]), reads=[gl, nbgl], writes=[sg])
                    op("pool", lambda e: e.tensor_tensor(out=yg[:], in0=yg[:], in1=sg[:], op=ALU.mult), reads=[yg, sg], writes=[yg])
                    zs = bigb.get()
                    dma(zs[:], sc_zs_s[samp] if sm else sc_zs[i], writes=[zs])
                    sz = t_b
                    op("act", lambda e: e.activation(out=sz[:], in_=zs[:], func=AF.Silu), reads=[zs], writes=[sz])
                    op("dve", lambda e: e.tensor_tensor(out=ysT[:], in0=yg[:], in1=sz[:], op=ALU.mult), reads=[yg, sz], writes=[ysT])
                    yield

                ysT = ysTr.get()
                rB = readout_B()
                if sm:
                    for _ in rB:
                        pass
                else:
                    def _chain(a, b):
                        for _ in a:
                            yield
                        if b is not None:
                            for _ in b:
                                yield
                    TICK["gens"].insert(0, _chain(rB, TICK.get("post")))

                qT = qTr.get()
                dma(qT[:], sc_q_s[samp] if sm else sc_q[i], writes=[qT])
                qz = qzr.get()
                op("pool", lambda e: e.memset(qz[:], 0.0), writes=[qz])
                for h in range(2):
                    rows = slice(h * 64, (h + 1) * 64)
                    op("pool", lambda e: e.tensor_copy(out=qz[rows, h, :], in_=qT[rows, :]), reads=[qT], writes=[qz])
                gn = gnr.get()
                dma(gn[:], sc_gn_s[samp] if sm else sc_gn[i], writes=[gn])
                gate = gnr.get()
                sigmoid_from(gate, gn[:], [gn], big_recip=False)
                oattn = oattn_r.get()
                if sm:
                    sample_attention(samp, qz, gate, oattn)
                    for _ in out_chain(i, samp, oattn, ysT):
                        pass
                    return None
                prompt_attention(i, bg, qz, gate, oattn)
                for _ in rB:
                    pass
                return out_chain(i, samp, oattn, ysT)

            def prompt_attention(i, bg, qz, gate, oattn):
                nt_hi = (8 * bg + 7) // 128
                cbs = []
                for nt in range(nt_hi + 1):
                    cb = cbr.get()
                    dma(cb[:], cmpb_d[i, nt], writes=[cb])
                    cbs.append(cb)
                ta, tb_ = tkr.get(), tkr.get()
                dma(ta[:], tka_d[i], writes=[ta])
                dma(tb_[:], tkb_d[i], writes=[tb_])
                selmT = selmT_r.get()
                selmbs = []
                for h in range(2):
                    units = []
                    for nt in range(nt_hi + 1):
                        units.append((kcT[:, nt * 128:(nt + 1) * 128], [kcT],
                                      [(identb[:], cbs[nt].ap[:].unsqueeze(1).to_broadcast([128, 4, 128]), [identb, cbs[nt]])],
                                      vc[:, nt, h, :], [vc]))
                    acc = accO.get()
                    attn_pass(h, qz, units, acc, extras=[(accI, lambda ui: cmpsel[:, ui, :], 0, nt_hi, [cmpsel])])
                    rz = finish_pass(h, 0, acc, gate, oattn, first=True, clampz=True)
                    IT = OTr.get()
                    op("act", lambda e: e.copy(out=IT[:], in_=accI[:, :]), reads=[accI], writes=[IT])
                    tpi = ring.get()
                    tpiv = tpi.ap[:].rearrange("p (g c) -> p g c", c=128)
                    for g in range(4):
                        op("pe", lambda e: e.transpose(out=tpiv[:, g, :], in_=IT[:, g * 128:(g + 1) * 128], identity=identf[:]),
                           reads=[IT, identf], writes=[tpi])
                    imp = impr.get()
                    op("dve", lambda e: e.tensor_scalar(out=imp[:], in0=tpiv[:, 0, :], scalar1=rz[:, 0:1], scalar2=None,
                                                        op0=ALU.mult), reads=[tpi, rz], writes=[imp])
                    for g in range(1, 4):
                        op("dve", lambda e: e.scalar_tensor_tensor(out=imp[:], in0=tpiv[:, g, :], scalar=rz[:, g:g + 1],
                                                                   in1=imp[:], op0=ALU.mult, op1=ALU.add),
                           reads=[tpi, rz, imp], writes=[imp])
                    op("dve", lambda e: e.tensor_tensor(out=imp[:], in0=imp[:], in1=ta[:], op=ALU.mult), reads=[imp, ta], writes=[imp])
                    op("dve", lambda e: e.tensor_tensor(out=imp[:], in0=imp[:], in1=tb_[:], op=ALU.add), reads=[imp, tb_], writes=[imp])
                    m1, m2 = smr.get(), smr.get()
                    imp2 = impr.get()
                    op("dve", lambda e: e.max(out=m1[:], in_=imp[:]), reads=[imp], writes=[m1])
                    op("dve", lambda e: e.match_replace(out=imp2[:], in_to_replace=m1[:], in_values=imp[:], imm_value=-1e9),
                       reads=[m1, imp], writes=[imp2])
                    op("dve", lambda e: e.max(out=m2[:], in_=imp2[:]), reads=[imp2], writes=[m2])
                    op("dve", lambda e: e.scalar_tensor_tensor(out=imp2[:], in0=imp[:], scalar=m2[:, 7:8], in1=ta[:],
                                                               op0=ALU.is_ge, op1=ALU.mult), reads=[imp, m2, ta], writes=[imp2])
                    selmb = selmb_r.get()
                    op("dve", lambda e: e.tensor_scalar(out=selmb[:], in0=imp2[:], scalar1=-1.0, scalar2=None, op0=ALU.add),
                       reads=[imp2], writes=[selmb])
                    selmbs.append(selmb)
                wbs = {}
                for r in range(5):
                    kt = bg - 4 + r
                    if kt < 0:
                        continue
                    cb = cbr.get()
                    dma(cb[:], winb_d[i, r], writes=[cb])
                    wbs[kt] = cb
                kt0 = max(0, bg - 4)
                nwt = bg - kt0 + 1
                wkt, wvt = wkr.get(), wvr.get()
                dma(wkt[:, 0:nwt, :], sc_wk[kt0:bg + 1].rearrange("n p c -> p n c"), writes=[wkt])
                dma(wvt[:, 0:nwt, :], sc_wv[kt0:bg + 1].rearrange("n p c -> p n c"), writes=[wvt])
                for h in range(2):
                    units = []
                    for kt in sorted(wbs):
                        ws = kt - kt0
                        units.append((wkt[:, ws, :], [wkt],
                                      [(identb[:], wbs[kt].ap[:].unsqueeze(1).to_broadcast([128, 4, 128]), [identb, wbs[kt]])],
                                      wvt[:, ws, h * 65:(h + 1) * 65], [wvt]))
                    acc = accO.get()
                    attn_pass(h, qz, units, acc)
                    finish_pass(h, 2, acc, gate, oattn, first=False)
                for h in range(2):
                    pst = ring.get()
                    pstv = bfview(pst)[:, 0:128]
                    op("pe", lambda e: e.transpose(out=pstv, in_=selmbs[h][:], identity=identb[:]), reads=[selmbs[h], identb], writes=[pst])
                    op("act", lambda e: e.copy(out=selmT[:, h, :], in_=pstv), reads=[pst], writes=[selmT])
                for h in range(2):
                    units = []
                    for kt in range(bg + 1):
                        biases = [(Esb[:, kt * 128:(kt + 1) * 128], selmT.ap[:, h:h + 1, :].to_broadcast([128, 4, 128]),
                                   [Esb, selmT])]
                        if kt == bg:
                            biases.append((identb[:], caus.ap[:].unsqueeze(1).to_broadcast([128, 4, 128]), [identb, caus]))
                        units.append((selKT[:, kt * 128:(kt + 1) * 128], [selKT], biases, selV[:, kt, h, :], [selV]))
                    acc = accO.get()
                    attn_pass(h, qz, units, acc)
                    finish_pass(h, 1, acc, gate, oattn, first=False)

            def out_chain(i, samp, oattn, ysT):
                sm = samp is not None
                za = bigb.get()
                dma(za[:], sc_za_s[samp] if sm else sc_za[i], writes=[za])
                sza = big.get()
                op("act", lambda e: e.activation(out=sza[:], in_=za[:], func=AF.Silu), reads=[za], writes=[sza])
                ozb = bigb.get()
                op("dve", lambda e: e.tensor_tensor(out=ozb[:], in0=oattn[:], in1=sza[:], op=ALU.mult), reads=[oattn, sza], writes=[ozb])
                gmb = gmr2.get()
                dma(gmb[:], sc_gm_s[samp] if sm else sc_gm[i], writes=[gmb])
                op("act", lambda e: e.activation(out=sgm[:], in_=gmb[:], func=AF.Sigmoid), reads=[gmb], writes=[sgm])
                yield
                for _ in range(4):
                    yield
                pzt = ring.get()
                pztv = bfview(pzt)[:, 0:512].rearrange("p (a b) -> p a b", b=128)
                for k4 in range(4):
                    op("pe", lambda e: e.transpose(out=pztv[:, k4, :], in_=ozb[:, k4 * 128:(k4 + 1) * 128], identity=identb[:]),
                       reads=[ozb, identb], writes=[pzt])
                ozT = bigb.get()
                op("act", lambda e: e.copy(out=ozT[:].rearrange("p (a b) -> p a b", b=128), in_=pztv), reads=[pzt], writes=[ozT])
                ozT3 = ozT.ap[:].rearrange("p (a b) -> p a b", b=128)
                ysT3 = ysT.ap[:].rearrange("p (a b) -> p a b", b=128)
                yield
                for cg in range(2):
                    yield
                    pa = ring.get()
                    for k4 in range(4):
                        op("pe", lambda e: e.matmul(pa[:, :], lhsT=ozT3[:, k4, :], rhs=wla[:, k4, cg * 512:(cg + 1) * 512],
                                                    start=(k4 == 0), stop=(k4 == 3)), reads=[ozT, wla], writes=[pa])
                    pb_ = ring.get()
                    for k4 in range(4):
                        op("pe", lambda e: e.matmul(pb_[:, :], lhsT=ysT3[:, k4, :], rhs=wls[:, k4, cg * 512:(cg + 1) * 512],
                                                    start=(k4 == 0), stop=(k4 == 3)), reads=[ysT, wls], writes=[pb_])
                    m1_, m2_ = big.get(), big.get()
                    op("dve", lambda e: e.tensor_tensor(out=m1_[:], in0=pa[:, :], in1=sgm[:, cg * 512:(cg + 1) * 512], op=ALU.mult),
                       reads=[pa, sgm], writes=[m1_])
                    op("dve", lambda e: e.tensor_tensor(out=m2_[:], in0=pb_[:, :], in1=sgm[:, 1024 + cg * 512:1024 + (cg + 1) * 512],
                                                        op=ALU.mult), reads=[pb_, sgm], writes=[m2_])
                    op("pool", lambda e: e.tensor_tensor(out=mbt[:, cg * 512:(cg + 1) * 512], in0=m1_[:], in1=m2_[:], op=ALU.add),
                       reads=[m1_, m2_], writes=[mbt])
                for _ in range(3):
                    yield
                pmt = ring.get()
                pmtv = bfview(pmt).rearrange("p (a b) -> p a b", b=128)
                for k8 in range(8):
                    op("pe", lambda e: e.transpose(out=pmtv[:, k8, :], in_=mbt[:, k8 * 128:(k8 + 1) * 128], identity=identb[:]),
                       reads=[mbt, identb], writes=[pmt])
                op("act", lambda e: e.copy(out=mT[:], in_=pmtv), reads=[pmt], writes=[mT])
                xt = xr2.get()
                dma(xt[:], xs_d[samp] if sm else xo[i * 128:(i + 1) * 128, :], writes=[xt])
                res = xt
                yield
                for cg in range(2):
                    yield
                    py = ring.get()
                    for k8 in range(8):
                        op("pe", lambda e: e.matmul(py[:, :], lhsT=mT[:, k8, :], rhs=wo[:, k8, cg * 512:(cg + 1) * 512],
                                                    start=(k8 == 0), stop=(k8 == 7)), reads=[mT, wo], writes=[py])
                    op("dve", lambda e: e.tensor_tensor(out=res[:, cg * 512:(cg + 1) * 512], in0=py[:, :],
                                                        in1=xt[:, cg * 512:(cg + 1) * 512], op=ALU.add), reads=[py, xt], writes=[xt])
                yield
                ss = smr.get()
                op("act", lambda e: e.activation(out=mbt[:], in_=res[:], func=AF.Square, accum_out=ss[:, 0:1]),
                   reads=[res], writes=[mbt, ss])
                op("act", lambda e: e.activation(out=ss[:, 0:1], in_=ss[:, 0:1], func=AF.Ln, scale=1.0 / D, bias=1e-6),
                   reads=[ss], writes=[ss])
                op("act", lambda e: e.activation(out=ss[:, 0:1], in_=ss[:, 0:1], func=AF.Exp, scale=-0.5), reads=[ss], writes=[ss])
                op("dve", lambda e: e.scalar_tensor_tensor(out=res[:], in0=res[:], scalar=ss[:, 0:1], in1=fg[:],
                                                           op0=ALU.mult, op1=ALU.mult), reads=[res, ss, fg], writes=[res])
                if sm:
                    dma(ys_o[samp], res[0:8, :], reads=[res])
                else:
                    dma(y_o[i * 128:(i + 1) * 128, :], res[:], reads=[res])

            if NSB > 0:
                class _View:
                    def __init__(self, ap):
                        self.ap = ap

                    def __getitem__(self, idx):
                        return self.ap[idx]
                kcs_p = vcTs_p = sgm
                kcs = _View(sgm.ap[:, 1024:2048])
                vcTs = _View(sgm.ap[:, 0:1024])
                cmpsel_p = gmr2.bufs[0]
                cmpsel_s = _View(cmpsel_p.ap[:].rearrange("p (a b) -> p a b", b=256))
                vcs = fw.sbuf("vcs", [128, 8, 2, 65], BF16)
                cb0 = fw.sbuf("cb0_t", [128, 128], BF16)
                swb0 = fw.sbuf("swb0_t", [128, 128], BF16)
                ptb2 = fw.sbuf("ptb2", [128, 128], I32)
                ptf2 = fw.sbuf("ptf2", [128, 128], F32)
                io3 = fw.sbuf("io3", [128, 1], F32)
                idxs = fw.sbuf("idxs", [128, 128], I32)
                pgr2 = Ring([fw.sbuf(f"pgs{i}", [128, 256], F32) for i in range(2)])
                kbr = Ring([fw.sbuf(f"kbs{i}", [128, 128], BF16) for i in range(2)])
                KTr = Ring([fw.sbuf(f"KTs{i}", [128, 128], BF16) for i in range(3)])
                Vtr = Ring([fw.sbuf(f"Vts{i}", [128, 2, 65], BF16) for i in range(3)])
                selmT_s = fw.sbuf("selmT_s", [128, 4, 128], BF16)
                selmb_s = fw.sbuf("selmb_s", [128, 256], BF16)
                nkt = fw.sbuf("nkt", [128, 256], BF16)
                nvt = fw.sbuf("nvt", [128, 260], BF16)
                stt = fw.sbuf("stt", [128, 2, 16], F32)
                op("pool", lambda e: e.memset(vcs[:], 1.0), writes=[vcs])
                for vt_ in Vtr.bufs:
                    op("pool", lambda e: e.memset(vt_[:], 1.0), writes=[vt_])
                op("pool", lambda e: e.iota(io3[:], pattern=[[0, 1]], base=1, channel_multiplier=2,
                                            allow_small_or_imprecise_dtypes=True), writes=[io3])
                dma(cb0[:], cb0_d, writes=[cb0])
                dma(swb0[:], swb0_d, writes=[swb0])

            def stream_pass(ntiles, prep, biasfn, qz, gate, oattn, br):
                accs = [ps_bufs[0], ps_bufs[1]]

                def scores(kt):
                    KT_ap, kreads, Vfn = prep(kt)
                    return [emit_scores(KT_ap, kreads, biasfn(kt, h), qz, h) for h in range(2)], Vfn

                nxt = scores(0)
                for kt in range(ntiles):
                    Ss, Vfn = nxt
                    if kt + 1 < ntiles:
                        nxt = scores(kt + 1)
                    for h in range(2):
                        P = Pr.get()
                        op("act", lambda e: e.activation(out=P[:], in_=Ss[h][:, :], func=AF.Exp), reads=[Ss[h]], writes=[P])
                        V_ap, vreads = Vfn(h)
                        op("pe", lambda e: e.matmul(accs[h][0:65, :], lhsT=V_ap, rhs=P[:], start=(kt == 0), stop=(kt == ntiles - 1)),
                           reads=vreads + [P], writes=[accs[h]])
                for h in range(2):
                    finish_pass(h, br, accs[h], gate, oattn, first=False)

            def sample_attention(b, qz, gate, oattn):
                dma(kcs[:], sc_kc[b], writes=[sgm])
                dma(vcTs[:], sc_vcT[b], writes=[sgm])
                dma(cmpsel_s[:], cmpsel_s_d, writes=[cmpsel_p])
                dma(nkt[:], sc_nk[b], writes=[nkt])
                dma(nvt[:], sc_nv[b], writes=[nvt])
                for nt in range(8):
                    pv = ring.get()
                    pvv = bfview(pv)[:, 0:128]
                    op("pe", lambda e: e.transpose(out=pvv, in_=vcTs[:, nt * 128:(nt + 1) * 128], identity=identb[:]),
                       reads=[sgm, identb], writes=[pv])
                    op("dve", lambda e: e.tensor_copy(out=vcs[:, nt, :, 0:64], in_=pvv.rearrange("p (h d) -> p h d", d=64)),
                       reads=[pv], writes=[vcs])
                for h in range(2):
                    units = []
                    for nt in range(8):
                        bl = [(identb[:], cb0.ap[:].unsqueeze(1).to_broadcast([128, 4, 128]), [identb, cb0])] if nt == 0 else []
                        units.append((kcs[:, nt * 128:(nt + 1) * 128], [sgm], bl, vcs[:, nt, h, :], [vcs]))
                    acc = accO.get()
                    attn_pass(h, qz, units, acc, extras=[(accI, lambda ui: cmpsel_s[:, ui, 0:128], 0, 3, [cmpsel_p]),
                                                         (ypsb, lambda ui: cmpsel_s[:, ui, 128:256], 3, 7, [cmpsel_p])])
                    rz = finish_pass(h, 0, acc, gate, oattn, first=True, clampz=True)
                    imp_p, imp2_p = big.bufs[0], big.bufs[1]
                    imp, imp2 = _View(imp_p.ap[:, 0:256]), _View(imp2_p.ap[:, 0:256])
                    for t_, accb in enumerate((accI, ypsb)):
                        IT = OTr.get()
                        op("act", lambda e: e.copy(out=IT[:], in_=accb[:, :]), reads=[accb], writes=[IT])
                        tpi = ring.get()
                        tpiv = tpi.ap[:].rearrange("p (g c) -> p g c", c=128)
                        for g in range(4):
                            op("pe", lambda e: e.transpose(out=tpiv[:, g, :], in_=IT[:, g * 128:(g + 1) * 128], identity=identf[:]),
                               reads=[IT, identf], writes=[tpi])
                        dsti = imp[:, t_ * 128:(t_ + 1) * 128]
                        op("dve", lambda e: e.tensor_scalar(out=dsti, in0=tpiv[:, 0, :], scalar1=rz[:, 0:1], scalar2=None,
                                                            op0=ALU.mult), reads=[tpi, rz], writes=[imp_p])
                        for g in range(1, 4):
                            op("dve", lambda e: e.scalar_tensor_tensor(out=dsti, in0=tpiv[:, g, :], scalar=rz[:, g:g + 1],
                                                                       in1=dsti, op0=ALU.mult, op1=ALU.add),
                               reads=[tpi, rz, imp_p], writes=[imp_p])
                    op("dve", lambda e: e.memset(imp[:, 0:1], 100.0), writes=[imp_p])
                    m1, m2 = smr.get(), smr.get()
                    op("dve", lambda e: e.max(out=m1[:], in_=imp[:]), reads=[imp_p], writes=[m1])
                    op("dve", lambda e: e.match_replace(out=imp2[:], in_to_replace=m1[:], in_values=imp[:], imm_value=-1e9),
                       reads=[m1, imp_p], writes=[imp2_p])
                    op("dve", lambda e: e.max(out=m2[:], in_=imp2[:]), reads=[imp2_p], writes=[m2])
                    op("dve", lambda e: e.tensor_scalar(out=selmb_s[:], in0=imp[:], scalar1=m2[:, 6:7], scalar2=-1.0,
                                                        op0=ALU.is_ge, op1=ALU.add), reads=[imp_p, m2], writes=[selmb_s])
                    for t_ in range(2):
                        pst = ring.get()
                        pstv = bfview(pst)[:, 0:128]
                        op("pe", lambda e: e.transpose(out=pstv, in_=selmb_s[:, t_ * 128:(t_ + 1) * 128], identity=identb[:]),
                           reads=[selmb_s, identb], writes=[pst])
                        op("act", lambda e: e.copy(out=selmT_s[:, t_ * 2 + h, :], in_=pstv), reads=[pst], writes=[selmT_s])
                dma(ptb2[:], pt_d[b], writes=[ptb2])
                op("dve", lambda e: e.tensor_copy(out=ptf2[:], in_=ptb2[:]), reads=[ptb2], writes=[ptf2])
                op("dve", lambda e: e.tensor_scalar(out=ptf2[:], in0=ptf2[:], scalar1=256.0, scalar2=io3[:, 0:1],
                                                    op0=ALU.mult, op1=ALU.add), reads=[ptf2, io3], writes=[ptf2])
                op("dve", lambda e: e.tensor_copy(out=idxs[:], in_=ptf2[:]), reads=[ptf2], writes=[idxs])

                ringT = Ring([accI, ypsb])

                def tile_from_rows(pg, k):
                    kb = kbr.get()
                    Vt = Vtr.get()
                    if k % 2 == 0:
                        op("dve", lambda e: e.tensor_copy(out=kb[:], in_=pg[:, 0:128]), reads=[pg], writes=[kb])
                        op("act", lambda e: e.copy(out=Vt[:, :, 0:64], in_=pg[:, 128:256].rearrange("p (h d) -> p h d", d=64)),
                           reads=[pg], writes=[Vt])
                    else:
                        op("act", lambda e: e.copy(out=kb[:], in_=pg[:, 0:128]), reads=[pg], writes=[kb])
                        op("dve", lambda e: e.tensor_copy(out=Vt[:, :, 0:64], in_=pg[:, 128:256].rearrange("p (h d) -> p h d", d=64)),
                           reads=[pg], writes=[Vt])
                    pt2 = ringT.get()
                    p2v = bfview(pt2)[:, 0:128]
                    op("pe", lambda e: e.transpose(out=p2v, in_=kb[:], identity=identb[:]), reads=[kb, identb], writes=[pt2])
                    KT = KTr.get()
                    if k % 2 == 0:
                        op("act", lambda e: e.copy(out=KT[:], in_=p2v), reads=[pt2], writes=[KT])
                    else:
                        op("dve", lambda e: e.tensor_copy(out=KT[:], in_=p2v), reads=[pt2], writes=[KT])
                    return KT, Vt

                def prep_sel(kt):
                    if kt < 128:
                        pg = pgr2.get()
                        fw.idma(pg[:], cache_d, idxs[:, kt:kt + 1], reads=[idxs], writes=[pg])
                        KT, Vt = tile_from_rows(pg, kt)
                        return KT[:], [KT], (lambda h: (Vt[:, h, :], [Vt]))
                    return nkt[:, 0:128], [nkt], (lambda h: (nvt[:, h * 65:(h + 1) * 65], [nvt]))

                def bias_sel(kt, h):
                    if kt < 128:
                        t_ = kt // 64
                        return [(Esb[:, (kt % 64) * 128:(kt % 64 + 1) * 128],
                                 selmT_s.ap[:, t_ * 2 + h:t_ * 2 + h + 1, :].to_broadcast([128, 4, 128]), [Esb, selmT_s])]
                    return [(identb[:], caus.ap[:].unsqueeze(1).to_broadcast([128, 4, 128]), [identb, caus])]

                stream_pass(129, prep_sel, bias_sel, qz, gate, oattn, 1)

                def prep_win(r):
                    if r < 4:
                        pg = pgr2.get()
                        dma(pg[:], cwin_d[b, r * 128:(r + 1) * 128, :], writes=[pg])
                        KT, Vt = tile_from_rows(pg, r)
                        return KT[:], [KT], (lambda h: (Vt[:, h, :], [Vt]))
                    return nkt[:, 128:256], [nkt], (lambda h: (nvt[:, 130 + h * 65:130 + (h + 1) * 65], [nvt]))

                def bias_win(r, h):
                    if r == 0:
                        return [(identb[:], swb0.ap[:].unsqueeze(1).to_broadcast([128, 4, 128]), [identb, swb0])]
                    if r == 4:
                        return [(identb[:], caus.ap[:].unsqueeze(1).to_broadcast([128, 4, 128]), [identb, caus])]
                    return []

                stream_pass(5, prep_win, bias_win, qz, gate, oattn, 2)

            uT_cur = s5_block(0)
            post = None
            for bg in range(NBT):
                nxt_out = {}
                gen_ = s5_gen(bg + 1, None, nxt_out) if bg + 1 < NBT else None
                if bg >= NBP:
                    if gen_ is not None:
                        next(gen_)
                    TICK["gens"] = [g for g in (gen_,) if g is not None]
                    TICK["post"] = post
                    TICK["n"] = 0
                    newpost = own_block(bg - NBP, uT_cur)
                    if os.environ.get('DEFER_OUT', '1') == '0' and newpost is not None:
                        for _ in newpost:
                            pass
                        newpost = None
                    TICK["gens"] = []
                    if post is not None:
                        for _ in post:
                            pass
                    post = newpost
                if gen_ is not None:
                    for _ in gen_:
                        pass
                    uT_cur = nxt_out["uT"]
            if post is not None:
                for _ in post:
                    pass
            sso = fw.sbuf("sso", [128, 2, 16], F32)
            op("dve", lambda e: e.tensor_copy(out=sso[:, 0, :], in_=hre_p[:]), reads=[hre_p], writes=[sso])
            op("dve", lambda e: e.tensor_copy(out=sso[:, 1, :], in_=him_p[:]), reads=[him_p], writes=[sso])
            dma(ssm_o, sso[:], reads=[sso])
            for b in range(NSB):
                dma(stt[:], sst_d[b], writes=[stt])
                op("dve", lambda e: e.tensor_copy(out=hre_p[:], in_=stt[:, 0, :]), reads=[stt], writes=[hre_p])
                op("dve", lambda e: e.tensor_copy(out=him_p[:], in_=stt[:, 1, :]), reads=[stt], writes=[him_p])
                uT = s5_block(None, samp=b)
                op("dve", lambda e: e.tensor_copy(out=sso[:, 0, :], in_=hre_p[:]), reads=[hre_p], writes=[sso])
                op("dve", lambda e: e.tensor_copy(out=sso[:, 1, :], in_=him_p[:]), reads=[him_p], writes=[sso])
                dma(ssms_o[b], sso[:], reads=[sso])
                own_block(b, uT, samp=b)
            fw.barrier()
        fw.stack = top
        pstore.close()
        fw.finish()
    return nc, fw


def _bf(a):
    return np.ascontiguousarray(a).astype(ml_dtypes.bfloat16)


def host_consts(NBP, NBO, half, NSB=4):
    NBT = NBP + NBO
    c = {}
    n = np.arange(512) - 1
    blk = np.arange(128)
    c0 = n[:, None] * 16
    s0 = blk[None, :] * 64
    shared = np.minimum(c0 + 32, s0 + 64) - np.maximum(c0, s0)
    cs = np.clip(shared, 0, None).astype(np.float32) / 32.0
    cs[0, :] = 0.0
    c["cmpsel"] = _bf(cs.reshape(4, 128, 128).transpose(1, 0, 2))
    first_real_tok = 0 if half == 1 else NBP * 128
    q = np.arange(128)
    cmpb = np.zeros((NBO, 4, 128, 128), np.float32)
    for i in range(NBO):
        qpos = (NBP + i) * 128 + q
        for nt in range(4):
            nn = nt * 128 + np.arange(128) - 1
            vis = (nn[:, None] * 16 + 31 <= qpos[None, :]) & (nn[:, None] >= 0) & (nn[:, None] * 16 >= first_real_tok)
            cmpb[i, nt] = np.where(vis, 0.0, NEG)
    c["cmpb"] = _bf(cmpb)
    winb = np.zeros((NBO, 5, 128, 128), np.float32)
    for i in range(NBO):
        bg = NBP + i
        qpos = bg * 128 + q
        for r in range(5):
            kpos = (bg - 4 + r) * 128 + np.arange(128)
            dist = qpos[None, :] - kpos[:, None]
            vis = (kpos[:, None] >= first_real_tok) & (dist >= 0) & (dist < 512)
            winb[i, r] = np.where(vis, 0.0, NEG)
    c["winb"] = _bf(winb)
    tka = np.zeros((NBO, 128, 128), np.float32)
    tkb = np.zeros((NBO, 128, 128), np.float32)
    fb = first_real_tok // 64
    for i in range(NBO):
        qpos = (NBP + i) * 128 + q
        valid = (blk[None, :] * 64 <= qpos[:, None]) & (blk[None, :] >= fb)
        forced = (blk[None, :] == fb) | (blk[None, :] == (qpos // 64)[:, None])
        tka[i] = valid.astype(np.float32)
        tkb[i] = np.where(valid, np.where(forced, 100.0, 0.0), -100.0)
    c["tka"], c["tkb"] = tka, tkb
    key = np.arange(128)
    c["caus"] = _bf(np.where(key[:, None] <= q[None, :], 0.0, NEG))
    EW = max(NBT, 64 if NSB else 0) * 128
    E = np.zeros((128, EW), np.float32)
    kk = np.arange(EW)
    E[kk // 64, kk] = -NEG
    c["E"] = _bf(E)
    mgh = np.zeros((128, 2), np.float32)
    mgh[:64, 0] = 1.0
    mgh[64:, 1] = 1.0
    c["mgh"] = mgh
    n = np.arange(1024) - 1
    blk = np.arange(256)
    c0 = n[:, None] * 16
    s0 = blk[None, :] * 64
    shared = np.minimum(c0 + 32, s0 + 64) - np.maximum(c0, s0)
    cs = np.clip(shared, 0, None).astype(np.float32) / 32.0
    cs[0, :] = 0.0
    c["cmpsel_s"] = _bf(cs.reshape(8, 128, 256).transpose(1, 0, 2))
    cb0 = np.zeros((128, 128), np.float32)
    cb0[0, :] = NEG
    c["cb0"] = _bf(cb0)
    c["swb0"] = _bf(np.where(key[:, None] > q[None, :], 0.0, NEG))
    return c


def host_weights(inp):
    w = {}
    w_in = inp["w_in"][0]
    wf = w_in[:, wf_cols()]
    wt = w_in[:, wt_cols()]
    w["wf"] = np.ascontiguousarray(wf.reshape(8, 128, NWF).transpose(1, 0, 2))
    w["wt"] = np.ascontiguousarray(wt.reshape(8, 128, NWT).transpose(1, 0, 2))
    w["ng"] = np.ascontiguousarray(inp["norm_g"][0].reshape(8, 128).T)
    w["fg"] = np.ascontiguousarray(np.broadcast_to(inp["final_g"][None, :], (128, D)))
    w1 = inp["cmp_w1"][0]
    w1l = w1.transpose(2, 0, 1, 3)
    w["w1"] = np.ascontiguousarray(np.concatenate([w1l, w1l], 0))
    w2 = inp["cmp_w2"][0].transpose(1, 0, 2)
    w["w2"] = np.ascontiguousarray(np.concatenate([w2, w2], 0))
    pe = inp["cmp_pe"][0].transpose(2, 0, 1)
    w["pe"] = np.ascontiguousarray(np.concatenate([pe, pe], 0))
    b1 = inp["cmp_b1"][0].T
    w["b1"] = np.ascontiguousarray(np.concatenate([b1, b1], 0))

    def st_layout(a):
        a = a.reshape((16, 2, 64) + a.shape[2:])
        a = np.moveaxis(a, 0, 2)
        return np.ascontiguousarray(a.reshape((128, 16) + a.shape[3:]))
    w["lamre"] = st_layout(inp["ssm_lam_re"][0])
    w["lamim"] = st_layout(inp["ssm_lam_im"][0])
    w["logdt"] = st_layout(np.broadcast_to(inp["ssm_log_dt"][0][:, None], (32, 64)))
    w["bre"] = st_layout(inp["ssm_b_re"][0])
    w["bim"] = st_layout(inp["ssm_b_im"][0])
    for nm, key in (("cre", "ssm_c_re"), ("cim", "ssm_c_im")):
        cc = inp[key][0].transpose(0, 2, 1)
        cl = st_layout(cc)
        bd = np.zeros((128, 16, 4, 2, 16), np.float32)
        for st in range(16):
            bd[:64, st, st % 4, 0, :] = cl[:64, st]
            bd[64:, st, st % 4, 1, :] = cl[64:, st]
        w[nm] = bd.reshape(128, 16, 128)
    w["dsk"] = np.ascontiguousarray(inp["ssm_d"][0].reshape(4, 128).T)
    w["bgl"] = np.ascontiguousarray(inp["b_glu"][0].reshape(4, 128).T)
    w["wglu"] = np.ascontiguousarray(inp["w_glu"][0].reshape(4, 128, 512).transpose(1, 0, 2))
    w["wla"] = np.ascontiguousarray(inp["w_lift_attn"][0].reshape(4, 128, D).transpose(1, 0, 2))
    w["wls"] = np.ascontiguousarray(inp["w_lift_ssm"][0].reshape(4, 128, D).transpose(1, 0, 2))
    w["wo"] = np.ascontiguousarray(inp["w_out"][0].reshape(8, 128, D).transpose(1, 0, 2))
    return w


def make_in_maps(inp, NBP, NBO, NSB=4):
    w = host_weights(inp)
    consts = [host_consts(NBP, NBO, 0, NSB), host_consts(NBP, NBO, 1, NSB)]
    NS1 = max(NSB, 1)
    cache = (np.ascontiguousarray(inp["cache_kv"][0]).reshape(-1, 256) if NSB > 0
             else np.zeros((256, 256), np.float32))

    def st_layout(a):
        a = a.reshape((16, 2, 64))
        a = np.moveaxis(a, 0, 2)
        return a.reshape(128, 16)
    maps = []
    TP, TO = NBP * 128, NBO * 128
    for c in range(8):
        s, half = c // 2, c % 2
        m = dict(w)
        m.update(consts[half])
        xs = inp["x_prompt"][s]
        m["xo"] = np.ascontiguousarray(xs[half * TP:half * TP + TO])
        m["xp"] = np.ascontiguousarray(xs[0:TP]) if half == 1 else np.zeros((TP, D), np.float32)
        bs = [min(NS1 * c + j, 31) for j in range(NS1)]
        xsp = np.zeros((NS1, 128, D), np.float32)
        xsp[:, 0:8, :] = inp["x_sample"][bs]
        m["xs"] = xsp
        m["cache"] = cache
        m["ptab"] = np.ascontiguousarray(np.broadcast_to(inp["page_table"][bs][:, None, :], (NS1, 128, 128))).astype(np.int32)
        m["cwin"] = np.ascontiguousarray(inp["cache_win_kv"][0][bs]).reshape(NS1, 512, 256)
        sst = np.zeros((NS1, 128, 2, 16), np.float32)
        for j, b in enumerate(bs):
            sst[j, :, 0, :] = st_layout(inp["state_ssm_re"][0, b])
            sst[j, :, 1, :] = st_layout(inp["state_ssm_im"][0, b])
        m["sst"] = sst
        maps.append(m)
    return maps


def _unstate(a):
    return a.reshape(2, 64, 16).transpose(2, 0, 1).reshape(32, 64)


_CACHE = {}


def kernel(**inp):
    inp = {k: np.asarray(v) for k, v in inp.items()}
    NB, NSB = 32, 4
    if "nc" not in _CACHE:
        _CACHE["nc"] = build(NB, NB, NSB, inp["cache_kv"].shape[1])[0]
    nc = _CACHE["nc"]
    maps = make_in_maps(inp, NB, NB, NSB)
    res = run_bass_kernel_spmd(nc, maps, core_ids=list(range(8))).results
    T = NB * 128
    y_p = np.zeros((4, 8192, D), np.float32)
    kv_p = np.zeros((1, 4, 8192, 4, 2, 64), np.float32)
    win_p = np.zeros((1, 4, 512, 2, 2, 64), np.float32)
    sre_p = np.zeros((1, 4, 32, 64), np.float32)
    sim_p = np.zeros((1, 4, 32, 64), np.float32)
    y_s = np.zeros((32, 8, D), np.float32)
    kv_s = np.zeros((1, 32, 8, 4, 2, 64), np.float32)
    win_s = np.zeros((1, 32, 512, 2, 2, 64), np.float32)
    sre_s = np.zeros((1, 32, 32, 64), np.float32)
    sim_s = np.zeros((1, 32, 32, 64), np.float32)
    for c in range(8):
        s_, half = c // 2, c % 2
        r = res[c]
        y_p[s_, half * T:(half + 1) * T] = r["y"]
        kv_p[0, s_, half * T:(half + 1) * T] = r["kvp"].reshape(T, 4, 2, 64)
        if half == 1:
            win_p[0, s_] = r["winp"].reshape(512, 2, 2, 64)
            sre_p[0, s_] = _unstate(r["ssmp"][:, 0, :])
            sim_p[0, s_] = _unstate(r["ssmp"][:, 1, :])
        for j in range(NSB):
            b = NSB * c + j
            y_s[b] = r["ys"][j]
            kv_s[0, b] = r["kvs"][j].reshape(8, 4, 2, 64)
            win_s[0, b] = r["wins"][j].reshape(512, 2, 2, 64)
            sre_s[0, b] = _unstate(r["ssms"][j, :, 0, :])
            sim_s[0, b] = _unstate(r["ssms"][j, :, 1, :])
    return (y_p, y_s, kv_p, win_p, sre_p, sim_p, kv_s, win_s, sre_s, sim_s)
```

```python
import contextlib
import os
import math
import numpy as np
import ml_dtypes
import concourse.bass as bass
import concourse.mybir as mybir
from concourse.bass_utils import run_bass_kernel_spmd

F32 = mybir.dt.float32
BF16 = mybir.dt.bfloat16
I32 = mybir.dt.int32
AF = mybir.ActivationFunctionType
ALU = mybir.AluOpType
AX = mybir.AxisListType

EPOCH = int(os.environ.get("EPOCH", "3000"))
NEG = -30000.0
GC = math.sqrt(2.0 / math.pi)


class Eng:
    def __init__(self, fw, name, raw):
        self.fw, self.name, self.raw = fw, name, raw
        self.sems = []
        self.count = 0
        self.known = {}

    def sem_for(self, epoch):
        while len(self.sems) <= epoch:
            self.sems.append(self.fw.new_sem(f"s_{self.name}_{len(self.sems)}"))
        return self.sems[epoch]


class Buf:
    def __init__(self, ap, name="", excl=False):
        self.ap = ap
        self.name = name
        self.writer = None
        self.readers = {}
        self.excl = excl

    def __getitem__(self, idx):
        return self.ap[idx]


class FW:
    def __init__(self, nc, stack):
        self.nc = nc
        self.stack = stack
        self.semstack = stack
        self.engs = {}
        for n, raw in (("pe", nc.tensor), ("act", nc.scalar), ("dve", nc.vector),
                       ("pool", nc.gpsimd), ("sp", nc.sync)):
            self.engs[n] = Eng(self, n, raw)
        self.dma_slots = []
        self.dma_next = 0
        self.n_dma_slots = 32
        self.ninst = 0

    def new_sem(self, name):
        return self.semstack.enter_context(self.nc.semaphore(name))

    def barrier(self):
        for e in self.engs.values():
            for e2 in self.engs.values():
                if e2 is e or e2.count == 0:
                    continue
                self._need(e, (e2, e2.count))
            for si, slot in enumerate(self.dma_slots):
                if slot[1] > 0:
                    self._need(e, ('dma', slot[0], slot[1], f"dma{si}"))

    def sbuf(self, name, shape, dtype):
        t = self.stack.enter_context(self.nc.sbuf_tensor("sb_" + name, list(shape), dtype))
        return Buf(t, name)

    def psum(self, name, shape, dtype):
        t = self.stack.enter_context(self.nc.psum_tensor("pp_" + name, list(shape), dtype))
        return Buf(t, name, excl=True)

    def _need(self, eng, dep):
        if dep is None:
            return
        if dep[0] == 'dma':
            _, sem, val, key = dep
            if eng.known.get(key, 0) >= val:
                return
            eng.raw.wait_ge(sem, val)
            eng.known[key] = val
            return
        e2, cnt = dep
        if e2 is eng and eng.name == "pe":
            return
        if eng.known.get(e2.name, 0) >= cnt:
            return
        ep = (cnt - 1) // EPOCH
        eng.raw.wait_ge(e2.sem_for(ep), cnt - ep * EPOCH)
        eng.known[e2.name] = cnt

    def _deps(self, eng, reads, writes):
        for b in reads:
            self._need(eng, b.writer)
            if b.excl:
                for r in list(b.readers.values()):
                    self._need(eng, r)
        for b in writes:
            self._need(eng, b.writer)
            for r in list(b.readers.values()):
                self._need(eng, r)

    def op(self, engname, fn, reads=(), writes=()):
        eng = self.engs[engname]
        self._deps(eng, reads, writes)
        inst = fn(eng.raw)
        eng.count += 1
        ep = (eng.count - 1) // EPOCH
        inst.then_inc(eng.sem_for(ep), 1)
        tag = (eng, eng.count)
        for b in reads:
            b.readers[eng.name] = tag
        for b in writes:
            b.writer = tag
            b.readers = {}
        self.ninst += 1
        return inst

    def dma(self, out_ap, in_ap, reads=(), writes=(), q="sp", **kw):
        eng = self.engs[q]
        self._deps(eng, reads, writes)
        if len(self.dma_slots) < self.n_dma_slots:
            self.dma_slots.append([self.new_sem(f"d{len(self.dma_slots)}"), 0])
        si = self.dma_next % self.n_dma_slots
        self.dma_next += 1
        slot = self.dma_slots[si]
        key = f"dma{si}"
        if slot[1] > 0 and eng.known.get(key, 0) < slot[1]:
            eng.raw.wait_ge(slot[0], slot[1])
            eng.known[key] = slot[1]
        slot[1] += 16
        eng.raw.dma_start(out=out_ap, in_=in_ap, **kw).then_inc(slot[0], 16)
        tag = ('dma', slot[0], slot[1], key)
        for b in reads:
            b.readers[key] = tag
        for b in writes:
            b.writer = tag
            b.readers = {}
        self.ninst += 1

    def idma(self, out_ap, rows_ap, idx_ap, reads=(), writes=()):
        eng = self.engs["pool"]
        self._deps(eng, reads, writes)
        if len(self.dma_slots) < self.n_dma_slots:
            self.dma_slots.append([self.new_sem(f"d{len(self.dma_slots)}"), 0])
        si = self.dma_next % self.n_dma_slots
        self.dma_next += 1
        slot = self.dma_slots[si]
        key = f"dma{si}"
        if slot[1] > 0 and eng.known.get(key, 0) < slot[1]:
            eng.raw.wait_ge(slot[0], slot[1])
            eng.known[key] = slot[1]
        slot[1] += 16
        eng.raw.indirect_dma_start(out=out_ap, out_offset=None, in_=rows_ap,
                                   in_offset=bass.IndirectOffsetOnAxis(ap=idx_ap, axis=0)).then_inc(slot[0], 16)
        tag = ('dma', slot[0], slot[1], key)
        for b in reads:
            b.readers[key] = tag
        for b in writes:
            b.writer = tag
            b.readers = {}
        self.ninst += 1

    def finish(self):
        eng = self.engs["sp"]
        for slot in self.dma_slots:
            if slot[1] > 0:
                eng.raw.wait_ge(slot[0], slot[1])


class Ring:
    def __init__(self, bufs):
        self.bufs = bufs
        self.i = 0

    def get(self):
        b = self.bufs[self.i % len(self.bufs)]
        self.i += 1
        return b


D = 1024
NH, NKV, G, DH = 8, 2, 4, 64
D_ATTN = 512
D_SSM = 512
NST = 16
O_Q, O_KV, O_GN, O_ZA, O_U, O_ZS, O_GM = 0, 512, 1280, 1304, 1816, 2328, 2840
NWF = 1536
NWT = 3352
T_GROUPS = [(0, 512), (512, 792), (792, 1304), (1304, 1816), (1816, 2328), (2328, 2840), (2840, 3352)]


def wf_cols():
    q = [h * 256 + g * 64 + d for g in range(4) for h in range(2) for d in range(64)]
    return np.array(q + list(range(O_U, O_U + 512)) + list(range(O_ZS, O_ZS + 512)))


def wt_cols():
    return np.array(list(range(O_KV, O_KV + 768)) + list(range(O_GN, O_GN + 24)) +
                    list(range(O_ZA, O_ZA + 512)) + list(range(O_GM, O_GM + 2048)))


import os
DBG = int(os.environ.get("DBG_STAGE", "99"))
SKIP = os.environ.get("DBG_SKIP", "")


def build(NBP, NBO, NSB=4, NPHYS=5120):
    NBT = NBP + NBO
    TP, TO = NBP * 128, NBO * 128
    NWIN = min(4, NBO)
    nc = bass.Bass("TRN2", target_bir_lowering=False)

    def din(name, shape, dt=F32):
        return nc.dram_tensor(name, list(shape), dt, kind="ExternalInput").ap()

    def dout(name, shape, dt=F32):
        return nc.dram_tensor(name, list(shape), dt, kind="ExternalOutput").ap()

    def dscr(name, shape, dt):
        return nc.dram_tensor(name, list(shape), dt, kind="Internal").ap()

    xo = din("xo", [TO, D])
    xp = din("xp", [TP, D])
    wf_d = din("wf", [128, 8, NWF])
    wt_d = din("wt", [128, 8, NWT])
    ng_d = din("ng", [128, 8])
    fg_d = din("fg", [128, D])
    w1_d = din("w1", [128, 2, 32, 64])
    w2_d = din("w2", [128, 2, 64])
    pe_d = din("pe", [128, 2, 32])
    b1_d = din("b1", [128, 2])
    cmpsel_d = din("cmpsel", [128, 4, 128], BF16)
    cmpb_d = din("cmpb", [NBO, 4, 128, 128], BF16)
    winb_d = din("winb", [NBO, 5, 128, 128], BF16)
    tka_d = din("tka", [NBO, 128, 128])
    tkb_d = din("tkb", [NBO, 128, 128])
    caus_d = din("caus", [128, 128], BF16)
    EW = max(NBT, 64 if NSB else 0) * 128
    E_d = din("E", [128, EW], BF16)
    lamre_d = din("lamre", [128, 16])
    lamim_d = din("lamim", [128, 16])
    logdt_d = din("logdt", [128, 16])
    bre_d = din("bre", [128, 16, 16])
    bim_d = din("bim", [128, 16, 16])
    cre_d = din("cre", [128, 16, 128])
    cim_d = din("cim", [128, 16, 128])
    mgh_d = din("mgh", [128, 2])
    dsk_d = din("dsk", [128, 4])
    bgl_d = din("bgl", [128, 4])
    wglu_d = din("wglu", [128, 4, 512])
    wla_d = din("wla", [128, 4, D])
    wls_d = din("wls", [128, 4, D])
    wo_d = din("wo", [128, 8, D])

    y_o = dout("y", [TO, D])
    kv_o = dout("kvp", [TO, 512])
    win_o = dout("winp", [NWIN * 128, 256])
    ssm_o = dout("ssmp", [128, 2, 16])

    sc_q = dscr("sc_q", [NBO, 128, 512], BF16)
    sc_u = dscr("sc_u", [NBT, 128, 512], BF16)
    sc_zs = dscr("sc_zs", [NBO, 128, 512], BF16)
    sc_za = dscr("sc_za", [NBO, 128, 512], BF16)
    sc_gm = dscr("sc_gm", [NBO, 128, 2048], BF16)
    sc_gn = dscr("sc_gn", [NBO, 128, 24], F32)
    NS1 = max(NSB, 1)
    xs_d = din("xs", [NS1, 128, D])
    cache_d = din("cache", [NPHYS * 256, 256])
    pt_d = din("ptab", [NS1, 128, 128], I32)
    cwin_d = din("cwin", [NS1, 512, 256])
    sst_d = din("sst", [NS1, 128, 2, 16])
    cmpsel_s_d = din("cmpsel_s", [128, 8, 256], BF16)
    cb0_d = din("cb0", [128, 128], BF16)
    swb0_d = din("swb0", [128, 128], BF16)
    ys_o = dout("ys", [NS1, 8, D])
    kvs_o = dout("kvs", [NS1, 8, 512])
    wins_o = dout("wins", [NS1, 512, 256])
    ssms_o = dout("ssms", [NS1, 128, 2, 16])
    sc_q_s = dscr("sc_q_s", [NS1, 128, 512], BF16)
    sc_u_s = dscr("sc_u_s", [NS1, 128, 512], BF16)
    sc_zs_s = dscr("sc_zs_s", [NS1, 128, 512], BF16)
    sc_za_s = dscr("sc_za_s", [NS1, 128, 512], BF16)
    sc_gm_s = dscr("sc_gm_s", [NS1, 128, 2048], BF16)
    sc_gn_s = dscr("sc_gn_s", [NS1, 128, 24], F32)
    sc_nk = dscr("sc_nk", [NS1, 128, 256], BF16)
    sc_nv = dscr("sc_nv", [NS1, 128, 260], BF16)
    sc_kc = dscr("sc_kc", [NS1, 128, 1024], BF16)
    sc_vcT = dscr("sc_vcT", [NS1, 128, 1024], BF16)
    sc_ys_s = dscr("sc_ys_s", [NS1, 128, 512], BF16)
    sc_wk = dscr("sc_wk", [NBT, 128, 128], BF16)
    sc_wv = dscr("sc_wv", [NBT, 128, 130], BF16)

    with contextlib.ExitStack() as top:
        fw = FW(nc, top)
        op, dma = fw.op, fw.dma

        identb = fw.sbuf("identb", [128, 128], BF16)
        identf = fw.sbuf("identf", [128, 128], F32)
        ps_bufs = [fw.psum(f"ps{i}", [128, 512], F32) for i in range(8)]
        pstore = contextlib.ExitStack()
        pstore.__enter__()
        fw.stack = pstore
        selKT = fw.sbuf("selKT", [128, NBT * 128], BF16)
        selV = fw.sbuf("selV", [128, NBT, 2, 65], BF16)
        winKT = fw.sbuf("winKT", [128, 8 * 128], BF16)
        winV = fw.sbuf("winV", [128, 8, 2, 65], BF16)
        kcT = fw.sbuf("kcT", [128, 512], BF16)
        vcT = fw.sbuf("vcT", [128, 512], BF16)
        vc = fw.sbuf("vc", [128, 4, 2, 65], BF16)
        cmpsel = fw.sbuf("cmpsel", [128, 4, 128], BF16)
        ring = Ring(ps_bufs[4:])
        accO = Ring(ps_bufs[0:2])
        accI = ps_bufs[2]
        ypsb = ps_bufs[3]

        def bfview(b):
            return b.ap[:].bitcast(BF16)

        op("pool", lambda e: e.memset(identf[:], 1.0), writes=[identf])
        op("pool", lambda e: e.affine_select(out=identf[:], in_=identf[:], pattern=[[-1, 128]],
                                             compare_op=ALU.is_equal, fill=0.0, base=0,
                                             channel_multiplier=1), reads=[identf], writes=[identf])
        op("dve", lambda e: e.tensor_copy(out=identb[:], in_=identf[:]), reads=[identf], writes=[identb])
        op("pool", lambda e: e.memset(selV[:], 1.0), writes=[selV])
        op("pool", lambda e: e.memset(winV[:], 1.0), writes=[winV])
        op("pool", lambda e: e.memset(vc[:], 1.0), writes=[vc])
        op("pool", lambda e: e.memset(kcT[:], 0.0), writes=[kcT])
        op("pool", lambda e: e.memset(vcT[:], 0.0), writes=[vcT])
        op("pool", lambda e: e.memset(winKT[:], 0.0), writes=[winKT])
        dma(cmpsel[:], cmpsel_d, writes=[cmpsel])

        with contextlib.ExitStack() as p1:
            fw.stack = p1
            WF = fw.sbuf("WF", [128, 8, NWF], BF16)
            WT = fw.sbuf("WT", [128, 8, NWT], BF16)
            ngt = fw.sbuf("ngt", [128, 8], F32)
            W1r = fw.sbuf("W1r", [128, 2, 32, 64], BF16)
            W2r = fw.sbuf("W2r", [128, 2, 64], BF16)
            peT = fw.sbuf("peT", [128, 2, 32], BF16)
            b1p = fw.sbuf("b1p", [128, 2], F32)
            cmpraw = fw.sbuf("cmpraw", [128, 2, 144], BF16)
            stg = Ring([fw.sbuf(f"stg{i}", [128, 1024], F32) for i in range(2)])
            dma(ngt[:], ng_d, writes=[ngt])
            k = 0
            for (wd, W, ncol) in ((wf_d, WF, NWF), (wt_d, WT, NWT)):
                for c in range(8):
                    for c0 in range(0, ncol, 1024):
                        c1 = min(ncol, c0 + 1024)
                        s = stg.get()
                        dma(s[:, 0:c1 - c0], wd[:, c, c0:c1], writes=[s])
                        en = "dve" if k % 2 == 0 else "pool"
                        k += 1
                        op(en, lambda e: e.tensor_scalar(out=W[:, c, c0:c1], in0=s[:, 0:c1 - c0],
                                                         scalar1=ngt[:, c:c + 1], scalar2=1.0, op0=ALU.mult, op1=ALU.mult),
                           reads=[s, ngt], writes=[W])
            for kv in range(2):
                for jh in range(2):
                    s = stg.get()
                    dma(s[:, 0:1024].rearrange("p (j e) -> p j e", e=64), w1_d[:, kv, jh * 16:(jh + 1) * 16], writes=[s])
                    op("dve", lambda e: e.tensor_copy(out=W1r[:, kv, jh * 16:(jh + 1) * 16],
                                                      in_=s[:, 0:1024].rearrange("p (j e) -> p j e", e=64)),
                       reads=[s], writes=[W1r])
            s = stg.get()
            dma(s[:, 0:128].rearrange("p (k e) -> p k e", e=64), w2_d, writes=[s])
            op("dve", lambda e: e.tensor_copy(out=W2r[:], in_=s[:, 0:128].rearrange("p (k e) -> p k e", e=64)),
               reads=[s], writes=[W2r])
            s = stg.get()
            dma(s[:, 0:64].rearrange("p (k e) -> p k e", e=32), pe_d, writes=[s])
            op("dve", lambda e: e.tensor_copy(out=peT[:], in_=s[:, 0:64].rearrange("p (k e) -> p k e", e=32)),
               reads=[s], writes=[peT])
            dma(b1p[:], b1_d, writes=[b1p])
            op("pool", lambda e: e.memset(cmpraw[:], 0.0), writes=[cmpraw])
            pb = ring.get()
            for kv in range(2 if DBG >= 2 else 0):
                for h in range(2):
                    rows = slice(h * 64, (h + 1) * 64)
                    for j in range(32):
                        op("pe", lambda e: e.matmul(pb[rows, kv:kv + 1], lhsT=W1r[rows, kv, j, :],
                                                    rhs=peT[rows, kv, j:j + 1], start=(j == 0), stop=(j == 31)),
                           reads=[W1r, peT], writes=[pb])
            if DBG >= 2:
                op("dve", lambda e: e.tensor_tensor(out=b1p[:], in0=pb[:, 0:2], in1=b1p[:], op=ALU.add),
                   reads=[pb, b1p], writes=[b1p])

            xr = Ring([fw.sbuf(f"xt{i}", [128, D], F32) for i in range(2)])
            junk = fw.sbuf("junk", [128, D], BF16)
            ssr = Ring([fw.sbuf(f"ss{i}", [128, 1], F32) for i in range(2)])
            xbr = Ring([fw.sbuf(f"xb{i}", [128, D], BF16) for i in range(2)])
            xTr = Ring([fw.sbuf(f"xT{i}", [128, 8, 128], BF16) for i in range(2)])
            kvor = Ring([fw.sbuf(f"kvo{i}", [128, 512], F32) for i in range(2)])
            kv1r = Ring([fw.sbuf(f"kv1f{i}", [128, 280], F32) for i in range(2)])
            kvbr = Ring([fw.sbuf(f"kvb{i}", [128, 768], BF16) for i in range(2)])
            zar = Ring([fw.sbuf(f"za{i}", [128, 512], BF16) for i in range(2)])
            gmr = Ring([fw.sbuf(f"gmb{i}", [128, 2048], BF16) for i in range(2)])
            f4r = Ring([fw.sbuf(f"f4{i}", [128, 4, 128], BF16) for i in range(3)])
            hidr = Ring([fw.sbuf(f"hid{i}", [128, 4, 128], F32) for i in range(2)])
            gTr = Ring([fw.sbuf(f"gT{i}", [128, 2, 64], BF16) for i in range(2)])
            tmpK = fw.sbuf("tmpK", [128, 2, 128], BF16)
            tmpV = fw.sbuf("tmpV", [128, 2, 2, 65], BF16)
            op("pool", lambda e: e.memset(tmpV[:], 1.0), writes=[tmpV])

            def compress(raw, nb, dst_k, dst_v, col0):
                W = 2 * nb
                hp = ring.get()
                hv = hp.ap[:, 0:W].rearrange("p (k m) -> p k m", m=nb)
                cr = raw.ap[:].rearrange("p k (m s) -> p k m s", s=16)
                for kv in range(2):
                    for h in range(2):
                        rows = slice(h * 64, (h + 1) * 64)
                        for j in range(32):
                            rhs = cr[rows, kv, 0:nb, j] if j < 16 else cr[rows, kv, 1:nb + 1, j - 16]
                            op("pe", lambda e: e.matmul(hv[rows, kv, :], lhsT=W1r[rows, kv, j, :], rhs=rhs,
                                                        start=(j == 0), stop=(j == 31)),
                               reads=[W1r, raw], writes=[hp])
                hd = hidr.get()
                op("dve", lambda e: e.tensor_tensor(out=hd[:, 0, 0:W].rearrange("p (k m) -> p k m", m=nb), in0=hv,
                                                    in1=b1p[:].unsqueeze(2).to_broadcast([128, 2, nb]),
                                                    op=ALU.add), reads=[hp, b1p], writes=[hd])
                op("dve", lambda e: e.tensor_tensor(out=hd[:, 1, 0:W], in0=hd[:, 0, 0:W], in1=hd[:, 0, 0:W], op=ALU.mult),
                   reads=[hd], writes=[hd])
                op("dve", lambda e: e.tensor_scalar(out=hd[:, 1, 0:W], in0=hd[:, 1, 0:W], scalar1=0.044715, scalar2=1.0,
                                                    op0=ALU.mult, op1=ALU.add), reads=[hd], writes=[hd])
                op("dve", lambda e: e.tensor_tensor(out=hd[:, 2, 0:W], in0=hd[:, 1, 0:W], in1=hd[:, 0, 0:W], op=ALU.mult),
                   reads=[hd], writes=[hd])
                op("act", lambda e: e.activation(out=hd[:, 3, 0:W], in_=hd[:, 2, 0:W], func=AF.Exp, scale=-2.0 * GC),
                   reads=[hd], writes=[hd])
                op("dve", lambda e: e.tensor_scalar(out=hd[:, 3, 0:W], in0=hd[:, 3, 0:W], scalar1=1.0, scalar2=None,
                                                    op0=ALU.add), reads=[hd], writes=[hd])
                op("dve", lambda e: e.reciprocal(out=hd[:, 3, 0:W], in_=hd[:, 3, 0:W]), reads=[hd], writes=[hd])
                gT = gTr.get()
                op("dve", lambda e: e.tensor_tensor(out=gT[:, :, 0:nb], in0=hd[:, 0, 0:W].rearrange("p (k m) -> p k m", m=nb),
                                                    in1=hd[:, 3, 0:W].rearrange("p (k m) -> p k m", m=nb), op=ALU.mult),
                   reads=[hd], writes=[gT])
                kp = ring.get()
                kpv = kp.ap[:, 0:W].rearrange("p (k m) -> p k m", m=nb)
                for kv in range(2):
                    for h in range(2):
                        rows = slice(h * 64, (h + 1) * 64)
                        op("pe", lambda e: e.matmul(kpv[rows, kv, :], lhsT=W2r[rows, kv, :], rhs=gT[rows, kv, 0:nb],
                                                    start=True, stop=True), reads=[W2r, gT], writes=[kp])
                op("act", lambda e: e.copy(out=dst_k[:, col0:col0 + nb], in_=kpv[:, 0, :]), reads=[kp], writes=[dst_k])
                op("act", lambda e: e.copy(out=dst_v[:, col0:col0 + nb], in_=kpv[:, 1, :]), reads=[kp], writes=[dst_v])

            def front(bg, samp=None):
                sm = samp is not None
                own = sm or bg >= NBP
                i = samp if sm else bg - NBP
                d_gn, d_za, d_gm, d_q, d_zs = ((sc_gn_s, sc_za_s, sc_gm_s, sc_q_s, sc_zs_s) if sm
                                               else (sc_gn, sc_za, sc_gm, sc_q, sc_zs))
                xt = xr.get()
                if sm:
                    src = xs_d[samp]
                else:
                    src = xo[i * 128:(i + 1) * 128, :] if own else xp[bg * 128:(bg + 1) * 128, :]
                dma(xt[:], src, writes=[xt])
                ss = ssr.get()
                op("act", lambda e: e.activation(out=junk[:], in_=xt[:], func=AF.Square, accum_out=ss[:]),
                   reads=[xt], writes=[junk, ss])
                op("act", lambda e: e.activation(out=ss[:], in_=ss[:], func=AF.Ln, scale=1.0 / D, bias=1e-6),
                   reads=[ss], writes=[ss])
                op("act", lambda e: e.activation(out=ss[:], in_=ss[:], func=AF.Exp, scale=-0.5),
                   reads=[ss], writes=[ss])
                xb = xbr.get()
                op("dve", lambda e: e.tensor_scalar(out=xb[:], in0=xt[:], scalar1=ss[:, 0:1], scalar2=None,
                                                    op0=ALU.mult), reads=[xt, ss], writes=[xb])
                pt = ring.get()
                ptv = bfview(pt).rearrange("p (a b) -> p a b", b=128)
                for c in range(8):
                    op("pe", lambda e: e.transpose(out=ptv[:, c, :], in_=xb[:, c * 128:(c + 1) * 128],
                                                   identity=identb[:]), reads=[xb, identb], writes=[pt])
                xT = xTr.get()
                op("act", lambda e: e.copy(out=xT[:], in_=ptv), reads=[pt], writes=[xT])

                def tproj(gi):
                    c0, c1 = T_GROUPS[gi]
                    ps = ring.get()
                    for c in range(8):
                        op("pe", lambda e: e.matmul(ps[:, 0:c1 - c0], lhsT=xT[:, c, :], rhs=WT[:, c, c0:c1],
                                                    start=(c == 0), stop=(c == 7)), reads=[xT, WT], writes=[ps])
                    return ps

                kvb = kvbr.get()
                ps = tproj(0)
                if own:
                    kvo = kvor.get()
                    op("dve", lambda e: e.tensor_copy(out=kvo[:], in_=ps[:, :]), reads=[ps], writes=[kvo])
                    if sm:
                        dma(kvs_o[samp], kvo[0:8, :], reads=[kvo])
                    else:
                        dma(kv_o[i * 128:(i + 1) * 128, :], kvo[:], reads=[kvo])
                op("act", lambda e: e.copy(out=kvb[:, 0:512], in_=ps[:, :]), reads=[ps], writes=[kvb])
                ps = tproj(1)
                op("act", lambda e: e.copy(out=kvb[:, 512:768], in_=ps[:, 0:256]), reads=[ps], writes=[kvb])
                if own:
                    kv1 = kv1r.get()
                    op("dve", lambda e: e.tensor_copy(out=kv1[:], in_=ps[:, 0:280]), reads=[ps], writes=[kv1])
                    dma(d_gn[i], kv1[:, 256:280], reads=[kv1])
                    if sm:
                        dma(wins_o[samp, 504:512, :], kv1[0:8, 0:256], reads=[kv1])
                        dma(wins_o[samp, 0:504, :], cwin_d[samp, 8:512, :])
                    elif i >= NBO - NWIN:
                        w = i - (NBO - NWIN)
                        dma(win_o[w * 128:(w + 1) * 128, :], kv1[:, 0:256], reads=[kv1])
                    ps = tproj(2)
                    za = zar.get()
                    op("act", lambda e: e.copy(out=za[:], in_=ps[:, :]), reads=[ps], writes=[za])
                    dma(d_za[i], za[:], reads=[za])
                    gmb = gmr.get()
                    for q4 in range(4):
                        ps = tproj(3 + q4)
                        if q4 % 2 == 0:
                            op("dve", lambda e: e.tensor_copy(out=gmb[:, q4 * 512:(q4 + 1) * 512], in_=ps[:, :]),
                               reads=[ps], writes=[gmb])
                        else:
                            op("act", lambda e: e.copy(out=gmb[:, q4 * 512:(q4 + 1) * 512], in_=ps[:, :]),
                               reads=[ps], writes=[gmb])
                    dma(d_gm[i], gmb[:], reads=[gmb])
                pt2 = ring.get()
                p2v = bfview(pt2).rearrange("p (a b) -> p a b", b=128)
                for kk, c0 in enumerate((0, 128, 256, 512)):
                    op("pe", lambda e: e.transpose(out=p2v[:, kk, :], in_=kvb[:, c0:c0 + 128], identity=identb[:]),
                       reads=[kvb, identb], writes=[pt2])
                if sm:
                    op("act", lambda e: e.copy(out=tmpK[:], in_=p2v[:, 2:4, :]), reads=[pt2], writes=[tmpK])
                    op("pool", lambda e: e.tensor_copy(out=tmpV[:, 0, :, 0:64],
                                                       in_=kvb[:, 384:512].rearrange("p (h d) -> p h d", d=64)),
                       reads=[kvb], writes=[tmpV])
                    op("pool", lambda e: e.tensor_copy(out=tmpV[:, 1, :, 0:64],
                                                       in_=kvb[:, 640:768].rearrange("p (h d) -> p h d", d=64)),
                       reads=[kvb], writes=[tmpV])
                    dma(sc_nk[samp], tmpK[:].rearrange("p a b -> p (a b)"), reads=[tmpK])
                    dma(sc_nv[samp], tmpV[:].rearrange("p a h c -> p (a h c)"), reads=[tmpV])
                else:
                    op("pool", lambda e: e.tensor_copy(out=cmpraw[:, :, 0:16], in_=cmpraw[:, :, 128:144]),
                       reads=[cmpraw], writes=[cmpraw])
                    op("dve", lambda e: e.tensor_copy(out=cmpraw[:, :, 16:144], in_=p2v[:, 0:2, :]),
                       reads=[pt2, cmpraw], writes=[cmpraw])
                    op("act", lambda e: e.copy(out=selKT[:, bg * 128:(bg + 1) * 128], in_=p2v[:, 2, :]),
                       reads=[pt2], writes=[selKT])
                    wslot = bg % 8
                    op("act", lambda e: e.copy(out=winKT[:, wslot * 128:(wslot + 1) * 128], in_=p2v[:, 3, :]),
                       reads=[pt2], writes=[winKT])
                    op("pool", lambda e: e.tensor_copy(out=selV[:, bg, :, 0:64],
                                                       in_=kvb[:, 384:512].rearrange("p (h d) -> p h d", d=64)),
                       reads=[kvb], writes=[selV])
                    op("pool", lambda e: e.tensor_copy(out=winV[:, wslot, :, 0:64],
                                                       in_=kvb[:, 640:768].rearrange("p (h d) -> p h d", d=64)),
                       reads=[kvb], writes=[winV])
                    if bg >= NBP - 4:
                        dma(sc_wk[bg], winKT[:, wslot * 128:(wslot + 1) * 128], reads=[winKT])
                        dma(sc_wv[bg], winV[:, wslot].rearrange("p h c -> p (h c)"), reads=[winV])

                def fproj(col0, scale):
                    ps4 = ring.get()
                    p4 = ps4.ap[:].rearrange("p (a b) -> p a b", b=128)
                    for k4 in range(4):
                        for c in range(8):
                            op("pe", lambda e: e.matmul(p4[:, k4, :], lhsT=WF[:, c, col0 + 128 * k4:col0 + 128 * (k4 + 1)],
                                                        rhs=xT[:, c, :], start=(c == 0), stop=(c == 7)),
                               reads=[WF, xT], writes=[ps4])
                    f4 = f4r.get()
                    op("act", lambda e: e.activation(out=f4[:], in_=p4, func=AF.Copy, scale=scale),
                       reads=[ps4], writes=[f4])
                    return f4

                f4 = fproj(512, 1.0)
                dma((sc_u_s[samp] if sm else sc_u[bg]).rearrange("p (a b) -> p a b", b=128), f4[:], reads=[f4])
                if own:
                    f4 = fproj(0, 0.125)
                    dma(d_q[i].rearrange("p (a b) -> p a b", b=128), f4[:], reads=[f4])
                    f4 = fproj(1024, 1.0)
                    dma(d_zs[i].rearrange("p (a b) -> p a b", b=128), f4[:], reads=[f4])
                if sm:
                    return
                compress(cmpraw, 8, kcT, vcT, 8 * bg)
                nt = bg // 16
                pv = ring.get()
                pvv = bfview(pv)[:, 0:128]
                op("pe", lambda e: e.transpose(out=pvv, in_=vcT[:, nt * 128:(nt + 1) * 128], identity=identb[:]),
                   reads=[vcT, identb], writes=[pv])
                op("dve", lambda e: e.tensor_copy(out=vc[:, nt, :, 0:64], in_=pvv.rearrange("p (h d) -> p h d", d=64)),
                   reads=[pv], writes=[vc])

            if NSB > 0:
                rawS = fw.sbuf("rawS", [128, 2, 1040], BF16)
                kcS = fw.sbuf("kcS", [128, 1024], BF16)
                vcS = fw.sbuf("vcS", [128, 1024], BF16)
                ptb = fw.sbuf("ptb", [128, 128], I32)
                ptf = fw.sbuf("ptf", [128, 128], F32)
                io2 = fw.sbuf("io2", [128, 1], F32)
                idxc = fw.sbuf("idxc", [128, 128], I32)
                pgr = Ring([fw.sbuf(f"pg{i}", [128, 256], F32) for i in range(4)])
                pgbr = Ring([fw.sbuf(f"pgb{i}", [128, 256], BF16) for i in range(2)])
                op("pool", lambda e: e.memset(rawS[:], 0.0), writes=[rawS])
                op("pool", lambda e: e.iota(io2[:], pattern=[[0, 1]], base=0, channel_multiplier=2,
                                            allow_small_or_imprecise_dtypes=True), writes=[io2])

            def sample_p1():
                for b in range(NSB):
                    front(None, samp=b)
                    dma(ptb[:], pt_d[b], writes=[ptb])
                    op("dve", lambda e: e.tensor_copy(out=ptf[:], in_=ptb[:]), reads=[ptb], writes=[ptf])
                    op("dve", lambda e: e.tensor_scalar(out=ptf[:], in0=ptf[:], scalar1=256.0, scalar2=io2[:, 0:1],
                                                        op0=ALU.mult, op1=ALU.add), reads=[ptf, io2], writes=[ptf])
                    op("dve", lambda e: e.tensor_copy(out=idxc[:], in_=ptf[:]), reads=[ptf], writes=[idxc])
                    yield
                    for kt in range(128):
                        pg = pgr.get()
                        fw.idma(pg[:], cache_d, idxc[:, kt:kt + 1], reads=[idxc], writes=[pg])
                        pb_ = pgbr.get()
                        if kt % 2 == 0:
                            op("dve", lambda e: e.tensor_copy(out=pb_[:], in_=pg[:]), reads=[pg], writes=[pb_])
                        else:
                            op("act", lambda e: e.copy(out=pb_[:], in_=pg[:]), reads=[pg], writes=[pb_])
                        pt2 = ring.get()
                        p2v = bfview(pt2).rearrange("p (a b) -> p a b", b=128)
                        for kk in range(2):
                            op("pe", lambda e: e.transpose(out=p2v[:, kk, :], in_=pb_[:, kk * 128:(kk + 1) * 128],
                                                           identity=identb[:]), reads=[pb_, identb], writes=[pt2])
                        slot = kt % 8
                        if kt % 2 == 0:
                            op("act", lambda e: e.copy(out=rawS[:, :, 16 + 128 * slot:16 + 128 * (slot + 1)], in_=p2v[:, 0:2, :]),
                               reads=[pt2], writes=[rawS])
                        else:
                            op("dve", lambda e: e.tensor_copy(out=rawS[:, :, 16 + 128 * slot:16 + 128 * (slot + 1)],
                                                              in_=p2v[:, 0:2, :]), reads=[pt2], writes=[rawS])
                        if slot == 7:
                            compress(rawS, 64, kcS, vcS, 64 * (kt // 8))
                            op("dve", lambda e: e.tensor_copy(out=rawS[:, :, 0:16], in_=rawS[:, :, 1024:1040]),
                               reads=[rawS], writes=[rawS])
                        yield
                    dma(sc_kc[b], kcS[:], reads=[kcS])
                    dma(sc_vcT[b], vcS[:], reads=[vcS])

            sgen = sample_p1()
            per = max(1, (NSB * 129 + NBT - 1) // max(NBT, 1))
            for bg in range(NBT):
                front(bg)
                for _ in range(per):
                    try:
                        next(sgen)
                    except StopIteration:
                        break
            for _ in sgen:
                pass
        fw.barrier()
        fw.stack = pstore


        with contextlib.ExitStack() as p2:
            fw.stack = p2
            wla = fw.sbuf("wla", [128, 4, D], BF16)
            wls = fw.sbuf("wls", [128, 4, D], BF16)
            wo = fw.sbuf("wo", [128, 8, D], BF16)
            wglu = fw.sbuf("wglu", [128, 4, 512], BF16)
            fg = fw.sbuf("fgt", [128, D], F32)
            Esb = fw.sbuf("Esb", [128, EW], BF16)
            caus = fw.sbuf("caust", [128, 128], BF16)
            pset = contextlib.ExitStack()
            pset.__enter__()
            BbT = fw.sbuf("BbT", [128, 2, 16, 128], BF16)
            CTre = fw.sbuf("CTre", [128, 16, 128], BF16)
            CTimn = fw.sbuf("CTimn", [128, 16, 128], BF16)
            cosT = fw.sbuf("cosT", [128, 16, 128], F32)
            sinT = fw.sbuf("sinT", [128, 16, 128], F32)
            p2small = [fw.sbuf(f"p2s{i}", [128, 16], F32) for i in range(24)]
            dsk = fw.sbuf("dsk", [128, 4], F32)
            nbgl = fw.sbuf("nbgl", [128, 4], F32)
            fw.stack = pset
            stg = Ring([fw.sbuf(f"stg2_{i}", [128, 1024], F32) for i in range(2)])
            dma(fg[:], fg_d, writes=[fg])
            dma(Esb[:], E_d, writes=[Esb])
            dma(caus[:], caus_d, writes=[caus])
            k = 0
            for (wd, W, nk, ncol) in ((wla_d, wla, 4, D), (wls_d, wls, 4, D), (wo_d, wo, 8, D), (wglu_d, wglu, 4, 512)):
                for c in range(nk):
                    s_ = stg.get()
                    dma(s_[:, 0:ncol], wd[:, c, :], writes=[s_])
                    en = "dve" if k % 2 == 0 else "pool"
                    k += 1
                    op(en, lambda e: e.tensor_copy(out=W[:, c, :], in_=s_[:, 0:ncol]), reads=[s_], writes=[W])

            def small(name, shape=(128, 16), dt=F32):
                if tuple(shape) == (128, 16) and dt == F32 and p2small:
                    return p2small.pop()
                return fw.sbuf(name, list(shape), dt)

            lre, lim, ldt = small("lre"), small("lim"), small("ldt")
            dma(lre[:], lamre_d, writes=[lre])
            dma(lim[:], lamim_d, writes=[lim])
            dma(ldt[:], logdt_d, writes=[ldt])
            mgh = small("mgh", (128, 2))
            dma(dsk[:], dsk_d, writes=[dsk])
            dma(nbgl[:], bgl_d, writes=[nbgl])
            dma(mgh[:], mgh_d, writes=[mgh])
            tmp = [small(f"s5t{i}") for i in range(8)]
            rho, c1, s1 = small("rho"), small("c1"), small("s1")
            ki = small("ki", dt=I32)

            def tt(out, a, b, o, eng="dve"):
                op(eng, lambda e: e.tensor_tensor(out=out[:], in0=a[:], in1=b[:], op=o), reads=[a, b], writes=[out])

            def ts(out, a, s1_, s2_, o0, o1=None, eng="dve"):
                if o1 is None:
                    op(eng, lambda e: e.tensor_scalar(out=out[:], in0=a[:], scalar1=s1_, scalar2=None, op0=o0),
                       reads=[a], writes=[out])
                else:
                    op(eng, lambda e: e.tensor_scalar(out=out[:], in0=a[:], scalar1=s1_, scalar2=s2_, op0=o0, op1=o1),
                       reads=[a], writes=[out])

            dtt, lr, th, r_ = tmp[0], tmp[1], tmp[2], tmp[3]
            op("act", lambda e: e.activation(out=dtt[:], in_=ldt[:], func=AF.Exp), reads=[ldt], writes=[dtt])
            tt(lr, lre, dtt, ALU.mult)
            tt(th, lim, dtt, ALU.mult)
            op("act", lambda e: e.activation(out=rho[:], in_=lr[:], func=AF.Exp), reads=[lr], writes=[rho])
            ts(tmp[4], th, 1.0 / (2 * math.pi), None, ALU.mult)
            op("dve", lambda e: e.tensor_copy(out=ki[:], in_=tmp[4][:]), reads=[tmp[4]], writes=[ki])
            op("dve", lambda e: e.tensor_copy(out=tmp[4][:], in_=ki[:]), reads=[ki], writes=[tmp[4]])
            op("dve", lambda e: e.scalar_tensor_tensor(out=r_[:], in0=tmp[4][:], scalar=-2.0 * math.pi, in1=th[:],
                                                       op0=ALU.mult, op1=ALU.add), reads=[tmp[4], th], writes=[r_])
            s4, c4, s2, c2 = tmp[4], tmp[5], tmp[6], tmp[7]
            hp_ = small("halfpi", (128, 1))
            op("pool", lambda e: e.memset(hp_[:], math.pi / 2), writes=[hp_])
            op("act", lambda e: e.activation(out=s4[:], in_=r_[:], func=AF.Sin, scale=0.25), reads=[r_], writes=[s4])
            op("act", lambda e: e.activation(out=c4[:], in_=r_[:], func=AF.Sin, scale=0.25, bias=hp_[:, 0:1]),
               reads=[r_, hp_], writes=[c4])
            op("dve", lambda e: e.scalar_tensor_tensor(out=s2[:], in0=s4[:], scalar=2.0, in1=c4[:], op0=ALU.mult,
                                                       op1=ALU.mult), reads=[s4, c4], writes=[s2])
            tt(c2, s4, s4, ALU.mult)
            ts(c2, c2, -2.0, 1.0, ALU.mult, ALU.add)
            op("dve", lambda e: e.scalar_tensor_tensor(out=s1[:], in0=s2[:], scalar=2.0, in1=c2[:], op0=ALU.mult,
                                                       op1=ALU.mult), reads=[s2, c2], writes=[s1])
            tt(c1, s2, s2, ALU.mult)
            ts(c1, c1, -2.0, 1.0, ALU.mult, ALU.add)
            are, aim, am1, den = tmp[0], tmp[1], tmp[2], tmp[3]
            tt(are, rho, c1, ALU.mult)
            tt(aim, rho, s1, ALU.mult)
            ts(am1, are, -1.0, None, ALU.add)
            tt(den, lre, lre, ALU.mult)
            tt(tmp[4], lim, lim, ALU.mult)
            tt(den, den, tmp[4], ALU.add)
            op("dve", lambda e: e.reciprocal(out=den[:], in_=den[:]), reads=[den], writes=[den])
            fre, fim = small("fre"), small("fim")
            tt(tmp[4], am1, lre, ALU.mult)
            tt(tmp[5], aim, lim, ALU.mult)
            tt(tmp[4], tmp[4], tmp[5], ALU.add)
            tt(fre, tmp[4], den, ALU.mult)
            tt(tmp[4], aim, lre, ALU.mult)
            tt(tmp[5], am1, lim, ALU.mult)
            tt(tmp[4], tmp[4], tmp[5], ALU.subtract)
            tt(fim, tmp[4], den, ALU.mult)
            bre, bim = small("bre", (128, 16, 16)), small("bim", (128, 16, 16))
            dma(bre[:], bre_d, writes=[bre])
            dma(bim[:], bim_d, writes=[bim])
            bt = [small(f"bt{i}", (128, 16, 16)) for i in range(3)]
            Mb = fw.sbuf("Mb", [128, 16, 2, 16], BF16)
            Mz = fw.sbuf("Mz", [128, 16, 4, 32], BF16)
            op("pool", lambda e: e.memset(Mz[:], 0.0), writes=[Mz])

            def bc16(t_):
                return t_.ap[:].unsqueeze(2).to_broadcast([128, 16, 16])

            for ri in range(2):
                fa, fb_, sgn = (fre, fim, ALU.subtract) if ri == 0 else (fim, fre, ALU.add)
                op("dve", lambda e: e.tensor_tensor(out=bt[0][:], in0=bre[:], in1=bc16(fa), op=ALU.mult),
                   reads=[bre, fa], writes=[bt[0]])
                op("dve", lambda e: e.tensor_tensor(out=bt[1][:], in0=bim[:], in1=bc16(fb_), op=ALU.mult),
                   reads=[bim, fb_], writes=[bt[1]])
                op("dve", lambda e: e.tensor_tensor(out=bt[2][:], in0=bt[0][:], in1=bt[1][:], op=sgn),
                   reads=[bt[0], bt[1]], writes=[bt[2]])
                op("dve", lambda e: e.tensor_tensor(
                    out=Mb[:], in0=bt[2].ap[:].unsqueeze(2).to_broadcast([128, 16, 2, 16]),
                    in1=mgh.ap[:].unsqueeze(1).unsqueeze(3).to_broadcast([128, 16, 2, 16]), op=ALU.mult),
                   reads=[bt[2], mgh], writes=[Mb])
                Mz5 = Mz.ap[:].rearrange("p (a q) r c -> p a q r c", q=4)
                Mb4 = Mb.ap[:].rearrange("p (a q) g c -> p a q (g c)", q=4)
                for q_ in range(4):
                    op("dve", lambda e: e.tensor_copy(out=Mz5[:, :, q_, q_, :], in_=Mb4[:, :, q_, :]),
                       reads=[Mb], writes=[Mz])
                for st in range(16):
                    pz = ring.get()
                    pzv = bfview(pz)[:, 0:128]
                    op("pe", lambda e: e.transpose(out=pzv, in_=Mz[:, st].rearrange("p r c -> p (r c)"),
                                                   identity=identb[:]), reads=[Mz, identb], writes=[pz])
                    op("act", lambda e: e.copy(out=BbT[:, ri, st, :], in_=pzv), reads=[pz], writes=[BbT])
            for half_ in range(2):
                s_ = stg.get()
                dma(s_[:, 0:1024].rearrange("p (a b) -> p a b", b=128), cre_d[:, half_ * 8:(half_ + 1) * 8, :], writes=[s_])
                op("dve", lambda e: e.tensor_copy(out=CTre[:, half_ * 8:(half_ + 1) * 8, :],
                                                  in_=s_[:, 0:1024].rearrange("p (a b) -> p a b", b=128)),
                   reads=[s_], writes=[CTre])
                s_ = stg.get()
                dma(s_[:, 0:1024].rearrange("p (a b) -> p a b", b=128), cim_d[:, half_ * 8:(half_ + 1) * 8, :], writes=[s_])
                op("dve", lambda e: e.tensor_scalar(out=CTimn[:, half_ * 8:(half_ + 1) * 8, :],
                                                    in0=s_[:, 0:1024].rearrange("p (a b) -> p a b", b=128),
                                                    scalar1=-1.0, scalar2=None, op0=ALU.mult), reads=[s_], writes=[CTimn])
            op("pool", lambda e: e.memset(cosT[:], 1.0), writes=[cosT])
            op("pool", lambda e: e.memset(sinT[:], 0.0), writes=[sinT])
            emr, emi = small("emr"), small("emi")
            op("dve", lambda e: e.tensor_copy(out=emr[:], in_=c1[:]), reads=[c1], writes=[emr])
            op("dve", lambda e: e.tensor_copy(out=emi[:], in_=s1[:]), reads=[s1], writes=[emi])
            tb = [fw.sbuf(f"tb{i}", [128, 16, 64], F32) for i in range(2)]
            for kk in range(7):
                m = 1 << kk
                ebr = emr.ap[:].unsqueeze(2).to_broadcast([128, 16, m])
                ebi = emi.ap[:].unsqueeze(2).to_broadcast([128, 16, m])
                op("dve", lambda e: e.tensor_tensor(out=tb[0][:, :, 0:m], in0=cosT[:, :, 0:m], in1=ebr, op=ALU.mult),
                   reads=[cosT, emr], writes=[tb[0]])
                op("dve", lambda e: e.tensor_tensor(out=tb[1][:, :, 0:m], in0=sinT[:, :, 0:m], in1=ebi, op=ALU.mult),
                   reads=[sinT, emi], writes=[tb[1]])
                op("dve", lambda e: e.tensor_tensor(out=cosT[:, :, m:2 * m], in0=tb[0][:, :, 0:m], in1=tb[1][:, :, 0:m],
                                                    op=ALU.subtract), reads=[tb[0], tb[1]], writes=[cosT])
                op("dve", lambda e: e.tensor_tensor(out=tb[0][:, :, 0:m], in0=cosT[:, :, 0:m], in1=ebi, op=ALU.mult),
                   reads=[cosT, emi], writes=[tb[0]])
                op("dve", lambda e: e.tensor_tensor(out=tb[1][:, :, 0:m], in0=sinT[:, :, 0:m], in1=ebr, op=ALU.mult),
                   reads=[sinT, emr], writes=[tb[1]])
                op("dve", lambda e: e.tensor_tensor(out=sinT[:, :, m:2 * m], in0=tb[0][:, :, 0:m], in1=tb[1][:, :, 0:m],
                                                    op=ALU.add), reads=[tb[0], tb[1]], writes=[sinT])
                tt(tmp[0], emr, emr, ALU.mult)
                tt(tmp[1], emi, emi, ALU.mult)
                tt(tmp[2], emr, emi, ALU.mult)
                tt(emr, tmp[0], tmp[1], ALU.subtract)
                ts(emi, tmp[2], 2.0, None, ALU.mult)

            hre_p, him_p = small("hre_p"), small("him_p")
            g0r, g0i = small("g0r"), small("g0i")
            glr, gli = small("glr"), small("gli")
            fw.barrier()
            pset.close()
            fw.stack = p2
            op("pool", lambda e: e.memset(hre_p[:], 0.0), writes=[hre_p])
            op("pool", lambda e: e.memset(him_p[:], 0.0), writes=[him_p])

            uTr = Ring([fw.sbuf(f"uT{i}", [128, 4, 128], BF16) for i in range(2)])
            wk = Ring([fw.sbuf(f"wk{i}", [128, 128], F32) for i in range(8)])
            gbr = Ring([fw.sbuf(f"gb{i}", [128, 128], F32) for i in range(4)])
            hbr = Ring([fw.sbuf(f"hb{i}", [128, 2, 128], BF16) for i in range(4)])

            def cmul(or_, oi_, ar, ai, br_, bi_):
                tt(tmp[0], ar, br_, ALU.mult)
                tt(tmp[1], ai, bi_, ALU.mult, eng="pool")
                tt(tmp[2], ar, bi_, ALU.mult)
                tt(tmp[3], ai, br_, ALU.mult, eng="pool")
                tt(or_, tmp[0], tmp[1], ALU.subtract)
                tt(oi_, tmp[2], tmp[3], ALU.add, eng="pool")

            class T16:
                def __init__(self, b):
                    self.b = b

            def s5_gen(bg, samp, out):
                own = samp is not None or bg >= NBP
                lc = 7 if samp is not None else 127
                uT = uTr.get()
                out["uT"] = uT
                dma(uT[:], (sc_u_s[samp] if samp is not None else sc_u[bg]).rearrange("p (a b) -> p a b", b=128), writes=[uT])
                cmul(g0r, g0i, c1, s1, hre_p, him_p)
                yield
                pend = []
                for st in range(16):
                    ct, q_ = st // 4, st % 4
                    bps = ring.get()
                    bv = bps.ap[:, 0:256].rearrange("p (a b) -> p a b", b=128)
                    for ri in range(2):
                        op("pe", lambda e: e.matmul(bv[:, ri, :], lhsT=BbT[:, ri, st, :], rhs=uT[:, ct, :],
                                                    start=True, stop=True), reads=[BbT, uT], writes=[bps])
                    t1, t2, t3, t4 = wk.get(), wk.get(), wk.get(), wk.get()
                    cs, sn = cosT.ap[:, st, :], sinT.ap[:, st, :]
                    op("dve", lambda e: e.tensor_tensor(out=t1[:], in0=bv[:, 0, :], in1=cs, op=ALU.mult),
                       reads=[bps, cosT], writes=[t1])
                    op("dve", lambda e: e.tensor_tensor(out=t2[:], in0=bv[:, 1, :], in1=sn, op=ALU.mult),
                       reads=[bps, sinT], writes=[t2])
                    op("dve", lambda e: e.tensor_tensor(out=t3[:], in0=bv[:, 1, :], in1=cs, op=ALU.mult),
                       reads=[bps, cosT], writes=[t3])
                    op("dve", lambda e: e.tensor_tensor(out=t4[:], in0=bv[:, 0, :], in1=sn, op=ALU.mult),
                       reads=[bps, sinT], writes=[t4])
                    op("pool", lambda e: e.tensor_tensor(out=t1[:], in0=t1[:], in1=t2[:], op=ALU.add),
                       reads=[t1, t2], writes=[t1])
                    op("pool", lambda e: e.tensor_tensor(out=t3[:], in0=t3[:], in1=t4[:], op=ALU.subtract),
                       reads=[t3, t4], writes=[t3])
                    gr, gi = gbr.get(), gbr.get()
                    rb = rho.ap[:, st:st + 1].to_broadcast([128, 128])
                    op("dve", lambda e: e.tensor_tensor_scan(out=gr[:], data0=rb, data1=t1[:], initial=g0r[:, st:st + 1],
                                                             op0=ALU.mult, op1=ALU.add), reads=[rho, t1, g0r], writes=[gr])
                    op("dve", lambda e: e.tensor_tensor_scan(out=gi[:], data0=rb, data1=t3[:], initial=g0i[:, st:st + 1],
                                                             op0=ALU.mult, op1=ALU.add), reads=[rho, t3, g0i], writes=[gi])
                    op("pool", lambda e: e.tensor_copy(out=glr[:, st:st + 1], in_=gr[:, lc:lc + 1]), reads=[gr], writes=[glr])
                    op("pool", lambda e: e.tensor_copy(out=gli[:, st:st + 1], in_=gi[:, lc:lc + 1]), reads=[gi], writes=[gli])
                    this_y = None
                    if own:
                        t5, t6, t7, t8 = wk.get(), wk.get(), wk.get(), wk.get()
                        hb = hbr.get()
                        op("dve", lambda e: e.tensor_tensor(out=t5[:], in0=gr[:], in1=cs, op=ALU.mult),
                           reads=[gr, cosT], writes=[t5])
                        op("pool", lambda e: e.tensor_tensor(out=t6[:], in0=gi[:], in1=sn, op=ALU.mult),
                           reads=[gi, sinT], writes=[t6])
                        op("dve", lambda e: e.tensor_tensor(out=t7[:], in0=gi[:], in1=cs, op=ALU.mult),
                           reads=[gi, cosT], writes=[t7])
                        op("pool", lambda e: e.tensor_tensor(out=t8[:], in0=gr[:], in1=sn, op=ALU.mult),
                           reads=[gr, sinT], writes=[t8])
                        op("dve", lambda e: e.tensor_tensor(out=hb[:, 0, :], in0=t5[:], in1=t6[:], op=ALU.subtract),
                           reads=[t5, t6], writes=[hb])
                        op("pool", lambda e: e.tensor_tensor(out=hb[:, 1, :], in0=t7[:], in1=t8[:], op=ALU.add),
                           reads=[t7, t8], writes=[hb])

                        def this_y(hb=hb, st=st, ct=ct, q_=q_):
                            yv_ = ypsb.ap[:].rearrange("p (a b) -> p a b", b=128)
                            op("pe", lambda e: e.matmul(yv_[:, ct, :], lhsT=CTre[:, st, :], rhs=hb[:, 0, :],
                                                        start=(q_ == 0), stop=False), reads=[CTre, hb], writes=[ypsb])
                            op("pe", lambda e: e.matmul(yv_[:, ct, :], lhsT=CTimn[:, st, :], rhs=hb[:, 1, :],
                                                        start=False, stop=(q_ == 3)), reads=[CTimn, hb], writes=[ypsb])
                    yield
                    pend.append(this_y)
                    if len(pend) > int(os.environ.get('YDEF', '2')):
                        y_ = pend.pop(0)
                        if y_ is not None:
                            y_()
                while pend:
                    y_ = pend.pop(0)
                    if y_ is not None:
                        yield
                        y_()
                c127 = tmp[4]
                s127 = tmp[5]
                op("dve", lambda e: e.tensor_copy(out=c127[:], in_=cosT[:, :, lc]), reads=[cosT], writes=[c127])
                op("dve", lambda e: e.tensor_copy(out=s127[:], in_=sinT[:, :, lc]), reads=[sinT], writes=[s127])
                cmul(hre_p, him_p, c127, s127, glr, gli)

            def s5_block(bg, samp=None):
                out = {}
                for _ in s5_gen(bg, samp, out):
                    pass
                return out["uT"]

            wkr = Ring([fw.sbuf(f"wkt{i}", [128, 5, 128], BF16) for i in range(2)])
            wvr = Ring([fw.sbuf(f"wvt{i}", [128, 5, 130], BF16) for i in range(2)])
            qzr = Ring([fw.sbuf(f"qz{i}", [128, 2, 512], BF16) for i in range(2)])
            for qz_ in qzr.bufs:
                op("pool", lambda e: e.memset(qz_[:], 0.0), writes=[qz_])
            gnr = Ring([fw.sbuf(f"gn{i}", [128, 24], F32) for i in range(2)])
            cbr = Ring([fw.sbuf(f"cb{i}", [128, 128], BF16) for i in range(10)])
            tkr = Ring([fw.sbuf(f"tk{i}", [128, 128], F32) for i in range(2)])
            Pr = Ring([fw.sbuf(f"P{i}", [128, 512], BF16) for i in range(2)])
            OTr = Ring([fw.sbuf(f"OT{i}", [128, 512], F32) for i in range(1)])
            oattn_r = Ring([fw.sbuf(f"oat{i}", [128, 512], F32) for i in range(2)])
            smr = Ring([fw.sbuf(f"sm{i}", [128, 8], F32) for i in range(12)])
            impr = Ring([fw.sbuf(f"imp{i}", [128, 128], F32) for i in range(2)])
            selmb_r = Ring([fw.sbuf(f"selmb{i}", [128, 128], BF16) for i in range(2)])
            selmT_r = Ring([fw.sbuf(f"selmT{i}", [128, 2, 128], BF16) for i in range(2)])
            big = Ring([fw.sbuf(f"big{i}", [128, 512], F32) for i in range(4)])
            bigb = Ring([fw.sbuf(f"bigb{i}", [128, 512], BF16) for i in range(3)])
            ysTr = Ring([fw.sbuf(f"ysT{i}", [128, 512], BF16) for i in range(2)])
            gmr2 = Ring([fw.sbuf(f"gm2_{i}", [128, 2048], BF16) for i in range(1)])
            sgm = fw.sbuf("sgm", [128, 2048], BF16)
            sgt = fw.sbuf("sgt", [128, 512], F32)
            mbt = fw.sbuf("mbt", [128, D], BF16)
            mT = fw.sbuf("mT", [128, 8, 128], BF16)
            xr2 = Ring([fw.sbuf(f"xr2_{i}", [128, D], F32) for i in range(1)])

            def sigmoid_from(out, src_ap, src_bufs, scale_in=1.0, big_recip=True):
                op("act", lambda e: e.activation(out=out[:], in_=src_ap, func=AF.Sigmoid, scale=scale_in),
                   reads=src_bufs, writes=[out])

            TICK = {"gens": [], "n": 0}

            def tick():
                gs = TICK["gens"]
                if not gs:
                    return
                TICK["n"] += 1
                g_ = gs[TICK["n"] % len(gs)]
                try:
                    next(g_)
                except StopIteration:
                    gs.remove(g_)

            def emit_scores(kT, kreads, biases, qz, h):
                S = ring.get()
                op("pe", lambda e: e.matmul(S[:, :], lhsT=kT, rhs=qz[:, h, :], start=True, stop=(len(biases) == 0)),
                   reads=kreads + [qz], writes=[S])
                for bi_, (l_, r_ap, rd) in enumerate(biases):
                    op("pe", lambda e: e.matmul(S.ap[:].rearrange("p (a b) -> p a b", b=128), lhsT=l_, rhs=r_ap,
                                                start=False, stop=(bi_ == len(biases) - 1)), reads=rd, writes=[S])
                return S

            def attn_pass(h, qz, units, acc, extras=()):
                n = len(units)
                LA = int(os.environ.get('LA', '1'))
                Sq = [emit_scores(units[j][0], units[j][1], units[j][2], qz, h) for j in range(min(LA, n))]
                for ui, (kT, kreads, biases, V, vreads) in enumerate(units):
                    S = Sq.pop(0)
                    if ui + LA < n:
                        u2 = units[ui + LA]
                        Sq.append(emit_scores(u2[0], u2[1], u2[2], qz, h))
                    P = Pr.get()
                    op("act", lambda e: e.activation(out=P[:], in_=S[:, :], func=AF.Exp), reads=[S], writes=[P])
                    op("pe", lambda e: e.matmul(acc[0:65, :], lhsT=V, rhs=P[:], start=(ui == 0), stop=(ui == n - 1)),
                       reads=vreads + [P], writes=[acc])
                    for (xacc, xfn, lo, hi, xrd) in extras:
                        if lo <= ui <= hi:
                            op("pe", lambda e: e.matmul(xacc[:, :], lhsT=xfn(ui), rhs=P[:], start=(ui == lo),
                                                        stop=(ui == hi)), reads=xrd + [P], writes=[xacc])
                    tick()

            def finish_pass(h, br, acc, gate, oattn, first, clampz=False):
                OT = OTr.get()
                op("act", lambda e: e.copy(out=OT[0:65, :], in_=acc[0:65, :]), reads=[acc], writes=[OT])
                tp = ring.get()
                tpv = tp.ap[:, 0:260].rearrange("p (g c) -> p g c", c=65)
                for g in range(4):
                    op("pe", lambda e: e.transpose(out=tpv[:, g, :], in_=OT[0:65, g * 128:(g + 1) * 128],
                                                   identity=identf[0:65, 0:65]), reads=[OT, identf], writes=[tp])
                rz = smr.get()
                if clampz:
                    op("dve", lambda e: e.tensor_scalar(out=rz[:, 0:4], in0=tpv[:, :, 64], scalar1=1e-30, scalar2=None,
                                                        op0=ALU.max), reads=[tp], writes=[rz])
                    op("dve", lambda e: e.reciprocal(out=rz[:, 0:4], in_=rz[:, 0:4]), reads=[rz], writes=[rz])
                else:
                    op("dve", lambda e: e.reciprocal(out=rz[:, 0:4], in_=tpv[:, :, 64]), reads=[tp], writes=[rz])
                w_ = smr.get()
                gv = gate.ap[:].rearrange("p (h g b) -> p h g b", h=2, g=4)[:, h, :, br]
                op("dve", lambda e: e.tensor_tensor(out=w_[:, 0:4], in0=rz[:, 0:4], in1=gv, op=ALU.mult),
                   reads=[rz, gate], writes=[w_])
                for g in range(4):
                    dst = oattn[:, (h * 4 + g) * 64:(h * 4 + g + 1) * 64]
                    if first:
                        op("dve", lambda e: e.tensor_scalar(out=dst, in0=tpv[:, g, 0:64], scalar1=w_[:, g:g + 1],
                                                            scalar2=None, op0=ALU.mult), reads=[tp, w_], writes=[oattn])
                    else:
                        op("dve", lambda e: e.scalar_tensor_tensor(out=dst, in0=tpv[:, g, 0:64], scalar=w_[:, g:g + 1],
                                                                   in1=dst, op0=ALU.mult, op1=ALU.add),
                           reads=[tp, w_, oattn], writes=[oattn])
                return rz

            def own_block(i, uT, samp=None):
                sm = samp is not None
                bg = NBP + i if not sm else None
                yv_ = ypsb.ap[:].rearrange("p (a b) -> p a b", b=128)
                yv = big.get()
                yv3 = yv.ap[:].rearrange("p (a b) -> p a b", b=128)
                for ct in range(4):
                    op("dve", lambda e: e.scalar_tensor_tensor(out=yv3[:, ct, :], in0=uT[:, ct, :], scalar=dsk[:, ct:ct + 1],
                                                               in1=yv_[:, ct, :], op0=ALU.mult, op1=ALU.add),
                       reads=[uT, dsk, ypsb], writes=[yv])
                t_a, t_b = big.get(), big.get()
                op("pool", lambda e: e.tensor_tensor(out=t_a[:], in0=yv[:], in1=yv[:], op=ALU.mult), reads=[yv], writes=[t_a])
                op("pool", lambda e: e.tensor_scalar(out=t_a[:], in0=t_a[:], scalar1=0.044715, scalar2=1.0, op0=ALU.mult,
                                                     op1=ALU.add), reads=[t_a], writes=[t_a])
                op("pool", lambda e: e.tensor_tensor(out=t_a[:], in0=t_a[:], in1=yv[:], op=ALU.mult), reads=[t_a, yv], writes=[t_a])
                sigmoid_from(t_b, t_a[:], [t_a], scale_in=2.0 * GC)
                yg = big.get()
                op("dve", lambda e: e.tensor_tensor(out=yg[:], in0=yv[:], in1=t_b[:], op=ALU.mult), reads=[yv, t_b], writes=[yg])
                ygb = bigb.get()
                op("act", lambda e: e.copy(out=ygb[:], in_=yg[:]), reads=[yg], writes=[ygb])
                def readout_B():
                    for _ in range(6):
                        yield
                    gl = ring.get()
                    glv = gl.ap[:].rearrange("p (a b) -> p a b", b=128)
                    ygb3 = ygb.ap[:].rearrange("p (a b) -> p a b", b=128)
                    for co in range(4):
                        for ci in range(4):
                            op("pe", lambda e: e.matmul(glv[:, co, :], lhsT=wglu[:, ci, co * 128:(co + 1) * 128], rhs=ygb3[:, ci, :],
                                                        start=(ci == 0), stop=(ci == 3)), reads=[wglu, ygb], writes=[gl])
                    sg = t_a
                    sg3 = sg.ap[:].rearrange("p (a b) -> p a b", b=128)
                    for co in range(4):
                        op("act", lambda e: e.activation(out=sg3[:, co, :], in_=glv[:, co, :], func=AF.Sigmoid, scale=1.0,
                                                         bias=nbgl[:, co:co + 1]), reads=[gl, nbgl], writes=[sg])
                    op("pool", lambda e: e.tensor_tensor(out=yg[:], in0=yg[:], in1=sg[:], op=ALU.mult), reads=[yg, sg], writes=[yg])
                    zs = bigb.get()
                    dma(zs[:], sc_zs_s[samp] if sm else sc_zs[i], writes=[zs])
                    sz = t_b
                    op("act", lambda e: e.activation(out=sz[:], in_=zs[:], func=AF.Silu), reads=[zs], writes=[sz])
                    op("dve", lambda e: e.tensor_tensor(out=ysT[:], in0=yg[:], in1=sz[:], op=ALU.mult), reads=[yg, sz], writes=[ysT])
                    yield

                ysT = ysTr.get()
                rB = readout_B()
                if sm:
                    for _ in rB:
                        pass
                else:
                    def _chain(a, b):
                        for _ in a:
                            yield
                        if b is not None:
                            for _ in b:
                                yield
                    TICK["gens"].insert(0, _chain(rB, TICK.get("post")))

                qz = qzr.get()
                qsrc = sc_q_s[samp] if sm else sc_q[i]
                for h in range(2):
                    rows = slice(h * 64, (h + 1) * 64)
                    dma(qz[rows, h, :], qsrc[rows, :], writes=[qz])
                gn = gnr.get()
                dma(gn[:], sc_gn_s[samp] if sm else sc_gn[i], writes=[gn])
                gate = gnr.get()
                sigmoid_from(gate, gn[:], [gn], big_recip=False)
                oattn = oattn_r.get()
                if sm:
                    sample_attention(samp, qz, gate, oattn)
                    for _ in out_chain(i, samp, oattn, ysT):
                        pass
                    return None
                prompt_attention(i, bg, qz, gate, oattn)
                for _ in rB:
                    pass
                return out_chain(i, samp, oattn, ysT)

            def prompt_attention(i, bg, qz, gate, oattn):
                nt_hi = (8 * bg + 7) // 128
                cbs = []
                for nt in range(nt_hi + 1):
                    cb = cbr.get()
                    dma(cb[:], cmpb_d[i, nt], writes=[cb])
                    cbs.append(cb)
                ta, tb_ = tkr.get(), tkr.get()
                dma(ta[:], tka_d[i], writes=[ta])
                dma(tb_[:], tkb_d[i], writes=[tb_])
                selmT = selmT_r.get()
                selmbs = []
                for h in range(2):
                    units = []
                    for nt in range(nt_hi + 1):
                        units.append((kcT[:, nt * 128:(nt + 1) * 128], [kcT],
                                      [(identb[:], cbs[nt].ap[:].unsqueeze(1).to_broadcast([128, 4, 128]), [identb, cbs[nt]])],
                                      vc[:, nt, h, :], [vc]))
                    acc = accO.get()
                    attn_pass(h, qz, units, acc, extras=[(accI, lambda ui: cmpsel[:, ui, :], 0, nt_hi, [cmpsel])])
                    rz = finish_pass(h, 0, acc, gate, oattn, first=True, clampz=True)
                    IT = OTr.get()
                    op("act", lambda e: e.copy(out=IT[:], in_=accI[:, :]), reads=[accI], writes=[IT])
                    tpi = ring.get()
                    tpiv = tpi.ap[:].rearrange("p (g c) -> p g c", c=128)
                    for g in range(4):
                        op("pe", lambda e: e.transpose(out=tpiv[:, g, :], in_=IT[:, g * 128:(g + 1) * 128], identity=identf[:]),
                           reads=[IT, identf], writes=[tpi])
                    imp = impr.get()
                    op("dve", lambda e: e.tensor_scalar(out=imp[:], in0=tpiv[:, 0, :], scalar1=rz[:, 0:1], scalar2=None,
                                                        op0=ALU.mult), reads=[tpi, rz], writes=[imp])
                    for g in range(1, 4):
                        op("dve", lambda e: e.scalar_tensor_tensor(out=imp[:], in0=tpiv[:, g, :], scalar=rz[:, g:g + 1],
                                                                   in1=imp[:], op0=ALU.mult, op1=ALU.add),
                           reads=[tpi, rz, imp], writes=[imp])
                    op("dve", lambda e: e.tensor_tensor(out=imp[:], in0=imp[:], in1=ta[:], op=ALU.mult), reads=[imp, ta], writes=[imp])
                    op("dve", lambda e: e.tensor_tensor(out=imp[:], in0=imp[:], in1=tb_[:], op=ALU.add), reads=[imp, tb_], writes=[imp])
                    m1, m2 = smr.get(), smr.get()
                    imp2 = impr.get()
                    op("dve", lambda e: e.max(out=m1[:], in_=imp[:]), reads=[imp], writes=[m1])
                    op("dve", lambda e: e.match_replace(out=imp2[:], in_to_replace=m1[:], in_values=imp[:], imm_value=-1e9),
                       reads=[m1, imp], writes=[imp2])
                    op("dve", lambda e: e.max(out=m2[:], in_=imp2[:]), reads=[imp2], writes=[m2])
                    op("dve", lambda e: e.scalar_tensor_tensor(out=imp2[:], in0=imp[:], scalar=m2[:, 7:8], in1=ta[:],
                                                               op0=ALU.is_ge, op1=ALU.mult), reads=[imp, m2, ta], writes=[imp2])
                    selmb = selmb_r.get()
                    op("dve", lambda e: e.tensor_scalar(out=selmb[:], in0=imp2[:], scalar1=-1.0, scalar2=None, op0=ALU.add),
                       reads=[imp2], writes=[selmb])
                    selmbs.append(selmb)
                wbs = {}
                for r in range(5):
                    kt = bg - 4 + r
                    if kt < 0:
                        continue
                    cb = cbr.get()
                    dma(cb[:], winb_d[i, r], writes=[cb])
                    wbs[kt] = cb
                kt0 = max(0, bg - 4)
                nwt = bg - kt0 + 1
                wkt, wvt = wkr.get(), wvr.get()
                dma(wkt[:, 0:nwt, :], sc_wk[kt0:bg + 1].rearrange("n p c -> p n c"), writes=[wkt])
                dma(wvt[:, 0:nwt, :], sc_wv[kt0:bg + 1].rearrange("n p c -> p n c"), writes=[wvt])
                for h in range(2):
                    units = []
                    for kt in sorted(wbs):
                        ws = kt - kt0
                        units.append((wkt[:, ws, :], [wkt],
                                      [(identb[:], wbs[kt].ap[:].unsqueeze(1).to_broadcast([128, 4, 128]), [identb, wbs[kt]])],
                                      wvt[:, ws, h * 65:(h + 1) * 65], [wvt]))
                    acc = accO.get()
                    attn_pass(h, qz, units, acc)
                    finish_pass(h, 2, acc, gate, oattn, first=False)
                for h in range(2):
                    pst = ring.get()
                    pstv = bfview(pst)[:, 0:128]
                    op("pe", lambda e: e.transpose(out=pstv, in_=selmbs[h][:], identity=identb[:]), reads=[selmbs[h], identb], writes=[pst])
                    op("act", lambda e: e.copy(out=selmT[:, h, :], in_=pstv), reads=[pst], writes=[selmT])
                for h in range(2):
                    units = []
                    for kt in range(bg + 1):
                        biases = [(Esb[:, kt * 128:(kt + 1) * 128], selmT.ap[:, h:h + 1, :].to_broadcast([128, 4, 128]),
                                   [Esb, selmT])]
                        if kt == bg:
                            biases.append((identb[:], caus.ap[:].unsqueeze(1).to_broadcast([128, 4, 128]), [identb, caus]))
                        units.append((selKT[:, kt * 128:(kt + 1) * 128], [selKT], biases, selV[:, kt, h, :], [selV]))
                    acc = accO.get()
                    attn_pass(h, qz, units, acc)
                    finish_pass(h, 1, acc, gate, oattn, first=False)

            def out_chain(i, samp, oattn, ysT):
                sm = samp is not None
                za = bigb.get()
                dma(za[:], sc_za_s[samp] if sm else sc_za[i], writes=[za])
                sza = big.get()
                op("act", lambda e: e.activation(out=sza[:], in_=za[:], func=AF.Silu), reads=[za], writes=[sza])
                ozb = bigb.get()
                op("dve", lambda e: e.tensor_tensor(out=ozb[:], in0=oattn[:], in1=sza[:], op=ALU.mult), reads=[oattn, sza], writes=[ozb])
                gmb = gmr2.get()
                dma(gmb[:], sc_gm_s[samp] if sm else sc_gm[i], writes=[gmb])
                op("act", lambda e: e.activation(out=sgm[:], in_=gmb[:], func=AF.Sigmoid), reads=[gmb], writes=[sgm])
                yield
                for _ in range(4):
                    yield
                pzt = ring.get()
                pztv = bfview(pzt)[:, 0:512].rearrange("p (a b) -> p a b", b=128)
                for k4 in range(4):
                    op("pe", lambda e: e.transpose(out=pztv[:, k4, :], in_=ozb[:, k4 * 128:(k4 + 1) * 128], identity=identb[:]),
                       reads=[ozb, identb], writes=[pzt])
                ozT = bigb.get()
                op("act", lambda e: e.copy(out=ozT[:].rearrange("p (a b) -> p a b", b=128), in_=pztv), reads=[pzt], writes=[ozT])
                ozT3 = ozT.ap[:].rearrange("p (a b) -> p a b", b=128)
                ysT3 = ysT.ap[:].rearrange("p (a b) -> p a b", b=128)
                yield
                for cg in range(2):
                    yield
                    pa = ring.get()
                    for k4 in range(4):
                        op("pe", lambda e: e.matmul(pa[:, :], lhsT=ozT3[:, k4, :], rhs=wla[:, k4, cg * 512:(cg + 1) * 512],
                                                    start=(k4 == 0), stop=(k4 == 3)), reads=[ozT, wla], writes=[pa])
                    pb_ = ring.get()
                    for k4 in range(4):
                        op("pe", lambda e: e.matmul(pb_[:, :], lhsT=ysT3[:, k4, :], rhs=wls[:, k4, cg * 512:(cg + 1) * 512],
                                                    start=(k4 == 0), stop=(k4 == 3)), reads=[ysT, wls], writes=[pb_])
                    m1_, m2_ = big.get(), big.get()
                    op("dve", lambda e: e.tensor_tensor(out=m1_[:], in0=pa[:, :], in1=sgm[:, cg * 512:(cg + 1) * 512], op=ALU.mult),
                       reads=[pa, sgm], writes=[m1_])
                    op("dve", lambda e: e.tensor_tensor(out=m2_[:], in0=pb_[:, :], in1=sgm[:, 1024 + cg * 512:1024 + (cg + 1) * 512],
                                                        op=ALU.mult), reads=[pb_, sgm], writes=[m2_])
                    op("pool", lambda e: e.tensor_tensor(out=mbt[:, cg * 512:(cg + 1) * 512], in0=m1_[:], in1=m2_[:], op=ALU.add),
                       reads=[m1_, m2_], writes=[mbt])
                for _ in range(3):
                    yield
                pmt = ring.get()
                pmtv = bfview(pmt).rearrange("p (a b) -> p a b", b=128)
                for k8 in range(8):
                    op("pe", lambda e: e.transpose(out=pmtv[:, k8, :], in_=mbt[:, k8 * 128:(k8 + 1) * 128], identity=identb[:]),
                       reads=[mbt, identb], writes=[pmt])
                op("act", lambda e: e.copy(out=mT[:], in_=pmtv), reads=[pmt], writes=[mT])
                xt = xr2.get()
                dma(xt[:], xs_d[samp] if sm else xo[i * 128:(i + 1) * 128, :], writes=[xt])
                res = xt
                yield
                for cg in range(2):
                    yield
                    py = ring.get()
                    for k8 in range(8):
                        op("pe", lambda e: e.matmul(py[:, :], lhsT=mT[:, k8, :], rhs=wo[:, k8, cg * 512:(cg + 1) * 512],
                                                    start=(k8 == 0), stop=(k8 == 7)), reads=[mT, wo], writes=[py])
                    op("dve", lambda e: e.tensor_tensor(out=res[:, cg * 512:(cg + 1) * 512], in0=py[:, :],
                                                        in1=xt[:, cg * 512:(cg + 1) * 512], op=ALU.add), reads=[py, xt], writes=[xt])
                yield
                ss = smr.get()
                op("act", lambda e: e.activation(out=mbt[:], in_=res[:], func=AF.Square, accum_out=ss[:, 0:1]),
                   reads=[res], writes=[mbt, ss])
                op("act", lambda e: e.activation(out=ss[:, 0:1], in_=ss[:, 0:1], func=AF.Ln, scale=1.0 / D, bias=1e-6),
                   reads=[ss], writes=[ss])
                op("act", lambda e: e.activation(out=ss[:, 0:1], in_=ss[:, 0:1], func=AF.Exp, scale=-0.5), reads=[ss], writes=[ss])
                op("dve", lambda e: e.scalar_tensor_tensor(out=res[:], in0=res[:], scalar=ss[:, 0:1], in1=fg[:],
                                                           op0=ALU.mult, op1=ALU.mult), reads=[res, ss, fg], writes=[res])
                if sm:
                    dma(ys_o[samp], res[0:8, :], reads=[res])
                else:
                    dma(y_o[i * 128:(i + 1) * 128, :], res[:], reads=[res])

            if NSB > 0:
                class _View:
                    def __init__(self, ap):
                        self.ap = ap

                    def __getitem__(self, idx):
                        return self.ap[idx]
                kcs_p = vcTs_p = sgm
                kcs = _View(sgm.ap[:, 1024:2048])
                vcTs = _View(sgm.ap[:, 0:1024])
                cmpsel_p = gmr2.bufs[0]
                cmpsel_s = _View(cmpsel_p.ap[:].rearrange("p (a b) -> p a b", b=256))
                vcs = fw.sbuf("vcs", [128, 8, 2, 65], BF16)
                cb0 = fw.sbuf("cb0_t", [128, 128], BF16)
                swb0 = fw.sbuf("swb0_t", [128, 128], BF16)
                ptb2 = fw.sbuf("ptb2", [128, 128], I32)
                ptf2 = fw.sbuf("ptf2", [128, 128], F32)
                io3 = fw.sbuf("io3", [128, 1], F32)
                idxs = fw.sbuf("idxs", [128, 128], I32)
                pgr2 = Ring([fw.sbuf(f"pgs{i}", [128, 256], F32) for i in range(2)])
                kbr = Ring([fw.sbuf(f"kbs{i}", [128, 128], BF16) for i in range(2)])
                KTr = Ring([fw.sbuf(f"KTs{i}", [128, 128], BF16) for i in range(3)])
                Vtr = Ring([fw.sbuf(f"Vts{i}", [128, 2, 65], BF16) for i in range(3)])
                selmT_s = fw.sbuf("selmT_s", [128, 4, 128], BF16)
                selmb_s = fw.sbuf("selmb_s", [128, 256], BF16)
                nkt = fw.sbuf("nkt", [128, 256], BF16)
                nvt = fw.sbuf("nvt", [128, 260], BF16)
                stt = fw.sbuf("stt", [128, 2, 16], F32)
                op("pool", lambda e: e.memset(vcs[:], 1.0), writes=[vcs])
                for vt_ in Vtr.bufs:
                    op("pool", lambda e: e.memset(vt_[:], 1.0), writes=[vt_])
                op("pool", lambda e: e.iota(io3[:], pattern=[[0, 1]], base=1, channel_multiplier=2,
                                            allow_small_or_imprecise_dtypes=True), writes=[io3])
                dma(cb0[:], cb0_d, writes=[cb0])
                dma(swb0[:], swb0_d, writes=[swb0])

            def stream_pass(ntiles, prep, biasfn, qz, gate, oattn, br):
                accs = [ps_bufs[0], ps_bufs[1]]

                def scores(kt):
                    KT_ap, kreads, Vfn = prep(kt)
                    return [emit_scores(KT_ap, kreads, biasfn(kt, h), qz, h) for h in range(2)], Vfn

                nxt = scores(0)
                for kt in range(ntiles):
                    Ss, Vfn = nxt
                    if kt + 1 < ntiles:
                        nxt = scores(kt + 1)
                    for h in range(2):
                        P = Pr.get()
                        op("act", lambda e: e.activation(out=P[:], in_=Ss[h][:, :], func=AF.Exp), reads=[Ss[h]], writes=[P])
                        V_ap, vreads = Vfn(h)
                        op("pe", lambda e: e.matmul(accs[h][0:65, :], lhsT=V_ap, rhs=P[:], start=(kt == 0), stop=(kt == ntiles - 1)),
                           reads=vreads + [P], writes=[accs[h]])
                for h in range(2):
                    finish_pass(h, br, accs[h], gate, oattn, first=False)

            def sample_attention(b, qz, gate, oattn):
                dma(kcs[:], sc_kc[b], writes=[sgm])
                dma(vcTs[:], sc_vcT[b], writes=[sgm])
                dma(cmpsel_s[:], cmpsel_s_d, writes=[cmpsel_p])
                dma(nkt[:], sc_nk[b], writes=[nkt])
                dma(nvt[:], sc_nv[b], writes=[nvt])
                for nt in range(8):
                    pv = ring.get()
                    pvv = bfview(pv)[:, 0:128]
                    op("pe", lambda e: e.transpose(out=pvv, in_=vcTs[:, nt * 128:(nt + 1) * 128], identity=identb[:]),
                       reads=[sgm, identb], writes=[pv])
                    op("dve", lambda e: e.tensor_copy(out=vcs[:, nt, :, 0:64], in_=pvv.rearrange("p (h d) -> p h d", d=64)),
                       reads=[pv], writes=[vcs])
                for h in range(2):
                    units = []
                    for nt in range(8):
                        bl = [(identb[:], cb0.ap[:].unsqueeze(1).to_broadcast([128, 4, 128]), [identb, cb0])] if nt == 0 else []
                        units.append((kcs[:, nt * 128:(nt + 1) * 128], [sgm], bl, vcs[:, nt, h, :], [vcs]))
                    acc = accO.get()
                    attn_pass(h, qz, units, acc, extras=[(accI, lambda ui: cmpsel_s[:, ui, 0:128], 0, 3, [cmpsel_p]),
                                                         (ypsb, lambda ui: cmpsel_s[:, ui, 128:256], 3, 7, [cmpsel_p])])
                    rz = finish_pass(h, 0, acc, gate, oattn, first=True, clampz=True)
                    imp_p, imp2_p = big.bufs[0], big.bufs[1]
                    imp, imp2 = _View(imp_p.ap[:, 0:256]), _View(imp2_p.ap[:, 0:256])
                    for t_, accb in enumerate((accI, ypsb)):
                        IT = OTr.get()
                        op("act", lambda e: e.copy(out=IT[:], in_=accb[:, :]), reads=[accb], writes=[IT])
                        tpi = ring.get()
                        tpiv = tpi.ap[:].rearrange("p (g c) -> p g c", c=128)
                        for g in range(4):
                            op("pe", lambda e: e.transpose(out=tpiv[:, g, :], in_=IT[:, g * 128:(g + 1) * 128], identity=identf[:]),
                               reads=[IT, identf], writes=[tpi])
                        dsti = imp[:, t_ * 128:(t_ + 1) * 128]
                        op("dve", lambda e: e.tensor_scalar(out=dsti, in0=tpiv[:, 0, :], scalar1=rz[:, 0:1], scalar2=None,
                                                            op0=ALU.mult), reads=[tpi, rz], writes=[imp_p])
                        for g in range(1, 4):
                            op("dve", lambda e: e.scalar_tensor_tensor(out=dsti, in0=tpiv[:, g, :], scalar=rz[:, g:g + 1],
                                                                       in1=dsti, op0=ALU.mult, op1=ALU.add),
                               reads=[tpi, rz, imp_p], writes=[imp_p])
                    op("dve", lambda e: e.memset(imp[:, 0:1], 100.0), writes=[imp_p])
                    m1, m2 = smr.get(), smr.get()
                    op("dve", lambda e: e.max(out=m1[:], in_=imp[:]), reads=[imp_p], writes=[m1])
                    op("dve", lambda e: e.match_replace(out=imp2[:], in_to_replace=m1[:], in_values=imp[:], imm_value=-1e9),
                       reads=[m1, imp_p], writes=[imp2_p])
                    op("dve", lambda e: e.max(out=m2[:], in_=imp2[:]), reads=[imp2_p], writes=[m2])
                    op("dve", lambda e: e.tensor_scalar(out=selmb_s[:], in0=imp[:], scalar1=m2[:, 6:7], scalar2=-1.0,
                                                        op0=ALU.is_ge, op1=ALU.add), reads=[imp_p, m2], writes=[selmb_s])
                    for t_ in range(2):
                        pst = ring.get()
                        pstv = bfview(pst)[:, 0:128]
                        op("pe", lambda e: e.transpose(out=pstv, in_=selmb_s[:, t_ * 128:(t_ + 1) * 128], identity=identb[:]),
                           reads=[selmb_s, identb], writes=[pst])
                        op("act", lambda e: e.copy(out=selmT_s[:, t_ * 2 + h, :], in_=pstv), reads=[pst], writes=[selmT_s])
                dma(ptb2[:], pt_d[b], writes=[ptb2])
                op("dve", lambda e: e.tensor_copy(out=ptf2[:], in_=ptb2[:]), reads=[ptb2], writes=[ptf2])
                op("dve", lambda e: e.tensor_scalar(out=ptf2[:], in0=ptf2[:], scalar1=256.0, scalar2=io3[:, 0:1],
                                                    op0=ALU.mult, op1=ALU.add), reads=[ptf2, io3], writes=[ptf2])
                op("dve", lambda e: e.tensor_copy(out=idxs[:], in_=ptf2[:]), reads=[ptf2], writes=[idxs])

                ringT = Ring([accI, ypsb])

                def tile_from_rows(pg, k):
                    kb = kbr.get()
                    Vt = Vtr.get()
                    if k % 2 == 0:
                        op("dve", lambda e: e.tensor_copy(out=kb[:], in_=pg[:, 0:128]), reads=[pg], writes=[kb])
                        op("act", lambda e: e.copy(out=Vt[:, :, 0:64], in_=pg[:, 128:256].rearrange("p (h d) -> p h d", d=64)),
                           reads=[pg], writes=[Vt])
                    else:
                        op("act", lambda e: e.copy(out=kb[:], in_=pg[:, 0:128]), reads=[pg], writes=[kb])
                        op("dve", lambda e: e.tensor_copy(out=Vt[:, :, 0:64], in_=pg[:, 128:256].rearrange("p (h d) -> p h d", d=64)),
                           reads=[pg], writes=[Vt])
                    pt2 = ringT.get()
                    p2v = bfview(pt2)[:, 0:128]
                    op("pe", lambda e: e.transpose(out=p2v, in_=kb[:], identity=identb[:]), reads=[kb, identb], writes=[pt2])
                    KT = KTr.get()
                    if k % 2 == 0:
                        op("act", lambda e: e.copy(out=KT[:], in_=p2v), reads=[pt2], writes=[KT])
                    else:
                        op("dve", lambda e: e.tensor_copy(out=KT[:], in_=p2v), reads=[pt2], writes=[KT])
                    return KT, Vt

                def prep_sel(kt):
                    if kt < 128:
                        pg = pgr2.get()
                        fw.idma(pg[:], cache_d, idxs[:, kt:kt + 1], reads=[idxs], writes=[pg])
                        KT, Vt = tile_from_rows(pg, kt)
                        return KT[:], [KT], (lambda h: (Vt[:, h, :], [Vt]))
                    return nkt[:, 0:128], [nkt], (lambda h: (nvt[:, h * 65:(h + 1) * 65], [nvt]))

                def bias_sel(kt, h):
                    if kt < 128:
                        t_ = kt // 64
                        return [(Esb[:, (kt % 64) * 128:(kt % 64 + 1) * 128],
                                 selmT_s.ap[:, t_ * 2 + h:t_ * 2 + h + 1, :].to_broadcast([128, 4, 128]), [Esb, selmT_s])]
                    return [(identb[:], caus.ap[:].unsqueeze(1).to_broadcast([128, 4, 128]), [identb, caus])]

                stream_pass(129, prep_sel, bias_sel, qz, gate, oattn, 1)

                def prep_win(r):
                    if r < 4:
                        pg = pgr2.get()
                        dma(pg[:], cwin_d[b, r * 128:(r + 1) * 128, :], writes=[pg])
                        KT, Vt = tile_from_rows(pg, r)
                        return KT[:], [KT], (lambda h: (Vt[:, h, :], [Vt]))
                    return nkt[:, 128:256], [nkt], (lambda h: (nvt[:, 130 + h * 65:130 + (h + 1) * 65], [nvt]))

                def bias_win(r, h):
                    if r == 0:
                        return [(identb[:], swb0.ap[:].unsqueeze(1).to_broadcast([128, 4, 128]), [identb, swb0])]
                    if r == 4:
                        return [(identb[:], caus.ap[:].unsqueeze(1).to_broadcast([128, 4, 128]), [identb, caus])]
                    return []

                stream_pass(5, prep_win, bias_win, qz, gate, oattn, 2)

            uT_cur = s5_block(0)
            post = None
            for bg in range(NBT):
                nxt_out = {}
                gen_ = s5_gen(bg + 1, None, nxt_out) if bg + 1 < NBT else None
                if bg >= NBP:
                    if gen_ is not None:
                        next(gen_)
                    TICK["gens"] = [g for g in (gen_,) if g is not None]
                    TICK["post"] = post
                    TICK["n"] = 0
                    newpost = own_block(bg - NBP, uT_cur)
                    if os.environ.get('DEFER_OUT', '1') == '0' and newpost is not None:
                        for _ in newpost:
                            pass
                        newpost = None
                    TICK["gens"] = []
                    if post is not None:
                        for _ in post:
                            pass
                    post = newpost
                if gen_ is not None:
                    for _ in gen_:
                        pass
                    uT_cur = nxt_out["uT"]
            if post is not None:
                for _ in post:
                    pass
            sso = fw.sbuf("sso", [128, 2, 16], F32)
            op("dve", lambda e: e.tensor_copy(out=sso[:, 0, :], in_=hre_p[:]), reads=[hre_p], writes=[sso])
            op("dve", lambda e: e.tensor_copy(out=sso[:, 1, :], in_=him_p[:]), reads=[him_p], writes=[sso])
            dma(ssm_o, sso[:], reads=[sso])
            for b in range(NSB):
                dma(stt[:], sst_d[b], writes=[stt])
                op("dve", lambda e: e.tensor_copy(out=hre_p[:], in_=stt[:, 0, :]), reads=[stt], writes=[hre_p])
                op("dve", lambda e: e.tensor_copy(out=him_p[:], in_=stt[:, 1, :]), reads=[stt], writes=[him_p])
                uT = s5_block(None, samp=b)
                op("dve", lambda e: e.tensor_copy(out=sso[:, 0, :], in_=hre_p[:]), reads=[hre_p], writes=[sso])
                op("dve", lambda e: e.tensor_copy(out=sso[:, 1, :], in_=him_p[:]), reads=[him_p], writes=[sso])
                dma(ssms_o[b], sso[:], reads=[sso])
                own_block(b, uT, samp=b)
            fw.barrier()
        fw.stack = top
        pstore.close()
        fw.finish()
    return nc, fw


def _bf(a):
    return np.ascontiguousarray(a).astype(ml_dtypes.bfloat16)


def host_consts(NBP, NBO, half, NSB=4):
    NBT = NBP + NBO
    c = {}
    n = np.arange(512) - 1
    blk = np.arange(128)
    c0 = n[:, None] * 16
    s0 = blk[None, :] * 64
    shared = np.minimum(c0 + 32, s0 + 64) - np.maximum(c0, s0)
    cs = np.clip(shared, 0, None).astype(np.float32) / 32.0
    cs[0, :] = 0.0
    c["cmpsel"] = _bf(cs.reshape(4, 128, 128).transpose(1, 0, 2))
    first_real_tok = 0 if half == 1 else NBP * 128
    q = np.arange(128)
    cmpb = np.zeros((NBO, 4, 128, 128), np.float32)
    for i in range(NBO):
        qpos = (NBP + i) * 128 + q
        for nt in range(4):
            nn = nt * 128 + np.arange(128) - 1
            vis = (nn[:, None] * 16 + 31 <= qpos[None, :]) & (nn[:, None] >= 0) & (nn[:, None] * 16 >= first_real_tok)
            cmpb[i, nt] = np.where(vis, 0.0, NEG)
    c["cmpb"] = _bf(cmpb)
    winb = np.zeros((NBO, 5, 128, 128), np.float32)
    for i in range(NBO):
        bg = NBP + i
        qpos = bg * 128 + q
        for r in range(5):
            kpos = (bg - 4 + r) * 128 + np.arange(128)
            dist = qpos[None, :] - kpos[:, None]
            vis = (kpos[:, None] >= first_real_tok) & (dist >= 0) & (dist < 512)
            winb[i, r] = np.where(vis, 0.0, NEG)
    c["winb"] = _bf(winb)
    tka = np.zeros((NBO, 128, 128), np.float32)
    tkb = np.zeros((NBO, 128, 128), np.float32)
    fb = first_real_tok // 64
    for i in range(NBO):
        qpos = (NBP + i) * 128 + q
        valid = (blk[None, :] * 64 <= qpos[:, None]) & (blk[None, :] >= fb)
        forced = (blk[None, :] == fb) | (blk[None, :] == (qpos // 64)[:, None])
        tka[i] = valid.astype(np.float32)
        tkb[i] = np.where(valid, np.where(forced, 100.0, 0.0), -100.0)
    c["tka"], c["tkb"] = tka, tkb
    key = np.arange(128)
    c["caus"] = _bf(np.where(key[:, None] <= q[None, :], 0.0, NEG))
    EW = max(NBT, 64 if NSB else 0) * 128
    E = np.zeros((128, EW), np.float32)
    kk = np.arange(EW)
    E[kk // 64, kk] = -NEG
    c["E"] = _bf(E)
    mgh = np.zeros((128, 2), np.float32)
    mgh[:64, 0] = 1.0
    mgh[64:, 1] = 1.0
    c["mgh"] = mgh
    n = np.arange(1024) - 1
    blk = np.arange(256)
    c0 = n[:, None] * 16
    s0 = blk[None, :] * 64
    shared = np.minimum(c0 + 32, s0 + 64) - np.maximum(c0, s0)
    cs = np.clip(shared, 0, None).astype(np.float32) / 32.0
    cs[0, :] = 0.0
    c["cmpsel_s"] = _bf(cs.reshape(8, 128, 256).transpose(1, 0, 2))
    cb0 = np.zeros((128, 128), np.float32)
    cb0[0, :] = NEG
    c["cb0"] = _bf(cb0)
    c["swb0"] = _bf(np.where(key[:, None] > q[None, :], 0.0, NEG))
    return c


def host_weights(inp):
    w = {}
    w_in = inp["w_in"][0]
    wf = w_in[:, wf_cols()]
    wt = w_in[:, wt_cols()]
    w["wf"] = np.ascontiguousarray(wf.reshape(8, 128, NWF).transpose(1, 0, 2))
    w["wt"] = np.ascontiguousarray(wt.reshape(8, 128, NWT).transpose(1, 0, 2))
    w["ng"] = np.ascontiguousarray(inp["norm_g"][0].reshape(8, 128).T)
    w["fg"] = np.ascontiguousarray(np.broadcast_to(inp["final_g"][None, :], (128, D)))
    w1 = inp["cmp_w1"][0]
    w1l = w1.transpose(2, 0, 1, 3)
    w["w1"] = np.ascontiguousarray(np.concatenate([w1l, w1l], 0))
    w2 = inp["cmp_w2"][0].transpose(1, 0, 2)
    w["w2"] = np.ascontiguousarray(np.concatenate([w2, w2], 0))
    pe = inp["cmp_pe"][0].transpose(2, 0, 1)
    w["pe"] = np.ascontiguousarray(np.concatenate([pe, pe], 0))
    b1 = inp["cmp_b1"][0].T
    w["b1"] = np.ascontiguousarray(np.concatenate([b1, b1], 0))

    def st_layout(a):
        a = a.reshape((16, 2, 64) + a.shape[2:])
        a = np.moveaxis(a, 0, 2)
        return np.ascontiguousarray(a.reshape((128, 16) + a.shape[3:]))
    w["lamre"] = st_layout(inp["ssm_lam_re"][0])
    w["lamim"] = st_layout(inp["ssm_lam_im"][0])
    w["logdt"] = st_layout(np.broadcast_to(inp["ssm_log_dt"][0][:, None], (32, 64)))
    w["bre"] = st_layout(inp["ssm_b_re"][0])
    w["bim"] = st_layout(inp["ssm_b_im"][0])
    for nm, key in (("cre", "ssm_c_re"), ("cim", "ssm_c_im")):
        cc = inp[key][0].transpose(0, 2, 1)
        cl = st_layout(cc)
        bd = np.zeros((128, 16, 4, 2, 16), np.float32)
        for st in range(16):
            bd[:64, st, st % 4, 0, :] = cl[:64, st]
            bd[64:, st, st % 4, 1, :] = cl[64:, st]
        w[nm] = bd.reshape(128, 16, 128)
    w["dsk"] = np.ascontiguousarray(inp["ssm_d"][0].reshape(4, 128).T)
    w["bgl"] = np.ascontiguousarray(inp["b_glu"][0].reshape(4, 128).T)
    w["wglu"] = np.ascontiguousarray(inp["w_glu"][0].reshape(4, 128, 512).transpose(1, 0, 2))
    w["wla"] = np.ascontiguousarray(inp["w_lift_attn"][0].reshape(4, 128, D).transpose(1, 0, 2))
    w["wls"] = np.ascontiguousarray(inp["w_lift_ssm"][0].reshape(4, 128, D).transpose(1, 0, 2))
    w["wo"] = np.ascontiguousarray(inp["w_out"][0].reshape(8, 128, D).transpose(1, 0, 2))
    return w


def make_in_maps(inp, NBP, NBO, NSB=4):
    w = host_weights(inp)
    consts = [host_consts(NBP, NBO, 0, NSB), host_consts(NBP, NBO, 1, NSB)]
    NS1 = max(NSB, 1)
    cache = (np.ascontiguousarray(inp["cache_kv"][0]).reshape(-1, 256) if NSB > 0
             else np.zeros((256, 256), np.float32))

    def st_layout(a):
        a = a.reshape((16, 2, 64))
        a = np.moveaxis(a, 0, 2)
        return a.reshape(128, 16)
    maps = []
    TP, TO = NBP * 128, NBO * 128
    for c in range(8):
        s, half = c // 2, c % 2
        m = dict(w)
        m.update(consts[half])
        xs = inp["x_prompt"][s]
        m["xo"] = np.ascontiguousarray(xs[half * TP:half * TP + TO])
        m["xp"] = np.ascontiguousarray(xs[0:TP]) if half == 1 else np.zeros((TP, D), np.float32)
        bs = [min(NS1 * c + j, 31) for j in range(NS1)]
        xsp = np.zeros((NS1, 128, D), np.float32)
        xsp[:, 0:8, :] = inp["x_sample"][bs]
        m["xs"] = xsp
        m["cache"] = cache
        m["ptab"] = np.ascontiguousarray(np.broadcast_to(inp["page_table"][bs][:, None, :], (NS1, 128, 128))).astype(np.int32)
        m["cwin"] = np.ascontiguousarray(inp["cache_win_kv"][0][bs]).reshape(NS1, 512, 256)
        sst = np.zeros((NS1, 128, 2, 16), np.float32)
        for j, b in enumerate(bs):
            sst[j, :, 0, :] = st_layout(inp["state_ssm_re"][0, b])
            sst[j, :, 1, :] = st_layout(inp["state_ssm_im"][0, b])
        m["sst"] = sst
        maps.append(m)
    return maps


def _unstate(a):
    return a.reshape(2, 64, 16).transpose(2, 0, 1).reshape(32, 64)


_CACHE = {}


def kernel(**inp):
    inp = {k: np.asarray(v) for k, v in inp.items()}
    NB, NSB = 32, 4
    if "nc" not in _CACHE:
        _CACHE["nc"] = build(NB, NB, NSB, inp["cache_kv"].shape[1])[0]
    nc = _CACHE["nc"]
    maps = make_in_maps(inp, NB, NB, NSB)
    res = run_bass_kernel_spmd(nc, maps, core_ids=list(range(8))).results
    T = NB * 128
    y_p = np.zeros((4, 8192, D), np.float32)
    kv_p = np.zeros((1, 4, 8192, 4, 2, 64), np.float32)
    win_p = np.zeros((1, 4, 512, 2, 2, 64), np.float32)
    sre_p = np.zeros((1, 4, 32, 64), np.float32)
    sim_p = np.zeros((1, 4, 32, 64), np.float32)
    y_s = np.zeros((32, 8, D), np.float32)
    kv_s = np.zeros((1, 32, 8, 4, 2, 64), np.float32)
    win_s = np.zeros((1, 32, 512, 2, 2, 64), np.float32)
    sre_s = np.zeros((1, 32, 32, 64), np.float32)
    sim_s = np.zeros((1, 32, 32, 64), np.float32)
    for c in range(8):
        s_, half = c // 2, c % 2
        r = res[c]
        y_p[s_, half * T:(half + 1) * T] = r["y"]
        kv_p[0, s_, half * T:(half + 1) * T] = r["kvp"].reshape(T, 4, 2, 64)
        if half == 1:
            win_p[0, s_] = r["winp"].reshape(512, 2, 2, 64)
            sre_p[0, s_] = _unstate(r["ssmp"][:, 0, :])
            sim_p[0, s_] = _unstate(r["ssmp"][:, 1, :])
        for j in range(NSB):
            b = NSB * c + j
            y_s[b] = r["ys"][j]
            kv_s[0, b] = r["kvs"][j].reshape(8, 4, 2, 64)
            win_s[0, b] = r["wins"][j].reshape(512, 2, 2, 64)
            sre_s[0, b] = _unstate(r["ssms"][j, :, 0, :])
            sim_s[0, b] = _unstate(r["ssms"][j, :, 1, :])
    return (y_p, y_s, kv_p, win_p, sre_p, sim_p, kv_s, win_s, sre_s, sim_s)
```
